# Optimizing a Trainium2 kernel written in Bass

```python
import jax
import jax.numpy as jnp
from jax import lax
import numpy as np

D_MODEL = 1024
BATCH = 4
SEQ = 4096
DEPTH = 4
DEC_BATCH = 32
DEC_SEQ = 1
PAST_LEN = 8192
PAGE_SIZE = 128

N_A_LAYERS = DEPTH // 2
N_B_LAYERS = DEPTH - N_A_LAYERS
D_PLE = 256
EPS = 1e-6
D_RNN = (5 * D_MODEL) // 4
N_RG_BLOCKS = 16
RG_BLOCK = D_RNN // N_RG_BLOCKS
RG_CONV_W = 4
RG_C = 8.0
D_FF = 3 * D_MODEL
FFN_CONV_W = 3
HEAD_DIM = 64
N_HEADS = D_MODEL // HEAD_DIM
N_KV_HEADS = 2
HPG = N_HEADS // N_KV_HEADS
L_CMP = 32
D_CMP = 16
CMP_HIDDEN = 2 * HEAD_DIM
L_SLC = 64
N_SEL = 16
WINDOW = 512
Q_BLOCK = 128
ROPE_THETA = 10000.0
NEG = -1e30

kernel_name = 'hawk_nsa_yoco_decoder_step'


def rms_norm(x, g):
    xf = x.astype(jnp.float32)
    y = xf * lax.rsqrt(jnp.mean(xf * xf, axis=-1, keepdims=True) + EPS)
    return (y * g.astype(jnp.float32)).astype(x.dtype)


def rope(x, pos):
    half = HEAD_DIM // 2
    inv_freq = ROPE_THETA ** (-jnp.arange(half, dtype=jnp.float32) / half)
    ang = pos.astype(jnp.float32)[:, None] * inv_freq[None, :]
    cos = jnp.cos(ang)[None, :, None, :].astype(x.dtype)
    sin = jnp.sin(ang)[None, :, None, :].astype(x.dtype)
    x1, x2 = x[..., :half], x[..., half:]
    return jnp.concatenate([x1 * cos - x2 * sin, x2 * cos + x1 * sin], axis=-1)


def causal_dwconv(x_hist, w, b):
    width = w.shape[0]
    t = x_hist.shape[1] - width + 1
    out = b + x_hist[:, 0:t] * w[0]
    for k in range(1, width):
        out = out + x_hist[:, k:k + t] * w[k]
    return out


def masked_softmax(s, mask):
    s = jnp.where(mask, s.astype(jnp.float32), NEG)
    return jnp.where(mask, jax.nn.softmax(s, axis=-1), 0.0)


def linear_combine(left, right):
    a1, b1 = left
    a2, b2 = right
    return a1 * a2, a2 * b1 + b2


def rg_lru_block(h, conv_hist, h0, pos, w_in, conv_w, conv_b, w_a, b_a, w_x, b_x, lam, w_out):
    b, t, _ = h.shape
    proj = h @ w_in
    y = jax.nn.gelu(proj[..., :D_RNN])
    xr = proj[..., D_RNN:]
    xh = jnp.concatenate([conv_hist.astype(xr.dtype), xr], axis=1)
    new_hist = xh[:, xh.shape[1] - (RG_CONV_W - 1):]
    xc = causal_dwconv(xh, conv_w, conv_b)
    xb = xc.reshape(b, t, N_RG_BLOCKS, RG_BLOCK)
    r = jax.nn.sigmoid(jnp.einsum('btnk,nkj->btnj', xb, w_a) + b_a).reshape(b, t, D_RNN)
    i = jax.nn.sigmoid(jnp.einsum('btnk,nkj->btnj', xb, w_x) + b_x).reshape(b, t, D_RNN)
    log_a = -RG_C * r.astype(jnp.float32) * jax.nn.softplus(-lam.astype(jnp.float32))
    a = jnp.exp(log_a)
    mult = jnp.where((pos == 0)[None, :, None], 1.0, jnp.sqrt(-jnp.expm1(2.0 * log_a)))
    u = mult * (i * xc).astype(jnp.float32)
    u = u.at[:, 0].add(a[:, 0] * h0.astype(jnp.float32))
    _, hs = lax.associative_scan(linear_combine, (a, u), axis=1)
    out = (y * hs.astype(h.dtype)) @ w_out
    return out, new_hist, hs[:, -1].astype(h.dtype)


def conv_ffn(h, conv_hist, w_up, conv_w, conv_b, w_down):
    up = h @ w_up
    uh = jnp.concatenate([conv_hist.astype(up.dtype), up], axis=1)
    new_hist = uh[:, uh.shape[1] - (FFN_CONV_W - 1):]
    uc = causal_dwconv(uh, conv_w, conv_b)
    return (jax.nn.gelu(uc[..., :D_FF]) * uc[..., D_FF:]) @ w_down, new_hist


def per_layer_embed(x, p, g, w_in, w_gate):
    return (p @ w_in) * jax.nn.sigmoid(rms_norm(x, g) @ w_gate)


def shared_kv(u, pos, w_kv):
    b, t, _ = u.shape
    kv = (u @ w_kv).reshape(b, t, 6, N_KV_HEADS, HEAD_DIM)
    cmp_kv = kv[:, :, 0:2]
    slc_kv = jnp.stack([rope(kv[:, :, 2], pos), kv[:, :, 3]], axis=2)
    win_kv = jnp.stack([rope(kv[:, :, 4], pos), kv[:, :, 5]], axis=2)
    return cmp_kv, slc_kv, win_kv


def compress(raw_kv, pos_emb, w1, b1, w2):
    b, t = raw_kv.shape[0], raw_kv.shape[1]
    n_cb = (t - L_CMP) // D_CMP + 1
    idx = jnp.arange(n_cb, dtype=jnp.int32)[:, None] * D_CMP + jnp.arange(L_CMP, dtype=jnp.int32)[None, :]
    blk = raw_kv[:, idx] + pos_emb[None, None, :, :, None, :]
    flat = jnp.transpose(blk, (0, 1, 4, 3, 2, 5)).reshape(b, n_cb, N_KV_HEADS, 2, L_CMP * HEAD_DIM)
    hid = jax.nn.gelu(jnp.einsum('bcgjf,jfe->bcgje', flat, w1) + b1)
    out = jnp.einsum('bcgje,jed->bcgjd', hid, w2)
    ends = jnp.arange(n_cb, dtype=jnp.int32) * D_CMP + (L_CMP - 1)
    return out[..., 0, :], out[..., 1, :], ends


def cmp_to_slc(n_cb, n_sb):
    c0 = jnp.arange(n_cb, dtype=jnp.int32)[:, None] * D_CMP
    s0 = jnp.arange(n_sb, dtype=jnp.int32)[None, :] * L_SLC
    return ((c0 < s0 + L_SLC) & (c0 + L_CMP > s0)).astype(jnp.float32)


def nsa_query(h, pos, w_qg):
    b, t, _ = h.shape
    proj = h @ w_qg
    q = proj[..., :N_HEADS * HEAD_DIM].reshape(b, t, N_HEADS, HEAD_DIM)
    gates = jax.nn.sigmoid(proj[..., N_HEADS * HEAD_DIM:].reshape(b, t, N_HEADS, 3))
    return q, rope(q, pos), gates


def nsa_core(q, q_rot, gates, q_pos, kc, vc, c_end, slc_blk, win_kv, w_pos):
    b, t = q.shape[0], q.shape[1]
    scale = HEAD_DIM ** -0.5
    qg = q.reshape(b, t, N_KV_HEADS, HPG, HEAD_DIM)
    qr = q_rot.reshape(b, t, N_KV_HEADS, HPG, HEAD_DIM)
    s_c = jnp.einsum('btghd,bcgd->bgthc', qg, kc, preferred_element_type=jnp.float32) * scale
    m_c = c_end[None, :] <= q_pos[:, None]
    p_c = masked_softmax(s_c, m_c[None, None, :, None, :])
    o_c = jnp.einsum('bgthc,bcgd->btghd', p_c.astype(vc.dtype), vc)
    n_sb = slc_blk.shape[1]
    imp = jnp.einsum('bgthc,cs->bgts', p_c, cmp_to_slc(kc.shape[1], n_sb))
    blk = jnp.arange(n_sb, dtype=jnp.int32)[None, :]
    cur = (q_pos // L_SLC)[:, None]
    forced = (blk == 0) | (blk == cur) | (blk == cur - 1)
    imp = jnp.where(forced, jnp.inf, imp)
    imp = jnp.where(blk * L_SLC <= q_pos[:, None], imp, -jnp.inf)
    n_top = min(N_SEL, n_sb)
    _, sel = lax.top_k(imp, n_top)
    blk_g = jnp.transpose(slc_blk, (0, 4, 1, 2, 3, 5))
    gath = jax.vmap(jax.vmap(lambda kb, ib: kb[ib]))(blk_g, sel)
    n_rows = n_top * L_SLC
    k_s = gath[..., 0, :]
    v_s = gath[..., 1, :].reshape(b, N_KV_HEADS, t, n_rows, HEAD_DIM)
    kpos = sel[..., None] * L_SLC + jnp.arange(L_SLC, dtype=jnp.int32)
    m_s = (kpos <= q_pos[None, None, :, None, None]).reshape(b, N_KV_HEADS, t, 1, n_rows)
    s_s = jnp.einsum('btghd,bgtkld->bgthkl', qr, k_s, preferred_element_type=jnp.float32) * scale
    p_s = masked_softmax(s_s.reshape(b, N_KV_HEADS, t, HPG, n_rows), m_s)
    o_s = jnp.einsum('bgthn,bgtnd->btghd', p_s.astype(v_s.dtype), v_s)
    k_w = win_kv[:, :, 0]
    v_w = win_kv[:, :, 1]
    dist = q_pos[:, None] - w_pos[None, :]
    m_w = (dist >= 0) & (dist < WINDOW) & (w_pos[None, :] >= 0)
    s_w = jnp.einsum('btghd,bwgd->bgthw', qr, k_w, preferred_element_type=jnp.float32) * scale
    p_w = masked_softmax(s_w, m_w[None, None, :, None, :])
    o_w = jnp.einsum('bgthw,bwgd->btghd', p_w.astype(v_w.dtype), v_w)
    g = gates.reshape(b, t, N_KV_HEADS, HPG, 3)
    o = o_c * g[..., 0:1] + o_s * g[..., 1:2] + o_w * g[..., 2:3]
    return o.reshape(b, t, N_HEADS * HEAD_DIM)


def gather_pages(cache, page_table):
    rows = cache[page_table]
    return rows.reshape(page_table.shape[0], page_table.shape[1] * PAGE_SIZE, *cache.shape[2:])


def setup_inputs(seed: int = 0) -> dict:
    key = jax.random.key(seed)
    ks = iter(jax.random.split(key, 48))

    def nrm(shape, scale):
        return jax.random.normal(next(ks), shape, jnp.float32) * scale

    def gain(shape):
        return 1.0 + nrm(shape, 0.01)

    n_pages = PAST_LEN // PAGE_SIZE
    n_used = DEC_BATCH * n_pages
    n_phys = (5 * n_used + 3) // 4
    page_table = jax.random.permutation(next(ks), n_phys)[:n_used].reshape(DEC_BATCH, n_pages).astype(jnp.int32)
    w_buf = min(WINDOW, PAST_LEN)
    a0 = jax.random.uniform(next(ks), (N_A_LAYERS, D_RNN), jnp.float32, 0.9, 0.999)
    a_base = a0 ** (1.0 / RG_C)
    rg_lambda = jnp.log(a_base) - jnp.log1p(-a_base)
    return {
        'x_prompt': nrm((BATCH, SEQ, D_MODEL), 1.0),
        'x_sample': nrm((DEC_BATCH, DEC_SEQ, D_MODEL), 1.0),
        'p_prompt': nrm((DEPTH, BATCH, SEQ, D_PLE), 1.0),
        'p_sample': nrm((DEPTH, DEC_BATCH, DEC_SEQ, D_PLE), 1.0),
        'cache_cmp_kv': nrm((n_phys, PAGE_SIZE, 2, N_KV_HEADS, HEAD_DIM), 1.0),
        'cache_slc_kv': nrm((n_phys, PAGE_SIZE, 2, N_KV_HEADS, HEAD_DIM), 1.0),
        'cache_win_kv': nrm((DEC_BATCH, w_buf, 2, N_KV_HEADS, HEAD_DIM), 1.0),
        'state_rg_conv': nrm((N_A_LAYERS, DEC_BATCH, RG_CONV_W - 1, D_RNN), 1.0),
        'state_rg_h': nrm((N_A_LAYERS, DEC_BATCH, D_RNN), 0.5),
        'state_ffn_conv': nrm((DEPTH, DEC_BATCH, FFN_CONV_W - 1, 2 * D_FF), 1.0),
        'page_table': page_table,
        'g_mix': gain((DEPTH, D_MODEL)),
        'g_ffn': gain((DEPTH, D_MODEL)),
        'g_ple': gain((DEPTH, D_MODEL)),
        'g_final': gain((D_MODEL,)),
        'rg_w_in': nrm((N_A_LAYERS, D_MODEL, 2 * D_RNN), D_MODEL ** -0.5),
        'rg_conv_w': nrm((N_A_LAYERS, RG_CONV_W, D_RNN), RG_CONV_W ** -0.5),
        'rg_conv_b': nrm((N_A_LAYERS, D_RNN), 0.01),
        'rg_w_a': nrm((N_A_LAYERS, N_RG_BLOCKS, RG_BLOCK, RG_BLOCK), RG_BLOCK ** -0.5),
        'rg_b_a': nrm((N_A_LAYERS, N_RG_BLOCKS, RG_BLOCK), 0.01),
        'rg_w_x': nrm((N_A_LAYERS, N_RG_BLOCKS, RG_BLOCK, RG_BLOCK), RG_BLOCK ** -0.5),
        'rg_b_x': nrm((N_A_LAYERS, N_RG_BLOCKS, RG_BLOCK), 0.01),
        'rg_lambda': rg_lambda,
        'rg_w_out': nrm((N_A_LAYERS, D_RNN, D_MODEL), D_RNN ** -0.5),
        'g_kv': gain((D_MODEL,)),
        'w_kv': nrm((D_MODEL, 6 * N_KV_HEADS * HEAD_DIM), D_MODEL ** -0.5),
        'cmp_pos': nrm((L_CMP, 2, HEAD_DIM), 0.1),
        'cmp_w1': nrm((2, L_CMP * HEAD_DIM, CMP_HIDDEN), (L_CMP * HEAD_DIM) ** -0.5),
        'cmp_b1': nrm((2, CMP_HIDDEN), 0.01),
        'cmp_w2': nrm((2, CMP_HIDDEN, HEAD_DIM), CMP_HIDDEN ** -0.5),
        'attn_w_qg': nrm((N_B_LAYERS, D_MODEL, N_HEADS * HEAD_DIM + 3 * N_HEADS), D_MODEL ** -0.5),
        'attn_w_o': nrm((N_B_LAYERS, N_HEADS * HEAD_DIM, D_MODEL), (N_HEADS * HEAD_DIM) ** -0.5),
        'ffn_w_up': nrm((DEPTH, D_MODEL, 2 * D_FF), D_MODEL ** -0.5),
        'ffn_conv_w': nrm((DEPTH, FFN_CONV_W, 2 * D_FF), FFN_CONV_W ** -0.5),
        'ffn_conv_b': nrm((DEPTH, 2 * D_FF), 0.01),
        'ffn_w_down': nrm((DEPTH, D_FF, D_MODEL), D_FF ** -0.5),
        'ple_w_in': nrm((DEPTH, D_PLE, D_MODEL), D_PLE ** -0.5),
        'ple_w_gate': nrm((DEPTH, D_MODEL, D_MODEL), D_MODEL ** -0.5),
    }


def reference(x_prompt, x_sample, p_prompt, p_sample,
              cache_cmp_kv, cache_slc_kv, cache_win_kv,
              state_rg_conv, state_rg_h, state_ffn_conv, page_table,
              g_mix, g_ffn, g_ple, g_final,
              rg_w_in, rg_conv_w, rg_conv_b, rg_w_a, rg_b_a, rg_w_x, rg_b_x, rg_lambda, rg_w_out,
              g_kv, w_kv, cmp_pos, cmp_w1, cmp_b1, cmp_w2,
              attn_w_qg, attn_w_o,
              ffn_w_up, ffn_conv_w, ffn_conv_b, ffn_w_down,
              ple_w_in, ple_w_gate):

    def prompt_kv_side(stream):
        b, t = stream.shape[0], stream.shape[1]
        pos = jnp.arange(t, dtype=jnp.int32)
        cmp_kv, slc_kv, win_kv = shared_kv(rms_norm(stream, g_kv), pos, w_kv)
        kc, vc, c_end = compress(cmp_kv, cmp_pos, cmp_w1, cmp_b1, cmp_w2)
        slc_blk = slc_kv.reshape(b, t // L_SLC, L_SLC, 2, N_KV_HEADS, HEAD_DIM)
        win_pad = jnp.pad(win_kv, ((0, 0), (WINDOW, 0), (0, 0), (0, 0), (0, 0)))
        nqb = t // Q_BLOCK

        def blockify(a):
            return a.reshape(b, nqb, Q_BLOCK, *a.shape[2:]).swapaxes(0, 1)

        def one_block(args):
            qb, qrb, gb, start = args
            qpos = start + jnp.arange(Q_BLOCK, dtype=jnp.int32)
            kw = lax.dynamic_slice_in_dim(win_pad, start, WINDOW + Q_BLOCK, axis=1)
            wpos = start - WINDOW + jnp.arange(WINDOW + Q_BLOCK, dtype=jnp.int32)
            return nsa_core(qb, qrb, gb, qpos, kc, vc, c_end, slc_blk, kw, wpos)

        def attend(h, w_qg, w_o):
            q, qr, gates = nsa_query(h, pos, w_qg)
            starts = jnp.arange(nqb, dtype=jnp.int32) * Q_BLOCK
            o = lax.map(one_block, (blockify(q), blockify(qr), blockify(gates), starts))
            return o.swapaxes(0, 1).reshape(b, t, N_HEADS * HEAD_DIM) @ w_o

        return attend, (cmp_kv, slc_kv, win_kv[:, t - min(WINDOW, t):])

    def sample_kv_side(stream):
        db, tn = stream.shape[0], stream.shape[1]
        pos = PAST_LEN + jnp.arange(tn, dtype=jnp.int32)
        cmp_new, slc_new, win_new = shared_kv(rms_norm(stream, g_kv), pos, w_kv)
        t = PAST_LEN + tn
        cmp_full = jnp.concatenate([gather_pages(cache_cmp_kv, page_table).astype(cmp_new.dtype), cmp_new], axis=1)
        slc_full = jnp.concatenate([gather_pages(cache_slc_kv, page_table).astype(slc_new.dtype), slc_new], axis=1)
        kc, vc, c_end = compress(cmp_full, cmp_pos, cmp_w1, cmp_b1, cmp_w2)
        n_sb = -(-t // L_SLC)
        slc_blk = jnp.pad(slc_full, ((0, 0), (0, n_sb * L_SLC - t), (0, 0), (0, 0), (0, 0)))
        slc_blk = slc_blk.reshape(db, n_sb, L_SLC, 2, N_KV_HEADS, HEAD_DIM)
        w_buf = cache_win_kv.shape[1]
        win_full = jnp.concatenate([cache_win_kv.astype(win_new.dtype), win_new], axis=1)
        w_pos = PAST_LEN - w_buf + jnp.arange(w_buf + tn, dtype=jnp.int32)

        def attend(h, w_qg, w_o):
            q, qr, gates = nsa_query(h, pos, w_qg)
            return nsa_core(q, qr, gates, pos, kc, vc, c_end, slc_blk, win_full, w_pos) @ w_o

        keep = min(WINDOW, t)
        return attend, (cmp_new, slc_new, win_full[:, win_full.shape[1] - keep:])

    def run_group(x, p, pos, rg_conv0, rg_h0, ffn_conv0, kv_side):
        rg_conv_new, rg_h_new, ffn_conv_new = [], [], []
        attend, kv_state = None, None
        for i in range(DEPTH):
            if i == N_A_LAYERS:
                attend, kv_state = kv_side(x)
            h = rms_norm(x, g_mix[i])
            if i < N_A_LAYERS:
                o, c_hist, h_last = rg_lru_block(h, rg_conv0[i], rg_h0[i], pos, rg_w_in[i], rg_conv_w[i], rg_conv_b[i],
                                                 rg_w_a[i], rg_b_a[i], rg_w_x[i], rg_b_x[i], rg_lambda[i], rg_w_out[i])
                rg_conv_new.append(c_hist)
                rg_h_new.append(h_last)
            else:
                j = i - N_A_LAYERS
                o = attend(h, attn_w_qg[j], attn_w_o[j])
            x = x + o
            f, f_hist = conv_ffn(rms_norm(x, g_ffn[i]), ffn_conv0[i], ffn_w_up[i], ffn_conv_w[i], ffn_conv_b[i], ffn_w_down[i])
            x = x + f
            ffn_conv_new.append(f_hist)
            x = x + per_layer_embed(x, p[i], g_ple[i], ple_w_in[i], ple_w_gate[i])
        return rms_norm(x, g_final), jnp.stack(rg_conv_new), jnp.stack(rg_h_new), jnp.stack(ffn_conv_new), kv_state

    bp, sp = x_prompt.shape[0], x_prompt.shape[1]
    dt = x_prompt.dtype
    y_prompt, rgc_p, rgh_p, ffc_p, kv_p = run_group(
        x_prompt, p_prompt, jnp.arange(sp, dtype=jnp.int32),
        jnp.zeros((N_A_LAYERS, bp, RG_CONV_W - 1, D_RNN), dt),
        jnp.zeros((N_A_LAYERS, bp, D_RNN), dt),
        jnp.zeros((DEPTH, bp, FFN_CONV_W - 1, 2 * D_FF), dt),
        prompt_kv_side)
    y_sample, rgc_s, rgh_s, ffc_s, kv_s = run_group(
        x_sample, p_sample, PAST_LEN + jnp.arange(x_sample.shape[1], dtype=jnp.int32),
        state_rg_conv, state_rg_h, state_ffn_conv, sample_kv_side)
    cmp_p, slc_p, win_p = kv_p
    cmp_s, slc_s, win_s = kv_s
    return (y_prompt, y_sample, cmp_p, cmp_s, slc_p, slc_s, win_p, win_s,
            rgc_p, rgc_s, rgh_p, rgh_s, ffc_p, ffc_s)
```

```python
import numpy as np
import ml_dtypes
from contextlib import ExitStack
import concourse.bass as bass
import concourse.mybir as mybir
from concourse.bass_utils import run_bass_kernel_spmd

F32 = mybir.dt.float32
BF16 = mybir.dt.bfloat16
AF = mybir.ActivationFunctionType
ALU = mybir.AluOpType

D = 1024
NT_TOK = 512
DR = 1280
DFF = 3072
EPS = 1e-6


class Buf:
    __slots__ = ("w", "r")

    def __init__(self):
        self.w = None
        self.r = []


class Sched:
    ND = 12

    def __init__(self, nc, es):
        self.nc = nc
        self.E = dict(pe=nc.tensor, act=nc.scalar, dve=nc.vector, pool=nc.gpsimd, sp=nc.sync)
        self.sem = {}
        self.cnt = {}
        for e in ("pe", "act", "dve", "pool"):
            self.sem[e] = es.enter_context(nc.semaphore("s_" + e))
            self.cnt[e] = 0
        self.seen = {e: {} for e in self.E}
        self.dsem = {}
        self.dval = {}
        self.didx = {}
        for q in ("sp", "pool", "act"):
            self.dsem[q] = [es.enter_context(nc.semaphore(f"d_{q}{i}")) for i in range(self.ND)]
            self.dval[q] = [0] * self.ND
            self.didx[q] = 0
        self.all_dma = []

    def _wait(self, e, tk):
        key, sem, val = tk
        if key == e and e == "pe":
            return
        if self.seen[e].get(key, 0) >= val:
            return
        self.E[e].wait_ge(sem, val)
        self.seen[e][key] = val

    def _deps(self, e, reads, writes):
        for b in reads:
            if b.w is not None:
                for tk in (b.w if isinstance(b.w, list) else [b.w]):
                    self._wait(e, tk)
        for b in writes:
            if b.w is not None:
                for tk in (b.w if isinstance(b.w, list) else [b.w]):
                    self._wait(e, tk)
            for tk in b.r:
                self._wait(e, tk)

    def _mark(self, tk, reads, writes):
        for b in reads:
            b.r.append(tk)
            if len(b.r) > 24:
                b.r = b.r[-24:]
        for b in writes:
            b.w = tk
            b.r = []

    def op(self, e, fn, reads=(), writes=()):
        self._deps(e, reads, writes)
        inst = fn(self.E[e])
        inst.then_inc(self.sem[e], 1)
        self.cnt[e] += 1
        tk = (e, self.sem[e], self.cnt[e])
        self._mark(tk, reads, writes)
        return tk

    def dma(self, q, out, in_, reads=(), writes=(), **kw):
        slot = self.didx[q] % self.ND
        self.didx[q] += 1
        sem = self.dsem[q][slot]
        key = (q, slot)
        if self.dval[q][slot] > 0:
            self._wait(q, (key, sem, self.dval[q][slot]))
        self._deps(q, reads, writes)
        inst = self.E[q].dma_start(out=out, in_=in_, **kw)
        inst.then_inc(sem, 16)
        self.dval[q][slot] += 16
        tk = (key, sem, self.dval[q][slot])
        self._mark(tk, reads, writes)
        self.all_dma.append(tk)
        return tk

    def dma_ind(self, out, in_, idx_ap):
        q = "pool"
        slot = self.didx[q] % self.ND
        self.didx[q] += 1
        sem = self.dsem[q][slot]
        key = (q, slot)
        if self.dval[q][slot] > 0:
            self._wait(q, (key, sem, self.dval[q][slot]))
        inst = self.nc.gpsimd.indirect_dma_start(out=out, out_offset=None, in_=in_, in_offset=bass.IndirectOffsetOnAxis(ap=idx_ap, axis=0))
        inst.then_inc(sem, 16)
        self.dval[q][slot] += 16
        return (key, sem, self.dval[q][slot])

    def barrier(self):
        tks = [(e, self.sem[e], self.cnt[e]) for e in self.cnt if self.cnt[e] > 0]
        for q in self.dsem:
            for slot in range(self.ND):
                if self.dval[q][slot] > 0:
                    tks.append(((q, slot), self.dsem[q][slot], self.dval[q][slot]))
        for e in self.E:
            for tk in tks:
                if tk[0] == e and e == "pe":
                    continue
                self._wait(e, tk)

    def finish(self):
        for q in self.dsem:
            for slot in range(self.ND):
                if self.dval[q][slot] > 0:
                    self._wait("sp", ((q, slot), self.dsem[q][slot], self.dval[q][slot]))


class Wt:
    def __init__(self, nc, name, src, K, M, MW):
        self.KC = K // 128
        self.MW = MW
        self.NP = M // MW
        assert self.KC * MW <= 4096
        self.src = src
        self.dst = nc.dram_tensor("wb_" + name, [self.NP, 128, self.KC * MW], BF16, kind="Internal").ap()
        self.buf = Buf()


def build(cfg):
    NT = cfg.get("NT", 8)
    OWN0 = cfg.get("OWN0", 4)
    NL_A = cfg.get("NL_A", 2)
    DO_B = cfg.get("DO_B", False)
    DBG = cfg.get("DBG", 99)
    NTOK = NT * NT_TOK
    NOWN = (NT - OWN0) * NT_TOK

    nc = bass.Bass("TRN2", target_bir_lowering=False)

    def din(name, shape, dt=F32):
        return nc.dram_tensor(name, list(shape), dt, kind="ExternalInput").ap()

    def dout(name, shape, dt=F32):
        return nc.dram_tensor(name, list(shape), dt, kind="ExternalOutput").ap()

    xs = din("xs", [NTOK, D])
    ps_in = din("ps", [4, NTOK, 256])
    flag = din("flag", [128, 2])
    ident_d = din("ident", [128, 128])
    ropec = din("ropec", [128, NTOK])
    ropes = din("ropes", [128, NTOK])
    g_mix = din("g_mix", [4, D]); g_ffn = din("g_ffn", [4, D]); g_ple = din("g_ple", [4, D])
    g_final = din("g_final", [D]); g_kv = din("g_kv", [D])
    rg_w_in = din("rg_w_in", [2, D, 2 * DR]); rg_conv_w = din("rg_conv_w", [2, 4, DR]); rg_conv_b = din("rg_conv_b", [2, DR])
    rg_w_a = din("rg_w_a", [2, 16, 80, 80]); rg_b_a = din("rg_b_a", [2, DR])
    rg_w_x = din("rg_w_x", [2, 16, 80, 80]); rg_b_x = din("rg_b_x", [2, DR])
    rg_lambda = din("rg_lambda", [2, DR]); rg_w_out = din("rg_w_out", [2, DR, D])
    w_kv = din("w_kv", [D, 768])
    ffn_w_up = din("ffn_w_up", [4, D, 2 * DFF]); ffn_conv_w = din("ffn_conv_w", [4, 3, 2 * DFF]); ffn_conv_b = din("ffn_conv_b", [4, 2 * DFF])
    ffn_w_down = din("ffn_w_down", [4, DFF, D])
    ple_w_in = din("ple_w_in", [4, 256, D]); ple_w_gate = din("ple_w_gate", [4, D, D])

    attn_w_qg = din("attn_w_qg", [2, D, 1072]); attn_w_o = din("attn_w_o", [2, D, D])
    cmp_pos = din("cmp_pos", [32, 2, 64]); cmp_w1 = din("cmp_w1", [2, 2048, 128]); cmp_b1 = din("cmp_b1", [2, 128]); cmp_w2 = din("cmp_w2", [2, 128, 64])
    NQB = NOWN // 128 + 1
    t_cb = din("t_cb", [NQB, 128, 256]); t_wb = din("t_wb", [NQB, 128, 640]); t_va = din("t_va", [NQB, 128, 2, 64])
    t_mcs = din("t_mcs", [128, 2, 64]); t_tri = din("t_tri", [128, 128])

    SMP = cfg.get("SMP", False)
    if SMP:
        xs_s = din("xs_s", [128, 8, 512]); ps_s = din("ps_s", [4, 128, 2, 512])
        s_rgh_d = din("s_rgh", [128, 2, 10, 4, 3]); s_h0_d = din("s_h0", [128, 2, 10, 4]); s_ffh_d = din("s_ffh", [128, 4, 48, 4, 2])
        ropec_s = din("ropec_s", [128, 512]); ropes_s = din("ropes_s", [128, 512])
        pg_s = din("pg_s", [4, 64], mybir.dt.int32)
        NPHYS = cfg.get("NPHYS", 2560)
        c_cmp = din("cache_cmp_kv", [NPHYS * 128, 256]); c_slc = din("cache_slc_kv", [NPHYS * 128, 256]); c_win = din("c_win", [4, 512, 256])
        st_rgc = din("st_rgc", [2, 4, 3, DR]); st_ffc = din("st_ffc", [4, 4, 2, 2 * DFF])
        t_mcs_s = din("t_mcs_s", [128, 4, 2, 128]); t_as = din("t_as", [128, 2, 256]); t_brhs = din("t_brhs", [128, 16])
        t_b64 = din("t_b64", [128, 4, 128]); t_wb0 = din("t_wb0", [128, 128]); t_cb0 = din("t_cb0", [128, 128])
        o_s_y = dout("o_s_y", [128, 8, 4]); o_s_kv = dout("o_s_kv", [128, 6, 4]); o_s_win = dout("o_s_win", [4, 511, 256])
        o_s_rgc_new = dout("o_s_rgc_new", [128, 2, 10, 4]); o_s_rgc_old = dout("o_s_rgc_old", [2, 4, 2, DR]); o_s_rgh = dout("o_s_rgh", [128, 2, 10, 4])
        o_s_ffc_new = dout("o_s_ffc_new", [128, 4, 48, 4]); o_s_ffc_old = dout("o_s_ffc_old", [4, 4, 2 * DFF])

    o_y = dout("o_y", [NOWN, D])
    o_cmp = dout("o_cmp", [NOWN, 256]); o_slc = dout("o_slc", [NOWN, 256]); o_win = dout("o_win", [NOWN, 256])
    o_rgc = dout("o_rgc", [2, 3, DR]); o_rgh = dout("o_rgh", [2, DR]); o_ffc = dout("o_ffc", [4, 2, 2 * DFF])

    zpad = nc.dram_tensor("zpad", [9, 128, 4096], BF16, kind="Internal").ap()
    zband = nc.dram_tensor("zband", [2, 2, 128, 10 * 384], BF16, kind="Internal").ap()

    es = ExitStack()
    with es:
        S = Sched(nc, es)

        uid = [0]

        def sb(name, shape, dt=F32):
            uid[0] += 1
            return es.enter_context(nc.sbuf_tensor(f"s{uid[0]}_{name}", list(shape), dt))

        ident = sb("ident", [128, 128]); b_ident = Buf()
        ones_bf = sb("ones_bf", [128, 128], BF16); b_ones = Buf()
        flag_sb = sb("flag_sb", [128, 2]); b_flag = Buf()
        gvec = sb("gvec", [128, 14, 8]); b_gvec = Buf()
        rgcw = sb("rgcw", [128, 2, 4, 10]); rgcb = sb("rgcb", [128, 2, 10]); b_rgc = Buf()
        rgba = sb("rgba", [128, 2, 10]); rgbx = sb("rgbx", [128, 2, 10]); rgc8 = sb("rgc8", [128, 2, 10]); b_rgp = Buf()
        ffcw = sb("ffcw", [128, 4, 3, 48]); ffcb = sb("ffcb", [128, 4, 48]); b_ffc = Buf()
        rg_hist = sb("rg_hist", [128, 2, 10, 3]); b_rghist = [[Buf() for _ in range(10)] for _ in range(2)]
        rg_hc = sb("rg_hc", [128, 2, 10]); b_rghc = [[Buf() for _ in range(10)] for _ in range(2)]
        ff_hist = sb("ff_hist", [128, 4, 48, 2]); b_ffhist = [[Buf() for _ in range(48)] for _ in range(4)]
        epsb = sb("epsb", [128, 1]); b_eps = Buf()

        x = sb("x", [128, 8, 512]); b_x = [Buf() for _ in range(8)]
        xt = sb("xt", [128, 4, 1024]); b_xt = Buf()
        h = sb("h", [128, 8, 512], BF16); b_h = Buf()
        sq = sb("sq", [128, 8, 512], BF16); b_sq = Buf()
        rstd = sb("rstd", [128, 512]); b_rstd = Buf()
        NPAN = 4
        pan = [sb(f"pan{i}", [128, 4096], BF16) for i in range(NPAN)]
        b_pan = [Buf() for _ in range(NPAN)]
        psum2 = [es.enter_context(nc.psum_tensor(f"psum{i}", [128, 1024], F32)) for i in range(4)]
        psum = [psum2[i // 2][:, (i % 2) * 512:(i % 2 + 1) * 512] for i in range(8)]
        b_ps = [Buf() for _ in range(8)]
        ps_i = [0]

        def ps_next():
            i = ps_i[0] % 6
            ps_i[0] += 1
            return psum[i], b_ps[i]

        def ps_pair():
            if ps_i[0] % 2 == 1:
                ps_i[0] += 1
            i = ps_i[0] % 6
            ps_i[0] += 2
            return psum2[i // 2], [b_ps[i], b_ps[i + 1]]

        acc_ps = psum2[3]
        b_acc = [b_ps[6], b_ps[7]]

        W = {}
        band = {}

        def mkw(name, src, K, M, MW):
            w = Wt(nc, name, src, K, M, MW)
            W[name] = w
            tks = []
            for pnl in range(w.NP):
                srcv = src[:, pnl * MW:(pnl + 1) * MW].rearrange("(k p) m -> p k m", p=128)
                dstv = w.dst[pnl].rearrange("p (k m) -> p k m", k=w.KC)
                tks.append(S.dma("pool", dstv, srcv))
            w.buf.w = tks
            return w

        zt_full = sq[:].rearrange("p k t -> p (k t)")
        zt = zt_full[:, 0:3840]; b_zt = b_sq
        S.op("dve", lambda e: e.memset(sq[:], 0.0), writes=[b_zt])
        for l in range(NL_A):
            for gi, wsrc in enumerate((rg_w_a, rg_w_x)):
                bw = Buf()
                band[(l, gi)] = bw
                tz = S.dma("sp", zband[l, gi], zt, reads=[b_zt])
                S._wait("pool", tz)
                tks = []
                dv = zband[l, gi].rearrange("p (i b m) -> p i b m", i=10, b=3)
                for n in range(16):
                    r0, r1 = 80 * n, 80 * n + 80
                    for i in range(r0 // 128, (r1 - 1) // 128 + 1):
                        ra, rb = max(r0, 128 * i), min(r1, 128 * i + 128)
                        for j in range(r0 // 128, (r1 - 1) // 128 + 1):
                            ca, cb = max(r0, 128 * j), min(r1, 128 * j + 128)
                            tks.append(S.dma("pool", dv[ra - 128 * i:rb - 128 * i, i, j - i + 1, ca - 128 * j:cb - 128 * j],
                                             wsrc[l, n, ra - r0:rb - r0, ca - r0:cb - r0]))
                bw.w = tks
                wv = Wt.__new__(Wt)
                wv.KC = 30; wv.MW = 128; wv.NP = 1; wv.dst = zband[l, gi:gi + 1]; wv.buf = bw
                W[f"band{l}_{gi}"] = wv

        for l in range(NL_A):
            mkw(f"rgin{l}", rg_w_in[l], D, 2 * DR, 512)
            mkw(f"rgout{l}", rg_w_out[l], DR, D, 256)
        for l in range(4 if DO_B else NL_A):
            mkw(f"up{l}", ffn_w_up[l], D, 2 * DFF, 512)
            mkw(f"down{l}", ffn_w_down[l], DFF, D, 128)
            mkw(f"plein{l}", ple_w_in[l], 256, D, 1024)
            mkw(f"pleg{l}", ple_w_gate[l], D, D, 512)
        mkw("kv", w_kv, D, 768, 256)


        BIGV = 30000.0
        if DO_B:
            for j in range(2):
                mkw(f"q{j}", attn_w_qg[j][:, 0:1024], D, 1024, 512)
                mkw(f"wo{j}", attn_w_o[j], D, D, 512)
                wsw = Wt(nc, f"qsw{j}", None, D, 1024, 512)
                W[f"qsw{j}"] = wsw
                tks = []
                with nc.allow_non_contiguous_dma(reason="one-time rope column swap"):
                    for pnl in range(2):
                        for k in range(8):
                            srcv = attn_w_qg[j][k * 128:(k + 1) * 128, pnl * 512:(pnl + 1) * 512].rearrange("p (hd hh dd) -> p hd hh dd", hd=8, hh=2)
                            dstv = wsw.dst[pnl].rearrange("p (k hd hh dd) -> p k hd hh dd", k=8, hd=8, hh=2)
                            for hh in range(2):
                                tks.append(S.dma("pool", dstv[:, k, :, hh, :], srcv[:, :, 1 - hh, :]))
                wsw.buf.w = tks
                wg = Wt.__new__(Wt)
                wg.KC = 8; wg.MW = 128; wg.NP = 1; wg.dst = zpad[j:j + 1, :, 0:1024]; wg.buf = Buf()
                W[f"qg{j}"] = wg
                tz = S.dma("sp", zpad[j, :, 0:1024], zt[:, 0:1024], reads=[b_zt])
                S._wait("pool", tz)
                with nc.allow_non_contiguous_dma(reason="gate cols"):
                    tk = S.dma("pool", zpad[j, :, 0:1024].rearrange("p (k m) -> p k m", k=8)[:, :, 0:48],
                               attn_w_qg[j][:, 1024:1072].rearrange("(k p) m -> p k m", p=128))
                wg.buf.w = [tk]
            for jv in range(2):
                for g in range(2):
                    idx = 2 + jv * 2 + g
                    w1 = Wt.__new__(Wt)
                    w1.KC = 32; w1.MW = 128; w1.NP = 1; w1.dst = zpad[idx:idx + 1]; w1.buf = Buf()
                    W[f"w1_{jv}{g}"] = w1
                    tz = S.dma("sp", zpad[idx], zt_full, reads=[b_zt])
                    S._wait("pool", tz)
                    tk = S.dma("pool", zpad[idx, g * 64:(g + 1) * 64, :].rearrange("p (l e) -> p l e", l=32),
                               cmp_w1[jv].rearrange("(l d) e -> d l e", d=64))
                    w1.buf.w = [tk]

            slcK = sb("slcK", [128, 2, NTOK], BF16); b_slcK = Buf()
            slcV = sb("slcV", [128, NTOK // 128, 2, 128], BF16); b_slcV = Buf()
            winK = sb("winK", [128, 2, 1024], BF16); b_winK = Buf()
            winV = sb("winV", [128, 8, 2, 128], BF16); b_winV = Buf()
            cmpT = sb("cmpT", [128, 2, 528], BF16); b_cmpT = Buf()
            kcK = sb("kcK", [128, 2, 256], BF16); b_kcK = Buf()
            vcV = sb("vcV", [128, 2, 2, 128], F32); b_vcV = Buf()
            vcVb = sb("vcVb", [128, 2, 2, 128], BF16); b_vcVb = Buf()
            ident_bf = sb("ident_bf", [128, 128], BF16); b_identb = Buf()
            ones1 = sb("ones1", [128, 128], BF16); b_ones1 = Buf()
            mcs = sb("mcs", [128, 2, 64], BF16); b_mcs = Buf()
            tri = sb("tri", [128, 128], BF16); b_tri = Buf()
            w2k = sb("w2k", [128, 128], BF16); w2v = sb("w2v", [128, 128], BF16); b_w2 = Buf()
            cb1 = sb("cb1", [128, 2]); b_cb1 = Buf()
            b1t = sb("b1t", [128, 2]); posT = sb("posT", [128, 2, 32], BF16); b_posT = Buf()
            S.op("pool", lambda e: e.memset(slcV[:], 1.0), writes=[b_slcV])
            S.op("pool", lambda e: e.memset(winV[:], 1.0), writes=[b_winV])
            S.op("pool", lambda e: e.memset(slcK[:], 0.0), writes=[b_slcK])
            S.op("pool", lambda e: e.memset(winK[:], 0.0), writes=[b_winK])
            S.op("pool", lambda e: e.memset(cmpT[:], 0.0), writes=[b_cmpT])
            S.op("pool", lambda e: e.memset(kcK[:], 0.0), writes=[b_kcK])
            S.op("pool", lambda e: e.memset(vcV[:], 0.0), writes=[b_vcV])
            S.op("pool", lambda e: e.memset(vcVb[:], 0.0), writes=[b_vcVb])
            S.op("pool", lambda e: e.memset(ones1[:], 1.0), writes=[b_ones1])
            S.op("pool", lambda e: e.memset(w2v[:], 0.0), writes=[b_w2])
            S.op("pool", lambda e: e.memset(posT[:], 0.0), writes=[b_posT])
            S.dma("pool", ident_bf[:], ident_d, writes=[b_identb])
            S.dma("pool", mcs[:], t_mcs, writes=[b_mcs])
            S.dma("pool", tri[:], t_tri, writes=[b_tri])
            S.dma("pool", w2k[:, 0:64], cmp_w2[0], writes=[b_w2])
            S.dma("pool", w2k[:, 64:128], cmp_w2[0], writes=[b_w2])
            S.dma("pool", w2v[:, 0:64], cmp_w2[1], writes=[b_w2])
            with nc.allow_non_contiguous_dma(reason="small"):
                S.dma("sp", b1t[:], cmp_b1.rearrange("j e -> e j"), writes=[b_cb1])
                for jv_ in range(2):
                    S.dma("pool", posT[0:64, jv_, :], cmp_pos[:, jv_, :].rearrange("l d -> d l"), writes=[b_posT])

        S.dma("sp", ident[:], ident_d, writes=[b_ident])
        S.dma("sp", flag_sb[:], flag, writes=[b_flag])
        S.op("dve", lambda e: e.memset(ones_bf[:], 1.0 / D), writes=[b_ones])
        S.op("dve", lambda e: e.memset(epsb[:], EPS), writes=[b_eps])
        with nc.allow_non_contiguous_dma(reason="small param vectors to feature-major"):
            for i in range(4):
                S.dma("sp", gvec[:, i, :], g_mix[i].rearrange("(c p) -> p c", p=128), writes=[b_gvec])
                S.dma("sp", gvec[:, 4 + i, :], g_ffn[i].rearrange("(c p) -> p c", p=128), writes=[b_gvec])
                S.dma("sp", gvec[:, 8 + i, :], g_ple[i].rearrange("(c p) -> p c", p=128), writes=[b_gvec])
            S.dma("sp", gvec[:, 12, :], g_final.rearrange("(c p) -> p c", p=128), writes=[b_gvec])
            S.dma("sp", gvec[:, 13, :], g_kv.rearrange("(c p) -> p c", p=128), writes=[b_gvec])
            for l in range(2):
                for k in range(4):
                    S.dma("sp", rgcw[:, l, k, :], rg_conv_w[l, k].rearrange("(c p) -> p c", p=128), writes=[b_rgc])
                S.dma("sp", rgcb[:, l, :], rg_conv_b[l].rearrange("(c p) -> p c", p=128), writes=[b_rgc])
                S.dma("sp", rgba[:, l, :], rg_b_a[l].rearrange("(c p) -> p c", p=128), writes=[b_rgp])
                S.dma("sp", rgbx[:, l, :], rg_b_x[l].rearrange("(c p) -> p c", p=128), writes=[b_rgp])
                S.dma("sp", rgc8[:, l, :], rg_lambda[l].rearrange("(c p) -> p c", p=128), writes=[b_rgp])
            for l in range(4):
                for k in range(3):
                    S.dma("sp", ffcw[:, l, k, :], ffn_conv_w[l, k].rearrange("(c p) -> p c", p=128), writes=[b_ffc])
                S.dma("sp", ffcb[:, l, :], ffn_conv_b[l].rearrange("(c p) -> p c", p=128), writes=[b_ffc])
        S.op("act", lambda e: e.activation(out=rgc8[:], in_=rgc8[:], func=AF.Exp, scale=-1.0), reads=[b_rgp], writes=[b_rgp])
        S.op("act", lambda e: e.activation(out=rgc8[:], in_=rgc8[:], func=AF.Ln, bias=1.0), reads=[b_rgp], writes=[b_rgp])
        S.op("dve", lambda e: e.tensor_scalar(out=rgc8[:], in0=rgc8[:], scalar1=-8.0, scalar2=None, op0=ALU.mult), reads=[b_rgp], writes=[b_rgp])
        allh = [b for row in b_rghist for b in row] + [b for row in b_rghc for b in row] + [b for row in b_ffhist for b in row]
        S.op("dve", lambda e: e.memset(rg_hist[:], 0.0), writes=[b for row in b_rghist for b in row])
        S.op("dve", lambda e: e.memset(rg_hc[:], 0.0), writes=[b for row in b_rghc for b in row])
        S.op("dve", lambda e: e.memset(ff_hist[:], 0.0), writes=[b for row in b_ffhist for b in row])

        pan_i = [0]

        def load_panel(w, pnl):
            i = pan_i[0] % NPAN
            pan_i[0] += 1
            n = w.KC * w.MW
            S.dma("sp", pan[i][:, 0:n], w.dst[pnl], reads=[w.buf], writes=[b_pan[i]])
            return pan[i][:, 0:n].rearrange("p (k m) -> p k m", k=w.KC), b_pan[i]

        class Stream:
            def __init__(self, items, depth=2):
                self.items = items
                self.loaded = []
                self.depth = depth
                self.pos = 0

            def get(self):
                while len(self.loaded) < min(len(self.items), self.pos + 1 + self.depth):
                    w, pnl = self.items[len(self.loaded)]
                    self.loaded.append(load_panel(w, pnl))
                r = self.loaded[self.pos]
                self.pos += 1
                return r

        if DO_B:
            for jv in range(2):
                pnl, bpn = load_panel(W[f"w1_{jv}0"], 0)
                pt, bp = ps_next()
                for l_ in range(32):
                    S.op("pe", lambda e: e.matmul(pt[:, 0:1], lhsT=pnl[:, l_, :], rhs=posT[:, jv, l_:l_ + 1], start=(l_ == 0), stop=(l_ == 31)),
                         reads=[bpn, b_posT], writes=[bp])
                S.op("dve", lambda e: e.tensor_tensor(out=cb1[:, jv:jv + 1], in0=b1t[:, jv:jv + 1], in1=pt[:, 0:1], op=ALU.add), reads=[bp, b_cb1], writes=[b_cb1])

        def transpose_in(src_rows, ncols, dst, dst_bufs, dst_dt_bf16=False):
            nk = ncols // 128
            S.dma("sp", xt[:, :, 0:ncols], src_rows.rearrange("(b p) f -> p b f", p=128), writes=[b_xt])
            for k in range(nk):
                pt, bp = ps_next()
                for blk in range(4):
                    S.op("pe", lambda e: e.transpose(pt[:, blk * 128:(blk + 1) * 128], xt[:, blk, k * 128:(k + 1) * 128], ident[:]),
                         reads=[b_xt, b_ident], writes=[bp])
                S.op("act", lambda e: e.copy(out=dst[:, k, :], in_=pt[:]), reads=[bp], writes=[dst_bufs[k]])

        def rmsnorm(gi, out_t, out_buf):
            for k in range(8):
                S.op("act", lambda e: e.activation(out=sq[:, k, :], in_=x[:, k, :], func=AF.Square), reads=[b_x[k]], writes=[b_sq])
            pt, bp = ps_next()
            for k in range(8):
                S.op("pe", lambda e: e.matmul(pt[:], lhsT=ones_bf[:], rhs=sq[:, k, :], start=(k == 0), stop=(k == 7)),
                     reads=[b_sq, b_ones], writes=[bp])
            S.op("act", lambda e: e.activation(out=rstd[:], in_=pt[:], func=AF.Sqrt, bias=epsb[:, 0:1]), reads=[bp, b_eps], writes=[b_rstd])
            S.op("dve", lambda e: e.reciprocal(out=rstd[:], in_=rstd[:]), reads=[b_rstd], writes=[b_rstd])
            for k in range(8):
                S.op("dve", lambda e: e.scalar_tensor_tensor(out=out_t[:, k, :], in0=x[:, k, :], scalar=gvec[:, gi, k:k + 1], in1=rstd[:],
                                                              op0=ALU.mult, op1=ALU.mult),
                     reads=[b_x[k], b_gvec, b_rstd], writes=[out_buf])

        def proj_chunk(pnl_ap, pnl_buf, col0, mcols, rhs_t, rhs_buf, nk):
            pt, bp = ps_next()
            for k in range(nk):
                S.op("pe", lambda e: e.matmul(pt[0:mcols, :], lhsT=pnl_ap[:, k, col0:col0 + mcols], rhs=rhs_t[:, k, :],
                                              start=(k == 0), stop=(k == nk - 1)),
                     reads=[pnl_buf] + (rhs_buf if isinstance(rhs_buf, list) else [rhs_buf]), writes=[bp])
            return pt, bp

        def rg_layer(l, ti, es2, smp=False):
            def sb2(name, shape, dt=F32):
                uid[0] += 1
                return es2.enter_context(nc.sbuf_tensor(f"s{uid[0]}_{name}", list(shape), dt))
            y = sb2("rg_y", [128, 10, 512], BF16); b_y = [Buf() for _ in range(10)]
            xc = sb2("rg_xc", [128, 10, 512]); b_xc = [Buf() for _ in range(10)]
            xcb = sb2("rg_xcb", [128, 10, 512], BF16); b_xcb = [Buf() for _ in range(10)]
            xrh = [sb2(f"rg_xrh{i}", [128, 515]) for i in range(2)]; b_xrh = [Buf(), Buf()]
            t1 = [sb2(f"rg_t1{i}", [128, 512]) for i in range(2)]; b_t1 = [Buf(), Buf()]
            gr = [sb2(f"rg_r{i}", [128, 512]) for i in range(2)]; b_gr = [Buf(), Buf()]
            gi_ = [sb2(f"rg_i{i}", [128, 512]) for i in range(2)]; b_gi = [Buf(), Buf()]
            ga = [sb2(f"rg_a{i}", [128, 512]) for i in range(2)]; b_ga = [Buf(), Buf()]
            gm = [sb2(f"rg_m{i}", [128, 512]) for i in range(2)]; b_gm = [Buf(), Buf()]
            hs = [sb2(f"rg_hs{i}", [128, 512]) for i in range(2)]; b_hs = [Buf(), Buf()]
            if smp:
                s_rgh = sb2("s_rgh", [128, 10, 4, 3]); s_h0 = sb2("s_h0", [128, 10, 4]); b_sst = Buf()
                so_rgc = sb2("so_rgc", [128, 10, 4]); so_rgh = sb2("so_rgh", [128, 10, 4]); b_so = Buf()
                S.dma("sp", s_rgh[:], s_rgh_d[:, l], writes=[b_sst])
                S.dma("sp", s_h0[:], s_h0_d[:, l], writes=[b_sst])
            rmsnorm(l, h, b_h)
            st = Stream([(W[f"rgin{l}"], p) for p in range(5)] + [(W[f"band{l}_0"], 0), (W[f"band{l}_1"], 0)]
                        + [(W[f"rgout{l}"], p) for p in range(4)])
            for jc in range(20):
                if jc % 4 == 0:
                    pnl, bpn = st.get()
                pt, bp = proj_chunk(pnl, bpn, (jc % 4) * 128, 128, h, b_h, 8)
                if jc < 10:
                    S.op("act", lambda e: e.activation(out=y[:, jc, :], in_=pt[:], func=AF.Gelu), reads=[bp], writes=[b_y[jc]])
                else:
                    j = jc - 10
                    q = j % 2
                    S.op("act", lambda e: e.copy(out=xrh[q][:, 3:515], in_=pt[:]), reads=[bp], writes=[b_xrh[q]])
                    if ti == OWN0:
                        S.op("pool", lambda e: e.tensor_scalar(out=rg_hist[:, l, j, :], in0=rg_hist[:, l, j, :], scalar1=flag_sb[:, 0:1], scalar2=None, op0=ALU.mult),
                             reads=[b_rghist[l][j], b_flag], writes=[b_rghist[l][j]])
                    S.op("pool", lambda e: e.tensor_copy(out=xrh[q][:, 0:3], in_=rg_hist[:, l, j, :]), reads=[b_rghist[l][j]], writes=[b_xrh[q]])
                    S.op("pool", lambda e: e.tensor_copy(out=rg_hist[:, l, j, :], in_=xrh[q][:, 512:515]), reads=[b_xrh[q]], writes=[b_rghist[l][j]])
                    if smp:
                        S.op("dve", lambda e: e.tensor_copy(out=so_rgc[:, j, :], in_=xrh[q][:, 6:22:4]), reads=[b_xrh[q]], writes=[b_so])
                        S.op("dve", lambda e: e.tensor_copy(out=xrh[q][:, 3:19].rearrange("p (b k) -> p b k", k=4)[:, :, 0:3], in_=s_rgh[:, j, :, :]),
                             reads=[b_sst], writes=[b_xrh[q]])
                    S.op("dve", lambda e: e.tensor_scalar(out=t1[q][:], in0=xrh[q][:, 0:512], scalar1=rgcw[:, l, 0, j:j + 1], scalar2=rgcb[:, l, j:j + 1],
                                                           op0=ALU.mult, op1=ALU.add), reads=[b_xrh[q], b_rgc], writes=[b_t1[q]])
                    for k in (1, 2):
                        S.op("dve", lambda e: e.scalar_tensor_tensor(out=t1[q][:], in0=xrh[q][:, k:k + 512], scalar=rgcw[:, l, k, j:j + 1], in1=t1[q][:],
                                                                      op0=ALU.mult, op1=ALU.add), reads=[b_xrh[q], b_rgc, b_t1[q]], writes=[b_t1[q]])
                    S.op("dve", lambda e: e.scalar_tensor_tensor(out=xc[:, j, :], in0=xrh[q][:, 3:515], scalar=rgcw[:, l, 3, j:j + 1], in1=t1[q][:],
                                                                  op0=ALU.mult, op1=ALU.add), reads=[b_xrh[q], b_rgc, b_t1[q]], writes=[b_xc[j]])
                    S.op("pool", lambda e: e.tensor_copy(out=xcb[:, j, :], in_=xc[:, j, :]), reads=[b_xc[j]], writes=[b_xcb[j]])
            bands = [st.get(), st.get()]
            for j in range(10):
                q = j % 2
                ins = [i for i in (j - 1, j, j + 1) if 0 <= i < 10]
                for gidx, (dst, bdst, bias_t) in enumerate(((gr, b_gr, rgba), (gi_, b_gi, rgbx))):
                    pt, bp = ps_next()
                    bv, b_band = bands[gidx]
                    for n_, i in enumerate(ins):
                        S.op("pe", lambda e: e.matmul(pt[:], lhsT=bv[:, i * 3 + (j - i + 1), :], rhs=xcb[:, i, :], start=(n_ == 0), stop=(n_ == len(ins) - 1)),
                             reads=[b_band, b_xcb[i]], writes=[bp])
                    S.op("act", lambda e: e.activation(out=dst[q][:], in_=pt[:], func=AF.Sigmoid, bias=bias_t[:, l, j:j + 1]),
                         reads=[bp, b_rgp], writes=[bdst[q]])
                S.op("act", lambda e: e.activation(out=ga[q][:], in_=gr[q][:], func=AF.Exp, scale=rgc8[:, l, j:j + 1]),
                     reads=[b_gr[q], b_rgp], writes=[b_ga[q]])
                S.op("dve", lambda e: e.tensor_tensor(out=gm[q][:], in0=ga[q][:], in1=ga[q][:], op=ALU.mult), reads=[b_ga[q]], writes=[b_gm[q]])
                S.op("dve", lambda e: e.tensor_scalar(out=gm[q][:], in0=gm[q][:], scalar1=-1.0, scalar2=1.0, op0=ALU.mult, op1=ALU.add),
                     reads=[b_gm[q]], writes=[b_gm[q]])
                S.op("act", lambda e: e.activation(out=gm[q][:], in_=gm[q][:], func=AF.Sqrt), reads=[b_gm[q]], writes=[b_gm[q]])
                if ti == 0:
                    S.op("dve", lambda e: e.memset(gm[q][:, 0:1], 1.0), reads=[], writes=[b_gm[q]])
                elif ti == OWN0:
                    S.op("dve", lambda e: e.tensor_scalar(out=gm[q][:, 0:1], in0=gm[q][:, 0:1], scalar1=flag_sb[:, 1:2], scalar2=None, op0=ALU.max),
                         reads=[b_gm[q], b_flag], writes=[b_gm[q]])
                    S.op("dve", lambda e: e.tensor_scalar(out=rg_hc[:, l, j:j + 1], in0=rg_hc[:, l, j:j + 1], scalar1=flag_sb[:, 0:1], scalar2=None, op0=ALU.mult),
                         reads=[b_rghc[l][j], b_flag], writes=[b_rghc[l][j]])
                S.op("dve", lambda e: e.tensor_tensor(out=gi_[q][:], in0=gi_[q][:], in1=xc[:, j, :], op=ALU.mult), reads=[b_gi[q], b_xc[j]], writes=[b_gi[q]])
                S.op("dve", lambda e: e.tensor_tensor(out=gi_[q][:], in0=gi_[q][:], in1=gm[q][:], op=ALU.mult), reads=[b_gi[q], b_gm[q]], writes=[b_gi[q]])
                if smp:
                    S.op("dve", lambda e: e.memset(ga[q][:, 2:18:4], 0.0), reads=[], writes=[b_ga[q]])
                    S.op("dve", lambda e: e.tensor_copy(out=gi_[q][:, 2:18:4], in_=s_h0[:, j, :]), reads=[b_sst], writes=[b_gi[q]])
                S.op("dve", lambda e: e.tensor_tensor_scan(out=hs[q][:], data0=ga[q][:], data1=gi_[q][:], initial=rg_hc[:, l, j:j + 1], op0=ALU.mult, op1=ALU.add),
                     reads=[b_ga[q], b_gi[q], b_rghc[l][j]], writes=[b_hs[q]])
                if smp:
                    S.op("dve", lambda e: e.tensor_copy(out=so_rgh[:, j, :], in_=hs[q][:, 3:19:4]), reads=[b_hs[q]], writes=[b_so])
                S.op("dve", lambda e: e.tensor_copy(out=rg_hc[:, l, j:j + 1], in_=hs[q][:, 511:512]), reads=[b_hs[q]], writes=[b_rghc[l][j]])
                S.op("dve", lambda e: e.tensor_tensor(out=y[:, j, :], in0=y[:, j, :], in1=hs[q][:], op=ALU.mult), reads=[b_y[j], b_hs[q]], writes=[b_y[j]])
            for oc in range(8):
                if oc % 2 == 0:
                    pnl, bpn = st.get()
                pt, bp = ps_next()
                for k in range(10):
                    S.op("pe", lambda e: e.matmul(pt[:], lhsT=pnl[:, k, (oc % 2) * 128:(oc % 2 + 1) * 128], rhs=y[:, k, :], start=(k == 0), stop=(k == 9)),
                         reads=[bpn, b_y[k]], writes=[bp])
                S.op("dve", lambda e: e.tensor_tensor(out=x[:, oc, :], in0=x[:, oc, :], in1=pt[:], op=ALU.add), reads=[b_x[oc], bp], writes=[b_x[oc]])
            if smp:
                S.dma("sp", o_s_rgc_new[:, l], so_rgc[:], reads=[b_so])
                S.dma("sp", o_s_rgh[:, l], so_rgh[:], reads=[b_so])

        def ffn_ple(l, ti, es2, smp=False):
            def sb2(name, shape, dt=F32):
                uid[0] += 1
                return es2.enter_context(nc.sbuf_tensor(f"s{uid[0]}_{name}", list(shape), dt))
            gbuf = sb2("ff_g", [128, 24, 512], BF16); b_g = [Buf() for _ in range(24)]
            uh = [sb2(f"ff_uh{i}", [128, 514]) for i in range(4)]; b_uh = [Buf() for _ in range(4)]
            uc = [sb2(f"ff_uc{i}", [128, 512]) for i in range(4)]; b_uc = [Buf() for _ in range(4)]
            pT = sb2("ple_pT", [128, 2, 512], BF16); b_pT = [Buf(), Buf()]
            sig = [sb2(f"ple_sig{i}", [128, 512]) for i in range(2)]; b_sig = [Buf(), Buf()]

            if smp:
                s_ffh = sb2("s_ffh", [128, 48, 4, 2]); b_sst = Buf()
                so_ffc = sb2("so_ffc", [128, 48, 4]); b_so = Buf()
                S.dma("sp", s_ffh[:], s_ffh_d[:, l], writes=[b_sst])
            rmsnorm(4 + l, h, b_h)
            items = []
            for c4 in range(6):
                items += [(W[f"up{l}"], c4), (W[f"up{l}"], c4 + 6)]
            items += [(W[f"down{l}"], p) for p in range(8)]
            items += [(W[f"plein{l}"], 0), (W[f"pleg{l}"], 0), (W[f"pleg{l}"], 1)]
            st = Stream(items)
            if ti == OWN0:
                for cc in range(48):
                    S.op("pool", lambda e: e.tensor_scalar(out=ff_hist[:, l, cc, :], in0=ff_hist[:, l, cc, :], scalar1=flag_sb[:, 0:1], scalar2=None, op0=ALU.mult),
                         reads=[b_ffhist[l][cc], b_flag], writes=[b_ffhist[l][cc]])
            for c4 in range(6):
                pA = st.get()
                pB = st.get()
                for ci in range(4):
                    c = c4 * 4 + ci
                    for hf, (pnl, bpn) in enumerate((pA, pB)):
                        cc = c + 24 * hf
                        q = (c % 2) * 2 + hf
                        pt, bp = proj_chunk(pnl, bpn, ci * 128, 128, h, b_h, 8)
                        S.op("act", lambda e: e.copy(out=uh[q][:, 2:514], in_=pt[:]), reads=[bp], writes=[b_uh[q]])
                        S.op("pool", lambda e: e.tensor_copy(out=uh[q][:, 0:2], in_=ff_hist[:, l, cc, :]), reads=[b_ffhist[l][cc]], writes=[b_uh[q]])
                        S.op("pool", lambda e: e.tensor_copy(out=ff_hist[:, l, cc, :], in_=uh[q][:, 512:514]), reads=[b_uh[q]], writes=[b_ffhist[l][cc]])
                        if smp:
                            S.op("dve", lambda e: e.tensor_copy(out=so_ffc[:, cc, :], in_=uh[q][:, 5:21:4]), reads=[b_uh[q]], writes=[b_so])
                            S.op("dve", lambda e: e.tensor_copy(out=uh[q][:, 2:18].rearrange("p (b k) -> p b k", k=4)[:, :, 1:3], in_=s_ffh[:, cc, :, :]),
                                 reads=[b_sst], writes=[b_uh[q]])
                        S.op("dve", lambda e: e.tensor_scalar(out=uc[q][:], in0=uh[q][:, 0:512], scalar1=ffcw[:, l, 0, cc:cc + 1], scalar2=ffcb[:, l, cc:cc + 1],
                                                               op0=ALU.mult, op1=ALU.add), reads=[b_uh[q], b_ffc], writes=[b_uc[q]])
                        for k in (1, 2):
                            S.op("dve", lambda e: e.scalar_tensor_tensor(out=uc[q][:], in0=uh[q][:, k:k + 512], scalar=ffcw[:, l, k, cc:cc + 1], in1=uc[q][:],
                                                                          op0=ALU.mult, op1=ALU.add), reads=[b_uh[q], b_ffc, b_uc[q]], writes=[b_uc[q]])
                    q0 = (c % 2) * 2
                    S.op("act", lambda e: e.activation(out=uc[q0][:], in_=uc[q0][:], func=AF.Gelu), reads=[b_uc[q0]], writes=[b_uc[q0]])
                    S.op("dve", lambda e: e.tensor_tensor(out=gbuf[:, c, :], in0=uc[q0][:], in1=uc[q0 + 1][:], op=ALU.mult),
                         reads=[b_uc[q0], b_uc[q0 + 1]], writes=[b_g[c]])
            for oc in range(8):
                pnl, bpn = st.get()
                pt, bp = ps_next()
                for k in range(24):
                    S.op("pe", lambda e: e.matmul(pt[:], lhsT=pnl[:, k, :], rhs=gbuf[:, k, :], start=(k == 0), stop=(k == 23)),
                         reads=[bpn, b_g[k]], writes=[bp])
                S.op("dve", lambda e: e.tensor_tensor(out=x[:, oc, :], in0=x[:, oc, :], in1=pt[:], op=ALU.add), reads=[b_x[oc], bp], writes=[b_x[oc]])
            if smp:
                S.dma("sp", o_s_ffc_new[:, l], so_ffc[:], reads=[b_so])
            if smp:
                S.dma("pool", pT[:], ps_s[l], writes=b_pT)
            else:
                transpose_in(ps_in[l, ti * 512:(ti + 1) * 512, :], 256, pT, b_pT)
            rmsnorm(8 + l, h, b_h)
            pin, bpin = st.get()
            pg = [st.get(), st.get()]
            for oc in range(8):
                q = oc % 2
                pe_, bpe = proj_chunk(pin, bpin, oc * 128, 128, pT, b_pT, 2)
                pgp, bpg = pg[oc // 4]
                pt, bp = proj_chunk(pgp, bpg, (oc % 4) * 128, 128, h, b_h, 8)
                S.op("act", lambda e: e.activation(out=sig[q][:], in_=pt[:], func=AF.Sigmoid), reads=[bp], writes=[b_sig[q]])
                S.op("dve", lambda e: e.tensor_tensor(out=sig[q][:], in0=sig[q][:], in1=pe_[:], op=ALU.mult), reads=[b_sig[q], bpe], writes=[b_sig[q]])
                S.op("dve", lambda e: e.tensor_tensor(out=x[:, oc, :], in0=x[:, oc, :], in1=sig[q][:], op=ALU.add), reads=[b_x[oc], b_sig[q]], writes=[b_x[oc]])


        def attn_layer(l, ti, es2, qbs, smp=False):
            j = l - 2

            def sb2(name, shape, dt=F32):
                uid[0] += 1
                return es2.enter_context(nc.sbuf_tensor(f"s{uid[0]}_{name}", list(shape), dt))
            qT = sb2("qT", [128, 8, 512], BF16); b_qT = [Buf() for _ in range(8)]
            qrT = sb2("qrT", [128, 8, 512], BF16); b_qrT = [Buf() for _ in range(8)]
            gT = sb2("gT", [128, 512], BF16); b_gT = Buf()
            oT = sb2("oT", [128, 8, 512], BF16); b_oT = Buf()
            rl = sb2("rl", [128, 1024]); b_rl = Buf()
            rc_t = rl[:, 0:512]; rs_t = rl[:, 512:1024]; b_rope = b_rl
            rt1 = sb2("rt1", [128, 512]); rt2 = sb2("rt2", [128, 512]); b_rt1 = Buf(); b_rt2 = Buf()
            if smp:
                qbs = []
            PW = 8 if smp else 1024
            qpad = [sb2(f"qpad{i}", [128, 8, 128 if not smp else 2], BF16) for i in range(2)]; b_qpad = [Buf(), Buf()]
            qrpad = [sb2(f"qrpad{i}", [128, 8, 128 if not smp else 2], BF16) for i in range(2)]; b_qrpad = [Buf(), Buf()]
            cbt = [sb2(f"cbt{i}", [128, 256], BF16) for i in range(2)]; b_cbt = [Buf(), Buf()]
            wbt = [sb2(f"wbt{i}", [128, 640], BF16) for i in range(2)]; b_wbt = [Buf(), Buf()]
            vat = [sb2(f"vat{i}", [128, 2, 64]) for i in range(2)]; b_vat = [Buf(), Buf()]
            Pc = sb2("Pc", [128, 2, PW], BF16); b_Pc = [Buf(), Buf()]
            Pn = Pc; b_Pn = b_Pc
            sqf = sq[:].rearrange("p k t -> p (k t)")
            Pst = [sqf[:, i * 1024:(i + 1) * 1024] for i in range(2)]; b_Pst = [Buf() for _ in range(2)]
            pst_i = [0]
            osb = [sb2(f"osb{i}", [64, PW]) for i in range(3)]; b_osb = [Buf() for _ in range(3)]
            rlx = sb2("rlx", [64, PW]); b_rlx = Buf()
            impf = sb2("impf", [128, 64]); b_impf = Buf()
            impt = sb2("impt", [128, 64]); b_impt = Buf()
            m8 = sb2("m8", [128, 16]); b_m8 = Buf()
            thr = sb2("thr", [128, 1]); b_thr = Buf()
            nbf = sb2("nbf", [128, 64]); b_nbf = Buf()
            nb = sb2("nb", [128, 64], BF16); b_nb = Buf()
            nbd = sb2("nbd", [128, 128], BF16); b_nbd = Buf()
            nbe = [sb2(f"nbe{i}", [128, 128], BF16) for i in range(4)]; b_nbe = [Buf() for _ in range(4)]
            nbe_i = [0]
            for t_ in qpad + qrpad:
                S.op("pool", lambda e: e.memset(t_[:], 0.0), writes=b_qpad + b_qrpad)

            S.dma("sp", rc_t, ropec_s if smp else ropec[:, ti * 512:(ti + 1) * 512], writes=[b_rope])
            S.dma("sp", rs_t, ropes_s if smp else ropes[:, ti * 512:(ti + 1) * 512], writes=[b_rope])
            rmsnorm(l, h, b_h)
            for bq in b_Pst:
                bq.w = b_sq.w
                bq.r = list(b_sq.r)
            st = Stream([(W[f"q{j}"], 0), (W[f"qsw{j}"], 0), (W[f"q{j}"], 1), (W[f"qsw{j}"], 1), (W[f"qg{j}"], 0), (W[f"wo{j}"], 0), (W[f"wo{j}"], 1)], depth=2)
            for c in range(8):
                if c % 4 == 0:
                    pq, bpq = st.get()
                    psw, bpsw = st.get()
                pt, bp = proj_chunk(pq, bpq, (c % 4) * 128, 128, h, b_h, 8)
                pt2, bp2 = proj_chunk(psw, bpsw, (c % 4) * 128, 128, h, b_h, 8)
                S.op("dve", lambda e: e.tensor_copy(out=qT[:, c, :], in_=pt[:]), reads=[bp], writes=[b_qT[c]])
                S.op("dve", lambda e: e.tensor_tensor(out=rt1[:], in0=pt[:], in1=rc_t, op=ALU.mult), reads=[bp, b_rope], writes=[b_rt1])
                S.op("dve", lambda e: e.tensor_tensor(out=rt2[:], in0=pt2[:], in1=rs_t, op=ALU.mult), reads=[bp2, b_rope], writes=[b_rt2])
                S.op("dve", lambda e: e.tensor_tensor(out=qrT[:, c, :], in0=rt1[:], in1=rt2[:], op=ALU.add), reads=[b_rt1, b_rt2], writes=[b_qrT[c]])
            pg, bpg = st.get()
            pt, bp = proj_chunk(pg, bpg, 0, 128, h, b_h, 8)
            S.op("act", lambda e: e.activation(out=gT[:], in_=pt[:], func=AF.Sigmoid), reads=[bp], writes=[b_gT])
            identb4 = ident_bf[:].unsqueeze(1).broadcast_to([128, 4, 128])

            def scores(lhs_bias, bias_bufs, kmat, kbufs, qp, bqp):
                pp, bpp = ps_pair()
                for bank in range(2):
                    S.op("pe", lambda e: e.matmul(pp[:, bank * 512:(bank + 1) * 512], lhsT=lhs_bias, rhs=identb4, start=True, stop=False),
                         reads=bias_bufs + [b_identb], writes=[bpp[bank]])
                    for h4 in range(4):
                        hh = bank * 4 + h4
                        S.op("pe", lambda e: e.matmul(pp[:, hh * 128:(hh + 1) * 128], lhsT=kmat, rhs=qp[:, hh, :], start=False, stop=(h4 == 3)),
                             reads=kbufs + [bqp], writes=[bpp[bank]])
                return pp, bpp

            def pv_acc(vmat, vbufs, pt_, bpt_, first, last):
                for bank in range(2):
                    S.op("pe", lambda e: e.matmul(acc_ps[:, bank * 512:(bank + 1) * 512], lhsT=vmat, rhs=pt_[:, bank * 512:(bank + 1) * 512], start=first, stop=last),
                         reads=vbufs + [bpt_], writes=[b_acc[bank]])

            if len(qbs) < 4:
                S.op("pool", lambda e: e.memset(oT[:], 0.0), writes=[b_oT])
            for qb in (qbs if DBG >= 5 else []):
                qbo = (ti - OWN0) * 4 + qb + 1
                kd = 4 * ti + qb
                tb = qbo % 2
                S.dma("pool", cbt[tb][:], t_cb[qbo], writes=[b_cbt[tb]])
                S.dma("pool", wbt[tb][:], t_wb[qbo], writes=[b_wbt[tb]])
                S.dma("sp", vat[tb][:], t_va[qbo], writes=[b_vat[tb]])
                tsl = slice(qb * 128, (qb + 1) * 128)
                for g in range(2):
                    pi = g
                    S.op("pool", lambda e: e.tensor_copy(out=qpad[pi][0:64, 0:8:2, :], in_=qT[0:64, 4 * g:4 * g + 4, tsl]), reads=b_qT[4 * g:4 * g + 4], writes=[b_qpad[pi]])
                    S.op("pool", lambda e: e.tensor_copy(out=qpad[pi][64:128, 1:8:2, :], in_=qT[64:128, 4 * g:4 * g + 4, tsl]), reads=b_qT[4 * g:4 * g + 4], writes=[b_qpad[pi]])
                    S.op("pool", lambda e: e.tensor_copy(out=qrpad[pi][0:64, 0:8:2, :], in_=qrT[0:64, 4 * g:4 * g + 4, tsl]), reads=b_qrT[4 * g:4 * g + 4], writes=[b_qrpad[pi]])
                    S.op("pool", lambda e: e.tensor_copy(out=qrpad[pi][64:128, 1:8:2, :], in_=qrT[64:128, 4 * g:4 * g + 4, tsl]), reads=b_qrT[4 * g:4 * g + 4], writes=[b_qrpad[pi]])
                    nbc = 1 if 32 * ti + 31 < 128 else 2
                    for bc in range(nbc):
                        pp, bpp = scores(cbt[tb][:, bc * 128:(bc + 1) * 128], [b_cbt[tb]], kcK[:, g, bc * 128:(bc + 1) * 128], [b_kcK], qpad[pi], b_qpad[pi])
                        S.op("act", lambda e: e.activation(out=Pc[:, bc, :], in_=pp[:], func=AF.Exp, scale=0.125), reads=bpp, writes=[b_Pc[bc]])
                    lp, blp = ps_pair()
                    for bank in range(2):
                        for bc in range(nbc):
                            S.op("pe", lambda e: e.matmul(lp[:, bank * 512:(bank + 1) * 512], lhsT=ones1[:], rhs=Pc[:, bc, bank * 512:(bank + 1) * 512],
                                                          start=(bc == 0), stop=(bc == nbc - 1)), reads=[b_ones1, b_Pc[bc]], writes=[blp[bank]])
                    S.op("dve", lambda e: e.tensor_scalar(out=rl[:], in0=lp[:], scalar1=1e-30, scalar2=None, op0=ALU.add), reads=blp, writes=[b_rl])
                    S.op("dve", lambda e: e.reciprocal(out=rl[:], in_=rl[:]), reads=[b_rl], writes=[b_rl])
                    for bc in range(nbc):
                        S.op("dve", lambda e: e.tensor_tensor(out=Pn[:, bc, :], in0=Pc[:, bc, :], in1=rl[:], op=ALU.mult), reads=[b_Pc[bc], b_rl], writes=[b_Pn[bc]])
                    for bc in range(nbc):
                        pv_acc(vcVb[:, bc, g, :], [b_vcVb], Pn[:, bc, :], b_Pn[bc], bc == 0, bc == nbc - 1)
                    S.op("act", lambda e: e.copy(out=osb[0][:], in_=acc_ps[0:64, :]), reads=b_acc, writes=[b_osb[0]])
                    ip, bip = ps_next()
                    n_ = 0
                    for bc in range(nbc):
                        for hh in range(8):
                            S.op("pe", lambda e: e.matmul(ip[:, 0:64], lhsT=Pn[:, bc, hh * 128:(hh + 1) * 128], rhs=mcs[:, bc, :], start=(n_ == 0), stop=(n_ == 8 * nbc - 1)),
                                 reads=[b_Pn[bc], b_mcs], writes=[bip])
                            n_ += 1
                    S.op("dve", lambda e: e.tensor_tensor(out=impf[:], in0=ip[:, 0:64], in1=vat[tb][:, 0, :], op=ALU.mult), reads=[bip, b_vat[tb]], writes=[b_impf])
                    S.op("dve", lambda e: e.tensor_tensor(out=impf[:], in0=impf[:], in1=vat[tb][:, 1, :], op=ALU.add), reads=[b_impf, b_vat[tb]], writes=[b_impf])
                    S.op("dve", lambda e: e.max(out=m8[:, 0:8], in_=impf[:]), reads=[b_impf], writes=[b_m8])
                    S.op("dve", lambda e: e.match_replace(out=impt[:], in_to_replace=m8[:, 0:8], in_values=impf[:], imm_value=-2.0), reads=[b_impf, b_m8], writes=[b_impt])
                    S.op("dve", lambda e: e.max(out=m8[:, 8:16], in_=impt[:]), reads=[b_impt], writes=[b_m8])
                    S.op("dve", lambda e: e.tensor_scalar(out=thr[:], in0=m8[:, 15:16], scalar1=-0.5, scalar2=None, op0=ALU.max), reads=[b_m8], writes=[b_thr])
                    S.op("dve", lambda e: e.tensor_scalar(out=nbf[:], in0=impf[:], scalar1=thr[:, 0:1], scalar2=BIGV, op0=ALU.is_ge, op1=ALU.mult),
                         reads=[b_impf, b_thr], writes=[b_nbf])
                    S.op("dve", lambda e: e.tensor_scalar(out=nb[:], in0=nbf[:], scalar1=-BIGV, scalar2=None, op0=ALU.add), reads=[b_nbf], writes=[b_nb])
                    S.op("dve", lambda e: e.tensor_tensor(out=nbd[:].rearrange("p (s k) -> p s k", s=2), in0=nb[:, 2 * kd:2 * kd + 2].unsqueeze(2).broadcast_to([128, 2, 64]),
                                                           in1=tri[:].rearrange("p (s k) -> p s k", s=2), op=ALU.add), reads=[b_nb, b_tri], writes=[b_nbd])
                    def slc_front(kc):
                        if kc == kd:
                            lb, lbb = nbd[:], [b_nbd]
                        else:
                            z = nbe_i[0] % 4
                            nbe_i[0] += 1
                            S.op("dve", lambda e: e.tensor_copy(out=nbe[z][:].rearrange("p (s k) -> p s k", s=2),
                                                                 in_=nb[:, 2 * kc:2 * kc + 2].unsqueeze(2).broadcast_to([128, 2, 64])), reads=[b_nb], writes=[b_nbe[z]])
                            lb, lbb = nbe[z][:], [b_nbe[z]]
                        pp, bpp = scores(lb, lbb, slcK[:, g, kc * 128:(kc + 1) * 128], [b_slcK], qrpad[pi], b_qrpad[pi])
                        pz = pst_i[0] % 2
                        pst_i[0] += 1
                        S.op("act", lambda e: e.activation(out=Pst[pz], in_=pp[:], func=AF.Exp, scale=0.125), reads=bpp, writes=[b_Pst[pz]])
                        return pz
                    cur = slc_front(0)
                    for kc in range(kd + 1):
                        nxt = slc_front(kc + 1) if kc < kd else None
                        pv_acc(slcV[:, kc, g, :], [b_slcV], Pst[cur], b_Pst[cur], kc == 0, kc == kd)
                        cur = nxt
                    S.op("dve", lambda e: e.reciprocal(out=rlx[:], in_=acc_ps[64:128, :]), reads=b_acc, writes=[b_rlx])
                    S.op("dve", lambda e: e.tensor_tensor(out=osb[1][:], in0=acc_ps[0:64, :], in1=rlx[:], op=ALU.mult), reads=b_acc + [b_rlx], writes=[b_osb[1]])
                    def win_front(wi):
                        rch = (kd - 4 + wi) % 8
                        pp, bpp = scores(wbt[tb][:, wi * 128:(wi + 1) * 128], [b_wbt[tb]], winK[:, g, rch * 128:(rch + 1) * 128], [b_winK], qrpad[pi], b_qrpad[pi])
                        pz = pst_i[0] % 2
                        pst_i[0] += 1
                        S.op("act", lambda e: e.activation(out=Pst[pz], in_=pp[:], func=AF.Exp, scale=0.125), reads=bpp, writes=[b_Pst[pz]])
                        return pz
                    cur = win_front(0)
                    for wi in range(5):
                        nxt = win_front(wi + 1) if wi < 4 else None
                        rch = (kd - 4 + wi) % 8
                        pv_acc(winV[:, rch, g, :], [b_winV], Pst[cur], b_Pst[cur], wi == 0, wi == 4)
                        cur = nxt
                    S.op("dve", lambda e: e.reciprocal(out=rlx[:], in_=acc_ps[64:128, :]), reads=b_acc, writes=[b_rlx])
                    S.op("dve", lambda e: e.tensor_tensor(out=osb[2][:], in0=acc_ps[0:64, :], in1=rlx[:], op=ALU.mult), reads=b_acc + [b_rlx], writes=[b_osb[2]])
                    for br in range(3):
                        gp, bgp = ps_pair()
                        for hh in range(8):
                            f0 = 3 * (8 * g + hh) + br
                            S.op("pe", lambda e: e.matmul(gp[:, hh * 128:(hh + 1) * 128], lhsT=ident_bf[:, f0:f0 + 1].broadcast_to([128, 128]), rhs=gT[:, tsl], start=True, stop=True),
                                 reads=[b_identb, b_gT], writes=[bgp[hh // 4]])
                        S.op("dve", lambda e: e.tensor_tensor(out=osb[br][:], in0=osb[br][:], in1=gp[0:64, :], op=ALU.mult), reads=[b_osb[br]] + bgp, writes=[b_osb[br]])
                    S.op("dve", lambda e: e.tensor_tensor(out=osb[0][:], in0=osb[0][:], in1=osb[1][:], op=ALU.add), reads=[b_osb[0], b_osb[1]], writes=[b_osb[0]])
                    av = osb[0][:].rearrange("p (h t) -> p h t", h=8)
                    tv = osb[2][:].rearrange("p (h t) -> p h t", h=8)
                    S.op("dve", lambda e: e.tensor_tensor(out=oT[0:64, 4 * g:4 * g + 4, tsl], in0=av[:, 0:8:2, :], in1=tv[:, 0:8:2, :], op=ALU.add),
                         reads=[b_osb[0], b_osb[2]], writes=[b_oT])
                    S.op("dve", lambda e: e.tensor_tensor(out=oT[64:128, 4 * g:4 * g + 4, tsl], in0=av[:, 1:8:2, :], in1=tv[:, 1:8:2, :], op=ALU.add),
                         reads=[b_osb[0], b_osb[2]], writes=[b_oT])

            if smp:
                S.op("pool", lambda e: e.memset(oT[:], 0.0), writes=[b_oT])
                QS = sb2("QS", [128, 16], BF16); QRS = sb2("QRS", [128, 16], BF16); b_QS = Buf()
                S.op("pool", lambda e: e.memset(QS[:], 0.0), writes=[b_QS])
                S.op("pool", lambda e: e.memset(QRS[:], 0.0), writes=[b_QS])
                PcS = sb2("PcS", [128, 4, 16], BF16); b_PcS = Buf()
                PnG = sb2("PnG", [128, 4, 2], BF16); b_PnG = Buf()
                PnGf = sb2("PnGf", [128, 4, 2]); b_PnGf = Buf()
                rlS = sb2("rlS", [128, 16]); b_rlS = Buf()
                tpin = sb2("tpin", [128, 2, 128]); b_tpin = Buf()
                S.op("pool", lambda e: e.memset(tpin[:], 0.0), writes=[b_tpin])
                impS = sb2("impS", [128, 256]); impS2 = sb2("impS2", [128, 256]); b_impS = Buf()
                m8s = sb2("m8s", [128, 16]); thrs = sb2("thrs", [128, 1])
                nbS = sb2("nbS", [128, 256], BF16); b_nbS = Buf()
                gb = [sb2(f"gb{i}", [128, 256]) for i in range(4)]; b_gb = [Buf() for _ in range(4)]
                KTc = [sb2(f"KTc{i}", [128, 128], BF16) for i in range(3)]; b_KTc = [Buf() for _ in range(3)]
                wbuf = sb2("wbuf", [128, 4, 256]); b_wbuf = Buf()
                PsS = [sb2(f"PsS{i}", [128, 16], BF16) for i in range(3)]; b_PsS = [Buf() for _ in range(3)]
                GS = sb2("GS", [128, 48, 4]); b_GS = Buf()
                ocS = sb2("ocS", [64, 16]); osS = sb2("osS", [64, 2, 16]); rlS2 = sb2("rlS2", [64, 2, 16]); b_ocS = Buf()
                oS = sb2("oS", [64, 16]); tS = sb2("tS", [64, 16]); b_oS = Buf()
                gpt, bgpt = ps_next()
                for f0 in range(48):
                    S.op("pe", lambda e: e.matmul(gpt[:, f0 * 4:(f0 + 1) * 4], lhsT=ident_bf[:, f0:f0 + 1].broadcast_to([128, 128]), rhs=gT[:, 3:19:4], start=True, stop=True),
                         reads=[b_identb, b_gT], writes=[bgpt])
                S.op("dve", lambda e: e.tensor_copy(out=GS[:].rearrange("p f b -> p (f b)"), in_=gpt[:, 0:192]), reads=[bgpt], writes=[b_GS])
                gi_c = [0]
                kt_c = [0]
                ps_c = [0]
                ne_c = [0]
                for i in range(4):
                    col = 4 * i + 3
                    for g in range(2):
                        for par in range(2):
                            S.op("dve", lambda e: e.tensor_copy(out=QS[64 * g:64 * g + 64, 8 * g + par:8 * g + 8:2], in_=qT[64 * par:64 * par + 64, 4 * g:4 * g + 4, col]),
                                 reads=b_qT[4 * g:4 * g + 4], writes=[b_QS])
                            S.op("dve", lambda e: e.tensor_copy(out=QRS[64 * g:64 * g + 64, 8 * g + par:8 * g + 8:2], in_=qrT[64 * par:64 * par + 64, 4 * g:4 * g + 4, col]),
                                 reads=b_qrT[4 * g:4 * g + 4], writes=[b_QS])
                    for pc in range(4):
                        pp, bpp = ps_next()
                        if pc == 0:
                            S.op("pe", lambda e: e.matmul(pp[:, 0:16], lhsT=cb0, rhs=brhs, start=True, stop=False), reads=[b_stab], writes=[bpp])
                        S.op("pe", lambda e: e.matmul(pp[:, 0:16], lhsT=kcS[:, i, pc * 128:(pc + 1) * 128], rhs=QS[:], start=(pc != 0), stop=True),
                             reads=[b_kcS, b_QS], writes=[bpp])
                        S.op("act", lambda e: e.activation(out=PcS[:, pc, :], in_=pp[:, 0:16], func=AF.Exp, scale=0.125), reads=[bpp], writes=[b_PcS])
                    lp, blp = ps_next()
                    for pc in range(4):
                        S.op("pe", lambda e: e.matmul(lp[:, 0:16], lhsT=ones1[:], rhs=PcS[:, pc, :], start=(pc == 0), stop=(pc == 3)), reads=[b_ones1, b_PcS], writes=[blp])
                    S.op("dve", lambda e: e.tensor_scalar(out=rlS[:], in0=lp[:, 0:16], scalar1=1e-30, scalar2=None, op0=ALU.add), reads=[blp], writes=[b_rlS])
                    S.op("dve", lambda e: e.reciprocal(out=rlS[:], in_=rlS[:]), reads=[b_rlS], writes=[b_rlS])
                    for pc in range(4):
                        S.op("dve", lambda e: e.tensor_tensor(out=PcS[:, pc, :], in0=PcS[:, pc, :], in1=rlS[:], op=ALU.mult), reads=[b_PcS, b_rlS], writes=[b_PcS])
                    for pc in range(4):
                        S.op("pe", lambda e: e.matmul(acc_ps[:, 0:16], lhsT=vcS[:, i, pc, :], rhs=PcS[:, pc, :], start=(pc == 0), stop=(pc == 3)),
                             reads=[b_vcS, b_PcS], writes=[b_acc[0]])
                    S.op("dve", lambda e: e.tensor_copy(out=ocS[:, 0:8], in_=acc_ps[0:64, 0:8]), reads=[b_acc[0]], writes=[b_ocS])
                    S.op("dve", lambda e: e.tensor_copy(out=ocS[:, 8:16], in_=acc_ps[64:128, 8:16]), reads=[b_acc[0]], writes=[b_ocS])
                    for pc in range(4):
                        S.op("dve", lambda e: e.tensor_reduce(out=PnGf[:, pc, :], in_=PcS[:, pc, :].rearrange("p (g h) -> p g h", g=2), axis=mybir.AxisListType.X, op=ALU.add),
                             reads=[b_PcS], writes=[b_PnGf])
                    S.op("dve", lambda e: e.tensor_copy(out=PnG[:], in_=PnGf[:]), reads=[b_PnGf], writes=[b_PnG])
                    ipT, bipT = ps_next()
                    for sc in range(2):
                        for pc in range(4):
                            S.op("pe", lambda e: e.matmul(ipT[:, 2 * sc:2 * sc + 2], lhsT=mcsS[:, pc, sc, :], rhs=PnG[:, pc, :], start=(pc == 0), stop=(pc == 3)),
                                 reads=[b_stab, b_PnG], writes=[bipT])
                    S.op("dve", lambda e: e.tensor_copy(out=tpin[:, :, 0:2], in_=ipT[:, 0:4].rearrange("p (s g) -> p s g", s=2)), reads=[bipT], writes=[b_tpin])
                    tp, btp = ps_next()
                    for sc in range(2):
                        S.op("pe", lambda e: e.transpose(tp[:, sc * 128:(sc + 1) * 128], tpin[:, sc, :], ident[:]), reads=[b_tpin, b_ident], writes=[btp])
                    S.op("dve", lambda e: e.tensor_tensor(out=impS[:], in0=tp[:, 0:256], in1=tas[:, 0, :], op=ALU.mult), reads=[btp, b_stab], writes=[b_impS])
                    S.op("dve", lambda e: e.tensor_tensor(out=impS[:], in0=impS[:], in1=tas[:, 1, :], op=ALU.add), reads=[b_impS, b_stab], writes=[b_impS])
                    S.op("dve", lambda e: e.max(out=m8s[:, 0:8], in_=impS[:]), reads=[b_impS], writes=[b_impS])
                    S.op("dve", lambda e: e.match_replace(out=impS2[:], in_to_replace=m8s[:, 0:8], in_values=impS[:], imm_value=-2.0), reads=[b_impS], writes=[b_impS])
                    S.op("dve", lambda e: e.max(out=m8s[:, 8:16], in_=impS2[:]), reads=[b_impS], writes=[b_impS])
                    S.op("dve", lambda e: e.tensor_scalar(out=thrs[:], in0=m8s[:, 15:16], scalar1=-0.5, scalar2=None, op0=ALU.max), reads=[b_impS], writes=[b_impS])
                    S.op("dve", lambda e: e.tensor_scalar(out=impS2[:], in0=impS[:], scalar1=thrs[:, 0:1], scalar2=BIGV, op0=ALU.is_ge, op1=ALU.mult), reads=[b_impS], writes=[b_impS])
                    S.op("dve", lambda e: e.tensor_scalar(out=nbS[:], in0=impS2[:], scalar1=-BIGV, scalar2=None, op0=ALU.add), reads=[b_impS], writes=[b_nbS])

                    def s_front(bias_l, bias_b, kt, ktb):
                        pp, bpp = ps_next()
                        if bias_l is not None:
                            S.op("pe", lambda e: e.matmul(pp[:, 0:16], lhsT=bias_l, rhs=brhs, start=True, stop=False), reads=bias_b + [b_stab], writes=[bpp])
                        S.op("pe", lambda e: e.matmul(pp[:, 0:16], lhsT=kt, rhs=QRS[:], start=(bias_l is None), stop=True), reads=ktb + [b_QS], writes=[bpp])
                        z = ps_c[0] % 3
                        ps_c[0] += 1
                        S.op("act", lambda e: e.activation(out=PsS[z][:], in_=pp[:, 0:16], func=AF.Exp, scale=0.125), reads=[bpp], writes=[b_PsS[z]])
                        return z

                    def s_pv(z, v0, v1, vb, first, last):
                        for g, vv in enumerate((v0, v1)):
                            S.op("pe", lambda e: e.matmul(acc_ps[:, 512 + 8 * g:512 + 8 * g + 8], lhsT=vv, rhs=PsS[z][:, 8 * g:8 * g + 8],
                                                          start=(first and g == 0), stop=(last and g == 1)),
                                 reads=vb + [b_PsS[z]], writes=[b_acc[1]])

                    def gather(kc):
                        gz = gi_c[0] % 4
                        gi_c[0] += 1
                        S._deps("pool", [b_idxS], [b_gb[gz]])
                        tk = S.dma_ind(gb[gz][:], c_slc, idxS[:, i, kc:kc + 1])
                        S._mark(tk, [b_idxS], [b_gb[gz]])
                        return gz

                    def slc_front_s(kc, gz):
                        if kc == 64:
                            z = s_front(b64[:, i, :], [], KTn[:, 0, :], [b_KTn])
                            return (z, Vn[:, 0, 0, :], Vn[:, 0, 1, :], [b_Vn])
                        zz = ne_c[0] % 4
                        ne_c[0] += 1
                        tq, btq = ps_next()
                        S.op("pe", lambda e: e.transpose(tq[:, 0:128], gb[gz][:, 0:128], ident[:]), reads=[b_gb[gz], b_ident], writes=[btq])
                        kz = kt_c[0] % 3
                        kt_c[0] += 1
                        S.op("act", lambda e: e.copy(out=KTc[kz][:], in_=tq[:, 0:128]), reads=[btq], writes=[b_KTc[kz]])
                        vz = 2 + (kc % 4)
                        S.op("dve", lambda e: e.tensor_copy(out=slcV[:, vz, :, 0:64], in_=gb[gz][:, 128:256].rearrange("p (g d) -> p g d", g=2)),
                             reads=[b_gb[gz]], writes=[b_Vc[kc % 4]])
                        S.op("dve", lambda e: e.tensor_copy(out=nbe[zz][:].rearrange("p (s k) -> p s k", s=2),
                                                             in_=nbS[:, 2 * kc:2 * kc + 2].unsqueeze(2).broadcast_to([128, 2, 64])), reads=[b_nbS], writes=[b_nbe[zz]])
                        z = s_front(nbe[zz][:], [b_nbe[zz]], KTc[kz][:], [b_KTc[kz]])
                        return (z, slcV[:, vz, 0, :], slcV[:, vz, 1, :], [b_Vc[kc % 4]])

                    gq = [gather(kc) for kc in range(3)]
                    cur = slc_front_s(0, gq[0])
                    for kc in range(65):
                        if kc + 3 < 64:
                            gq.append(gather(kc + 3))
                        nxt = slc_front_s(kc + 1, gq[kc + 1] if kc + 1 < 64 else None) if kc < 64 else None
                        s_pv(cur[0], cur[1], cur[2], cur[3], kc == 0, kc == 64)
                        cur = nxt
                    S.op("dve", lambda e: e.reciprocal(out=rlS2[:, 0, :], in_=acc_ps[64:128, 512:528]), reads=[b_acc[1]], writes=[b_ocS])
                    S.op("dve", lambda e: e.tensor_tensor(out=osS[:, 0, :], in0=acc_ps[0:64, 512:528], in1=rlS2[:, 0, :], op=ALU.mult), reads=[b_acc[1], b_ocS], writes=[b_ocS])
                    S.dma("sp", wbuf[:], c_win[i].rearrange("(c p) f -> p c f", p=128), writes=[b_wbuf])

                    def win_front_s(wc):
                        if wc == 4:
                            z = s_front(b64[:, i, :], [], KTn[:, 1, :], [b_KTn])
                            return (z, Vn[:, 1, 0, :], Vn[:, 1, 1, :], [b_Vn])
                        tq, btq = ps_next()
                        S.op("pe", lambda e: e.transpose(tq[:, 0:128], wbuf[:, wc, 0:128], ident[:]), reads=[b_wbuf, b_ident], writes=[btq])
                        kz = kt_c[0] % 3
                        kt_c[0] += 1
                        S.op("act", lambda e: e.copy(out=KTc[kz][:], in_=tq[:, 0:128]), reads=[btq], writes=[b_KTc[kz]])
                        vz = 2 + wc
                        S.op("dve", lambda e: e.tensor_copy(out=slcV[:, vz, :, 0:64], in_=wbuf[:, wc, 128:256].rearrange("p (g d) -> p g d", g=2)),
                             reads=[b_wbuf], writes=[b_Vc[wc]])
                        z = s_front(wb0 if wc == 0 else None, [], KTc[kz][:], [b_KTc[kz]])
                        return (z, slcV[:, vz, 0, :], slcV[:, vz, 1, :], [b_Vc[wc]])
                    cur = win_front_s(0)
                    for wc in range(5):
                        nxt = win_front_s(wc + 1) if wc < 4 else None
                        s_pv(cur[0], cur[1], cur[2], cur[3], wc == 0, wc == 4)
                        cur = nxt
                    S.op("dve", lambda e: e.reciprocal(out=rlS2[:, 1, :], in_=acc_ps[64:128, 512:528]), reads=[b_acc[1]], writes=[b_ocS])
                    S.op("dve", lambda e: e.tensor_tensor(out=osS[:, 1, :], in0=acc_ps[0:64, 512:528], in1=rlS2[:, 1, :], op=ALU.mult), reads=[b_acc[1], b_ocS], writes=[b_ocS])
                    Gv = GS[0:64, :, i].rearrange("p (h r) -> p h r", r=3)
                    S.op("dve", lambda e: e.tensor_tensor(out=oS[:], in0=ocS[:], in1=Gv[:, :, 0], op=ALU.mult), reads=[b_ocS, b_GS], writes=[b_oS])
                    for br in (1, 2):
                        S.op("dve", lambda e: e.tensor_tensor(out=tS[:], in0=osS[:, br - 1, :], in1=Gv[:, :, br], op=ALU.mult), reads=[b_ocS, b_GS, b_oS], writes=[b_oS])
                        S.op("dve", lambda e: e.tensor_tensor(out=oS[:], in0=oS[:], in1=tS[:], op=ALU.add), reads=[b_oS], writes=[b_oS])
                    S.op("dve", lambda e: e.tensor_copy(out=oT[0:64, :, col], in_=oS[:, 0:16:2]), reads=[b_oS], writes=[b_oT])
                    S.op("dve", lambda e: e.tensor_copy(out=oT[64:128, :, col], in_=oS[:, 1:16:2]), reads=[b_oS], writes=[b_oT])
            for oc in range(8):
                if oc % 4 == 0:
                    pnl, bpn = st.get()
                pt, bp = ps_next()
                for k in range(8):
                    S.op("pe", lambda e: e.matmul(pt[:], lhsT=pnl[:, k, (oc % 4) * 128:(oc % 4 + 1) * 128], rhs=oT[:, k, :], start=(k == 0), stop=(k == 7)),
                         reads=[bpn, b_oT], writes=[bp])
                S.op("dve", lambda e: e.tensor_tensor(out=x[:, oc, :], in0=x[:, oc, :], in1=pt[:], op=ALU.add), reads=[b_x[oc], bp], writes=[b_x[oc]])

        def final_out(ti, es2, smp=False):
            uid[0] += 1
            yf = es2.enter_context(nc.sbuf_tensor(f"s{uid[0]}_yf", [128, 8, 512], F32)); b_yf = Buf()
            rmsnorm_f32(12, yf, b_yf)
            if smp:
                uid[0] += 1
                yo = es2.enter_context(nc.sbuf_tensor(f"s{uid[0]}_yo", [128, 8, 4], F32)); b_yo = Buf()
                S.op("dve", lambda e: e.tensor_copy(out=yo[:], in_=yf[:, :, 3:19:4]), reads=[b_yf], writes=[b_yo])
                S.dma("sp", o_s_y, yo[:], reads=[b_yo])
                return
            for blk in range(4):
                for kh in range(2):
                    pt, bp = ps_next()
                    for k4 in range(4):
                        k = kh * 4 + k4
                        S.op("pe", lambda e: e.transpose(pt[:, k4 * 128:(k4 + 1) * 128], yf[:, k, blk * 128:(blk + 1) * 128], ident[:]),
                             reads=[b_yf, b_ident], writes=[bp])
                    S.op("act", lambda e: e.copy(out=xt[:, blk, kh * 512:(kh + 1) * 512], in_=pt[:]), reads=[bp], writes=[b_xt])
            r0 = (ti - OWN0) * 512
            S.dma("sp", o_y[r0:r0 + 512, :].rearrange("(b p) f -> p b f", p=128), xt[:], reads=[b_xt])

        def rmsnorm_f32(gi, yf, b_yf):
            for k in range(8):
                S.op("act", lambda e: e.activation(out=sq[:, k, :], in_=x[:, k, :], func=AF.Square), reads=[b_x[k]], writes=[b_sq])
            pt, bp = ps_next()
            for k in range(8):
                S.op("pe", lambda e: e.matmul(pt[:], lhsT=ones_bf[:], rhs=sq[:, k, :], start=(k == 0), stop=(k == 7)),
                     reads=[b_sq, b_ones], writes=[bp])
            S.op("act", lambda e: e.activation(out=rstd[:], in_=pt[:], func=AF.Sqrt, bias=epsb[:, 0:1]), reads=[bp, b_eps], writes=[b_rstd])
            S.op("dve", lambda e: e.reciprocal(out=rstd[:], in_=rstd[:]), reads=[b_rstd], writes=[b_rstd])
            for k in range(8):
                S.op("dve", lambda e: e.scalar_tensor_tensor(out=yf[:, k, :], in0=x[:, k, :], scalar=gvec[:, gi, k:k + 1], in1=rstd[:],
                                                              op0=ALU.mult, op1=ALU.mult),
                     reads=[b_x[k], b_gvec, b_rstd], writes=[b_yf])


        def compress_round(sb2, cT, b_cT, P0, kdst, vdst, w2vs, pre=None):
            if pre is None:
                hid0 = sb2("hid0", [128, 32], BF16); b_hid0 = Buf()
                hidp = sb2("hidp", [128, 128], BF16); b_hidp = Buf()
            else:
                hid0, b_hid0, hidp, b_hidp = pre
            off = P0 % 128
            stc = Stream([(W["w1_00"], 0), (W["w1_01"], 0), (W["w1_10"], 0), (W["w1_11"], 0)], depth=2)
            for jv in range(2):
                for g in range(2):
                    pnl, bpn = stc.get()
                    pt, bp = ps_next()
                    for l_ in range(32):
                        S.op("pe", lambda e: e.matmul(pt[:, 0:32], lhsT=pnl[:, l_, :], rhs=cT[:, jv, l_:l_ + 497:16], start=(l_ == 0), stop=(l_ == 31)),
                             reads=[bpn, b_cT], writes=[bp])
                    if jv == 0:
                        S.op("act", lambda e: e.activation(out=hid0[:], in_=pt[:, 0:32], func=AF.Gelu, bias=cb1[:, 0:1]), reads=[bp, b_cb1], writes=[b_hid0])
                        pt2, bp2 = ps_next()
                        S.op("pe", lambda e: e.matmul(pt2[:, 0:32], lhsT=w2k[:], rhs=hid0[:], start=True, stop=True), reads=[b_w2, b_hid0], writes=[bp2])
                        kdst(g, pt2, bp2)
                    else:
                        S.op("dve", lambda e: e.memset(hidp[:], 0.0), writes=[b_hidp])
                        S.op("act", lambda e: e.activation(out=hidp[:, off:off + 32], in_=pt[:, 0:32], func=AF.Gelu, bias=cb1[:, 1:2]), reads=[bp, b_cb1], writes=[b_hidp])
                        pt2, bp2 = ps_next()
                        S.op("pe", lambda e: e.matmul(pt2[:, 0:128], lhsT=hidp[:], rhs=w2vs[g], start=True, stop=True), reads=[b_w2, b_hidp], writes=[bp2])
                        vdst(g, pt2, bp2)

        kvsw = Wt(nc, "kvsw", None, D, 256, 256)
        W["kvsw"] = kvsw
        tks = []
        with nc.allow_non_contiguous_dma(reason="one-time rope column swap"):
            for c, cb in enumerate((256, 512)):
                for k in range(8):
                    srcv = w_kv[k * 128:(k + 1) * 128, cb:cb + 128].rearrange("p (g hh dd) -> p g hh dd", g=2, hh=2)
                    dstv = kvsw.dst[0].rearrange("p (k c g hh dd) -> p k c g hh dd", k=8, c=2, g=2, hh=2)
                    for hh in range(2):
                        tks.append(S.dma("pool", dstv[:, k, c, :, hh, :], srcv[:, :, 1 - hh, :]))
        kvsw.buf.w = tks
        def kv_gen(ti, es2, smp=False):
            def sb2(name, shape, dt=F32):
                uid[0] += 1
                return es2.enter_context(nc.sbuf_tensor(f"s{uid[0]}_{name}", list(shape), dt))
            kvf = sb2("kvf", [128, 6, 512]); b_kvf = [Buf() for _ in range(6)]
            rc_t = sb2("rc_t", [128, 512]); rs_t = sb2("rs_t", [128, 512]); b_rope = Buf()
            rt1 = sb2("rt1", [128, 512]); b_rt1 = Buf()
            S.dma("sp", rc_t[:], ropec_s if smp else ropec[:, ti * 512:(ti + 1) * 512], writes=[b_rope])
            S.dma("sp", rs_t[:], ropes_s if smp else ropes[:, ti * 512:(ti + 1) * 512], writes=[b_rope])
            rmsnorm(13, h, b_h)
            st = Stream([(W["kv"], 0), (W["kv"], 1), (W["kv"], 2), (W["kvsw"], 0)], depth=3)
            pk = [st.get(), st.get(), st.get()]
            psw, bpsw = st.get()
            for c in range(6):
                pnl, bpn = pk[c // 2]
                pt, bp = proj_chunk(pnl, bpn, (c % 2) * 128, 128, h, b_h, 8)
                if c in (2, 4):
                    pt2, bp2 = proj_chunk(psw, bpsw, (c // 2 - 1) * 128, 128, h, b_h, 8)
                    S.op("dve", lambda e: e.tensor_tensor(out=kvf[:, c, :], in0=pt[:], in1=rc_t[:], op=ALU.mult), reads=[bp, b_rope], writes=[b_kvf[c]])
                    S.op("dve", lambda e: e.tensor_tensor(out=rt1[:], in0=pt2[:], in1=rs_t[:], op=ALU.mult), reads=[bp2, b_rope], writes=[b_rt1])
                    S.op("dve", lambda e: e.tensor_tensor(out=kvf[:, c, :], in0=kvf[:, c, :], in1=rt1[:], op=ALU.add), reads=[b_kvf[c], b_rt1], writes=[b_kvf[c]])
                else:
                    S.op("act", lambda e: e.copy(out=kvf[:, c, :], in_=pt[:]), reads=[bp], writes=[b_kvf[c]])
            if smp:
                kvo = sb2("kvo", [128, 6, 4]); b_kvo = Buf()
                S.op("dve", lambda e: e.tensor_copy(out=kvo[:], in_=kvf[:, :, 3:19:4]), reads=b_kvf, writes=[b_kvo])
                S.dma("sp", o_s_kv, kvo[:], reads=[b_kvo])
                S.op("dve", lambda e: e.tensor_copy(out=KTn[:, 0, :], in_=kvf[:, 2, 0:128]), reads=[b_kvf[2]], writes=[b_KTn])
                S.op("dve", lambda e: e.tensor_copy(out=KTn[:, 1, :], in_=kvf[:, 4, 0:128]), reads=[b_kvf[4]], writes=[b_KTn])
                for ci, c in enumerate((3, 5)):
                    pt, bp = ps_next()
                    S.op("pe", lambda e: e.transpose(pt[:, 0:128], kvf[:, c, 0:128], ident[:]), reads=[b_kvf[c], b_ident], writes=[bp])
                    S.op("dve", lambda e: e.tensor_copy(out=Vn[:, ci, :, 0:64], in_=pt[:, 0:128].rearrange("p (g d) -> p g d", g=2)), reads=[bp], writes=[b_Vn])
                return
            if ti >= OWN0 or DO_B:
                for blk in range(4):
                    for hf in range(2):
                        pt, bp = ps_next()
                        ncn = 4 if hf == 0 else 2
                        for c4 in range(ncn):
                            c = hf * 4 + c4
                            S.op("pe", lambda e: e.transpose(pt[:, c4 * 128:(c4 + 1) * 128], kvf[:, c, blk * 128:(blk + 1) * 128], ident[:]),
                                 reads=[b_kvf[c], b_ident], writes=[bp])
                        S.op("act", lambda e: e.copy(out=xt[:, blk, hf * 512:hf * 512 + ncn * 128], in_=pt[:, 0:ncn * 128]), reads=[bp], writes=[b_xt])
                        if DO_B and DBG >= 2 and not cfg.get("NO_V"):
                            kch = 4 * ti + blk
                            if hf == 0:
                                S.op("dve", lambda e: e.tensor_copy(out=slcV[:, kch, :, 0:64], in_=xt[:, blk, 384:512].rearrange("p (g d) -> p g d", g=2)),
                                     reads=[b_xt], writes=[b_slcV])
                            else:
                                S.op("dve", lambda e: e.tensor_copy(out=winV[:, kch % 8, :, 0:64], in_=xt[:, blk, 640:768].rearrange("p (g d) -> p g d", g=2)),
                                     reads=[b_xt], writes=[b_winV])
            if DO_B and DBG >= 2 and not cfg.get("NO_K"):
                c0 = ti * 512
                r0w = (ti % 2) * 512
                for g in range(2):
                    for hfp in range(2):
                        S.op("dve", lambda e: e.tensor_copy(out=slcK[hfp * 64:(hfp + 1) * 64, g, c0:c0 + 512], in_=kvf[g * 64:(g + 1) * 64, 2, :]),
                             reads=[b_kvf[2]], writes=[b_slcK])
                        S.op("dve", lambda e: e.tensor_copy(out=winK[hfp * 64:(hfp + 1) * 64, g, r0w:r0w + 512], in_=kvf[g * 64:(g + 1) * 64, 4, :]),
                             reads=[b_kvf[4]], writes=[b_winK])
            if DO_B and DBG >= 3:
                P0 = 32 * ti
                S.op("pool", lambda e: e.tensor_copy(out=cmpT[:, :, 0:16], in_=cmpT[:, :, 512:528]), reads=[b_cmpT], writes=[b_cmpT])
                for jv in range(2):
                    S.op("pool", lambda e: e.tensor_copy(out=cmpT[:, jv, 16:528], in_=kvf[:, jv, :]), reads=[b_kvf[jv], b_cmpT], writes=[b_cmpT])
                def kdst(g, pt2, bp2):
                    S.op("act", lambda e: e.copy(out=kcK[:, g, P0:P0 + 32], in_=pt2[:, 0:32]), reads=[bp2], writes=[b_kcK])

                def vdst(g, pt2, bp2):
                    pch = P0 // 128
                    S.op("dve", lambda e: e.tensor_tensor(out=vcV[:, pch, g, :], in0=vcV[:, pch, g, :], in1=pt2[:, 0:128], op=ALU.add),
                         reads=[bp2, b_vcV], writes=[b_vcV])
                    S.op("dve", lambda e: e.tensor_copy(out=vcVb[:, pch, g, :], in_=vcV[:, pch, g, :]), reads=[b_vcV], writes=[b_vcVb])
                compress_round(sb2, cmpT, b_cmpT, P0, kdst, vdst, [w2v[:], w2v[:]])
            if ti >= OWN0:
                r0 = (ti - OWN0) * 512
                for oi, od in enumerate((o_cmp, o_slc, o_win)):
                    S.dma("sp", od[r0:r0 + 512, :].rearrange("(b p) f -> p b f", p=128), xt[:, :, oi * 256:(oi + 1) * 256], reads=[b_xt])

        for ti in range(NT):
            transpose_in(xs[ti * 512:(ti + 1) * 512, :], 1024, x, b_x)
            for l in range(NL_A):
                with ExitStack() as es2:
                    rg_layer(l, ti, es2)
                    S.barrier()
                with ExitStack() as es2:
                    ffn_ple(l, ti, es2)
                    S.barrier()
            with ExitStack() as es2:
                kv_gen(ti, es2)
                S.barrier()
            if DO_B and ti >= OWN0 - 1:
                for l in (2, 3):
                    if DBG >= 4:
                        with ExitStack() as es2:
                            attn_layer(l, ti, es2, [3] if ti == OWN0 - 1 else [0, 1, 2, 3])
                            S.barrier()
                    with ExitStack() as es2:
                        ffn_ple(l, ti, es2)
                        S.barrier()
            if ti >= OWN0:
                with ExitStack() as es2:
                    final_out(ti, es2)
                    S.barrier()

        with nc.allow_non_contiguous_dma(reason="small state outputs"):
            for l in range(2):
                for k in range(3):
                    S.dma("sp", o_rgc[l, k].rearrange("(c p) -> p c", p=128), rg_hist[:, l, :, k], reads=b_rghist[l])
                S.dma("sp", o_rgh[l].rearrange("(c p) -> p c", p=128), rg_hc[:, l, :], reads=b_rghc[l])
            for l in range(4):
                for k in range(2):
                    S.dma("sp", o_ffc[l, k].rearrange("(c p) -> p c", p=128), ff_hist[:, l, :, k], reads=b_ffhist[l])

        if SMP:
            S.barrier()
            with ExitStack() as es3:
                def sb3(name, shape, dt=F32):
                    uid[0] += 1
                    return es3.enter_context(nc.sbuf_tensor(f"s{uid[0]}_{name}", list(shape), dt))
                assert NTOK >= 2048
                flatK = slcK[:].rearrange("p g t -> p (g t)")
                flatV = slcV[:].rearrange("p c g f -> p (c g f)")
                kcS = flatK[:, 0:2048].rearrange("p (b t) -> p b t", b=4)
                vcS = flatK[:, 2048:4096].rearrange("p (b c f) -> p b c f", b=4, c=4)
                b_kcS = Buf(); b_vcS = Buf(); b_KTn = Buf()
                Vn = slcV[:, 0:2, :, :]; b_Vn = Buf()
                b_Vc = [Buf() for _ in range(4)]
                vo = [1536]

                def vview(n):
                    a = vo[0]
                    vo[0] += n
                    return flatV[:, a:a + n]
                mcsS = vview(1024).rearrange("p (a b c) -> p a b c", a=4, b=2)
                b64 = vview(512).rearrange("p (a c) -> p a c", a=4)
                KTn = vview(256).rearrange("p (a t) -> p a t", a=2)
                cb0 = vview(128); wb0 = vview(128); w2v1 = vview(128); brhs = vview(16)
                xtf = xt[:].rearrange("p b f -> p (b f)")
                tas = xtf[:, 0:512].rearrange("p (a s) -> p a s", a=2)
                b_stab = Buf()
                S.op("pool", lambda e: e.memset(w2v1, 0.0), writes=[b_stab])
                S.dma("pool", w2v1[:, 64:128], cmp_w2[1], writes=[b_stab])
                S.dma("pool", cb0, t_cb0, writes=[b_stab]); S.dma("pool", brhs, t_brhs, writes=[b_stab])
                S.dma("pool", mcsS, t_mcs_s, writes=[b_stab]); S.dma("sp", tas, t_as, writes=[b_stab])
                S.dma("pool", b64, t_b64, writes=[b_stab]); S.dma("pool", wb0, t_wb0, writes=[b_stab])
                pgi = xtf[:, 512:768].bitcast(mybir.dt.int32); pgf = xtf[:, 768:1024]; pidf = sb3("pidf", [128, 1])
                idxS_t = xtf[:, 1024:1280].bitcast(mybir.dt.int32); b_idxS = Buf()
                idxS = idxS_t.rearrange("p (b g) -> p b g", b=4)
                S.dma("sp", pgi, pg_s.rearrange("b g -> (b g)").unsqueeze(0).broadcast_to([128, 256]), writes=[b_idxS])
                S.op("pool", lambda e: e.iota(pidf[:], pattern=[[0, 1]], base=0, channel_multiplier=1, allow_small_or_imprecise_dtypes=True), writes=[b_idxS])
                S.op("dve", lambda e: e.tensor_copy(out=pgf, in_=pgi), reads=[b_idxS], writes=[b_idxS])
                S.op("dve", lambda e: e.tensor_scalar(out=pgf, in0=pgf, scalar1=128.0, scalar2=pidf[:, 0:1], op0=ALU.mult, op1=ALU.add), reads=[b_idxS], writes=[b_idxS])
                S.op("dve", lambda e: e.tensor_copy(out=idxS_t, in_=pgf), reads=[b_idxS], writes=[b_idxS])
                cT = cmpT; b_cT = b_cmpT
                hid_pre = (sb3("hid0s", [128, 32], BF16), Buf(), sb3("hidps", [128, 128], BF16), Buf())
                vcSf = xtf[:, 1280:1792].rearrange("p (c f) -> p c f", c=4); b_vcSf = Buf()
                gbc = [xtf[:, 1792 + 256 * i:2048 + 256 * i] for i in range(4)]; b_gbc = [Buf() for _ in range(4)]
                for i in range(4):
                    S.op("pool", lambda e: e.memset(cT[:], 0.0), writes=[b_cT])
                    S.op("pool", lambda e: e.memset(vcSf, 0.0), writes=[b_vcSf])
                    for r in range(16):
                        for jq in range(4):
                            S._deps("pool", [b_idxS], [b_gbc[jq]])
                            tk = S.dma_ind(gbc[jq], c_cmp, idxS[:, i, 4 * r + jq:4 * r + jq + 1])
                            S._mark(tk, [b_idxS], [b_gbc[jq]])
                            tq, btq = ps_next()
                            for jv in range(2):
                                S.op("pe", lambda e: e.transpose(tq[:, jv * 128:(jv + 1) * 128], gbc[jq][:, jv * 128:(jv + 1) * 128], ident[:]),
                                     reads=[b_gbc[jq], b_ident], writes=[btq])
                            S.op("act", lambda e: e.copy(out=cT[:, :, 16 + 128 * jq:16 + 128 * (jq + 1)], in_=tq[:, 0:256].rearrange("p (j t) -> p j t", j=2)),
                                 reads=[btq], writes=[b_cT])
                        P0 = 32 * r

                        def kdst(g, pt2, bp2):
                            S.op("act", lambda e: e.copy(out=kcS[64 * g:64 * g + 64, i, P0:P0 + 32], in_=pt2[64 * g:64 * g + 64, 0:32]), reads=[bp2], writes=[b_kcS])

                        def vdst(g, pt2, bp2):
                            pch = P0 // 128
                            S.op("dve", lambda e: e.tensor_tensor(out=vcSf[:, pch, :], in0=vcSf[:, pch, :], in1=pt2[:, 0:128], op=ALU.add),
                                 reads=[bp2, b_vcSf], writes=[b_vcSf])
                        compress_round(None, cT, b_cT, P0, kdst, vdst, [w2v[:], w2v1], pre=hid_pre)
                        S.op("pool", lambda e: e.tensor_copy(out=cT[:, :, 0:16], in_=cT[:, :, 512:528]), reads=[b_cT], writes=[b_cT])
                    S.op("dve", lambda e: e.tensor_copy(out=vcS[:, i, :, :], in_=vcSf), reads=[b_vcSf], writes=[b_vcS])
                S.dma("sp", x[:], xs_s, writes=b_x)
                for l in range(2):
                    with ExitStack() as es2:
                        rg_layer(l, -1, es2, smp=True)
                        S.barrier()
                    with ExitStack() as es2:
                        ffn_ple(l, -1, es2, smp=True)
                        S.barrier()
                with ExitStack() as es2:
                    kv_gen(-1, es2, smp=True)
                    S.barrier()
                for l in (2, 3):
                    with ExitStack() as es2:
                        attn_layer(l, -1, es2, [], smp=True)
                        S.barrier()
                    with ExitStack() as es2:
                        ffn_ple(l, -1, es2, smp=True)
                        S.barrier()
                with ExitStack() as es2:
                    final_out(-1, es2, smp=True)
                    S.barrier()
                with nc.allow_non_contiguous_dma(reason="state passthrough"):
                    for i in range(4):
                        S.dma("sp", o_s_win[i], c_win[i, 1:512, :])
                    S.dma("sp", o_s_rgc_old, st_rgc[:, :, 1:3, :])
                    S.dma("sp", o_s_ffc_old, st_ffc[:, :, 1, :])
        S.finish()
    return nc

_WNAMES = ['g_mix', 'g_ffn', 'g_ple', 'g_final', 'g_kv', 'rg_w_in', 'rg_conv_w', 'rg_conv_b', 'rg_w_a', 'rg_w_x', 'rg_lambda',
           'rg_w_out', 'w_kv', 'ffn_w_up', 'ffn_conv_w', 'ffn_conv_b', 'ffn_w_down', 'ple_w_in', 'ple_w_gate',
           'attn_w_qg', 'attn_w_o', 'cmp_pos', 'cmp_w1', 'cmp_b1', 'cmp_w2']


def make_tables(half, NT, OWN0):
    BIG = 30000.0
    NTOK = NT * 512
    pre = OWN0 * 512
    nown = NTOK - pre + 128
    nqb = nown // 128
    i = np.arange(nown) - 128
    tau = pre + i
    hv = np.where(i < 0, 1, half)[:, None]
    P = np.arange(256)
    c = P - 1
    vc = (P >= 1)[None, :] & ((hv == 1) | (16 * c >= pre)[None, :])
    ok = vc & ((16 * c + 31)[None, :] <= tau[:, None])
    t_cb = np.where(ok, 0.0, -BIG).astype(np.float32).reshape(nqb, 128, 256)
    kk = np.arange(640)
    tau0 = pre + 128 * (i // 128)
    kap = tau0[:, None] - 512 + kk[None, :]
    dist = tau[:, None] - kap
    okw = (dist >= 0) & (dist < 512) & (kap >= 0) & ((hv == 1) | (kap >= pre))
    t_wb = np.where(okw, 0.0, -BIG).astype(np.float32).reshape(nqb, 128, 640)
    sblk = np.arange(64)
    vb = ((hv == 1) | (64 * sblk >= pre)[None, :]) & (64 * sblk < NTOK)[None, :]
    V = (vb & ((64 * sblk)[None, :] <= tau[:, None])).astype(np.float32)
    blk0 = np.where(hv[:, 0] == 1, 0, pre // 64)
    cur = tau // 64
    forced = (sblk[None, :] == blk0[:, None]) | (sblk[None, :] == cur[:, None]) | (sblk[None, :] == (cur - 1)[:, None])
    A = 100.0 * forced.astype(np.float32) * V + (V - 1.0)
    t_va = np.stack([V, A], axis=1).astype(np.float32).reshape(nqb, 128, 2, 64)
    Pm = np.arange(256)
    c0 = 16 * (Pm - 1)
    s0 = 64 * sblk
    m = ((c0[:, None] < s0[None, :] + 64) & (c0[:, None] + 32 > s0[None, :]) & (Pm[:, None] >= 1)).astype(np.float32)
    t_mcs = np.ascontiguousarray(m.reshape(2, 128, 64).transpose(1, 0, 2))
    t_tri = np.where(np.arange(128)[None, :] > np.arange(128)[:, None], -BIG, 0.0).astype(np.float32)
    return dict(t_cb=t_cb, t_wb=t_wb, t_va=t_va, t_mcs=t_mcs, t_tri=t_tri)


def core_inputs(inp, b, half, NT=8, OWN0=4):
    NTOK = NT * 512
    pre = OWN0 * 512
    own = NTOK - pre
    if half == 1:
        xs = inp['x_prompt'][b, :NTOK]
        ps = inp['p_prompt'][:, b, :NTOK]
        pos = np.arange(NTOK)
    else:
        xs = np.concatenate([inp['x_prompt'][b, :pre], inp['x_prompt'][b, :own]], 0)
        ps = np.concatenate([inp['p_prompt'][:, b, :pre], inp['p_prompt'][:, b, :own]], 1)
        pos = np.concatenate([np.arange(pre), np.arange(own)])
    d = np.arange(128) % 64
    inv = (10000.0 ** (-(d % 32).astype(np.float32) / 32)).astype(np.float32)
    ang = pos[None, :].astype(np.float32) * inv[:, None]
    sgn = np.where(d < 32, -1.0, 1.0)[:, None]
    m = dict(xs=np.ascontiguousarray(xs, dtype=np.float32), ps=np.ascontiguousarray(ps, dtype=np.float32),
             flag=np.tile(np.array([[half, 1 - half]], np.float32), (128, 1)),
             ident=np.eye(128, dtype=np.float32),
             ropec=np.cos(ang).astype(np.float32), ropes=(np.sin(ang) * sgn).astype(np.float32))
    for k in _WNAMES:
        m[k] = np.ascontiguousarray(inp[k], dtype=np.float32)
    m['rg_b_a'] = np.ascontiguousarray(inp['rg_b_a'], dtype=np.float32).reshape(2, 1280)
    m['rg_b_x'] = np.ascontiguousarray(inp['rg_b_x'], dtype=np.float32).reshape(2, 1280)
    m.update(make_tables(half, NT, OWN0))
    return m


def sample_inputs(inp, c):
    BIG = 30000.0
    b0 = 4 * c
    m = {}
    xs = np.zeros((128, 8, 512), np.float32)
    ps = np.zeros((4, 128, 2, 512), np.float32)
    for i in range(4):
        xs[:, :, 4 * i + 3] = inp['x_sample'][b0 + i, 0].reshape(8, 128).T
        for l in range(4):
            ps[l, :, :, 4 * i + 3] = inp['p_sample'][l, b0 + i, 0].reshape(2, 128).T
    m['xs_s'] = xs
    m['ps_s'] = ps
    rgc = inp['state_rg_conv'][:, b0:b0 + 4]
    m['s_rgh'] = np.ascontiguousarray(rgc.reshape(2, 4, 3, 10, 128).transpose(4, 0, 3, 1, 2), dtype=np.float32)
    rgh = inp['state_rg_h'][:, b0:b0 + 4]
    m['s_h0'] = np.ascontiguousarray(rgh.reshape(2, 4, 10, 128).transpose(3, 0, 2, 1), dtype=np.float32)
    ffc = inp['state_ffn_conv'][:, b0:b0 + 4]
    m['s_ffh'] = np.ascontiguousarray(ffc.reshape(4, 4, 2, 48, 128).transpose(4, 0, 3, 1, 2), dtype=np.float32)
    d = np.arange(128) % 64
    inv = (10000.0 ** (-(d % 32).astype(np.float32) / 32)).astype(np.float32)
    ang = np.full((1, 512), 8192.0, np.float32) * inv[:, None]
    sgn = np.where(d < 32, -1.0, 1.0)[:, None]
    m['ropec_s'] = np.cos(ang).astype(np.float32)
    m['ropes_s'] = (np.sin(ang) * sgn).astype(np.float32)
    m['pg_s'] = np.ascontiguousarray(inp['page_table'][b0:b0 + 4], dtype=np.int32)
    nphys = inp['cache_cmp_kv'].shape[0]
    m['cache_cmp_kv'] = np.ascontiguousarray(inp['cache_cmp_kv'], dtype=np.float32).reshape(nphys * 128, 256)
    m['cache_slc_kv'] = np.ascontiguousarray(inp['cache_slc_kv'], dtype=np.float32).reshape(nphys * 128, 256)
    m['c_win'] = np.ascontiguousarray(inp['cache_win_kv'][b0:b0 + 4], dtype=np.float32).reshape(4, 512, 256)
    m['st_rgc'] = np.ascontiguousarray(rgc, dtype=np.float32)
    m['st_ffc'] = np.ascontiguousarray(ffc, dtype=np.float32)
    P = np.arange(512)
    c0 = 16 * (P - 1)
    sb = np.arange(256)
    mm = ((c0[:, None] < 64 * sb[None, :] + 64) & (c0[:, None] + 32 > 64 * sb[None, :]) & (P[:, None] >= 1) & (sb[None, :] < 129)).astype(np.float32)
    m['t_mcs_s'] = np.ascontiguousarray(mm.reshape(4, 128, 2, 128).transpose(1, 0, 2, 3))
    V = (sb < 129).astype(np.float32)
    forced = ((sb == 0) | (sb == 128) | (sb == 127)).astype(np.float32)
    A = 100.0 * forced * V + (V - 1.0)
    m['t_as'] = np.ascontiguousarray(np.tile(np.stack([V, A])[None], (128, 1, 1)), dtype=np.float32)
    br = np.zeros((128, 16), np.float32); br[0, 0:8] = 1.0; br[1, 8:16] = 1.0
    m['t_brhs'] = br
    b64 = np.zeros((128, 4, 128), np.float32)
    for i in range(4):
        b64[0:2, i, :] = -BIG
        b64[0:2, i, 4 * i + 3] = 0.0
    m['t_b64'] = b64
    wb0 = np.zeros((128, 128), np.float32); wb0[0:2, 0] = -BIG
    m['t_wb0'] = wb0
    m['t_cb0'] = wb0.copy()
    return m


_BUILD_CACHE = {}


def kernel(**inputs):
    inp = {k: np.asarray(v) for k, v in inputs.items()}
    nphys = inp['cache_cmp_kv'].shape[0]
    nc = build(dict(NT=8, OWN0=4, NL_A=2, DO_B=True, SMP=True, NPHYS=nphys))
    in_maps = []
    for c in range(8):
        m = core_inputs(inp, c // 2, c % 2)
        m.update(sample_inputs(inp, c))
        in_maps.append(m)
    res = run_bass_kernel_spmd(nc, in_maps, core_ids=list(range(8)))
    R = res.results
    B, T = 4, 4096
    y_prompt = np.zeros((B, T, D), np.float32)
    cmp_p = np.zeros((B, T, 2, 2, 64), np.float32)
    slc_p = np.zeros((B, T, 2, 2, 64), np.float32)
    win_p = np.zeros((B, 512, 2, 2, 64), np.float32)
    rgc_p = np.zeros((2, B, 3, DR), np.float32)
    rgh_p = np.zeros((2, B, DR), np.float32)
    ffc_p = np.zeros((4, B, 2, 2 * DFF), np.float32)
    DB = 32
    y_sample = np.zeros((DB, 1, D), np.float32)
    cmp_s = np.zeros((DB, 1, 2, 2, 64), np.float32)
    slc_s = np.zeros((DB, 1, 2, 2, 64), np.float32)
    win_s = np.zeros((DB, 512, 2, 2, 64), np.float32)
    rgc_s = np.zeros((2, DB, 3, DR), np.float32)
    rgh_s = np.zeros((2, DB, DR), np.float32)
    ffc_s = np.zeros((4, DB, 2, 2 * DFF), np.float32)
    for c in range(8):
        b, half = c // 2, c % 2
        sl = slice(half * 2048, half * 2048 + 2048)
        y_prompt[b, sl] = R[c]['o_y']
        cmp_p[b, sl] = R[c]['o_cmp'].reshape(2048, 2, 2, 64)
        slc_p[b, sl] = R[c]['o_slc'].reshape(2048, 2, 2, 64)
        if half == 1:
            win_p[b] = R[c]['o_win'][-512:].reshape(512, 2, 2, 64)
            rgc_p[:, b] = R[c]['o_rgc']
            rgh_p[:, b] = R[c]['o_rgh']
            ffc_p[:, b] = R[c]['o_ffc']
        assemble_sample(R[c], c, y_sample, cmp_s, slc_s, win_s, rgc_s, rgh_s, ffc_s)
    return (y_prompt, y_sample, cmp_p, cmp_s, slc_p, slc_s, win_p, win_s, rgc_p, rgc_s, rgh_p, rgh_s, ffc_p, ffc_s)


def assemble_sample(r, c, y_sample, cmp_s, slc_s, win_s, rgc_s, rgh_s, ffc_s):
    b0 = 4 * c
    for i in range(4):
        b = b0 + i
        y_sample[b, 0] = r['o_s_y'][:, :, i].T.reshape(1024)
        kv = r['o_s_kv'][:, :, i]
        cmp_s[b, 0] = kv[:, 0:2].T.reshape(2, 2, 64)
        slc_s[b, 0] = kv[:, 2:4].T.reshape(2, 2, 64)
        win_s[b, 0:511] = r['o_s_win'][i].reshape(511, 2, 2, 64)
        win_s[b, 511] = kv[:, 4:6].T.reshape(2, 2, 64)
        for l in range(2):
            rgc_s[l, b, 0:2] = r['o_s_rgc_old'][l, i]
            rgc_s[l, b, 2] = r['o_s_rgc_new'][:, l, :, i].T.reshape(1280)
            rgh_s[l, b] = r['o_s_rgh'][:, l, :, i].T.reshape(1280)
        for l in range(4):
            ffc_s[l, b, 0] = r['o_s_ffc_old'][l, i]
            ffc_s[l, b, 1] = r['o_s_ffc_new'][:, l, :, i].T.reshape(6144)
```

```python
import numpy as np
import ml_dtypes
from contextlib import ExitStack
import concourse.bass as bass
import concourse.mybir as mybir
from concourse.bass_utils import run_bass_kernel_spmd

F32 = mybir.dt.float32
BF16 = mybir.dt.bfloat16
AF = mybir.ActivationFunctionType
ALU = mybir.AluOpType

D = 1024
NT_TOK = 512
DR = 1280
DFF = 3072
EPS = 1e-6


class Buf:
    __slots__ = ("w", "r")

    def __init__(self):
        self.w = None
        self.r = []


class Sched:
    ND = 12

    def __init__(self, nc, es):
        self.nc = nc
        self.E = dict(pe=nc.tensor, act=nc.scalar, dve=nc.vector, pool=nc.gpsimd, sp=nc.sync)
        self.sem = {}
        self.cnt = {}
        for e in ("pe", "act", "dve", "pool"):
            self.sem[e] = es.enter_context(nc.semaphore("s_" + e))
            self.cnt[e] = 0
        self.seen = {e: {} for e in self.E}
        self.dsem = {}
        self.dval = {}
        self.didx = {}
        self.NDQ = {"sp": 12, "pool": 40, "act": 4}
        for q in ("sp", "pool", "act"):
            self.dsem[q] = [es.enter_context(nc.semaphore(f"d_{q}{i}")) for i in range(self.NDQ[q])]
            self.dval[q] = [0] * self.NDQ[q]
            self.didx[q] = 0
        self.all_dma = []

    def _wait(self, e, tk):
        key, sem, val = tk
        if key == e and e == "pe":
            return
        if self.seen[e].get(key, 0) >= val:
            return
        self.E[e].wait_ge(sem, val)
        self.seen[e][key] = val

    def _deps(self, e, reads, writes):
        for b in reads:
            if b.w is not None:
                for tk in (b.w if isinstance(b.w, list) else [b.w]):
                    self._wait(e, tk)
        for b in writes:
            if b.w is not None:
                for tk in (b.w if isinstance(b.w, list) else [b.w]):
                    self._wait(e, tk)
            for tk in b.r:
                self._wait(e, tk)

    def _mark(self, tk, reads, writes):
        for b in reads:
            b.r.append(tk)
            if len(b.r) > 24:
                b.r = b.r[-24:]
        for b in writes:
            b.w = tk
            b.r = []

    def op(self, e, fn, reads=(), writes=()):
        self._deps(e, reads, writes)
        inst = fn(self.E[e])
        inst.then_inc(self.sem[e], 1)
        self.cnt[e] += 1
        tk = (e, self.sem[e], self.cnt[e])
        self._mark(tk, reads, writes)
        return tk

    def dma(self, q, out, in_, reads=(), writes=(), **kw):
        slot = self.didx[q] % self.NDQ[q]
        self.didx[q] += 1
        sem = self.dsem[q][slot]
        key = (q, slot)
        if self.dval[q][slot] > 0:
            self._wait(q, (key, sem, self.dval[q][slot]))
        self._deps(q, reads, writes)
        inst = self.E[q].dma_start(out=out, in_=in_, **kw)
        inst.then_inc(sem, 16)
        self.dval[q][slot] += 16
        tk = (key, sem, self.dval[q][slot])
        self._mark(tk, reads, writes)
        self.all_dma.append(tk)
        return tk

    def dma_ind(self, out, in_, idx_ap):
        q = "pool"
        slot = self.didx[q] % self.NDQ[q]
        self.didx[q] += 1
        sem = self.dsem[q][slot]
        key = (q, slot)
        if self.dval[q][slot] > 0:
            self._wait(q, (key, sem, self.dval[q][slot]))
        inst = self.nc.gpsimd.indirect_dma_start(out=out, out_offset=None, in_=in_, in_offset=bass.IndirectOffsetOnAxis(ap=idx_ap, axis=0))
        inst.then_inc(sem, 16)
        self.dval[q][slot] += 16
        return (key, sem, self.dval[q][slot])

    def barrier(self):
        tks = [(e, self.sem[e], self.cnt[e]) for e in self.cnt if self.cnt[e] > 0]
        for q in self.dsem:
            for slot in range(self.NDQ[q]):
                if self.dval[q][slot] > 0:
                    tks.append(((q, slot), self.dsem[q][slot], self.dval[q][slot]))
        for e in self.E:
            for tk in tks:
                if tk[0] == e and e == "pe":
                    continue
                self._wait(e, tk)

    def finish(self):
        for q in self.dsem:
            for slot in range(self.NDQ[q]):
                if self.dval[q][slot] > 0:
                    self._wait("sp", ((q, slot), self.dsem[q][slot], self.dval[q][slot]))


class Wt:
    def __init__(self, nc, name, src, K, M, MW):
        self.KC = K // 128
        self.MW = MW
        self.NP = M // MW
        assert self.KC * MW <= 4096
        self.src = src
        self.dst = nc.dram_tensor("wb_" + name, [self.NP, 128, self.KC * MW], BF16, kind="Internal").ap()
        self.buf = Buf()


def build(cfg):
    NT = cfg.get("NT", 8)
    OWN0 = cfg.get("OWN0", 4)
    NL_A = cfg.get("NL_A", 2)
    DO_B = cfg.get("DO_B", False)
    DBG = cfg.get("DBG", 99)
    NTOK = NT * NT_TOK
    NOWN = (NT - OWN0) * NT_TOK

    nc = bass.Bass("TRN2", target_bir_lowering=False)

    def din(name, shape, dt=F32):
        return nc.dram_tensor(name, list(shape), dt, kind="ExternalInput").ap()

    def dout(name, shape, dt=F32):
        return nc.dram_tensor(name, list(shape), dt, kind="ExternalOutput").ap()

    xs = din("xs", [NTOK, D])
    ps_in = din("ps", [4, NTOK, 256])
    flag = din("flag", [128, 2])
    ident_d = din("ident", [128, 128])
    ropec = din("ropec", [128, NTOK])
    ropes = din("ropes", [128, NTOK])
    g_mix = din("g_mix", [4, D]); g_ffn = din("g_ffn", [4, D]); g_ple = din("g_ple", [4, D])
    g_final = din("g_final", [D]); g_kv = din("g_kv", [D])
    rg_w_in = din("rg_w_in", [2, D, 2 * DR]); rg_conv_w = din("rg_conv_w", [2, 4, DR]); rg_conv_b = din("rg_conv_b", [2, DR])
    rg_w_a = din("rg_w_a", [2, 16, 80, 80]); rg_b_a = din("rg_b_a", [2, DR])
    rg_w_x = din("rg_w_x", [2, 16, 80, 80]); rg_b_x = din("rg_b_x", [2, DR])
    rg_lambda = din("rg_lambda", [2, DR]); rg_w_out = din("rg_w_out", [2, DR, D])
    w_kv = din("w_kv", [D, 768])
    ffn_w_up = din("ffn_w_up", [4, D, 2 * DFF]); ffn_conv_w = din("ffn_conv_w", [4, 3, 2 * DFF]); ffn_conv_b = din("ffn_conv_b", [4, 2 * DFF])
    ffn_w_down = din("ffn_w_down", [4, DFF, D])
    ple_w_in = din("ple_w_in", [4, 256, D]); ple_w_gate = din("ple_w_gate", [4, D, D])

    attn_w_qg = din("attn_w_qg", [2, D, 1072]); attn_w_o = din("attn_w_o", [2, D, D])
    cmp_pos = din("cmp_pos", [32, 2, 64]); cmp_w1 = din("cmp_w1", [2, 2048, 128]); cmp_b1 = din("cmp_b1", [2, 128]); cmp_w2 = din("cmp_w2", [2, 128, 64])
    NQB = NOWN // 128 + 1
    t_cb = din("t_cb", [NQB, 128, 256]); t_wb = din("t_wb", [NQB, 128, 640]); t_va = din("t_va", [NQB, 128, 2, 64])
    t_mcs = din("t_mcs", [128, 2, 64]); t_tri = din("t_tri", [128, 128])

    SMP = cfg.get("SMP", False)
    if SMP:
        xs_s = din("xs_s", [128, 8, 512]); ps_s = din("ps_s", [4, 128, 2, 512])
        s_rgh_d = din("s_rgh", [128, 2, 10, 4, 3]); s_h0_d = din("s_h0", [128, 2, 10, 4]); s_ffh_d = din("s_ffh", [128, 4, 48, 4, 2])
        ropec_s = din("ropec_s", [128, 512]); ropes_s = din("ropes_s", [128, 512])
        pg_s = din("pg_s", [4, 64], mybir.dt.int32)
        NPHYS = cfg.get("NPHYS", 2560)
        c_cmp = din("cache_cmp_kv", [NPHYS * 128, 256]); c_slc = din("cache_slc_kv", [NPHYS * 128, 256]); c_win = din("c_win", [4, 512, 256])
        st_rgc = din("st_rgc", [2, 4, 3, DR]); st_ffc = din("st_ffc", [4, 4, 2, 2 * DFF])
        t_mcs_s = din("t_mcs_s", [128, 4, 2, 128]); t_as = din("t_as", [128, 2, 256]); t_brhs = din("t_brhs", [128, 16])
        t_b64 = din("t_b64", [128, 4, 128]); t_wb0 = din("t_wb0", [128, 128]); t_cb0 = din("t_cb0", [128, 128])
        o_s_y = dout("o_s_y", [128, 8, 4]); o_s_kv = dout("o_s_kv", [128, 6, 4]); o_s_win = dout("o_s_win", [4, 511, 256])
        o_s_rgc_new = dout("o_s_rgc_new", [128, 2, 10, 4]); o_s_rgc_old = dout("o_s_rgc_old", [2, 4, 2, DR]); o_s_rgh = dout("o_s_rgh", [128, 2, 10, 4])
        o_s_ffc_new = dout("o_s_ffc_new", [128, 4, 48, 4]); o_s_ffc_old = dout("o_s_ffc_old", [4, 4, 2 * DFF])

    o_y = dout("o_y", [NOWN, D])
    o_cmp = dout("o_cmp", [NOWN, 256]); o_slc = dout("o_slc", [NOWN, 256]); o_win = dout("o_win", [NOWN, 256])
    o_rgc = dout("o_rgc", [2, 3, DR]); o_rgh = dout("o_rgh", [2, DR]); o_ffc = dout("o_ffc", [4, 2, 2 * DFF])

    zpad = nc.dram_tensor("zpad", [9, 128, 4096], BF16, kind="Internal").ap()
    zband = nc.dram_tensor("zband", [2, 2, 128, 10 * 384], BF16, kind="Internal").ap()

    es = ExitStack()
    with es:
        S = Sched(nc, es)

        uid = [0]

        def sb(name, shape, dt=F32):
            uid[0] += 1
            return es.enter_context(nc.sbuf_tensor(f"s{uid[0]}_{name}", list(shape), dt))

        ident = sb("ident", [128, 128]); b_ident = Buf()
        ones_bf = sb("ones_bf", [128, 128], BF16); b_ones = Buf()
        flag_sb = sb("flag_sb", [128, 2]); b_flag = Buf()
        gvec = sb("gvec", [128, 14, 8]); b_gvec = Buf()
        rgcw = sb("rgcw", [128, 2, 4, 10]); rgcb = sb("rgcb", [128, 2, 10]); b_rgc = Buf()
        rgba = sb("rgba", [128, 2, 10]); rgbx = sb("rgbx", [128, 2, 10]); rgc8 = sb("rgc8", [128, 2, 10]); b_rgp = Buf()
        ffcw = sb("ffcw", [128, 4, 3, 48]); ffcb = sb("ffcb", [128, 4, 48]); b_ffc = Buf()
        rg_hist = sb("rg_hist", [128, 2, 10, 3]); b_rghist = [[Buf() for _ in range(10)] for _ in range(2)]
        rg_hc = sb("rg_hc", [128, 2, 10]); b_rghc = [[Buf() for _ in range(10)] for _ in range(2)]
        ff_hist = sb("ff_hist", [128, 4, 48, 2]); b_ffhist = [[Buf() for _ in range(48)] for _ in range(4)]
        epsb = sb("epsb", [128, 1]); b_eps = Buf()

        x = sb("x", [128, 8, 512]); b_x = [Buf() for _ in range(8)]
        xt = sb("xt", [128, 4, 1024]); b_xt = Buf()
        h = sb("h", [128, 8, 512], BF16); b_h = Buf()
        sq = sb("sq", [128, 8, 512], BF16); b_sq = Buf()
        rstd = sb("rstd", [128, 512]); b_rstd = Buf()
        NPAN = 4
        pan = [sb(f"pan{i}", [128, 4096], BF16) for i in range(NPAN)]
        b_pan = [Buf() for _ in range(NPAN)]
        psum2 = [es.enter_context(nc.psum_tensor(f"psum{i}", [128, 1024], F32)) for i in range(4)]
        psum = [psum2[i // 2][:, (i % 2) * 512:(i % 2 + 1) * 512] for i in range(8)]
        b_ps = [Buf() for _ in range(8)]
        ps_i = [0]

        def ps_next():
            i = ps_i[0] % 6
            ps_i[0] += 1
            return psum[i], b_ps[i]

        def ps_pair():
            if ps_i[0] % 2 == 1:
                ps_i[0] += 1
            i = ps_i[0] % 6
            ps_i[0] += 2
            return psum2[i // 2], [b_ps[i], b_ps[i + 1]]

        acc_ps = psum2[3]
        b_acc = [b_ps[6], b_ps[7]]

        W = {}
        band = {}

        jobs = {}
        cast_order = []

        def ensure(name):
            if name in jobs:
                jobs.pop(name)()

        def prefetch_casts(k):
            n = 0
            for nm in cast_order:
                if n >= k:
                    break
                if nm in jobs:
                    ensure(nm)
                    n += 1

        def mkw(name, src, K, M, MW):
            w = Wt(nc, name, src, K, M, MW)
            w.name = name
            W[name] = w

            def job():
                tks = []
                for pnl in range(w.NP):
                    srcv = src[:, pnl * MW:(pnl + 1) * MW].rearrange("(k p) m -> p k m", p=128)
                    dstv = w.dst[pnl].rearrange("p (k m) -> p k m", k=w.KC)
                    tks.append(S.dma("pool", dstv, srcv))
                w.buf.w = tks
            jobs[name] = job
            return w

        zt_full = sq[:].rearrange("p k t -> p (k t)")
        zt = zt_full[:, 0:3840]; b_zt = b_sq
        S.op("dve", lambda e: e.memset(sq[:], 0.0), writes=[b_zt])
        for l in range(NL_A):
            for gi, wsrc in enumerate((rg_w_a, rg_w_x)):
                bw = Buf()
                band[(l, gi)] = bw
                tz = S.dma("sp", zband[l, gi], zt, reads=[b_zt])
                wv = Wt.__new__(Wt)
                wv.KC = 30; wv.MW = 128; wv.NP = 1; wv.dst = zband[l, gi:gi + 1]; wv.buf = bw; wv.name = f"band{l}_{gi}"
                W[wv.name] = wv

                def bjob(l=l, gi=gi, wsrc=wsrc, bw=bw, tz=tz):
                    S._wait("pool", tz)
                    tks = []
                    dv = zband[l, gi].rearrange("p (i b m) -> p i b m", i=10, b=3)
                    for n in range(16):
                        r0, r1 = 80 * n, 80 * n + 80
                        for i in range(r0 // 128, (r1 - 1) // 128 + 1):
                            ra, rb = max(r0, 128 * i), min(r1, 128 * i + 128)
                            for j in range(r0 // 128, (r1 - 1) // 128 + 1):
                                ca, cb = max(r0, 128 * j), min(r1, 128 * j + 128)
                                tks.append(S.dma("pool", dv[ra - 128 * i:rb - 128 * i, i, j - i + 1, ca - 128 * j:cb - 128 * j],
                                                 wsrc[l, n, ra - r0:rb - r0, ca - r0:cb - r0]))
                    bw.w = tks
                jobs[wv.name] = bjob

        for l in range(NL_A):
            mkw(f"rgin{l}", rg_w_in[l], D, 2 * DR, 512)
            mkw(f"rgout{l}", rg_w_out[l], DR, D, 256)
        for l in range(4 if DO_B else NL_A):
            mkw(f"up{l}", ffn_w_up[l], D, 2 * DFF, 512)
            mkw(f"down{l}", ffn_w_down[l], DFF, D, 128)
            mkw(f"plein{l}", ple_w_in[l], 256, D, 1024)
            mkw(f"pleg{l}", ple_w_gate[l], D, D, 512)
        mkw("kv", w_kv, D, 768, 256)


        BIGV = 30000.0
        if DO_B:
            for j in range(2):
                mkw(f"q{j}", attn_w_qg[j][:, 0:1024], D, 1024, 512)
                mkw(f"wo{j}", attn_w_o[j], D, D, 512)
                wsw = Wt(nc, f"qsw{j}", None, D, 1024, 512)
                wsw.name = f"qsw{j}"
                W[f"qsw{j}"] = wsw

                def qswjob(j=j, wsw=wsw):
                    tks = []
                    with nc.allow_non_contiguous_dma(reason="one-time rope column swap"):
                        for pnl in range(2):
                            for k in range(8):
                                srcv = attn_w_qg[j][k * 128:(k + 1) * 128, pnl * 512:(pnl + 1) * 512].rearrange("p (hd hh dd) -> p hd hh dd", hd=8, hh=2)
                                dstv = wsw.dst[pnl].rearrange("p (k hd hh dd) -> p k hd hh dd", k=8, hd=8, hh=2)
                                for hh in range(2):
                                    tks.append(S.dma("pool", dstv[:, k, :, hh, :], srcv[:, :, 1 - hh, :]))
                    wsw.buf.w = tks
                jobs[wsw.name] = qswjob
                wg = Wt.__new__(Wt)
                wg.KC = 8; wg.MW = 128; wg.NP = 1; wg.dst = zpad[j:j + 1, :, 0:1024]; wg.buf = Buf()
                W[f"qg{j}"] = wg
                wg.name = f"qg{j}"
                tz = S.dma("sp", zpad[j, :, 0:1024], zt[:, 0:1024], reads=[b_zt])

                def qgjob(j=j, wg=wg, tz=tz):
                    S._wait("pool", tz)
                    with nc.allow_non_contiguous_dma(reason="gate cols"):
                        tk = S.dma("pool", zpad[j, :, 0:1024].rearrange("p (k m) -> p k m", k=8)[:, :, 0:48],
                                   attn_w_qg[j][:, 1024:1072].rearrange("(k p) m -> p k m", p=128))
                    wg.buf.w = [tk]
                jobs[wg.name] = qgjob
            for jv in range(2):
                for g in range(2):
                    idx = 2 + jv * 2 + g
                    w1 = Wt.__new__(Wt)
                    w1.KC = 32; w1.MW = 128; w1.NP = 1; w1.dst = zpad[idx:idx + 1]; w1.buf = Buf()
                    W[f"w1_{jv}{g}"] = w1
                    w1.name = f"w1_{jv}{g}"
                    tz = S.dma("sp", zpad[idx], zt_full, reads=[b_zt])

                    def w1job(jv=jv, g=g, idx=idx, w1=w1, tz=tz):
                        S._wait("pool", tz)
                        tk = S.dma("pool", zpad[idx, g * 64:(g + 1) * 64, :].rearrange("p (l e) -> p l e", l=32),
                                   cmp_w1[jv].rearrange("(l d) e -> d l e", d=64))
                        w1.buf.w = [tk]
                    jobs[w1.name] = w1job

            slcK = sb("slcK", [128, 2, NTOK], BF16); b_slcK = Buf()
            slcV = sb("slcV", [128, NTOK // 128, 2, 128], BF16); b_slcV = Buf()
            winK = sb("winK", [128, 2, 1024], BF16); b_winK = Buf()
            winV = sb("winV", [128, 8, 2, 128], BF16); b_winV = Buf()
            cmpT = sb("cmpT", [128, 2, 528], BF16); b_cmpT = Buf()
            kcK = sb("kcK", [128, 2, 256], BF16); b_kcK = Buf()
            vcV = sb("vcV", [128, 2, 2, 128], F32); b_vcV = Buf()
            vcVb = sb("vcVb", [128, 2, 2, 128], BF16); b_vcVb = Buf()
            ident_bf = sb("ident_bf", [128, 128], BF16); b_identb = Buf()
            ones1 = sb("ones1", [128, 128], BF16); b_ones1 = Buf()
            mcs = sb("mcs", [128, 2, 64], BF16); b_mcs = Buf()
            tri = sb("tri", [128, 128], BF16); b_tri = Buf()
            w2k = sb("w2k", [128, 128], BF16); w2v = sb("w2v", [128, 128], BF16); b_w2 = Buf()
            cb1 = sb("cb1", [128, 2]); b_cb1 = Buf()
            b1t = sb("b1t", [128, 2]); posT = sb("posT", [128, 2, 32], BF16); b_posT = Buf()
            S.op("pool", lambda e: e.memset(slcV[:], 1.0), writes=[b_slcV])
            S.op("pool", lambda e: e.memset(winV[:], 1.0), writes=[b_winV])
            S.op("pool", lambda e: e.memset(slcK[:], 0.0), writes=[b_slcK])
            S.op("pool", lambda e: e.memset(winK[:], 0.0), writes=[b_winK])
            S.op("pool", lambda e: e.memset(cmpT[:], 0.0), writes=[b_cmpT])
            S.op("pool", lambda e: e.memset(kcK[:], 0.0), writes=[b_kcK])
            S.op("pool", lambda e: e.memset(vcV[:], 0.0), writes=[b_vcV])
            S.op("pool", lambda e: e.memset(vcVb[:], 0.0), writes=[b_vcVb])
            S.op("pool", lambda e: e.memset(ones1[:], 1.0), writes=[b_ones1])
            S.op("pool", lambda e: e.memset(w2v[:], 0.0), writes=[b_w2])
            S.op("pool", lambda e: e.memset(posT[:], 0.0), writes=[b_posT])
            S.dma("pool", ident_bf[:], ident_d, writes=[b_identb])
            S.dma("pool", mcs[:], t_mcs, writes=[b_mcs])
            S.dma("pool", tri[:], t_tri, writes=[b_tri])
            S.dma("pool", w2k[:, 0:64], cmp_w2[0], writes=[b_w2])
            S.dma("pool", w2k[:, 64:128], cmp_w2[0], writes=[b_w2])
            S.dma("pool", w2v[:, 0:64], cmp_w2[1], writes=[b_w2])
            with nc.allow_non_contiguous_dma(reason="small"):
                S.dma("sp", b1t[:], cmp_b1.rearrange("j e -> e j"), writes=[b_cb1])
                for jv_ in range(2):
                    S.dma("pool", posT[0:64, jv_, :], cmp_pos[:, jv_, :].rearrange("l d -> d l"), writes=[b_posT])

        S.dma("sp", ident[:], ident_d, writes=[b_ident])
        S.dma("sp", flag_sb[:], flag, writes=[b_flag])
        S.op("dve", lambda e: e.memset(ones_bf[:], 1.0 / D), writes=[b_ones])
        S.op("dve", lambda e: e.memset(epsb[:], EPS), writes=[b_eps])
        with nc.allow_non_contiguous_dma(reason="small param vectors to feature-major"):
            for i in range(4):
                S.dma("sp", gvec[:, i, :], g_mix[i].rearrange("(c p) -> p c", p=128), writes=[b_gvec])
                S.dma("sp", gvec[:, 4 + i, :], g_ffn[i].rearrange("(c p) -> p c", p=128), writes=[b_gvec])
                S.dma("sp", gvec[:, 8 + i, :], g_ple[i].rearrange("(c p) -> p c", p=128), writes=[b_gvec])
            S.dma("sp", gvec[:, 12, :], g_final.rearrange("(c p) -> p c", p=128), writes=[b_gvec])
            S.dma("sp", gvec[:, 13, :], g_kv.rearrange("(c p) -> p c", p=128), writes=[b_gvec])
            for l in range(2):
                for k in range(4):
                    S.dma("sp", rgcw[:, l, k, :], rg_conv_w[l, k].rearrange("(c p) -> p c", p=128), writes=[b_rgc])
                S.dma("sp", rgcb[:, l, :], rg_conv_b[l].rearrange("(c p) -> p c", p=128), writes=[b_rgc])
                S.dma("sp", rgba[:, l, :], rg_b_a[l].rearrange("(c p) -> p c", p=128), writes=[b_rgp])
                S.dma("sp", rgbx[:, l, :], rg_b_x[l].rearrange("(c p) -> p c", p=128), writes=[b_rgp])
                S.dma("sp", rgc8[:, l, :], rg_lambda[l].rearrange("(c p) -> p c", p=128), writes=[b_rgp])
            for l in range(4):
                for k in range(3):
                    S.dma("sp", ffcw[:, l, k, :], ffn_conv_w[l, k].rearrange("(c p) -> p c", p=128), writes=[b_ffc])
                S.dma("sp", ffcb[:, l, :], ffn_conv_b[l].rearrange("(c p) -> p c", p=128), writes=[b_ffc])
        S.op("act", lambda e: e.activation(out=rgc8[:], in_=rgc8[:], func=AF.Exp, scale=-1.0), reads=[b_rgp], writes=[b_rgp])
        S.op("act", lambda e: e.activation(out=rgc8[:], in_=rgc8[:], func=AF.Ln, bias=1.0), reads=[b_rgp], writes=[b_rgp])
        S.op("dve", lambda e: e.tensor_scalar(out=rgc8[:], in0=rgc8[:], scalar1=-8.0, scalar2=None, op0=ALU.mult), reads=[b_rgp], writes=[b_rgp])
        allh = [b for row in b_rghist for b in row] + [b for row in b_rghc for b in row] + [b for row in b_ffhist for b in row]
        S.op("dve", lambda e: e.memset(rg_hist[:], 0.0), writes=[b for row in b_rghist for b in row])
        S.op("dve", lambda e: e.memset(rg_hc[:], 0.0), writes=[b for row in b_rghc for b in row])
        S.op("dve", lambda e: e.memset(ff_hist[:], 0.0), writes=[b for row in b_ffhist for b in row])

        pan_i = [0]

        def load_panel(w, pnl):
            i = pan_i[0] % NPAN
            pan_i[0] += 1
            n = w.KC * w.MW
            S.dma("sp", pan[i][:, 0:n], w.dst[pnl], reads=[w.buf], writes=[b_pan[i]])
            return pan[i][:, 0:n].rearrange("p (k m) -> p k m", k=w.KC), b_pan[i]

        class Stream:
            def __init__(self, items, depth=2):
                for w_, _p in items:
                    ensure(w_.name)
                prefetch_casts(4)
                self.items = items
                self.loaded = []
                self.depth = depth
                self.pos = 0

            def get(self):
                while len(self.loaded) < min(len(self.items), self.pos + 1 + self.depth):
                    w, pnl = self.items[len(self.loaded)]
                    self.loaded.append(load_panel(w, pnl))
                r = self.loaded[self.pos]
                self.pos += 1
                return r

        if DO_B:
            for jv in range(2):
                ensure(f"w1_{jv}0")
                pnl, bpn = load_panel(W[f"w1_{jv}0"], 0)
                pt, bp = ps_next()
                for l_ in range(32):
                    S.op("pe", lambda e: e.matmul(pt[:, 0:1], lhsT=pnl[:, l_, :], rhs=posT[:, jv, l_:l_ + 1], start=(l_ == 0), stop=(l_ == 31)),
                         reads=[bpn, b_posT], writes=[bp])
                S.op("dve", lambda e: e.tensor_tensor(out=cb1[:, jv:jv + 1], in0=b1t[:, jv:jv + 1], in1=pt[:, 0:1], op=ALU.add), reads=[bp, b_cb1], writes=[b_cb1])

        def transpose_in(src_rows, ncols, dst, dst_bufs, dst_dt_bf16=False):
            nk = ncols // 128
            S.dma("sp", xt[:, :, 0:ncols], src_rows.rearrange("(b p) f -> p b f", p=128), writes=[b_xt])
            for k in range(nk):
                pt, bp = ps_next()
                for blk in range(4):
                    S.op("pe", lambda e: e.transpose(pt[:, blk * 128:(blk + 1) * 128], xt[:, blk, k * 128:(k + 1) * 128], ident[:]),
                         reads=[b_xt, b_ident], writes=[bp])
                S.op("act", lambda e: e.copy(out=dst[:, k, :], in_=pt[:]), reads=[bp], writes=[dst_bufs[k]])

        def rmsnorm(gi, out_t, out_buf):
            for k in range(8):
                S.op("act", lambda e: e.activation(out=sq[:, k, :], in_=x[:, k, :], func=AF.Square), reads=[b_x[k]], writes=[b_sq])
            pt, bp = ps_next()
            for k in range(8):
                S.op("pe", lambda e: e.matmul(pt[:], lhsT=ones_bf[:], rhs=sq[:, k, :], start=(k == 0), stop=(k == 7)),
                     reads=[b_sq, b_ones], writes=[bp])
            S.op("act", lambda e: e.activation(out=rstd[:], in_=pt[:], func=AF.Sqrt, bias=epsb[:, 0:1]), reads=[bp, b_eps], writes=[b_rstd])
            S.op("dve", lambda e: e.reciprocal(out=rstd[:], in_=rstd[:]), reads=[b_rstd], writes=[b_rstd])
            for k in range(8):
                S.op("dve", lambda e: e.scalar_tensor_tensor(out=out_t[:, k, :], in0=x[:, k, :], scalar=gvec[:, gi, k:k + 1], in1=rstd[:],
                                                              op0=ALU.mult, op1=ALU.mult),
                     reads=[b_x[k], b_gvec, b_rstd], writes=[out_buf])

        def proj_chunk(pnl_ap, pnl_buf, col0, mcols, rhs_t, rhs_buf, nk):
            pt, bp = ps_next()
            for k in range(nk):
                S.op("pe", lambda e: e.matmul(pt[0:mcols, :], lhsT=pnl_ap[:, k, col0:col0 + mcols], rhs=rhs_t[:, k, :],
                                              start=(k == 0), stop=(k == nk - 1)),
                     reads=[pnl_buf] + (rhs_buf if isinstance(rhs_buf, list) else [rhs_buf]), writes=[bp])
            return pt, bp

        def rg_layer(l, ti, es2, smp=False):
            def sb2(name, shape, dt=F32):
                uid[0] += 1
                return es2.enter_context(nc.sbuf_tensor(f"s{uid[0]}_{name}", list(shape), dt))
            y = sb2("rg_y", [128, 10, 512], BF16); b_y = [Buf() for _ in range(10)]
            xc = sb2("rg_xc", [128, 10, 512]); b_xc = [Buf() for _ in range(10)]
            xcb = sb2("rg_xcb", [128, 10, 512], BF16); b_xcb = [Buf() for _ in range(10)]
            xrh = [sb2(f"rg_xrh{i}", [128, 515]) for i in range(2)]; b_xrh = [Buf(), Buf()]
            t1 = [sb2(f"rg_t1{i}", [128, 512]) for i in range(2)]; b_t1 = [Buf(), Buf()]
            gr = [sb2(f"rg_r{i}", [128, 512]) for i in range(2)]; b_gr = [Buf(), Buf()]
            gi_ = [sb2(f"rg_i{i}", [128, 512]) for i in range(2)]; b_gi = [Buf(), Buf()]
            ga = [sb2(f"rg_a{i}", [128, 512]) for i in range(2)]; b_ga = [Buf(), Buf()]
            gm = [sb2(f"rg_m{i}", [128, 512]) for i in range(2)]; b_gm = [Buf(), Buf()]
            hs = [sb2(f"rg_hs{i}", [128, 512]) for i in range(2)]; b_hs = [Buf(), Buf()]
            if smp:
                s_rgh = sb2("s_rgh", [128, 10, 4, 3]); s_h0 = sb2("s_h0", [128, 10, 4]); b_sst = Buf()
                so_rgc = sb2("so_rgc", [128, 10, 4]); so_rgh = sb2("so_rgh", [128, 10, 4]); b_so = Buf()
                S.dma("sp", s_rgh[:], s_rgh_d[:, l], writes=[b_sst])
                S.dma("sp", s_h0[:], s_h0_d[:, l], writes=[b_sst])
            rmsnorm(l, h, b_h)
            st = Stream([(W[f"rgin{l}"], p) for p in range(5)] + [(W[f"band{l}_0"], 0), (W[f"band{l}_1"], 0)]
                        + [(W[f"rgout{l}"], p) for p in range(4)])
            for jc in range(20):
                if jc % 4 == 0:
                    pnl, bpn = st.get()
                pt, bp = proj_chunk(pnl, bpn, (jc % 4) * 128, 128, h, b_h, 8)
                if jc < 10:
                    S.op("act", lambda e: e.activation(out=y[:, jc, :], in_=pt[:], func=AF.Gelu), reads=[bp], writes=[b_y[jc]])
                else:
                    j = jc - 10
                    q = j % 2
                    S.op("act", lambda e: e.copy(out=xrh[q][:, 3:515], in_=pt[:]), reads=[bp], writes=[b_xrh[q]])
                    S.op("act", lambda e: e.activation(out=xc[:, j, :], in_=pt[:], func=AF.Identity, scale=rgcw[:, l, 3, j:j + 1], bias=rgcb[:, l, j:j + 1]),
                         reads=[bp, b_rgc], writes=[b_xc[j]])
                    if ti == OWN0:
                        S.op("pool", lambda e: e.tensor_scalar(out=rg_hist[:, l, j, :], in0=rg_hist[:, l, j, :], scalar1=flag_sb[:, 0:1], scalar2=None, op0=ALU.mult),
                             reads=[b_rghist[l][j], b_flag], writes=[b_rghist[l][j]])
                    S.op("pool", lambda e: e.tensor_copy(out=xrh[q][:, 0:3], in_=rg_hist[:, l, j, :]), reads=[b_rghist[l][j]], writes=[b_xrh[q]])
                    S.op("pool", lambda e: e.tensor_copy(out=rg_hist[:, l, j, :], in_=xrh[q][:, 512:515]), reads=[b_xrh[q]], writes=[b_rghist[l][j]])
                    if smp:
                        S.op("dve", lambda e: e.tensor_copy(out=so_rgc[:, j, :], in_=xrh[q][:, 6:22:4]), reads=[b_xrh[q]], writes=[b_so])
                        S.op("dve", lambda e: e.tensor_copy(out=xrh[q][:, 3:19].rearrange("p (b k) -> p b k", k=4)[:, :, 0:3], in_=s_rgh[:, j, :, :]),
                             reads=[b_sst], writes=[b_xrh[q]])
                    for k in (2, 1, 0):
                        S.op("dve", lambda e: e.scalar_tensor_tensor(out=xc[:, j, :], in0=xrh[q][:, k:k + 512], scalar=rgcw[:, l, k, j:j + 1], in1=xc[:, j, :],
                                                                      op0=ALU.mult, op1=ALU.add), reads=[b_xrh[q], b_rgc, b_xc[j]], writes=[b_xc[j]])
                    S.op("pool", lambda e: e.tensor_copy(out=xcb[:, j, :], in_=xc[:, j, :]), reads=[b_xc[j]], writes=[b_xcb[j]])
            bands = [st.get(), st.get()]
            for jp in range(5):
                js = (2 * jp, 2 * jp + 1)
                for j in js:
                    q = j % 2
                    ins = [i for i in (j - 1, j, j + 1) if 0 <= i < 10]
                    for gidx, (dst, bdst, bias_t) in enumerate(((gr, b_gr, rgba), (gi_, b_gi, rgbx))):
                        pt, bp = ps_next()
                        bv, b_band = bands[gidx]
                        for n_, i in enumerate(ins):
                            S.op("pe", lambda e: e.matmul(pt[:], lhsT=bv[:, i * 3 + (j - i + 1), :], rhs=xcb[:, i, :], start=(n_ == 0), stop=(n_ == len(ins) - 1)),
                                 reads=[b_band, b_xcb[i]], writes=[bp])
                        S.op("act", lambda e: e.activation(out=dst[q][:], in_=pt[:], func=AF.Sigmoid, bias=bias_t[:, l, j:j + 1]),
                             reads=[bp, b_rgp], writes=[bdst[q]])
                for j in js:
                    q = j % 2
                    S.op("act", lambda e: e.activation(out=ga[q][:], in_=gr[q][:], func=AF.Exp, scale=rgc8[:, l, j:j + 1]),
                         reads=[b_gr[q], b_rgp], writes=[b_ga[q]])
                    S.op("dve", lambda e: e.tensor_tensor(out=gm[q][:], in0=ga[q][:], in1=ga[q][:], op=ALU.mult), reads=[b_ga[q]], writes=[b_gm[q]])
                    S.op("dve", lambda e: e.tensor_scalar(out=gm[q][:], in0=gm[q][:], scalar1=-1.0, scalar2=1.0, op0=ALU.mult, op1=ALU.add),
                         reads=[b_gm[q]], writes=[b_gm[q]])
                    S.op("dve", lambda e: e.tensor_tensor(out=gi_[q][:], in0=gi_[q][:], in1=xc[:, j, :], op=ALU.mult), reads=[b_gi[q], b_xc[j]], writes=[b_gi[q]])
                for j in js:
                    q = j % 2
                    S.op("act", lambda e: e.activation(out=gm[q][:], in_=gm[q][:], func=AF.Sqrt), reads=[b_gm[q]], writes=[b_gm[q]])
                for j in js:
                    q = j % 2
                    if ti == 0:
                        S.op("dve", lambda e: e.memset(gm[q][:, 0:1], 1.0), reads=[], writes=[b_gm[q]])
                    elif ti == OWN0:
                        S.op("dve", lambda e: e.tensor_scalar(out=gm[q][:, 0:1], in0=gm[q][:, 0:1], scalar1=flag_sb[:, 1:2], scalar2=None, op0=ALU.max),
                             reads=[b_gm[q], b_flag], writes=[b_gm[q]])
                        S.op("dve", lambda e: e.tensor_scalar(out=rg_hc[:, l, j:j + 1], in0=rg_hc[:, l, j:j + 1], scalar1=flag_sb[:, 0:1], scalar2=None, op0=ALU.mult),
                             reads=[b_rghc[l][j], b_flag], writes=[b_rghc[l][j]])
                    S.op("dve", lambda e: e.tensor_tensor(out=gi_[q][:], in0=gi_[q][:], in1=gm[q][:], op=ALU.mult), reads=[b_gi[q], b_gm[q]], writes=[b_gi[q]])
                    if smp:
                        S.op("dve", lambda e: e.memset(ga[q][:, 2:18:4], 0.0), reads=[], writes=[b_ga[q]])
                        S.op("dve", lambda e: e.tensor_copy(out=gi_[q][:, 2:18:4], in_=s_h0[:, j, :]), reads=[b_sst], writes=[b_gi[q]])
                    S.op("dve", lambda e: e.tensor_tensor_scan(out=hs[q][:], data0=ga[q][:], data1=gi_[q][:], initial=rg_hc[:, l, j:j + 1], op0=ALU.mult, op1=ALU.add),
                         reads=[b_ga[q], b_gi[q], b_rghc[l][j]], writes=[b_hs[q]])
                    if smp:
                        S.op("dve", lambda e: e.tensor_copy(out=so_rgh[:, j, :], in_=hs[q][:, 3:19:4]), reads=[b_hs[q]], writes=[b_so])
                    S.op("dve", lambda e: e.tensor_copy(out=rg_hc[:, l, j:j + 1], in_=hs[q][:, 511:512]), reads=[b_hs[q]], writes=[b_rghc[l][j]])
                    S.op("dve", lambda e: e.tensor_tensor(out=y[:, j, :], in0=y[:, j, :], in1=hs[q][:], op=ALU.mult), reads=[b_y[j], b_hs[q]], writes=[b_y[j]])
            for oc in range(8):
                if oc % 2 == 0:
                    pnl, bpn = st.get()
                pt, bp = ps_next()
                for k in range(10):
                    S.op("pe", lambda e: e.matmul(pt[:], lhsT=pnl[:, k, (oc % 2) * 128:(oc % 2 + 1) * 128], rhs=y[:, k, :], start=(k == 0), stop=(k == 9)),
                         reads=[bpn, b_y[k]], writes=[bp])
                S.op("dve", lambda e: e.tensor_tensor(out=x[:, oc, :], in0=x[:, oc, :], in1=pt[:], op=ALU.add), reads=[b_x[oc], bp], writes=[b_x[oc]])
            if smp:
                S.dma("sp", o_s_rgc_new[:, l], so_rgc[:], reads=[b_so])
                S.dma("sp", o_s_rgh[:, l], so_rgh[:], reads=[b_so])

        def ffn_ple(l, ti, es2, smp=False):
            def sb2(name, shape, dt=F32):
                uid[0] += 1
                return es2.enter_context(nc.sbuf_tensor(f"s{uid[0]}_{name}", list(shape), dt))
            gbuf = sb2("ff_g", [128, 24, 512], BF16); b_g = [Buf() for _ in range(24)]
            uh = [sb2(f"ff_uh{i}", [128, 514]) for i in range(6)]; b_uh = [Buf() for _ in range(6)]
            uc = [sb2(f"ff_uc{i}", [128, 512]) for i in range(6)]; b_uc = [Buf() for _ in range(6)]
            pT = sb2("ple_pT", [128, 2, 512], BF16); b_pT = [Buf(), Buf()]
            sig = [sb2(f"ple_sig{i}", [128, 512]) for i in range(2)]; b_sig = [Buf(), Buf()]

            if smp:
                s_ffh = sb2("s_ffh", [128, 48, 4, 2]); b_sst = Buf()
                so_ffc = sb2("so_ffc", [128, 48, 4]); b_so = Buf()
                S.dma("sp", s_ffh[:], s_ffh_d[:, l], writes=[b_sst])
            rmsnorm(4 + l, h, b_h)
            items = []
            for c4 in range(6):
                items += [(W[f"up{l}"], c4), (W[f"up{l}"], c4 + 6)]
            items += [(W[f"down{l}"], p) for p in range(8)]
            items += [(W[f"plein{l}"], 0), (W[f"pleg{l}"], 0), (W[f"pleg{l}"], 1)]
            st = Stream(items)
            if ti == OWN0:
                for cc in range(48):
                    S.op("pool", lambda e: e.tensor_scalar(out=ff_hist[:, l, cc, :], in0=ff_hist[:, l, cc, :], scalar1=flag_sb[:, 0:1], scalar2=None, op0=ALU.mult),
                         reads=[b_ffhist[l][cc], b_flag], writes=[b_ffhist[l][cc]])
            pans = [None, None]

            def stageA(c):
                if c % 4 == 0:
                    pans[0] = st.get()
                    pans[1] = st.get()
                ci = c % 4
                for hf in range(2):
                    pnl, bpn = pans[hf]
                    cc = c + 24 * hf
                    q = (c % 3) * 2 + hf
                    pt, bp = proj_chunk(pnl, bpn, ci * 128, 128, h, b_h, 8)
                    S.op("act", lambda e: e.copy(out=uh[q][:, 2:514], in_=pt[:]), reads=[bp], writes=[b_uh[q]])
                    S.op("act", lambda e: e.activation(out=uc[q][:], in_=pt[:], func=AF.Identity, scale=ffcw[:, l, 2, cc:cc + 1], bias=ffcb[:, l, cc:cc + 1]),
                         reads=[bp, b_ffc], writes=[b_uc[q]])
                    S.op("pool", lambda e: e.tensor_copy(out=uh[q][:, 0:2], in_=ff_hist[:, l, cc, :]), reads=[b_ffhist[l][cc]], writes=[b_uh[q]])
                    S.op("pool", lambda e: e.tensor_copy(out=ff_hist[:, l, cc, :], in_=uh[q][:, 512:514]), reads=[b_uh[q]], writes=[b_ffhist[l][cc]])
                    if smp:
                        S.op("dve", lambda e: e.tensor_copy(out=so_ffc[:, cc, :], in_=uh[q][:, 5:21:4]), reads=[b_uh[q]], writes=[b_so])
                        S.op("dve", lambda e: e.tensor_copy(out=uh[q][:, 2:18].rearrange("p (b k) -> p b k", k=4)[:, :, 1:3], in_=s_ffh[:, cc, :, :]),
                             reads=[b_sst], writes=[b_uh[q]])

            def stageB(c):
                for hf in range(2):
                    cc = c + 24 * hf
                    q = (c % 3) * 2 + hf
                    for k in (1, 0):
                        S.op("dve", lambda e: e.scalar_tensor_tensor(out=uc[q][:], in0=uh[q][:, k:k + 512], scalar=ffcw[:, l, k, cc:cc + 1], in1=uc[q][:],
                                                                      op0=ALU.mult, op1=ALU.add), reads=[b_uh[q], b_ffc, b_uc[q]], writes=[b_uc[q]])
                q0 = (c % 3) * 2
                S.op("act", lambda e: e.activation(out=uc[q0][:], in_=uc[q0][:], func=AF.Gelu), reads=[b_uc[q0]], writes=[b_uc[q0]])
                S.op("pool", lambda e: e.tensor_tensor(out=gbuf[:, c, :], in0=uc[q0][:], in1=uc[q0 + 1][:], op=ALU.mult),
                     reads=[b_uc[q0], b_uc[q0 + 1]], writes=[b_g[c]])
            stageA(0)
            for c in range(24):
                if c + 1 < 24:
                    stageA(c + 1)
                stageB(c)
            for oc in range(8):
                pnl, bpn = st.get()
                pt, bp = ps_next()
                for k in range(24):
                    S.op("pe", lambda e: e.matmul(pt[:], lhsT=pnl[:, k, :], rhs=gbuf[:, k, :], start=(k == 0), stop=(k == 23)),
                         reads=[bpn, b_g[k]], writes=[bp])
                S.op("dve", lambda e: e.tensor_tensor(out=x[:, oc, :], in0=x[:, oc, :], in1=pt[:], op=ALU.add), reads=[b_x[oc], bp], writes=[b_x[oc]])
            if smp:
                S.dma("sp", o_s_ffc_new[:, l], so_ffc[:], reads=[b_so])
            if smp:
                S.dma("pool", pT[:], ps_s[l], writes=b_pT)
            else:
                transpose_in(ps_in[l, ti * 512:(ti + 1) * 512, :], 256, pT, b_pT)
            rmsnorm(8 + l, h, b_h)
            pin, bpin = st.get()
            pg = [st.get(), st.get()]
            for oc in range(8):
                q = oc % 2
                pe_, bpe = proj_chunk(pin, bpin, oc * 128, 128, pT, b_pT, 2)
                pgp, bpg = pg[oc // 4]
                pt, bp = proj_chunk(pgp, bpg, (oc % 4) * 128, 128, h, b_h, 8)
                S.op("act", lambda e: e.activation(out=sig[q][:], in_=pt[:], func=AF.Sigmoid), reads=[bp], writes=[b_sig[q]])
                S.op("dve", lambda e: e.tensor_tensor(out=sig[q][:], in0=sig[q][:], in1=pe_[:], op=ALU.mult), reads=[b_sig[q], bpe], writes=[b_sig[q]])
                S.op("dve", lambda e: e.tensor_tensor(out=x[:, oc, :], in0=x[:, oc, :], in1=sig[q][:], op=ALU.add), reads=[b_x[oc], b_sig[q]], writes=[b_x[oc]])


        def attn_layer(l, ti, es2, qbs, smp=False):
            j = l - 2

            def sb2(name, shape, dt=F32):
                uid[0] += 1
                return es2.enter_context(nc.sbuf_tensor(f"s{uid[0]}_{name}", list(shape), dt))
            qT = sb2("qT", [128, 8, 512], BF16); b_qT = [Buf() for _ in range(8)]
            qrT = sb2("qrT", [128, 8, 512], BF16); b_qrT = [Buf() for _ in range(8)]
            gT = sb2("gT", [128, 512], BF16); b_gT = Buf()
            oT = sb2("oT", [128, 8, 512], BF16); b_oT = Buf()
            rl = sb2("rl", [128, 1024]); b_rl = Buf()
            rc_t = rl[:, 0:512]; rs_t = rl[:, 512:1024]; b_rope = b_rl
            rt1 = sb2("rt1", [128, 512]); rt2 = sb2("rt2", [128, 512]); b_rt1 = Buf(); b_rt2 = Buf()
            if smp:
                qbs = []
            PW = 8 if smp else 1024
            qpad = [sb2(f"qpad{i}", [128, 8, 128 if not smp else 2], BF16) for i in range(2)]; b_qpad = [Buf(), Buf()]
            qrpad = [sb2(f"qrpad{i}", [128, 8, 128 if not smp else 2], BF16) for i in range(2)]; b_qrpad = [Buf(), Buf()]
            cbt = [sb2(f"cbt{i}", [128, 256], BF16) for i in range(2)]; b_cbt = [Buf(), Buf()]
            wbt = [sb2(f"wbt{i}", [128, 640], BF16) for i in range(2)]; b_wbt = [Buf(), Buf()]
            vat = [sb2(f"vat{i}", [128, 2, 64]) for i in range(2)]; b_vat = [Buf(), Buf()]
            Pc = sb2("Pc", [128, 2, PW], BF16); b_Pc = [Buf(), Buf()]
            Pn = Pc; b_Pn = b_Pc
            sqf = sq[:].rearrange("p k t -> p (k t)")
            Pst = [sqf[:, i * 1024:(i + 1) * 1024] for i in range(2)]; b_Pst = [Buf() for _ in range(2)]
            pst_i = [0]
            osb = [sb2(f"osb{i}", [64, PW]) for i in range(3)]; b_osb = [Buf() for _ in range(3)]
            rlx = sb2("rlx", [64, PW]); b_rlx = Buf()
            impf = sb2("impf", [128, 64]); b_impf = Buf()
            impt = sb2("impt", [128, 64]); b_impt = Buf()
            m8 = sb2("m8", [128, 16]); b_m8 = Buf()
            thr = sb2("thr", [128, 1]); b_thr = Buf()
            nbf = sb2("nbf", [128, 64]); b_nbf = Buf()
            nb = sb2("nb", [128, 64], BF16); b_nb = Buf()
            nbd = sb2("nbd", [128, 128], BF16); b_nbd = Buf()
            nbe = [sb2(f"nbe{i}", [128, 128], BF16) for i in range(4)]; b_nbe = [Buf() for _ in range(4)]
            nbe_i = [0]
            for t_ in qpad + qrpad:
                S.op("pool", lambda e: e.memset(t_[:], 0.0), writes=b_qpad + b_qrpad)

            S.dma("sp", rc_t, ropec_s if smp else ropec[:, ti * 512:(ti + 1) * 512], writes=[b_rope])
            S.dma("sp", rs_t, ropes_s if smp else ropes[:, ti * 512:(ti + 1) * 512], writes=[b_rope])
            rmsnorm(l, h, b_h)
            for bq in b_Pst:
                bq.w = b_sq.w
                bq.r = list(b_sq.r)
            st = Stream([(W[f"q{j}"], 0), (W[f"qsw{j}"], 0), (W[f"q{j}"], 1), (W[f"qsw{j}"], 1), (W[f"qg{j}"], 0), (W[f"wo{j}"], 0), (W[f"wo{j}"], 1)], depth=2)
            for c in range(8):
                if c % 4 == 0:
                    pq, bpq = st.get()
                    psw, bpsw = st.get()
                pt, bp = proj_chunk(pq, bpq, (c % 4) * 128, 128, h, b_h, 8)
                pt2, bp2 = proj_chunk(psw, bpsw, (c % 4) * 128, 128, h, b_h, 8)
                S.op("dve", lambda e: e.tensor_copy(out=qT[:, c, :], in_=pt[:]), reads=[bp], writes=[b_qT[c]])
                S.op("dve", lambda e: e.tensor_tensor(out=rt1[:], in0=pt[:], in1=rc_t, op=ALU.mult), reads=[bp, b_rope], writes=[b_rt1])
                S.op("dve", lambda e: e.tensor_tensor(out=rt2[:], in0=pt2[:], in1=rs_t, op=ALU.mult), reads=[bp2, b_rope], writes=[b_rt2])
                S.op("dve", lambda e: e.tensor_tensor(out=qrT[:, c, :], in0=rt1[:], in1=rt2[:], op=ALU.add), reads=[b_rt1, b_rt2], writes=[b_qrT[c]])
            pg, bpg = st.get()
            pt, bp = proj_chunk(pg, bpg, 0, 128, h, b_h, 8)
            S.op("act", lambda e: e.activation(out=gT[:], in_=pt[:], func=AF.Sigmoid), reads=[bp], writes=[b_gT])
            identb4 = ident_bf[:].unsqueeze(1).broadcast_to([128, 4, 128])

            def scores(lhs_bias, bias_bufs, kmat, kbufs, qp, bqp):
                pp, bpp = ps_pair()
                for bank in range(2):
                    S.op("pe", lambda e: e.matmul(pp[:, bank * 512:(bank + 1) * 512], lhsT=lhs_bias, rhs=identb4, start=True, stop=False),
                         reads=bias_bufs + [b_identb], writes=[bpp[bank]])
                    for h4 in range(4):
                        hh = bank * 4 + h4
                        S.op("pe", lambda e: e.matmul(pp[:, hh * 128:(hh + 1) * 128], lhsT=kmat, rhs=qp[:, hh, :], start=False, stop=(h4 == 3)),
                             reads=kbufs + [bqp], writes=[bpp[bank]])
                return pp, bpp

            def pv_acc(vmat, vbufs, pt_, bpt_, first, last):
                for bank in range(2):
                    S.op("pe", lambda e: e.matmul(acc_ps[:, bank * 512:(bank + 1) * 512], lhsT=vmat, rhs=pt_[:, bank * 512:(bank + 1) * 512], start=first, stop=last),
                         reads=vbufs + [bpt_], writes=[b_acc[bank]])

            if len(qbs) < 4:
                S.op("pool", lambda e: e.memset(oT[:], 0.0), writes=[b_oT])
            for qb in (qbs if DBG >= 5 else []):
                qbo = (ti - OWN0) * 4 + qb + 1
                kd = 4 * ti + qb
                tb = qbo % 2
                S.dma("pool", cbt[tb][:], t_cb[qbo], writes=[b_cbt[tb]])
                S.dma("pool", wbt[tb][:], t_wb[qbo], writes=[b_wbt[tb]])
                S.dma("sp", vat[tb][:], t_va[qbo], writes=[b_vat[tb]])
                tsl = slice(qb * 128, (qb + 1) * 128)
                for g in range(2):
                    pi = g
                    S.op("pool", lambda e: e.tensor_copy(out=qpad[pi][0:64, 0:8:2, :], in_=qT[0:64, 4 * g:4 * g + 4, tsl]), reads=b_qT[4 * g:4 * g + 4], writes=[b_qpad[pi]])
                    S.op("pool", lambda e: e.tensor_copy(out=qpad[pi][64:128, 1:8:2, :], in_=qT[64:128, 4 * g:4 * g + 4, tsl]), reads=b_qT[4 * g:4 * g + 4], writes=[b_qpad[pi]])
                    S.op("pool", lambda e: e.tensor_copy(out=qrpad[pi][0:64, 0:8:2, :], in_=qrT[0:64, 4 * g:4 * g + 4, tsl]), reads=b_qrT[4 * g:4 * g + 4], writes=[b_qrpad[pi]])
                    S.op("pool", lambda e: e.tensor_copy(out=qrpad[pi][64:128, 1:8:2, :], in_=qrT[64:128, 4 * g:4 * g + 4, tsl]), reads=b_qrT[4 * g:4 * g + 4], writes=[b_qrpad[pi]])
                    nbc = 1 if 32 * ti + 31 < 128 else 2
                    for bc in range(nbc):
                        pp, bpp = scores(cbt[tb][:, bc * 128:(bc + 1) * 128], [b_cbt[tb]], kcK[:, g, bc * 128:(bc + 1) * 128], [b_kcK], qpad[pi], b_qpad[pi])
                        S.op("act", lambda e: e.activation(out=Pc[:, bc, :], in_=pp[:], func=AF.Exp, scale=0.125), reads=bpp, writes=[b_Pc[bc]])
                    lp, blp = ps_pair()
                    for bank in range(2):
                        for bc in range(nbc):
                            S.op("pe", lambda e: e.matmul(lp[:, bank * 512:(bank + 1) * 512], lhsT=ones1[:], rhs=Pc[:, bc, bank * 512:(bank + 1) * 512],
                                                          start=(bc == 0), stop=(bc == nbc - 1)), reads=[b_ones1, b_Pc[bc]], writes=[blp[bank]])
                    S.op("dve", lambda e: e.tensor_scalar(out=rl[:], in0=lp[:], scalar1=1e-30, scalar2=None, op0=ALU.add), reads=blp, writes=[b_rl])
                    S.op("dve", lambda e: e.reciprocal(out=rl[:], in_=rl[:]), reads=[b_rl], writes=[b_rl])
                    for bc in range(nbc):
                        S.op("dve", lambda e: e.tensor_tensor(out=Pn[:, bc, :], in0=Pc[:, bc, :], in1=rl[:], op=ALU.mult), reads=[b_Pc[bc], b_rl], writes=[b_Pn[bc]])
                    for bc in range(nbc):
                        pv_acc(vcVb[:, bc, g, :], [b_vcVb], Pn[:, bc, :], b_Pn[bc], bc == 0, bc == nbc - 1)
                    S.op("act", lambda e: e.copy(out=osb[0][:], in_=acc_ps[0:64, :]), reads=b_acc, writes=[b_osb[0]])
                    ip, bip = ps_next()
                    n_ = 0
                    for bc in range(nbc):
                        for hh in range(8):
                            S.op("pe", lambda e: e.matmul(ip[:, 0:64], lhsT=Pn[:, bc, hh * 128:(hh + 1) * 128], rhs=mcs[:, bc, :], start=(n_ == 0), stop=(n_ == 8 * nbc - 1)),
                                 reads=[b_Pn[bc], b_mcs], writes=[bip])
                            n_ += 1
                    S.op("dve", lambda e: e.tensor_tensor(out=impf[:], in0=ip[:, 0:64], in1=vat[tb][:, 0, :], op=ALU.mult), reads=[bip, b_vat[tb]], writes=[b_impf])
                    S.op("dve", lambda e: e.tensor_tensor(out=impf[:], in0=impf[:], in1=vat[tb][:, 1, :], op=ALU.add), reads=[b_impf, b_vat[tb]], writes=[b_impf])
                    S.op("dve", lambda e: e.max(out=m8[:, 0:8], in_=impf[:]), reads=[b_impf], writes=[b_m8])
                    S.op("dve", lambda e: e.match_replace(out=impt[:], in_to_replace=m8[:, 0:8], in_values=impf[:], imm_value=-2.0), reads=[b_impf, b_m8], writes=[b_impt])
                    S.op("dve", lambda e: e.max(out=m8[:, 8:16], in_=impt[:]), reads=[b_impt], writes=[b_m8])
                    S.op("dve", lambda e: e.tensor_scalar(out=thr[:], in0=m8[:, 15:16], scalar1=-0.5, scalar2=None, op0=ALU.max), reads=[b_m8], writes=[b_thr])
                    S.op("dve", lambda e: e.tensor_scalar(out=nbf[:], in0=impf[:], scalar1=thr[:, 0:1], scalar2=BIGV, op0=ALU.is_ge, op1=ALU.mult),
                         reads=[b_impf, b_thr], writes=[b_nbf])
                    S.op("dve", lambda e: e.tensor_scalar(out=nb[:], in0=nbf[:], scalar1=-BIGV, scalar2=None, op0=ALU.add), reads=[b_nbf], writes=[b_nb])
                    S.op("dve", lambda e: e.tensor_tensor(out=nbd[:].rearrange("p (s k) -> p s k", s=2), in0=nb[:, 2 * kd:2 * kd + 2].unsqueeze(2).broadcast_to([128, 2, 64]),
                                                           in1=tri[:].rearrange("p (s k) -> p s k", s=2), op=ALU.add), reads=[b_nb, b_tri], writes=[b_nbd])
                    def slc_front(kc):
                        if kc == kd:
                            lb, lbb = nbd[:], [b_nbd]
                        else:
                            z = nbe_i[0] % 4
                            nbe_i[0] += 1
                            S.op("dve", lambda e: e.tensor_copy(out=nbe[z][:].rearrange("p (s k) -> p s k", s=2),
                                                                 in_=nb[:, 2 * kc:2 * kc + 2].unsqueeze(2).broadcast_to([128, 2, 64])), reads=[b_nb], writes=[b_nbe[z]])
                            lb, lbb = nbe[z][:], [b_nbe[z]]
                        pp, bpp = scores(lb, lbb, slcK[:, g, kc * 128:(kc + 1) * 128], [b_slcK], qrpad[pi], b_qrpad[pi])
                        pz = pst_i[0] % 2
                        pst_i[0] += 1
                        S.op("act", lambda e: e.activation(out=Pst[pz], in_=pp[:], func=AF.Exp, scale=0.125), reads=bpp, writes=[b_Pst[pz]])
                        return pz
                    cur = slc_front(0)
                    for kc in range(kd + 1):
                        nxt = slc_front(kc + 1) if kc < kd else None
                        pv_acc(slcV[:, kc, g, :], [b_slcV], Pst[cur], b_Pst[cur], kc == 0, kc == kd)
                        cur = nxt
                    S.op("dve", lambda e: e.reciprocal(out=rlx[:], in_=acc_ps[64:128, :]), reads=b_acc, writes=[b_rlx])
                    S.op("dve", lambda e: e.tensor_tensor(out=osb[1][:], in0=acc_ps[0:64, :], in1=rlx[:], op=ALU.mult), reads=b_acc + [b_rlx], writes=[b_osb[1]])
                    def win_front(wi):
                        rch = (kd - 4 + wi) % 8
                        pp, bpp = scores(wbt[tb][:, wi * 128:(wi + 1) * 128], [b_wbt[tb]], winK[:, g, rch * 128:(rch + 1) * 128], [b_winK], qrpad[pi], b_qrpad[pi])
                        pz = pst_i[0] % 2
                        pst_i[0] += 1
                        S.op("act", lambda e: e.activation(out=Pst[pz], in_=pp[:], func=AF.Exp, scale=0.125), reads=bpp, writes=[b_Pst[pz]])
                        return pz
                    cur = win_front(0)
                    for wi in range(5):
                        nxt = win_front(wi + 1) if wi < 4 else None
                        rch = (kd - 4 + wi) % 8
                        pv_acc(winV[:, rch, g, :], [b_winV], Pst[cur], b_Pst[cur], wi == 0, wi == 4)
                        cur = nxt
                    S.op("dve", lambda e: e.reciprocal(out=rlx[:], in_=acc_ps[64:128, :]), reads=b_acc, writes=[b_rlx])
                    S.op("dve", lambda e: e.tensor_tensor(out=osb[2][:], in0=acc_ps[0:64, :], in1=rlx[:], op=ALU.mult), reads=b_acc + [b_rlx], writes=[b_osb[2]])
                    for br in range(3):
                        gp, bgp = ps_pair()
                        for hh in range(8):
                            f0 = 3 * (8 * g + hh) + br
                            S.op("pe", lambda e: e.matmul(gp[:, hh * 128:(hh + 1) * 128], lhsT=ident_bf[:, f0:f0 + 1].broadcast_to([128, 128]), rhs=gT[:, tsl], start=True, stop=True),
                                 reads=[b_identb, b_gT], writes=[bgp[hh // 4]])
                        S.op("dve", lambda e: e.tensor_tensor(out=osb[br][:], in0=osb[br][:], in1=gp[0:64, :], op=ALU.mult), reads=[b_osb[br]] + bgp, writes=[b_osb[br]])
                    S.op("dve", lambda e: e.tensor_tensor(out=osb[0][:], in0=osb[0][:], in1=osb[1][:], op=ALU.add), reads=[b_osb[0], b_osb[1]], writes=[b_osb[0]])
                    av = osb[0][:].rearrange("p (h t) -> p h t", h=8)
                    tv = osb[2][:].rearrange("p (h t) -> p h t", h=8)
                    S.op("dve", lambda e: e.tensor_tensor(out=oT[0:64, 4 * g:4 * g + 4, tsl], in0=av[:, 0:8:2, :], in1=tv[:, 0:8:2, :], op=ALU.add),
                         reads=[b_osb[0], b_osb[2]], writes=[b_oT])
                    S.op("dve", lambda e: e.tensor_tensor(out=oT[64:128, 4 * g:4 * g + 4, tsl], in0=av[:, 1:8:2, :], in1=tv[:, 1:8:2, :], op=ALU.add),
                         reads=[b_osb[0], b_osb[2]], writes=[b_oT])

            if smp:
                S.op("pool", lambda e: e.memset(oT[:], 0.0), writes=[b_oT])
                QS = sb2("QS", [128, 16], BF16); QRS = sb2("QRS", [128, 16], BF16); b_QS = Buf()
                S.op("pool", lambda e: e.memset(QS[:], 0.0), writes=[b_QS])
                S.op("pool", lambda e: e.memset(QRS[:], 0.0), writes=[b_QS])
                PcS = sb2("PcS", [128, 4, 16], BF16); b_PcS = Buf()
                PnG = sb2("PnG", [128, 4, 2], BF16); b_PnG = Buf()
                PnGf = sb2("PnGf", [128, 4, 2]); b_PnGf = Buf()
                rlS = sb2("rlS", [128, 16]); b_rlS = Buf()
                tpin = sb2("tpin", [128, 2, 128]); b_tpin = Buf()
                S.op("pool", lambda e: e.memset(tpin[:], 0.0), writes=[b_tpin])
                impS = sb2("impS", [128, 256]); impS2 = sb2("impS2", [128, 256]); b_impS = Buf()
                m8s = sb2("m8s", [128, 16]); thrs = sb2("thrs", [128, 1])
                nbS = sb2("nbS", [128, 256], BF16); b_nbS = Buf()
                gb = [sb2(f"gb{i}", [128, 256]) for i in range(4)]; b_gb = [Buf() for _ in range(4)]
                KTc = [sb2(f"KTc{i}", [128, 128], BF16) for i in range(3)]; b_KTc = [Buf() for _ in range(3)]
                wbuf = sb2("wbuf", [128, 4, 256]); b_wbuf = Buf()
                PsS = [sb2(f"PsS{i}", [128, 16], BF16) for i in range(3)]; b_PsS = [Buf() for _ in range(3)]
                GS = sb2("GS", [128, 48, 4]); b_GS = Buf()
                ocS = sb2("ocS", [64, 16]); osS = sb2("osS", [64, 2, 16]); rlS2 = sb2("rlS2", [64, 2, 16]); b_ocS = Buf()
                oS = sb2("oS", [64, 16]); tS = sb2("tS", [64, 16]); b_oS = Buf()
                gpt, bgpt = ps_next()
                for f0 in range(48):
                    S.op("pe", lambda e: e.matmul(gpt[:, f0 * 4:(f0 + 1) * 4], lhsT=ident_bf[:, f0:f0 + 1].broadcast_to([128, 128]), rhs=gT[:, 3:19:4], start=True, stop=True),
                         reads=[b_identb, b_gT], writes=[bgpt])
                S.op("dve", lambda e: e.tensor_copy(out=GS[:].rearrange("p f b -> p (f b)"), in_=gpt[:, 0:192]), reads=[bgpt], writes=[b_GS])
                gi_c = [0]
                kt_c = [0]
                ps_c = [0]
                ne_c = [0]
                for i in range(4):
                    col = 4 * i + 3
                    for g in range(2):
                        for par in range(2):
                            S.op("dve", lambda e: e.tensor_copy(out=QS[64 * g:64 * g + 64, 8 * g + par:8 * g + 8:2], in_=qT[64 * par:64 * par + 64, 4 * g:4 * g + 4, col]),
                                 reads=b_qT[4 * g:4 * g + 4], writes=[b_QS])
                            S.op("dve", lambda e: e.tensor_copy(out=QRS[64 * g:64 * g + 64, 8 * g + par:8 * g + 8:2], in_=qrT[64 * par:64 * par + 64, 4 * g:4 * g + 4, col]),
                                 reads=b_qrT[4 * g:4 * g + 4], writes=[b_QS])
                    for pc in range(4):
                        pp, bpp = ps_next()
                        if pc == 0:
                            S.op("pe", lambda e: e.matmul(pp[:, 0:16], lhsT=cb0, rhs=brhs, start=True, stop=False), reads=[b_stab], writes=[bpp])
                        S.op("pe", lambda e: e.matmul(pp[:, 0:16], lhsT=kcS[:, i, pc * 128:(pc + 1) * 128], rhs=QS[:], start=(pc != 0), stop=True),
                             reads=[b_kcS, b_QS], writes=[bpp])
                        S.op("act", lambda e: e.activation(out=PcS[:, pc, :], in_=pp[:, 0:16], func=AF.Exp, scale=0.125), reads=[bpp], writes=[b_PcS])
                    lp, blp = ps_next()
                    for pc in range(4):
                        S.op("pe", lambda e: e.matmul(lp[:, 0:16], lhsT=ones1[:], rhs=PcS[:, pc, :], start=(pc == 0), stop=(pc == 3)), reads=[b_ones1, b_PcS], writes=[blp])
                    S.op("dve", lambda e: e.tensor_scalar(out=rlS[:], in0=lp[:, 0:16], scalar1=1e-30, scalar2=None, op0=ALU.add), reads=[blp], writes=[b_rlS])
                    S.op("dve", lambda e: e.reciprocal(out=rlS[:], in_=rlS[:]), reads=[b_rlS], writes=[b_rlS])
                    for pc in range(4):
                        S.op("dve", lambda e: e.tensor_tensor(out=PcS[:, pc, :], in0=PcS[:, pc, :], in1=rlS[:], op=ALU.mult), reads=[b_PcS, b_rlS], writes=[b_PcS])
                    for pc in range(4):
                        S.op("pe", lambda e: e.matmul(acc_ps[:, 0:16], lhsT=vcS[:, i, pc, :], rhs=PcS[:, pc, :], start=(pc == 0), stop=(pc == 3)),
                             reads=[b_vcS, b_PcS], writes=[b_acc[0]])
                    S.op("dve", lambda e: e.tensor_copy(out=ocS[:, 0:8], in_=acc_ps[0:64, 0:8]), reads=[b_acc[0]], writes=[b_ocS])
                    S.op("dve", lambda e: e.tensor_copy(out=ocS[:, 8:16], in_=acc_ps[64:128, 8:16]), reads=[b_acc[0]], writes=[b_ocS])
                    for pc in range(4):
                        S.op("dve", lambda e: e.tensor_reduce(out=PnGf[:, pc, :], in_=PcS[:, pc, :].rearrange("p (g h) -> p g h", g=2), axis=mybir.AxisListType.X, op=ALU.add),
                             reads=[b_PcS], writes=[b_PnGf])
                    S.op("dve", lambda e: e.tensor_copy(out=PnG[:], in_=PnGf[:]), reads=[b_PnGf], writes=[b_PnG])
                    ipT, bipT = ps_next()
                    for sc in range(2):
                        for pc in range(4):
                            S.op("pe", lambda e: e.matmul(ipT[:, 2 * sc:2 * sc + 2], lhsT=mcsS[:, pc, sc, :], rhs=PnG[:, pc, :], start=(pc == 0), stop=(pc == 3)),
                                 reads=[b_stab, b_PnG], writes=[bipT])
                    S.op("dve", lambda e: e.tensor_copy(out=tpin[:, :, 0:2], in_=ipT[:, 0:4].rearrange("p (s g) -> p s g", s=2)), reads=[bipT], writes=[b_tpin])
                    tp, btp = ps_next()
                    for sc in range(2):
                        S.op("pe", lambda e: e.transpose(tp[:, sc * 128:(sc + 1) * 128], tpin[:, sc, :], ident[:]), reads=[b_tpin, b_ident], writes=[btp])
                    S.op("dve", lambda e: e.tensor_tensor(out=impS[:], in0=tp[:, 0:256], in1=tas[:, 0, :], op=ALU.mult), reads=[btp, b_stab], writes=[b_impS])
                    S.op("dve", lambda e: e.tensor_tensor(out=impS[:], in0=impS[:], in1=tas[:, 1, :], op=ALU.add), reads=[b_impS, b_stab], writes=[b_impS])
                    S.op("dve", lambda e: e.max(out=m8s[:, 0:8], in_=impS[:]), reads=[b_impS], writes=[b_impS])
                    S.op("dve", lambda e: e.match_replace(out=impS2[:], in_to_replace=m8s[:, 0:8], in_values=impS[:], imm_value=-2.0), reads=[b_impS], writes=[b_impS])
                    S.op("dve", lambda e: e.max(out=m8s[:, 8:16], in_=impS2[:]), reads=[b_impS], writes=[b_impS])
                    S.op("dve", lambda e: e.tensor_scalar(out=thrs[:], in0=m8s[:, 15:16], scalar1=-0.5, scalar2=None, op0=ALU.max), reads=[b_impS], writes=[b_impS])
                    S.op("dve", lambda e: e.tensor_scalar(out=impS2[:], in0=impS[:], scalar1=thrs[:, 0:1], scalar2=BIGV, op0=ALU.is_ge, op1=ALU.mult), reads=[b_impS], writes=[b_impS])
                    S.op("dve", lambda e: e.tensor_scalar(out=nbS[:], in0=impS2[:], scalar1=-BIGV, scalar2=None, op0=ALU.add), reads=[b_impS], writes=[b_nbS])

                    def s_front(bias_l, bias_b, kt, ktb):
                        pp, bpp = ps_next()
                        if bias_l is not None:
                            S.op("pe", lambda e: e.matmul(pp[:, 0:16], lhsT=bias_l, rhs=brhs, start=True, stop=False), reads=bias_b + [b_stab], writes=[bpp])
                        S.op("pe", lambda e: e.matmul(pp[:, 0:16], lhsT=kt, rhs=QRS[:], start=(bias_l is None), stop=True), reads=ktb + [b_QS], writes=[bpp])
                        z = ps_c[0] % 3
                        ps_c[0] += 1
                        S.op("act", lambda e: e.activation(out=PsS[z][:], in_=pp[:, 0:16], func=AF.Exp, scale=0.125), reads=[bpp], writes=[b_PsS[z]])
                        return z

                    def s_pv(z, v0, v1, vb, first, last):
                        for g, vv in enumerate((v0, v1)):
                            S.op("pe", lambda e: e.matmul(acc_ps[:, 512 + 8 * g:512 + 8 * g + 8], lhsT=vv, rhs=PsS[z][:, 8 * g:8 * g + 8],
                                                          start=(first and g == 0), stop=(last and g == 1)),
                                 reads=vb + [b_PsS[z]], writes=[b_acc[1]])

                    def gather(kc):
                        gz = gi_c[0] % 4
                        gi_c[0] += 1
                        S._deps("pool", [b_idxS], [b_gb[gz]])
                        tk = S.dma_ind(gb[gz][:], c_slc, idxS[:, i, kc:kc + 1])
                        S._mark(tk, [b_idxS], [b_gb[gz]])
                        return gz

                    def slc_front_s(kc, gz):
                        if kc == 64:
                            z = s_front(b64[:, i, :], [], KTn[:, 0, :], [b_KTn])
                            return (z, Vn[:, 0, 0, :], Vn[:, 0, 1, :], [b_Vn])
                        zz = ne_c[0] % 4
                        ne_c[0] += 1
                        tq, btq = ps_next()
                        S.op("pe", lambda e: e.transpose(tq[:, 0:128], gb[gz][:, 0:128], ident[:]), reads=[b_gb[gz], b_ident], writes=[btq])
                        kz = kt_c[0] % 3
                        kt_c[0] += 1
                        S.op("act", lambda e: e.copy(out=KTc[kz][:], in_=tq[:, 0:128]), reads=[btq], writes=[b_KTc[kz]])
                        vz = 2 + (kc % 4)
                        S.op("dve", lambda e: e.tensor_copy(out=slcV[:, vz, :, 0:64], in_=gb[gz][:, 128:256].rearrange("p (g d) -> p g d", g=2)),
                             reads=[b_gb[gz]], writes=[b_Vc[kc % 4]])
                        S.op("dve", lambda e: e.tensor_copy(out=nbe[zz][:].rearrange("p (s k) -> p s k", s=2),
                                                             in_=nbS[:, 2 * kc:2 * kc + 2].unsqueeze(2).broadcast_to([128, 2, 64])), reads=[b_nbS], writes=[b_nbe[zz]])
                        z = s_front(nbe[zz][:], [b_nbe[zz]], KTc[kz][:], [b_KTc[kz]])
                        return (z, slcV[:, vz, 0, :], slcV[:, vz, 1, :], [b_Vc[kc % 4]])

                    gq = [gather(kc) for kc in range(3)]
                    cur = slc_front_s(0, gq[0])
                    for kc in range(65):
                        if kc + 3 < 64:
                            gq.append(gather(kc + 3))
                        nxt = slc_front_s(kc + 1, gq[kc + 1] if kc + 1 < 64 else None) if kc < 64 else None
                        s_pv(cur[0], cur[1], cur[2], cur[3], kc == 0, kc == 64)
                        cur = nxt
                    S.op("dve", lambda e: e.reciprocal(out=rlS2[:, 0, :], in_=acc_ps[64:128, 512:528]), reads=[b_acc[1]], writes=[b_ocS])
                    S.op("dve", lambda e: e.tensor_tensor(out=osS[:, 0, :], in0=acc_ps[0:64, 512:528], in1=rlS2[:, 0, :], op=ALU.mult), reads=[b_acc[1], b_ocS], writes=[b_ocS])
                    S.dma("sp", wbuf[:], c_win[i].rearrange("(c p) f -> p c f", p=128), writes=[b_wbuf])

                    def win_front_s(wc):
                        if wc == 4:
                            z = s_front(b64[:, i, :], [], KTn[:, 1, :], [b_KTn])
                            return (z, Vn[:, 1, 0, :], Vn[:, 1, 1, :], [b_Vn])
                        tq, btq = ps_next()
                        S.op("pe", lambda e: e.transpose(tq[:, 0:128], wbuf[:, wc, 0:128], ident[:]), reads=[b_wbuf, b_ident], writes=[btq])
                        kz = kt_c[0] % 3
                        kt_c[0] += 1
                        S.op("act", lambda e: e.copy(out=KTc[kz][:], in_=tq[:, 0:128]), reads=[btq], writes=[b_KTc[kz]])
                        vz = 2 + wc
                        S.op("dve", lambda e: e.tensor_copy(out=slcV[:, vz, :, 0:64], in_=wbuf[:, wc, 128:256].rearrange("p (g d) -> p g d", g=2)),
                             reads=[b_wbuf], writes=[b_Vc[wc]])
                        z = s_front(wb0 if wc == 0 else None, [], KTc[kz][:], [b_KTc[kz]])
                        return (z, slcV[:, vz, 0, :], slcV[:, vz, 1, :], [b_Vc[wc]])
                    cur = win_front_s(0)
                    for wc in range(5):
                        nxt = win_front_s(wc + 1) if wc < 4 else None
                        s_pv(cur[0], cur[1], cur[2], cur[3], wc == 0, wc == 4)
                        cur = nxt
                    S.op("dve", lambda e: e.reciprocal(out=rlS2[:, 1, :], in_=acc_ps[64:128, 512:528]), reads=[b_acc[1]], writes=[b_ocS])
                    S.op("dve", lambda e: e.tensor_tensor(out=osS[:, 1, :], in0=acc_ps[0:64, 512:528], in1=rlS2[:, 1, :], op=ALU.mult), reads=[b_acc[1], b_ocS], writes=[b_ocS])
                    Gv = GS[0:64, :, i].rearrange("p (h r) -> p h r", r=3)
                    S.op("dve", lambda e: e.tensor_tensor(out=oS[:], in0=ocS[:], in1=Gv[:, :, 0], op=ALU.mult), reads=[b_ocS, b_GS], writes=[b_oS])
                    for br in (1, 2):
                        S.op("dve", lambda e: e.tensor_tensor(out=tS[:], in0=osS[:, br - 1, :], in1=Gv[:, :, br], op=ALU.mult), reads=[b_ocS, b_GS, b_oS], writes=[b_oS])
                        S.op("dve", lambda e: e.tensor_tensor(out=oS[:], in0=oS[:], in1=tS[:], op=ALU.add), reads=[b_oS], writes=[b_oS])
                    S.op("dve", lambda e: e.tensor_copy(out=oT[0:64, :, col], in_=oS[:, 0:16:2]), reads=[b_oS], writes=[b_oT])
                    S.op("dve", lambda e: e.tensor_copy(out=oT[64:128, :, col], in_=oS[:, 1:16:2]), reads=[b_oS], writes=[b_oT])
            for oc in range(8):
                if oc % 4 == 0:
                    pnl, bpn = st.get()
                pt, bp = ps_next()
                for k in range(8):
                    S.op("pe", lambda e: e.matmul(pt[:], lhsT=pnl[:, k, (oc % 4) * 128:(oc % 4 + 1) * 128], rhs=oT[:, k, :], start=(k == 0), stop=(k == 7)),
                         reads=[bpn, b_oT], writes=[bp])
                S.op("dve", lambda e: e.tensor_tensor(out=x[:, oc, :], in0=x[:, oc, :], in1=pt[:], op=ALU.add), reads=[b_x[oc], bp], writes=[b_x[oc]])

        def final_out(ti, es2, smp=False):
            uid[0] += 1
            yf = es2.enter_context(nc.sbuf_tensor(f"s{uid[0]}_yf", [128, 8, 512], F32)); b_yf = Buf()
            rmsnorm_f32(12, yf, b_yf)
            if smp:
                uid[0] += 1
                yo = es2.enter_context(nc.sbuf_tensor(f"s{uid[0]}_yo", [128, 8, 4], F32)); b_yo = Buf()
                S.op("dve", lambda e: e.tensor_copy(out=yo[:], in_=yf[:, :, 3:19:4]), reads=[b_yf], writes=[b_yo])
                S.dma("sp", o_s_y, yo[:], reads=[b_yo])
                return
            for blk in range(4):
                for kh in range(2):
                    pt, bp = ps_next()
                    for k4 in range(4):
                        k = kh * 4 + k4
                        S.op("pe", lambda e: e.transpose(pt[:, k4 * 128:(k4 + 1) * 128], yf[:, k, blk * 128:(blk + 1) * 128], ident[:]),
                             reads=[b_yf, b_ident], writes=[bp])
                    S.op("act", lambda e: e.copy(out=xt[:, blk, kh * 512:(kh + 1) * 512], in_=pt[:]), reads=[bp], writes=[b_xt])
            r0 = (ti - OWN0) * 512
            S.dma("sp", o_y[r0:r0 + 512, :].rearrange("(b p) f -> p b f", p=128), xt[:], reads=[b_xt])

        def rmsnorm_f32(gi, yf, b_yf):
            for k in range(8):
                S.op("act", lambda e: e.activation(out=sq[:, k, :], in_=x[:, k, :], func=AF.Square), reads=[b_x[k]], writes=[b_sq])
            pt, bp = ps_next()
            for k in range(8):
                S.op("pe", lambda e: e.matmul(pt[:], lhsT=ones_bf[:], rhs=sq[:, k, :], start=(k == 0), stop=(k == 7)),
                     reads=[b_sq, b_ones], writes=[bp])
            S.op("act", lambda e: e.activation(out=rstd[:], in_=pt[:], func=AF.Sqrt, bias=epsb[:, 0:1]), reads=[bp, b_eps], writes=[b_rstd])
            S.op("dve", lambda e: e.reciprocal(out=rstd[:], in_=rstd[:]), reads=[b_rstd], writes=[b_rstd])
            for k in range(8):
                S.op("dve", lambda e: e.scalar_tensor_tensor(out=yf[:, k, :], in0=x[:, k, :], scalar=gvec[:, gi, k:k + 1], in1=rstd[:],
                                                              op0=ALU.mult, op1=ALU.mult),
                     reads=[b_x[k], b_gvec, b_rstd], writes=[b_yf])


        def compress_round(sb2, cT, b_cT, P0, kdst, vdst, w2vs, pre=None):
            if pre is None:
                hid0 = sb2("hid0", [128, 32], BF16); b_hid0 = Buf()
                hidp = sb2("hidp", [128, 128], BF16); b_hidp = Buf()
            else:
                hid0, b_hid0, hidp, b_hidp = pre
            off = P0 % 128
            stc = Stream([(W["w1_00"], 0), (W["w1_01"], 0), (W["w1_10"], 0), (W["w1_11"], 0)], depth=2)
            for jv in range(2):
                for g in range(2):
                    pnl, bpn = stc.get()
                    pt, bp = ps_next()
                    for l_ in range(32):
                        S.op("pe", lambda e: e.matmul(pt[:, 0:32], lhsT=pnl[:, l_, :], rhs=cT[:, jv, l_:l_ + 497:16], start=(l_ == 0), stop=(l_ == 31)),
                             reads=[bpn, b_cT], writes=[bp])
                    if jv == 0:
                        S.op("act", lambda e: e.activation(out=hid0[:], in_=pt[:, 0:32], func=AF.Gelu, bias=cb1[:, 0:1]), reads=[bp, b_cb1], writes=[b_hid0])
                        pt2, bp2 = ps_next()
                        S.op("pe", lambda e: e.matmul(pt2[:, 0:32], lhsT=w2k[:], rhs=hid0[:], start=True, stop=True), reads=[b_w2, b_hid0], writes=[bp2])
                        kdst(g, pt2, bp2)
                    else:
                        S.op("dve", lambda e: e.memset(hidp[:], 0.0), writes=[b_hidp])
                        S.op("act", lambda e: e.activation(out=hidp[:, off:off + 32], in_=pt[:, 0:32], func=AF.Gelu, bias=cb1[:, 1:2]), reads=[bp, b_cb1], writes=[b_hidp])
                        pt2, bp2 = ps_next()
                        S.op("pe", lambda e: e.matmul(pt2[:, 0:128], lhsT=hidp[:], rhs=w2vs[g], start=True, stop=True), reads=[b_w2, b_hidp], writes=[bp2])
                        vdst(g, pt2, bp2)

        kvsw = Wt(nc, "kvsw", None, D, 256, 256)
        kvsw.name = "kvsw"
        W["kvsw"] = kvsw

        def kvswjob():
            tks = []
            with nc.allow_non_contiguous_dma(reason="one-time rope column swap"):
                for c, cb in enumerate((256, 512)):
                    for k in range(8):
                        srcv = w_kv[k * 128:(k + 1) * 128, cb:cb + 128].rearrange("p (g hh dd) -> p g hh dd", g=2, hh=2)
                        dstv = kvsw.dst[0].rearrange("p (k c g hh dd) -> p k c g hh dd", k=8, c=2, g=2, hh=2)
                        for hh in range(2):
                            tks.append(S.dma("pool", dstv[:, k, c, :, hh, :], srcv[:, :, 1 - hh, :]))
            kvsw.buf.w = tks
        jobs["kvsw"] = kvswjob

        def kv_gen(ti, es2, smp=False):
            def sb2(name, shape, dt=F32):
                uid[0] += 1
                return es2.enter_context(nc.sbuf_tensor(f"s{uid[0]}_{name}", list(shape), dt))
            kvf = sb2("kvf", [128, 6, 512]); b_kvf = [Buf() for _ in range(6)]
            rc_t = sb2("rc_t", [128, 512]); rs_t = sb2("rs_t", [128, 512]); b_rope = Buf()
            rt1 = sb2("rt1", [128, 512]); b_rt1 = Buf()
            S.dma("sp", rc_t[:], ropec_s if smp else ropec[:, ti * 512:(ti + 1) * 512], writes=[b_rope])
            S.dma("sp", rs_t[:], ropes_s if smp else ropes[:, ti * 512:(ti + 1) * 512], writes=[b_rope])
            rmsnorm(13, h, b_h)
            st = Stream([(W["kv"], 0), (W["kv"], 1), (W["kv"], 2), (W["kvsw"], 0)], depth=3)
            pk = [st.get(), st.get(), st.get()]
            psw, bpsw = st.get()
            for c in range(6):
                pnl, bpn = pk[c // 2]
                pt, bp = proj_chunk(pnl, bpn, (c % 2) * 128, 128, h, b_h, 8)
                if c in (2, 4):
                    pt2, bp2 = proj_chunk(psw, bpsw, (c // 2 - 1) * 128, 128, h, b_h, 8)
                    S.op("dve", lambda e: e.tensor_tensor(out=kvf[:, c, :], in0=pt[:], in1=rc_t[:], op=ALU.mult), reads=[bp, b_rope], writes=[b_kvf[c]])
                    S.op("dve", lambda e: e.tensor_tensor(out=rt1[:], in0=pt2[:], in1=rs_t[:], op=ALU.mult), reads=[bp2, b_rope], writes=[b_rt1])
                    S.op("dve", lambda e: e.tensor_tensor(out=kvf[:, c, :], in0=kvf[:, c, :], in1=rt1[:], op=ALU.add), reads=[b_kvf[c], b_rt1], writes=[b_kvf[c]])
                else:
                    S.op("act", lambda e: e.copy(out=kvf[:, c, :], in_=pt[:]), reads=[bp], writes=[b_kvf[c]])
            if smp:
                kvo = sb2("kvo", [128, 6, 4]); b_kvo = Buf()
                S.op("dve", lambda e: e.tensor_copy(out=kvo[:], in_=kvf[:, :, 3:19:4]), reads=b_kvf, writes=[b_kvo])
                S.dma("sp", o_s_kv, kvo[:], reads=[b_kvo])
                S.op("dve", lambda e: e.tensor_copy(out=KTn[:, 0, :], in_=kvf[:, 2, 0:128]), reads=[b_kvf[2]], writes=[b_KTn])
                S.op("dve", lambda e: e.tensor_copy(out=KTn[:, 1, :], in_=kvf[:, 4, 0:128]), reads=[b_kvf[4]], writes=[b_KTn])
                for ci, c in enumerate((3, 5)):
                    pt, bp = ps_next()
                    S.op("pe", lambda e: e.transpose(pt[:, 0:128], kvf[:, c, 0:128], ident[:]), reads=[b_kvf[c], b_ident], writes=[bp])
                    S.op("dve", lambda e: e.tensor_copy(out=Vn[:, ci, :, 0:64], in_=pt[:, 0:128].rearrange("p (g d) -> p g d", g=2)), reads=[bp], writes=[b_Vn])
                return
            if ti >= OWN0 or DO_B:
                for blk in range(4):
                    for hf in range(2):
                        pt, bp = ps_next()
                        ncn = 4 if hf == 0 else 2
                        for c4 in range(ncn):
                            c = hf * 4 + c4
                            S.op("pe", lambda e: e.transpose(pt[:, c4 * 128:(c4 + 1) * 128], kvf[:, c, blk * 128:(blk + 1) * 128], ident[:]),
                                 reads=[b_kvf[c], b_ident], writes=[bp])
                        S.op("act", lambda e: e.copy(out=xt[:, blk, hf * 512:hf * 512 + ncn * 128], in_=pt[:, 0:ncn * 128]), reads=[bp], writes=[b_xt])
                        if DO_B and DBG >= 2 and not cfg.get("NO_V"):
                            kch = 4 * ti + blk
                            if hf == 0:
                                S.op("dve", lambda e: e.tensor_copy(out=slcV[:, kch, :, 0:64], in_=xt[:, blk, 384:512].rearrange("p (g d) -> p g d", g=2)),
                                     reads=[b_xt], writes=[b_slcV])
                            else:
                                S.op("dve", lambda e: e.tensor_copy(out=winV[:, kch % 8, :, 0:64], in_=xt[:, blk, 640:768].rearrange("p (g d) -> p g d", g=2)),
                                     reads=[b_xt], writes=[b_winV])
            if DO_B and DBG >= 2 and not cfg.get("NO_K"):
                c0 = ti * 512
                r0w = (ti % 2) * 512
                for g in range(2):
                    for hfp in range(2):
                        S.op("dve", lambda e: e.tensor_copy(out=slcK[hfp * 64:(hfp + 1) * 64, g, c0:c0 + 512], in_=kvf[g * 64:(g + 1) * 64, 2, :]),
                             reads=[b_kvf[2]], writes=[b_slcK])
                        S.op("dve", lambda e: e.tensor_copy(out=winK[hfp * 64:(hfp + 1) * 64, g, r0w:r0w + 512], in_=kvf[g * 64:(g + 1) * 64, 4, :]),
                             reads=[b_kvf[4]], writes=[b_winK])
            if DO_B and DBG >= 3:
                P0 = 32 * ti
                S.op("pool", lambda e: e.tensor_copy(out=cmpT[:, :, 0:16], in_=cmpT[:, :, 512:528]), reads=[b_cmpT], writes=[b_cmpT])
                for jv in range(2):
                    S.op("pool", lambda e: e.tensor_copy(out=cmpT[:, jv, 16:528], in_=kvf[:, jv, :]), reads=[b_kvf[jv], b_cmpT], writes=[b_cmpT])
                def kdst(g, pt2, bp2):
                    S.op("act", lambda e: e.copy(out=kcK[:, g, P0:P0 + 32], in_=pt2[:, 0:32]), reads=[bp2], writes=[b_kcK])

                def vdst(g, pt2, bp2):
                    pch = P0 // 128
                    S.op("dve", lambda e: e.tensor_tensor(out=vcV[:, pch, g, :], in0=vcV[:, pch, g, :], in1=pt2[:, 0:128], op=ALU.add),
                         reads=[bp2, b_vcV], writes=[b_vcV])
                    S.op("dve", lambda e: e.tensor_copy(out=vcVb[:, pch, g, :], in_=vcV[:, pch, g, :]), reads=[b_vcV], writes=[b_vcVb])
                compress_round(sb2, cmpT, b_cmpT, P0, kdst, vdst, [w2v[:], w2v[:]])
            if ti >= OWN0:
                r0 = (ti - OWN0) * 512
                for oi, od in enumerate((o_cmp, o_slc, o_win)):
                    S.dma("sp", od[r0:r0 + 512, :].rearrange("(b p) f -> p b f", p=128), xt[:, :, oi * 256:(oi + 1) * 256], reads=[b_xt])

        for l_ in range(NL_A):
            cast_order.extend([f"rgin{l_}", f"band{l_}_0", f"band{l_}_1", f"rgout{l_}", f"up{l_}", f"down{l_}", f"plein{l_}", f"pleg{l_}"])
        cast_order.extend(["kv", "kvsw"])
        if DO_B:
            cast_order.extend(["w1_00", "w1_01", "w1_10", "w1_11"])
            for j_ in range(2):
                cast_order.extend([f"q{j_}", f"qsw{j_}", f"qg{j_}", f"wo{j_}", f"up{2 + j_}", f"down{2 + j_}", f"plein{2 + j_}", f"pleg{2 + j_}"])
        for ti in range(NT):
            transpose_in(xs[ti * 512:(ti + 1) * 512, :], 1024, x, b_x)
            for l in range(NL_A):
                with ExitStack() as es2:
                    rg_layer(l, ti, es2)
                    S.barrier()
                with ExitStack() as es2:
                    ffn_ple(l, ti, es2)
                    S.barrier()
            with ExitStack() as es2:
                kv_gen(ti, es2)
                S.barrier()
            if DO_B and ti >= OWN0 - 1:
                for l in (2, 3):
                    if DBG >= 4:
                        with ExitStack() as es2:
                            attn_layer(l, ti, es2, [3] if ti == OWN0 - 1 else [0, 1, 2, 3])
                            S.barrier()
                    with ExitStack() as es2:
                        ffn_ple(l, ti, es2)
                        S.barrier()
            if ti >= OWN0:
                with ExitStack() as es2:
                    final_out(ti, es2)
                    S.barrier()

        with nc.allow_non_contiguous_dma(reason="small state outputs"):
            for l in range(2):
                for k in range(3):
                    S.dma("sp", o_rgc[l, k].rearrange("(c p) -> p c", p=128), rg_hist[:, l, :, k], reads=b_rghist[l])
                S.dma("sp", o_rgh[l].rearrange("(c p) -> p c", p=128), rg_hc[:, l, :], reads=b_rghc[l])
            for l in range(4):
                for k in range(2):
                    S.dma("sp", o_ffc[l, k].rearrange("(c p) -> p c", p=128), ff_hist[:, l, :, k], reads=b_ffhist[l])

        if SMP:
            S.barrier()
            with ExitStack() as es3:
                def sb3(name, shape, dt=F32):
                    uid[0] += 1
                    return es3.enter_context(nc.sbuf_tensor(f"s{uid[0]}_{name}", list(shape), dt))
                assert NTOK >= 2048
                flatK = slcK[:].rearrange("p g t -> p (g t)")
                flatV = slcV[:].rearrange("p c g f -> p (c g f)")
                kcS = flatK[:, 0:2048].rearrange("p (b t) -> p b t", b=4)
                vcS = flatK[:, 2048:4096].rearrange("p (b c f) -> p b c f", b=4, c=4)
                b_kcS = Buf(); b_vcS = Buf(); b_KTn = Buf()
                Vn = slcV[:, 0:2, :, :]; b_Vn = Buf()
                b_Vc = [Buf() for _ in range(4)]
                vo = [1536]

                def vview(n):
                    a = vo[0]
                    vo[0] += n
                    return flatV[:, a:a + n]
                mcsS = vview(1024).rearrange("p (a b c) -> p a b c", a=4, b=2)
                b64 = vview(512).rearrange("p (a c) -> p a c", a=4)
                KTn = vview(256).rearrange("p (a t) -> p a t", a=2)
                cb0 = vview(128); wb0 = vview(128); w2v1 = vview(128); brhs = vview(16)
                xtf = xt[:].rearrange("p b f -> p (b f)")
                tas = xtf[:, 0:512].rearrange("p (a s) -> p a s", a=2)
                b_stab = Buf()
                S.op("pool", lambda e: e.memset(w2v1, 0.0), writes=[b_stab])
                S.dma("pool", w2v1[:, 64:128], cmp_w2[1], writes=[b_stab])
                S.dma("pool", cb0, t_cb0, writes=[b_stab]); S.dma("pool", brhs, t_brhs, writes=[b_stab])
                S.dma("pool", mcsS, t_mcs_s, writes=[b_stab]); S.dma("sp", tas, t_as, writes=[b_stab])
                S.dma("pool", b64, t_b64, writes=[b_stab]); S.dma("pool", wb0, t_wb0, writes=[b_stab])
                pgi = xtf[:, 512:768].bitcast(mybir.dt.int32); pgf = xtf[:, 768:1024]; pidf = sb3("pidf", [128, 1])
                idxS_t = xtf[:, 1024:1280].bitcast(mybir.dt.int32); b_idxS = Buf()
                idxS = idxS_t.rearrange("p (b g) -> p b g", b=4)
                S.dma("sp", pgi, pg_s.rearrange("b g -> (b g)").unsqueeze(0).broadcast_to([128, 256]), writes=[b_idxS])
                S.op("pool", lambda e: e.iota(pidf[:], pattern=[[0, 1]], base=0, channel_multiplier=1, allow_small_or_imprecise_dtypes=True), writes=[b_idxS])
                S.op("dve", lambda e: e.tensor_copy(out=pgf, in_=pgi), reads=[b_idxS], writes=[b_idxS])
                S.op("dve", lambda e: e.tensor_scalar(out=pgf, in0=pgf, scalar1=128.0, scalar2=pidf[:, 0:1], op0=ALU.mult, op1=ALU.add), reads=[b_idxS], writes=[b_idxS])
                S.op("dve", lambda e: e.tensor_copy(out=idxS_t, in_=pgf), reads=[b_idxS], writes=[b_idxS])
                cT = cmpT; b_cT = b_cmpT
                hid_pre = (sb3("hid0s", [128, 32], BF16), Buf(), sb3("hidps", [128, 128], BF16), Buf())
                vcSf = xtf[:, 1280:1792].rearrange("p (c f) -> p c f", c=4); b_vcSf = Buf()
                gbc = [xtf[:, 1792 + 256 * i:2048 + 256 * i] for i in range(4)]; b_gbc = [Buf() for _ in range(4)]
                for i in range(4):
                    S.op("pool", lambda e: e.memset(cT[:], 0.0), writes=[b_cT])
                    S.op("pool", lambda e: e.memset(vcSf, 0.0), writes=[b_vcSf])
                    for r in range(16):
                        for jq in range(4):
                            S._deps("pool", [b_idxS], [b_gbc[jq]])
                            tk = S.dma_ind(gbc[jq], c_cmp, idxS[:, i, 4 * r + jq:4 * r + jq + 1])
                            S._mark(tk, [b_idxS], [b_gbc[jq]])
                            tq, btq = ps_next()
                            for jv in range(2):
                                S.op("pe", lambda e: e.transpose(tq[:, jv * 128:(jv + 1) * 128], gbc[jq][:, jv * 128:(jv + 1) * 128], ident[:]),
                                     reads=[b_gbc[jq], b_ident], writes=[btq])
                            S.op("act", lambda e: e.copy(out=cT[:, :, 16 + 128 * jq:16 + 128 * (jq + 1)], in_=tq[:, 0:256].rearrange("p (j t) -> p j t", j=2)),
                                 reads=[btq], writes=[b_cT])
                        P0 = 32 * r

                        def kdst(g, pt2, bp2):
                            S.op("act", lambda e: e.copy(out=kcS[64 * g:64 * g + 64, i, P0:P0 + 32], in_=pt2[64 * g:64 * g + 64, 0:32]), reads=[bp2], writes=[b_kcS])

                        def vdst(g, pt2, bp2):
                            pch = P0 // 128
                            S.op("dve", lambda e: e.tensor_tensor(out=vcSf[:, pch, :], in0=vcSf[:, pch, :], in1=pt2[:, 0:128], op=ALU.add),
                                 reads=[bp2, b_vcSf], writes=[b_vcSf])
                        compress_round(None, cT, b_cT, P0, kdst, vdst, [w2v[:], w2v1], pre=hid_pre)
                        S.op("pool", lambda e: e.tensor_copy(out=cT[:, :, 0:16], in_=cT[:, :, 512:528]), reads=[b_cT], writes=[b_cT])
                    S.op("dve", lambda e: e.tensor_copy(out=vcS[:, i, :, :], in_=vcSf), reads=[b_vcSf], writes=[b_vcS])
                S.dma("sp", x[:], xs_s, writes=b_x)
                for l in range(2):
                    with ExitStack() as es2:
                        rg_layer(l, -1, es2, smp=True)
                        S.barrier()
                    with ExitStack() as es2:
                        ffn_ple(l, -1, es2, smp=True)
                        S.barrier()
                with ExitStack() as es2:
                    kv_gen(-1, es2, smp=True)
                    S.barrier()
                for l in (2, 3):
                    with ExitStack() as es2:
                        attn_layer(l, -1, es2, [], smp=True)
                        S.barrier()
                    with ExitStack() as es2:
                        ffn_ple(l, -1, es2, smp=True)
                        S.barrier()
                with ExitStack() as es2:
                    final_out(-1, es2, smp=True)
                    S.barrier()
                with nc.allow_non_contiguous_dma(reason="state passthrough"):
                    for i in range(4):
                        S.dma("sp", o_s_win[i], c_win[i, 1:512, :])
                    S.dma("sp", o_s_rgc_old, st_rgc[:, :, 1:3, :])
                    S.dma("sp", o_s_ffc_old, st_ffc[:, :, 1, :])
        S.finish()
    return nc

_WNAMES = ['g_mix', 'g_ffn', 'g_ple', 'g_final', 'g_kv', 'rg_w_in', 'rg_conv_w', 'rg_conv_b', 'rg_w_a', 'rg_w_x', 'rg_lambda',
           'rg_w_out', 'w_kv', 'ffn_w_up', 'ffn_conv_w', 'ffn_conv_b', 'ffn_w_down', 'ple_w_in', 'ple_w_gate',
           'attn_w_qg', 'attn_w_o', 'cmp_pos', 'cmp_w1', 'cmp_b1', 'cmp_w2']


def make_tables(half, NT, OWN0):
    BIG = 30000.0
    NTOK = NT * 512
    pre = OWN0 * 512
    nown = NTOK - pre + 128
    nqb = nown // 128
    i = np.arange(nown) - 128
    tau = pre + i
    hv = np.where(i < 0, 1, half)[:, None]
    P = np.arange(256)
    c = P - 1
    vc = (P >= 1)[None, :] & ((hv == 1) | (16 * c >= pre)[None, :])
    ok = vc & ((16 * c + 31)[None, :] <= tau[:, None])
    t_cb = np.where(ok, 0.0, -BIG).astype(np.float32).reshape(nqb, 128, 256)
    kk = np.arange(640)
    tau0 = pre + 128 * (i // 128)
    kap = tau0[:, None] - 512 + kk[None, :]
    dist = tau[:, None] - kap
    okw = (dist >= 0) & (dist < 512) & (kap >= 0) & ((hv == 1) | (kap >= pre))
    t_wb = np.where(okw, 0.0, -BIG).astype(np.float32).reshape(nqb, 128, 640)
    sblk = np.arange(64)
    vb = ((hv == 1) | (64 * sblk >= pre)[None, :]) & (64 * sblk < NTOK)[None, :]
    V = (vb & ((64 * sblk)[None, :] <= tau[:, None])).astype(np.float32)
    blk0 = np.where(hv[:, 0] == 1, 0, pre // 64)
    cur = tau // 64
    forced = (sblk[None, :] == blk0[:, None]) | (sblk[None, :] == cur[:, None]) | (sblk[None, :] == (cur - 1)[:, None])
    A = 100.0 * forced.astype(np.float32) * V + (V - 1.0)
    t_va = np.stack([V, A], axis=1).astype(np.float32).reshape(nqb, 128, 2, 64)
    Pm = np.arange(256)
    c0 = 16 * (Pm - 1)
    s0 = 64 * sblk
    m = ((c0[:, None] < s0[None, :] + 64) & (c0[:, None] + 32 > s0[None, :]) & (Pm[:, None] >= 1)).astype(np.float32)
    t_mcs = np.ascontiguousarray(m.reshape(2, 128, 64).transpose(1, 0, 2))
    t_tri = np.where(np.arange(128)[None, :] > np.arange(128)[:, None], -BIG, 0.0).astype(np.float32)
    return dict(t_cb=t_cb, t_wb=t_wb, t_va=t_va, t_mcs=t_mcs, t_tri=t_tri)


def core_inputs(inp, b, half, NT=8, OWN0=4):
    NTOK = NT * 512
    pre = OWN0 * 512
    own = NTOK - pre
    if half == 1:
        xs = inp['x_prompt'][b, :NTOK]
        ps = inp['p_prompt'][:, b, :NTOK]
        pos = np.arange(NTOK)
    else:
        xs = np.concatenate([inp['x_prompt'][b, :pre], inp['x_prompt'][b, :own]], 0)
        ps = np.concatenate([inp['p_prompt'][:, b, :pre], inp['p_prompt'][:, b, :own]], 1)
        pos = np.concatenate([np.arange(pre), np.arange(own)])
    d = np.arange(128) % 64
    inv = (10000.0 ** (-(d % 32).astype(np.float32) / 32)).astype(np.float32)
    ang = pos[None, :].astype(np.float32) * inv[:, None]
    sgn = np.where(d < 32, -1.0, 1.0)[:, None]
    m = dict(xs=np.ascontiguousarray(xs, dtype=np.float32), ps=np.ascontiguousarray(ps, dtype=np.float32),
             flag=np.tile(np.array([[half, 1 - half]], np.float32), (128, 1)),
             ident=np.eye(128, dtype=np.float32),
             ropec=np.cos(ang).astype(np.float32), ropes=(np.sin(ang) * sgn).astype(np.float32))
    for k in _WNAMES:
        m[k] = np.ascontiguousarray(inp[k], dtype=np.float32)
    m['rg_b_a'] = np.ascontiguousarray(inp['rg_b_a'], dtype=np.float32).reshape(2, 1280)
    m['rg_b_x'] = np.ascontiguousarray(inp['rg_b_x'], dtype=np.float32).reshape(2, 1280)
    m.update(make_tables(half, NT, OWN0))
    return m


def sample_inputs(inp, c):
    BIG = 30000.0
    b0 = 4 * c
    m = {}
    xs = np.zeros((128, 8, 512), np.float32)
    ps = np.zeros((4, 128, 2, 512), np.float32)
    for i in range(4):
        xs[:, :, 4 * i + 3] = inp['x_sample'][b0 + i, 0].reshape(8, 128).T
        for l in range(4):
            ps[l, :, :, 4 * i + 3] = inp['p_sample'][l, b0 + i, 0].reshape(2, 128).T
    m['xs_s'] = xs
    m['ps_s'] = ps
    rgc = inp['state_rg_conv'][:, b0:b0 + 4]
    m['s_rgh'] = np.ascontiguousarray(rgc.reshape(2, 4, 3, 10, 128).transpose(4, 0, 3, 1, 2), dtype=np.float32)
    rgh = inp['state_rg_h'][:, b0:b0 + 4]
    m['s_h0'] = np.ascontiguousarray(rgh.reshape(2, 4, 10, 128).transpose(3, 0, 2, 1), dtype=np.float32)
    ffc = inp['state_ffn_conv'][:, b0:b0 + 4]
    m['s_ffh'] = np.ascontiguousarray(ffc.reshape(4, 4, 2, 48, 128).transpose(4, 0, 3, 1, 2), dtype=np.float32)
    d = np.arange(128) % 64
    inv = (10000.0 ** (-(d % 32).astype(np.float32) / 32)).astype(np.float32)
    ang = np.full((1, 512), 8192.0, np.float32) * inv[:, None]
    sgn = np.where(d < 32, -1.0, 1.0)[:, None]
    m['ropec_s'] = np.cos(ang).astype(np.float32)
    m['ropes_s'] = (np.sin(ang) * sgn).astype(np.float32)
    m['pg_s'] = np.ascontiguousarray(inp['page_table'][b0:b0 + 4], dtype=np.int32)
    nphys = inp['cache_cmp_kv'].shape[0]
    m['cache_cmp_kv'] = np.ascontiguousarray(inp['cache_cmp_kv'], dtype=np.float32).reshape(nphys * 128, 256)
    m['cache_slc_kv'] = np.ascontiguousarray(inp['cache_slc_kv'], dtype=np.float32).reshape(nphys * 128, 256)
    m['c_win'] = np.ascontiguousarray(inp['cache_win_kv'][b0:b0 + 4], dtype=np.float32).reshape(4, 512, 256)
    m['st_rgc'] = np.ascontiguousarray(rgc, dtype=np.float32)
    m['st_ffc'] = np.ascontiguousarray(ffc, dtype=np.float32)
    P = np.arange(512)
    c0 = 16 * (P - 1)
    sb = np.arange(256)
    mm = ((c0[:, None] < 64 * sb[None, :] + 64) & (c0[:, None] + 32 > 64 * sb[None, :]) & (P[:, None] >= 1) & (sb[None, :] < 129)).astype(np.float32)
    m['t_mcs_s'] = np.ascontiguousarray(mm.reshape(4, 128, 2, 128).transpose(1, 0, 2, 3))
    V = (sb < 129).astype(np.float32)
    forced = ((sb == 0) | (sb == 128) | (sb == 127)).astype(np.float32)
    A = 100.0 * forced * V + (V - 1.0)
    m['t_as'] = np.ascontiguousarray(np.tile(np.stack([V, A])[None], (128, 1, 1)), dtype=np.float32)
    br = np.zeros((128, 16), np.float32); br[0, 0:8] = 1.0; br[1, 8:16] = 1.0
    m['t_brhs'] = br
    b64 = np.zeros((128, 4, 128), np.float32)
    for i in range(4):
        b64[0:2, i, :] = -BIG
        b64[0:2, i, 4 * i + 3] = 0.0
    m['t_b64'] = b64
    wb0 = np.zeros((128, 128), np.float32); wb0[0:2, 0] = -BIG
    m['t_wb0'] = wb0
    m['t_cb0'] = wb0.copy()
    return m


_BUILD_CACHE = {}


def kernel(**inputs):
    inp = {k: np.asarray(v) for k, v in inputs.items()}
    nphys = inp['cache_cmp_kv'].shape[0]
    nc = build(dict(NT=8, OWN0=4, NL_A=2, DO_B=True, SMP=True, NPHYS=nphys))
    in_maps = []
    for c in range(8):
        m = core_inputs(inp, c // 2, c % 2)
        m.update(sample_inputs(inp, c))
        in_maps.append(m)
    res = run_bass_kernel_spmd(nc, in_maps, core_ids=list(range(8)))
    R = res.results
    B, T = 4, 4096
    y_prompt = np.zeros((B, T, D), np.float32)
    cmp_p = np.zeros((B, T, 2, 2, 64), np.float32)
    slc_p = np.zeros((B, T, 2, 2, 64), np.float32)
    win_p = np.zeros((B, 512, 2, 2, 64), np.float32)
    rgc_p = np.zeros((2, B, 3, DR), np.float32)
    rgh_p = np.zeros((2, B, DR), np.float32)
    ffc_p = np.zeros((4, B, 2, 2 * DFF), np.float32)
    DB = 32
    y_sample = np.zeros((DB, 1, D), np.float32)
    cmp_s = np.zeros((DB, 1, 2, 2, 64), np.float32)
    slc_s = np.zeros((DB, 1, 2, 2, 64), np.float32)
    win_s = np.zeros((DB, 512, 2, 2, 64), np.float32)
    rgc_s = np.zeros((2, DB, 3, DR), np.float32)
    rgh_s = np.zeros((2, DB, DR), np.float32)
    ffc_s = np.zeros((4, DB, 2, 2 * DFF), np.float32)
    for c in range(8):
        b, half = c // 2, c % 2
        sl = slice(half * 2048, half * 2048 + 2048)
        y_prompt[b, sl] = R[c]['o_y']
        cmp_p[b, sl] = R[c]['o_cmp'].reshape(2048, 2, 2, 64)
        slc_p[b, sl] = R[c]['o_slc'].reshape(2048, 2, 2, 64)
        if half == 1:
            win_p[b] = R[c]['o_win'][-512:].reshape(512, 2, 2, 64)
            rgc_p[:, b] = R[c]['o_rgc']
            rgh_p[:, b] = R[c]['o_rgh']
            ffc_p[:, b] = R[c]['o_ffc']
        assemble_sample(R[c], c, y_sample, cmp_s, slc_s, win_s, rgc_s, rgh_s, ffc_s)
    return (y_prompt, y_sample, cmp_p, cmp_s, slc_p, slc_s, win_p, win_s, rgc_p, rgc_s, rgh_p, rgh_s, ffc_p, ffc_s)


def assemble_sample(r, c, y_sample, cmp_s, slc_s, win_s, rgc_s, rgh_s, ffc_s):
    b0 = 4 * c
    for i in range(4):
        b = b0 + i
        y_sample[b, 0] = r['o_s_y'][:, :, i].T.reshape(1024)
        kv = r['o_s_kv'][:, :, i]
        cmp_s[b, 0] = kv[:, 0:2].T.reshape(2, 2, 64)
        slc_s[b, 0] = kv[:, 2:4].T.reshape(2, 2, 64)
        win_s[b, 0:511] = r['o_s_win'][i].reshape(511, 2, 2, 64)
        win_s[b, 511] = kv[:, 4:6].T.reshape(2, 2, 64)
        for l in range(2):
            rgc_s[l, b, 0:2] = r['o_s_rgc_old'][l, i]
            rgc_s[l, b, 2] = r['o_s_rgc_new'][:, l, :, i].T.reshape(1280)
            rgh_s[l, b] = r['o_s_rgh'][:, l, :, i].T.reshape(1280)
        for l in range(4):
            ffc_s[l, b, 0] = r['o_s_ffc_old'][l, i]
            ffc_s[l, b, 1] = r['o_s_ffc_new'][:, l, :, i].T.reshape(6144)
```

```python
import numpy as np
import ml_dtypes
from contextlib import ExitStack
import concourse.bass as bass
import concourse.mybir as mybir
from concourse.bass_utils import run_bass_kernel_spmd

F32 = mybir.dt.float32
BF16 = mybir.dt.bfloat16
AF = mybir.ActivationFunctionType
ALU = mybir.AluOpType

D = 1024
NT_TOK = 512
DR = 1280
DFF = 3072
EPS = 1e-6


class Buf:
    __slots__ = ("w", "r")

    def __init__(self):
        self.w = None
        self.r = []


class Sched:
    ND = 12

    def __init__(self, nc, es):
        self.nc = nc
        self.E = dict(pe=nc.tensor, act=nc.scalar, dve=nc.vector, pool=nc.gpsimd, sp=nc.sync)
        self.sem = {}
        self.cnt = {}
        for e in ("pe", "act", "dve", "pool"):
            self.sem[e] = es.enter_context(nc.semaphore("s_" + e))
            self.cnt[e] = 0
        self.seen = {e: {} for e in self.E}
        self.dsem = {}
        self.dval = {}
        self.didx = {}
        self.NDQ = {"sp": 12, "pool": 40, "act": 4}
        for q in ("sp", "pool", "act"):
            self.dsem[q] = [es.enter_context(nc.semaphore(f"d_{q}{i}")) for i in range(self.NDQ[q])]
            self.dval[q] = [0] * self.NDQ[q]
            self.didx[q] = 0
        self.all_dma = []

    def _wait(self, e, tk):
        key, sem, val = tk
        if key == e and e == "pe":
            return
        if self.seen[e].get(key, 0) >= val:
            return
        self.E[e].wait_ge(sem, val)
        self.seen[e][key] = val

    def _deps(self, e, reads, writes):
        for b in reads:
            if b.w is not None:
                for tk in (b.w if isinstance(b.w, list) else [b.w]):
                    self._wait(e, tk)
        for b in writes:
            if b.w is not None:
                for tk in (b.w if isinstance(b.w, list) else [b.w]):
                    self._wait(e, tk)
            for tk in b.r:
                self._wait(e, tk)

    def _mark(self, tk, reads, writes):
        for b in reads:
            b.r.append(tk)
            if len(b.r) > 24:
                b.r = b.r[-24:]
        for b in writes:
            b.w = tk
            b.r = []

    def op(self, e, fn, reads=(), writes=()):
        self._deps(e, reads, writes)
        inst = fn(self.E[e])
        inst.then_inc(self.sem[e], 1)
        self.cnt[e] += 1
        tk = (e, self.sem[e], self.cnt[e])
        self._mark(tk, reads, writes)
        return tk

    def dma(self, q, out, in_, reads=(), writes=(), **kw):
        slot = self.didx[q] % self.NDQ[q]
        self.didx[q] += 1
        sem = self.dsem[q][slot]
        key = (q, slot)
        if self.dval[q][slot] > 0:
            self._wait(q, (key, sem, self.dval[q][slot]))
        self._deps(q, reads, writes)
        inst = self.E[q].dma_start(out=out, in_=in_, **kw)
        inst.then_inc(sem, 16)
        self.dval[q][slot] += 16
        tk = (key, sem, self.dval[q][slot])
        self._mark(tk, reads, writes)
        self.all_dma.append(tk)
        return tk

    def dma_ind(self, out, in_, idx_ap):
        q = "pool"
        slot = self.didx[q] % self.NDQ[q]
        self.didx[q] += 1
        sem = self.dsem[q][slot]
        key = (q, slot)
        if self.dval[q][slot] > 0:
            self._wait(q, (key, sem, self.dval[q][slot]))
        inst = self.nc.gpsimd.indirect_dma_start(out=out, out_offset=None, in_=in_, in_offset=bass.IndirectOffsetOnAxis(ap=idx_ap, axis=0))
        inst.then_inc(sem, 16)
        self.dval[q][slot] += 16
        return (key, sem, self.dval[q][slot])

    def barrier(self):
        tks = [(e, self.sem[e], self.cnt[e]) for e in self.cnt if self.cnt[e] > 0]
        for q in self.dsem:
            for slot in range(self.NDQ[q]):
                if self.dval[q][slot] > 0:
                    tks.append(((q, slot), self.dsem[q][slot], self.dval[q][slot]))
        for e in self.E:
            for tk in tks:
                if tk[0] == e and e == "pe":
                    continue
                self._wait(e, tk)

    def finish(self):
        for q in self.dsem:
            for slot in range(self.NDQ[q]):
                if self.dval[q][slot] > 0:
                    self._wait("sp", ((q, slot), self.dsem[q][slot], self.dval[q][slot]))


class Wt:
    def __init__(self, nc, name, src, K, M, MW):
        self.KC = K // 128
        self.MW = MW
        self.NP = M // MW
        assert self.KC * MW <= 4096
        self.src = src
        self.dst = nc.dram_tensor("wb_" + name, [self.NP, 128, self.KC * MW], BF16, kind="Internal").ap()
        self.buf = Buf()


def build(cfg):
    NT = cfg.get("NT", 8)
    OWN0 = cfg.get("OWN0", 4)
    NL_A = cfg.get("NL_A", 2)
    DO_B = cfg.get("DO_B", False)
    DBG = cfg.get("DBG", 99)
    NTOK = NT * NT_TOK
    NOWN = (NT - OWN0) * NT_TOK

    nc = bass.Bass("TRN2", target_bir_lowering=False)

    def din(name, shape, dt=F32):
        return nc.dram_tensor(name, list(shape), dt, kind="ExternalInput").ap()

    def dout(name, shape, dt=F32):
        return nc.dram_tensor(name, list(shape), dt, kind="ExternalOutput").ap()

    xs = din("xs", [NTOK, D])
    ps_in = din("ps", [4, NTOK, 256])
    flag = din("flag", [128, 2])
    ident_d = din("ident", [128, 128])
    ropec = din("ropec", [128, NTOK])
    ropes = din("ropes", [128, NTOK])
    g_mix = din("g_mix", [4, D]); g_ffn = din("g_ffn", [4, D]); g_ple = din("g_ple", [4, D])
    g_final = din("g_final", [D]); g_kv = din("g_kv", [D])
    rg_w_in = din("rg_w_in", [2, D, 2 * DR]); rg_conv_w = din("rg_conv_w", [2, 4, DR]); rg_conv_b = din("rg_conv_b", [2, DR])
    rg_w_a = din("rg_w_a", [2, 16, 80, 80]); rg_b_a = din("rg_b_a", [2, DR])
    rg_w_x = din("rg_w_x", [2, 16, 80, 80]); rg_b_x = din("rg_b_x", [2, DR])
    rg_lambda = din("rg_lambda", [2, DR]); rg_w_out = din("rg_w_out", [2, DR, D])
    w_kv = din("w_kv", [D, 768])
    ffn_w_up = din("ffn_w_up", [4, D, 2 * DFF]); ffn_conv_w = din("ffn_conv_w", [4, 3, 2 * DFF]); ffn_conv_b = din("ffn_conv_b", [4, 2 * DFF])
    ffn_w_down = din("ffn_w_down", [4, DFF, D])
    ple_w_in = din("ple_w_in", [4, 256, D]); ple_w_gate = din("ple_w_gate", [4, D, D])

    attn_w_qg = din("attn_w_qg", [2, D, 1072]); attn_w_o = din("attn_w_o", [2, D, D])
    cmp_pos = din("cmp_pos", [32, 2, 64]); cmp_w1 = din("cmp_w1", [2, 2048, 128]); cmp_b1 = din("cmp_b1", [2, 128]); cmp_w2 = din("cmp_w2", [2, 128, 64])
    NQB = NOWN // 128 + 1
    t_cb = din("t_cb", [NQB, 128, 256]); t_wb = din("t_wb", [NQB, 128, 640]); t_va = din("t_va", [NQB, 128, 2, 64])
    t_mcs = din("t_mcs", [128, 2, 64]); t_tri = din("t_tri", [128, 128])

    SMP = cfg.get("SMP", False)
    if SMP:
        xs_s = din("xs_s", [128, 8, 512]); ps_s = din("ps_s", [4, 128, 2, 512])
        s_rgh_d = din("s_rgh", [128, 2, 10, 4, 3]); s_h0_d = din("s_h0", [128, 2, 10, 4]); s_ffh_d = din("s_ffh", [128, 4, 48, 4, 2])
        ropec_s = din("ropec_s", [128, 512]); ropes_s = din("ropes_s", [128, 512])
        pg_s = din("pg_s", [4, 64], mybir.dt.int32)
        NPHYS = cfg.get("NPHYS", 2560)
        c_cmp = din("cache_cmp_kv", [NPHYS * 128, 256]); c_slc = din("cache_slc_kv", [NPHYS * 128, 256]); c_win = din("c_win", [4, 512, 256])
        st_rgc = din("st_rgc", [2, 4, 3, DR]); st_ffc = din("st_ffc", [4, 4, 2, 2 * DFF])
        t_mcs_s = din("t_mcs_s", [128, 4, 2, 128]); t_as = din("t_as", [128, 2, 256]); t_brhs = din("t_brhs", [128, 16])
        t_b64 = din("t_b64", [128, 4, 128]); t_wb0 = din("t_wb0", [128, 128]); t_cb0 = din("t_cb0", [128, 128])
        o_s_y = dout("o_s_y", [128, 8, 4]); o_s_kv = dout("o_s_kv", [128, 6, 4]); o_s_win = dout("o_s_win", [4, 511, 256])
        o_s_rgc_new = dout("o_s_rgc_new", [128, 2, 10, 4]); o_s_rgc_old = dout("o_s_rgc_old", [2, 4, 2, DR]); o_s_rgh = dout("o_s_rgh", [128, 2, 10, 4])
        o_s_ffc_new = dout("o_s_ffc_new", [128, 4, 48, 4]); o_s_ffc_old = dout("o_s_ffc_old", [4, 4, 2 * DFF])

    o_y = dout("o_y", [NOWN, D])
    o_cmp = dout("o_cmp", [NOWN, 256]); o_slc = dout("o_slc", [NOWN, 256]); o_win = dout("o_win", [NOWN, 256])
    o_rgc = dout("o_rgc", [2, 3, DR]); o_rgh = dout("o_rgh", [2, DR]); o_ffc = dout("o_ffc", [4, 2, 2 * DFF])

    zpad = nc.dram_tensor("zpad", [9, 128, 4096], BF16, kind="Internal").ap()
    zband = nc.dram_tensor("zband", [2, 2, 128, 10 * 384], BF16, kind="Internal").ap()

    es = ExitStack()
    with es:
        S = Sched(nc, es)

        uid = [0]

        def sb(name, shape, dt=F32):
            uid[0] += 1
            return es.enter_context(nc.sbuf_tensor(f"s{uid[0]}_{name}", list(shape), dt))

        ident = sb("ident", [128, 128]); b_ident = Buf()
        ones_bf = sb("ones_bf", [128, 128], BF16); b_ones = Buf()
        flag_sb = sb("flag_sb", [128, 2]); b_flag = Buf()
        gvec = sb("gvec", [128, 14, 8]); b_gvec = Buf()
        rgcw = sb("rgcw", [128, 2, 4, 10]); rgcb = sb("rgcb", [128, 2, 10]); b_rgc = Buf()
        rgba = sb("rgba", [128, 2, 10]); rgbx = sb("rgbx", [128, 2, 10]); rgc8 = sb("rgc8", [128, 2, 10]); b_rgp = Buf()
        ffcw = sb("ffcw", [128, 4, 3, 48]); ffcb = sb("ffcb", [128, 4, 48]); b_ffc = Buf()
        rg_hist = sb("rg_hist", [128, 2, 10, 3]); b_rghist = [[Buf() for _ in range(10)] for _ in range(2)]
        rg_hc = sb("rg_hc", [128, 2, 10]); b_rghc = [[Buf() for _ in range(10)] for _ in range(2)]
        ff_hist = sb("ff_hist", [128, 4, 48, 2]); b_ffhist = [[Buf() for _ in range(48)] for _ in range(4)]
        epsb = sb("epsb", [128, 1]); b_eps = Buf()

        x = sb("x", [128, 8, 512]); b_x = [Buf() for _ in range(8)]
        xt = sb("xt", [128, 4, 1024]); b_xt = Buf()
        h = sb("h", [128, 8, 512], BF16); b_h = Buf()
        sq = sb("sq", [128, 8, 512], BF16); b_sq = Buf()
        rstd = sb("rstd", [128, 512]); b_rstd = Buf()
        NPAN = 4
        pan = [sb(f"pan{i}", [128, 4096], BF16) for i in range(NPAN)]
        b_pan = [Buf() for _ in range(NPAN)]
        psum2 = [es.enter_context(nc.psum_tensor(f"psum{i}", [128, 1024], F32)) for i in range(4)]
        psum = [psum2[i // 2][:, (i % 2) * 512:(i % 2 + 1) * 512] for i in range(8)]
        b_ps = [Buf() for _ in range(8)]
        ps_i = [0]

        def ps_next():
            i = ps_i[0] % 6
            ps_i[0] += 1
            return psum[i], b_ps[i]

        def ps_pair():
            if ps_i[0] % 2 == 1:
                ps_i[0] += 1
            i = ps_i[0] % 6
            ps_i[0] += 2
            return psum2[i // 2], [b_ps[i], b_ps[i + 1]]

        acc_ps = psum2[3]
        b_acc = [b_ps[6], b_ps[7]]

        W = {}
        band = {}

        jobs = {}
        cast_order = []

        def ensure(name):
            if name in jobs:
                jobs.pop(name)()

        def prefetch_casts(k):
            n = 0
            for nm in cast_order:
                if n >= k:
                    break
                if nm in jobs:
                    ensure(nm)
                    n += 1

        def mkw(name, src, K, M, MW):
            w = Wt(nc, name, src, K, M, MW)
            w.name = name
            W[name] = w

            def job():
                tks = []
                for pnl in range(w.NP):
                    srcv = src[:, pnl * MW:(pnl + 1) * MW].rearrange("(k p) m -> p k m", p=128)
                    dstv = w.dst[pnl].rearrange("p (k m) -> p k m", k=w.KC)
                    tks.append(S.dma("pool", dstv, srcv))
                w.buf.w = tks
            jobs[name] = job
            return w

        zt_full = sq[:].rearrange("p k t -> p (k t)")
        zt = zt_full[:, 0:3840]; b_zt = b_sq
        S.op("dve", lambda e: e.memset(sq[:], 0.0), writes=[b_zt])
        for l in range(NL_A):
            for gi, wsrc in enumerate((rg_w_a, rg_w_x)):
                bw = Buf()
                band[(l, gi)] = bw
                tz = S.dma("sp", zband[l, gi], zt, reads=[b_zt])
                wv = Wt.__new__(Wt)
                wv.KC = 30; wv.MW = 128; wv.NP = 1; wv.dst = zband[l, gi:gi + 1]; wv.buf = bw; wv.name = f"band{l}_{gi}"
                W[wv.name] = wv

                def bjob(l=l, gi=gi, wsrc=wsrc, bw=bw, tz=tz):
                    S._wait("pool", tz)
                    tks = []
                    dv = zband[l, gi].rearrange("p (i b m) -> p i b m", i=10, b=3)
                    for n in range(16):
                        r0, r1 = 80 * n, 80 * n + 80
                        for i in range(r0 // 128, (r1 - 1) // 128 + 1):
                            ra, rb = max(r0, 128 * i), min(r1, 128 * i + 128)
                            for j in range(r0 // 128, (r1 - 1) // 128 + 1):
                                ca, cb = max(r0, 128 * j), min(r1, 128 * j + 128)
                                tks.append(S.dma("pool", dv[ra - 128 * i:rb - 128 * i, i, j - i + 1, ca - 128 * j:cb - 128 * j],
                                                 wsrc[l, n, ra - r0:rb - r0, ca - r0:cb - r0]))
                    bw.w = tks
                jobs[wv.name] = bjob

        for l in range(NL_A):
            mkw(f"rgin{l}", rg_w_in[l], D, 2 * DR, 512)
            mkw(f"rgout{l}", rg_w_out[l], DR, D, 256)
        for l in range(4 if DO_B else NL_A):
            mkw(f"up{l}", ffn_w_up[l], D, 2 * DFF, 512)
            mkw(f"down{l}", ffn_w_down[l], DFF, D, 128)
            mkw(f"plein{l}", ple_w_in[l], 256, D, 1024)
            mkw(f"pleg{l}", ple_w_gate[l], D, D, 512)
        mkw("kv", w_kv, D, 768, 256)


        BIGV = 30000.0
        if DO_B:
            for j in range(2):
                mkw(f"q{j}", attn_w_qg[j][:, 0:1024], D, 1024, 512)
                mkw(f"wo{j}", attn_w_o[j], D, D, 512)
                wsw = Wt(nc, f"qsw{j}", None, D, 1024, 512)
                wsw.name = f"qsw{j}"
                W[f"qsw{j}"] = wsw

                def qswjob(j=j, wsw=wsw):
                    tks = []
                    with nc.allow_non_contiguous_dma(reason="one-time rope column swap"):
                        for pnl in range(2):
                            for k in range(8):
                                srcv = attn_w_qg[j][k * 128:(k + 1) * 128, pnl * 512:(pnl + 1) * 512].rearrange("p (hd hh dd) -> p hd hh dd", hd=8, hh=2)
                                dstv = wsw.dst[pnl].rearrange("p (k hd hh dd) -> p k hd hh dd", k=8, hd=8, hh=2)
                                for hh in range(2):
                                    tks.append(S.dma("pool", dstv[:, k, :, hh, :], srcv[:, :, 1 - hh, :]))
                    wsw.buf.w = tks
                jobs[wsw.name] = qswjob
                wg = Wt.__new__(Wt)
                wg.KC = 8; wg.MW = 128; wg.NP = 1; wg.dst = zpad[j:j + 1, :, 0:1024]; wg.buf = Buf()
                W[f"qg{j}"] = wg
                wg.name = f"qg{j}"
                tz = S.dma("sp", zpad[j, :, 0:1024], zt[:, 0:1024], reads=[b_zt])

                def qgjob(j=j, wg=wg, tz=tz):
                    S._wait("pool", tz)
                    with nc.allow_non_contiguous_dma(reason="gate cols"):
                        tk = S.dma("pool", zpad[j, :, 0:1024].rearrange("p (k m) -> p k m", k=8)[:, :, 0:48],
                                   attn_w_qg[j][:, 1024:1072].rearrange("(k p) m -> p k m", p=128))
                    wg.buf.w = [tk]
                jobs[wg.name] = qgjob
            for jv in range(2):
                for g in range(2):
                    idx = 2 + jv * 2 + g
                    w1 = Wt.__new__(Wt)
                    w1.KC = 32; w1.MW = 128; w1.NP = 1; w1.dst = zpad[idx:idx + 1]; w1.buf = Buf()
                    W[f"w1_{jv}{g}"] = w1
                    w1.name = f"w1_{jv}{g}"
                    tz = S.dma("sp", zpad[idx], zt_full, reads=[b_zt])

                    def w1job(jv=jv, g=g, idx=idx, w1=w1, tz=tz):
                        S._wait("pool", tz)
                        tk = S.dma("pool", zpad[idx, g * 64:(g + 1) * 64, :].rearrange("p (l e) -> p l e", l=32),
                                   cmp_w1[jv].rearrange("(l d) e -> d l e", d=64))
                        w1.buf.w = [tk]
                    jobs[w1.name] = w1job

            slcK = sb("slcK", [128, 2, NTOK], BF16); b_slcK = Buf()
            slcV = sb("slcV", [128, NTOK // 128, 2, 128], BF16); b_slcV = Buf()
            winK = sb("winK", [128, 2, 1024], BF16); b_winK = Buf()
            winV = sb("winV", [128, 8, 2, 128], BF16); b_winV = Buf()
            cmpT = sb("cmpT", [128, 2, 528], BF16); b_cmpT = Buf()
            kcK = sb("kcK", [128, 2, 256], BF16); b_kcK = Buf()
            vcV = sb("vcV", [128, 2, 2, 128], F32); b_vcV = Buf()
            vcVb = sb("vcVb", [128, 2, 2, 128], BF16); b_vcVb = Buf()
            ident_bf = sb("ident_bf", [128, 128], BF16); b_identb = Buf()
            ones1 = sb("ones1", [128, 128], BF16); b_ones1 = Buf()
            mcs = sb("mcs", [128, 2, 64], BF16); b_mcs = Buf()
            tri = sb("tri", [128, 128], BF16); b_tri = Buf()
            w2k = sb("w2k", [128, 128], BF16); w2v = sb("w2v", [128, 128], BF16); b_w2 = Buf()
            cb1 = sb("cb1", [128, 2]); b_cb1 = Buf()
            b1t = sb("b1t", [128, 2]); posT = sb("posT", [128, 2, 32], BF16); b_posT = Buf()
            S.op("pool", lambda e: e.memset(slcV[:], 1.0), writes=[b_slcV])
            S.op("pool", lambda e: e.memset(winV[:], 1.0), writes=[b_winV])
            S.op("pool", lambda e: e.memset(slcK[:], 0.0), writes=[b_slcK])
            S.op("pool", lambda e: e.memset(winK[:], 0.0), writes=[b_winK])
            S.op("pool", lambda e: e.memset(cmpT[:], 0.0), writes=[b_cmpT])
            S.op("pool", lambda e: e.memset(kcK[:], 0.0), writes=[b_kcK])
            S.op("pool", lambda e: e.memset(vcV[:], 0.0), writes=[b_vcV])
            S.op("pool", lambda e: e.memset(vcVb[:], 0.0), writes=[b_vcVb])
            S.op("pool", lambda e: e.memset(ones1[:], 1.0), writes=[b_ones1])
            S.op("pool", lambda e: e.memset(w2v[:], 0.0), writes=[b_w2])
            S.op("pool", lambda e: e.memset(posT[:], 0.0), writes=[b_posT])
            S.dma("pool", ident_bf[:], ident_d, writes=[b_identb])
            S.dma("pool", mcs[:], t_mcs, writes=[b_mcs])
            S.dma("pool", tri[:], t_tri, writes=[b_tri])
            S.dma("pool", w2k[:, 0:64], cmp_w2[0], writes=[b_w2])
            S.dma("pool", w2k[:, 64:128], cmp_w2[0], writes=[b_w2])
            S.dma("pool", w2v[:, 0:64], cmp_w2[1], writes=[b_w2])
            with nc.allow_non_contiguous_dma(reason="small"):
                S.dma("sp", b1t[:], cmp_b1.rearrange("j e -> e j"), writes=[b_cb1])
                for jv_ in range(2):
                    S.dma("pool", posT[0:64, jv_, :], cmp_pos[:, jv_, :].rearrange("l d -> d l"), writes=[b_posT])

        S.dma("sp", ident[:], ident_d, writes=[b_ident])
        S.dma("sp", flag_sb[:], flag, writes=[b_flag])
        S.op("dve", lambda e: e.memset(ones_bf[:], 1.0 / D), writes=[b_ones])
        S.op("dve", lambda e: e.memset(epsb[:], EPS), writes=[b_eps])
        with nc.allow_non_contiguous_dma(reason="small param vectors to feature-major"):
            for i in range(4):
                S.dma("sp", gvec[:, i, :], g_mix[i].rearrange("(c p) -> p c", p=128), writes=[b_gvec])
                S.dma("sp", gvec[:, 4 + i, :], g_ffn[i].rearrange("(c p) -> p c", p=128), writes=[b_gvec])
                S.dma("sp", gvec[:, 8 + i, :], g_ple[i].rearrange("(c p) -> p c", p=128), writes=[b_gvec])
            S.dma("sp", gvec[:, 12, :], g_final.rearrange("(c p) -> p c", p=128), writes=[b_gvec])
            S.dma("sp", gvec[:, 13, :], g_kv.rearrange("(c p) -> p c", p=128), writes=[b_gvec])
            for l in range(2):
                for k in range(4):
                    S.dma("sp", rgcw[:, l, k, :], rg_conv_w[l, k].rearrange("(c p) -> p c", p=128), writes=[b_rgc])
                S.dma("sp", rgcb[:, l, :], rg_conv_b[l].rearrange("(c p) -> p c", p=128), writes=[b_rgc])
                S.dma("sp", rgba[:, l, :], rg_b_a[l].rearrange("(c p) -> p c", p=128), writes=[b_rgp])
                S.dma("sp", rgbx[:, l, :], rg_b_x[l].rearrange("(c p) -> p c", p=128), writes=[b_rgp])
                S.dma("sp", rgc8[:, l, :], rg_lambda[l].rearrange("(c p) -> p c", p=128), writes=[b_rgp])
            for l in range(4):
                for k in range(3):
                    S.dma("sp", ffcw[:, l, k, :], ffn_conv_w[l, k].rearrange("(c p) -> p c", p=128), writes=[b_ffc])
                S.dma("sp", ffcb[:, l, :], ffn_conv_b[l].rearrange("(c p) -> p c", p=128), writes=[b_ffc])
        S.op("act", lambda e: e.activation(out=rgc8[:], in_=rgc8[:], func=AF.Exp, scale=-1.0), reads=[b_rgp], writes=[b_rgp])
        S.op("act", lambda e: e.activation(out=rgc8[:], in_=rgc8[:], func=AF.Ln, bias=1.0), reads=[b_rgp], writes=[b_rgp])
        S.op("dve", lambda e: e.tensor_scalar(out=rgc8[:], in0=rgc8[:], scalar1=-8.0, scalar2=None, op0=ALU.mult), reads=[b_rgp], writes=[b_rgp])
        allh = [b for row in b_rghist for b in row] + [b for row in b_rghc for b in row] + [b for row in b_ffhist for b in row]
        S.op("dve", lambda e: e.memset(rg_hist[:], 0.0), writes=[b for row in b_rghist for b in row])
        S.op("dve", lambda e: e.memset(rg_hc[:], 0.0), writes=[b for row in b_rghc for b in row])
        S.op("dve", lambda e: e.memset(ff_hist[:], 0.0), writes=[b for row in b_ffhist for b in row])

        pan_i = [0]

        def load_panel(w, pnl):
            i = pan_i[0] % NPAN
            pan_i[0] += 1
            n = w.KC * w.MW
            S.dma("sp", pan[i][:, 0:n], w.dst[pnl], reads=[w.buf], writes=[b_pan[i]])
            return pan[i][:, 0:n].rearrange("p (k m) -> p k m", k=w.KC), b_pan[i]

        class Stream:
            def __init__(self, items, depth=2):
                for w_, _p in items:
                    ensure(w_.name)
                prefetch_casts(4)
                self.items = items
                self.loaded = []
                self.depth = depth
                self.pos = 0

            def get(self):
                while len(self.loaded) < min(len(self.items), self.pos + 1 + self.depth):
                    w, pnl = self.items[len(self.loaded)]
                    self.loaded.append(load_panel(w, pnl))
                r = self.loaded[self.pos]
                self.pos += 1
                return r

        if DO_B:
            for jv in range(2):
                ensure(f"w1_{jv}0")
                pnl, bpn = load_panel(W[f"w1_{jv}0"], 0)
                pt, bp = ps_next()
                for l_ in range(32):
                    S.op("pe", lambda e: e.matmul(pt[:, 0:1], lhsT=pnl[:, l_, :], rhs=posT[:, jv, l_:l_ + 1], start=(l_ == 0), stop=(l_ == 31)),
                         reads=[bpn, b_posT], writes=[bp])
                S.op("dve", lambda e: e.tensor_tensor(out=cb1[:, jv:jv + 1], in0=b1t[:, jv:jv + 1], in1=pt[:, 0:1], op=ALU.add), reads=[bp, b_cb1], writes=[b_cb1])

        def transpose_in(src_rows, ncols, dst, dst_bufs, dst_dt_bf16=False):
            nk = ncols // 128
            S.dma("sp", xt[:, :, 0:ncols], src_rows.rearrange("(b p) f -> p b f", p=128), writes=[b_xt])
            for k in range(nk):
                pt, bp = ps_next()
                for blk in range(4):
                    S.op("pe", lambda e: e.transpose(pt[:, blk * 128:(blk + 1) * 128], xt[:, blk, k * 128:(k + 1) * 128], ident[:]),
                         reads=[b_xt, b_ident], writes=[bp])
                S.op("act", lambda e: e.copy(out=dst[:, k, :], in_=pt[:]), reads=[bp], writes=[dst_bufs[k]])

        def rmsnorm(gi, out_t, out_buf):
            for k in range(8):
                S.op("act", lambda e: e.activation(out=sq[:, k, :], in_=x[:, k, :], func=AF.Square), reads=[b_x[k]], writes=[b_sq])
            pt, bp = ps_next()
            for k in range(8):
                S.op("pe", lambda e: e.matmul(pt[:], lhsT=ones_bf[:], rhs=sq[:, k, :], start=(k == 0), stop=(k == 7)),
                     reads=[b_sq, b_ones], writes=[bp])
            S.op("act", lambda e: e.activation(out=rstd[:], in_=pt[:], func=AF.Sqrt, bias=epsb[:, 0:1]), reads=[bp, b_eps], writes=[b_rstd])
            S.op("dve", lambda e: e.reciprocal(out=rstd[:], in_=rstd[:]), reads=[b_rstd], writes=[b_rstd])
            for k in range(8):
                S.op("dve", lambda e: e.scalar_tensor_tensor(out=out_t[:, k, :], in0=x[:, k, :], scalar=gvec[:, gi, k:k + 1], in1=rstd[:],
                                                              op0=ALU.mult, op1=ALU.mult),
                     reads=[b_x[k], b_gvec, b_rstd], writes=[out_buf])

        def proj_chunk(pnl_ap, pnl_buf, col0, mcols, rhs_t, rhs_buf, nk):
            pt, bp = ps_next()
            for k in range(nk):
                S.op("pe", lambda e: e.matmul(pt[0:mcols, :], lhsT=pnl_ap[:, k, col0:col0 + mcols], rhs=rhs_t[:, k, :],
                                              start=(k == 0), stop=(k == nk - 1)),
                     reads=[pnl_buf] + (rhs_buf if isinstance(rhs_buf, list) else [rhs_buf]), writes=[bp])
            return pt, bp

        def rg_layer(l, ti, es2, smp=False):
            def sb2(name, shape, dt=F32):
                uid[0] += 1
                return es2.enter_context(nc.sbuf_tensor(f"s{uid[0]}_{name}", list(shape), dt))
            y = sb2("rg_y", [128, 10, 512], BF16); b_y = [Buf() for _ in range(10)]
            xc = sb2("rg_xc", [128, 10, 512]); b_xc = [Buf() for _ in range(10)]
            xcb = sb2("rg_xcb", [128, 10, 512], BF16); b_xcb = [Buf() for _ in range(10)]
            xrh = [sb2(f"rg_xrh{i}", [128, 515]) for i in range(2)]; b_xrh = [Buf(), Buf()]
            t1 = [sb2(f"rg_t1{i}", [128, 512]) for i in range(2)]; b_t1 = [Buf(), Buf()]
            gr = [sb2(f"rg_r{i}", [128, 512]) for i in range(2)]; b_gr = [Buf(), Buf()]
            gi_ = [sb2(f"rg_i{i}", [128, 512]) for i in range(2)]; b_gi = [Buf(), Buf()]
            ga = [sb2(f"rg_a{i}", [128, 512]) for i in range(2)]; b_ga = [Buf(), Buf()]
            gm = [sb2(f"rg_m{i}", [128, 512]) for i in range(2)]; b_gm = [Buf(), Buf()]
            hs = [sb2(f"rg_hs{i}", [128, 512]) for i in range(2)]; b_hs = [Buf(), Buf()]
            if smp:
                s_rgh = sb2("s_rgh", [128, 10, 4, 3]); s_h0 = sb2("s_h0", [128, 10, 4]); b_sst = Buf()
                so_rgc = sb2("so_rgc", [128, 10, 4]); so_rgh = sb2("so_rgh", [128, 10, 4]); b_so = Buf()
                S.dma("sp", s_rgh[:], s_rgh_d[:, l], writes=[b_sst])
                S.dma("sp", s_h0[:], s_h0_d[:, l], writes=[b_sst])
            rmsnorm(l, h, b_h)
            st = Stream([(W[f"rgin{l}"], p) for p in range(5)] + [(W[f"band{l}_0"], 0), (W[f"band{l}_1"], 0)]
                        + [(W[f"rgout{l}"], p) for p in range(4)])
            for jc in range(20):
                if jc % 4 == 0:
                    pnl, bpn = st.get()
                pt, bp = proj_chunk(pnl, bpn, (jc % 4) * 128, 128, h, b_h, 8)
                if jc < 10:
                    S.op("act", lambda e: e.activation(out=y[:, jc, :], in_=pt[:], func=AF.Gelu), reads=[bp], writes=[b_y[jc]])
                else:
                    j = jc - 10
                    q = j % 2
                    S.op("act", lambda e: e.copy(out=xrh[q][:, 3:515], in_=pt[:]), reads=[bp], writes=[b_xrh[q]])
                    S.op("act", lambda e: e.activation(out=xc[:, j, :], in_=pt[:], func=AF.Identity, scale=rgcw[:, l, 3, j:j + 1], bias=rgcb[:, l, j:j + 1]),
                         reads=[bp, b_rgc], writes=[b_xc[j]])
                    if ti == OWN0:
                        S.op("pool", lambda e: e.tensor_scalar(out=rg_hist[:, l, j, :], in0=rg_hist[:, l, j, :], scalar1=flag_sb[:, 0:1], scalar2=None, op0=ALU.mult),
                             reads=[b_rghist[l][j], b_flag], writes=[b_rghist[l][j]])
                    S.op("pool", lambda e: e.tensor_copy(out=xrh[q][:, 0:3], in_=rg_hist[:, l, j, :]), reads=[b_rghist[l][j]], writes=[b_xrh[q]])
                    S.op("pool", lambda e: e.tensor_copy(out=rg_hist[:, l, j, :], in_=xrh[q][:, 512:515]), reads=[b_xrh[q]], writes=[b_rghist[l][j]])
                    if smp:
                        S.op("dve", lambda e: e.tensor_copy(out=so_rgc[:, j, :], in_=xrh[q][:, 6:22:4]), reads=[b_xrh[q]], writes=[b_so])
                        S.op("dve", lambda e: e.tensor_copy(out=xrh[q][:, 3:19].rearrange("p (b k) -> p b k", k=4)[:, :, 0:3], in_=s_rgh[:, j, :, :]),
                             reads=[b_sst], writes=[b_xrh[q]])
                    for k in (2, 1, 0):
                        S.op("dve", lambda e: e.scalar_tensor_tensor(out=xc[:, j, :], in0=xrh[q][:, k:k + 512], scalar=rgcw[:, l, k, j:j + 1], in1=xc[:, j, :],
                                                                      op0=ALU.mult, op1=ALU.add), reads=[b_xrh[q], b_rgc, b_xc[j]], writes=[b_xc[j]])
                    S.op("pool", lambda e: e.tensor_copy(out=xcb[:, j, :], in_=xc[:, j, :]), reads=[b_xc[j]], writes=[b_xcb[j]])
            bands = [st.get(), st.get()]
            for jp in range(5):
                js = (2 * jp, 2 * jp + 1)
                for j in js:
                    q = j % 2
                    ins = [i for i in (j - 1, j, j + 1) if 0 <= i < 10]
                    for gidx, (dst, bdst, bias_t) in enumerate(((gr, b_gr, rgba), (gi_, b_gi, rgbx))):
                        pt, bp = ps_next()
                        bv, b_band = bands[gidx]
                        for n_, i in enumerate(ins):
                            S.op("pe", lambda e: e.matmul(pt[:], lhsT=bv[:, i * 3 + (j - i + 1), :], rhs=xcb[:, i, :], start=(n_ == 0), stop=(n_ == len(ins) - 1)),
                                 reads=[b_band, b_xcb[i]], writes=[bp])
                        S.op("act", lambda e: e.activation(out=dst[q][:], in_=pt[:], func=AF.Sigmoid, bias=bias_t[:, l, j:j + 1]),
                             reads=[bp, b_rgp], writes=[bdst[q]])
                for j in js:
                    q = j % 2
                    S.op("act", lambda e: e.activation(out=ga[q][:], in_=gr[q][:], func=AF.Exp, scale=rgc8[:, l, j:j + 1]),
                         reads=[b_gr[q], b_rgp], writes=[b_ga[q]])
                    S.op("dve", lambda e: e.tensor_tensor(out=gm[q][:], in0=ga[q][:], in1=ga[q][:], op=ALU.mult), reads=[b_ga[q]], writes=[b_gm[q]])
                    S.op("dve", lambda e: e.tensor_scalar(out=gm[q][:], in0=gm[q][:], scalar1=-1.0, scalar2=1.0, op0=ALU.mult, op1=ALU.add),
                         reads=[b_gm[q]], writes=[b_gm[q]])
                    S.op("dve", lambda e: e.tensor_tensor(out=gi_[q][:], in0=gi_[q][:], in1=xc[:, j, :], op=ALU.mult), reads=[b_gi[q], b_xc[j]], writes=[b_gi[q]])
                for j in js:
                    q = j % 2
                    S.op("act", lambda e: e.activation(out=gm[q][:], in_=gm[q][:], func=AF.Sqrt), reads=[b_gm[q]], writes=[b_gm[q]])
                for j in js:
                    q = j % 2
                    if ti == 0:
                        S.op("dve", lambda e: e.memset(gm[q][:, 0:1], 1.0), reads=[], writes=[b_gm[q]])
                    elif ti == OWN0:
                        S.op("dve", lambda e: e.tensor_scalar(out=gm[q][:, 0:1], in0=gm[q][:, 0:1], scalar1=flag_sb[:, 1:2], scalar2=None, op0=ALU.max),
                             reads=[b_gm[q], b_flag], writes=[b_gm[q]])
                        S.op("dve", lambda e: e.tensor_scalar(out=rg_hc[:, l, j:j + 1], in0=rg_hc[:, l, j:j + 1], scalar1=flag_sb[:, 0:1], scalar2=None, op0=ALU.mult),
                             reads=[b_rghc[l][j], b_flag], writes=[b_rghc[l][j]])
                    S.op("dve", lambda e: e.tensor_tensor(out=gi_[q][:], in0=gi_[q][:], in1=gm[q][:], op=ALU.mult), reads=[b_gi[q], b_gm[q]], writes=[b_gi[q]])
                    if smp:
                        S.op("dve", lambda e: e.memset(ga[q][:, 2:18:4], 0.0), reads=[], writes=[b_ga[q]])
                        S.op("dve", lambda e: e.tensor_copy(out=gi_[q][:, 2:18:4], in_=s_h0[:, j, :]), reads=[b_sst], writes=[b_gi[q]])
                    S.op("dve", lambda e: e.tensor_tensor_scan(out=hs[q][:], data0=ga[q][:], data1=gi_[q][:], initial=rg_hc[:, l, j:j + 1], op0=ALU.mult, op1=ALU.add),
                         reads=[b_ga[q], b_gi[q], b_rghc[l][j]], writes=[b_hs[q]])
                    if smp:
                        S.op("dve", lambda e: e.tensor_copy(out=so_rgh[:, j, :], in_=hs[q][:, 3:19:4]), reads=[b_hs[q]], writes=[b_so])
                    S.op("dve", lambda e: e.tensor_copy(out=rg_hc[:, l, j:j + 1], in_=hs[q][:, 511:512]), reads=[b_hs[q]], writes=[b_rghc[l][j]])
                    S.op("dve", lambda e: e.tensor_tensor(out=y[:, j, :], in0=y[:, j, :], in1=hs[q][:], op=ALU.mult), reads=[b_y[j], b_hs[q]], writes=[b_y[j]])
            for oc in range(8):
                if oc % 2 == 0:
                    pnl, bpn = st.get()
                pt, bp = ps_next()
                for k in range(10):
                    S.op("pe", lambda e: e.matmul(pt[:], lhsT=pnl[:, k, (oc % 2) * 128:(oc % 2 + 1) * 128], rhs=y[:, k, :], start=(k == 0), stop=(k == 9)),
                         reads=[bpn, b_y[k]], writes=[bp])
                S.op("dve", lambda e: e.tensor_tensor(out=x[:, oc, :], in0=x[:, oc, :], in1=pt[:], op=ALU.add), reads=[b_x[oc], bp], writes=[b_x[oc]])
            if smp:
                S.dma("sp", o_s_rgc_new[:, l], so_rgc[:], reads=[b_so])
                S.dma("sp", o_s_rgh[:, l], so_rgh[:], reads=[b_so])

        def ffn_ple(l, ti, es2, smp=False):
            def sb2(name, shape, dt=F32):
                uid[0] += 1
                return es2.enter_context(nc.sbuf_tensor(f"s{uid[0]}_{name}", list(shape), dt))
            gbuf = sb2("ff_g", [128, 24, 512], BF16); b_g = [Buf() for _ in range(24)]
            uh = [sb2(f"ff_uh{i}", [128, 514]) for i in range(6)]; b_uh = [Buf() for _ in range(6)]
            uc = [sb2(f"ff_uc{i}", [128, 512]) for i in range(6)]; b_uc = [Buf() for _ in range(6)]
            pT = sb2("ple_pT", [128, 2, 512], BF16); b_pT = [Buf(), Buf()]
            sig = [sb2(f"ple_sig{i}", [128, 512]) for i in range(2)]; b_sig = [Buf(), Buf()]

            if smp:
                s_ffh = sb2("s_ffh", [128, 48, 4, 2]); b_sst = Buf()
                so_ffc = sb2("so_ffc", [128, 48, 4]); b_so = Buf()
                S.dma("sp", s_ffh[:], s_ffh_d[:, l], writes=[b_sst])
            rmsnorm(4 + l, h, b_h)
            items = []
            for c4 in range(6):
                items += [(W[f"up{l}"], c4), (W[f"up{l}"], c4 + 6)]
            items += [(W[f"down{l}"], p) for p in range(8)]
            items += [(W[f"plein{l}"], 0), (W[f"pleg{l}"], 0), (W[f"pleg{l}"], 1)]
            st = Stream(items)
            if ti == OWN0:
                for cc in range(48):
                    S.op("pool", lambda e: e.tensor_scalar(out=ff_hist[:, l, cc, :], in0=ff_hist[:, l, cc, :], scalar1=flag_sb[:, 0:1], scalar2=None, op0=ALU.mult),
                         reads=[b_ffhist[l][cc], b_flag], writes=[b_ffhist[l][cc]])
            pans = [None, None]

            def stageA(c):
                if c % 4 == 0:
                    pans[0] = st.get()
                    pans[1] = st.get()
                ci = c % 4
                for hf in range(2):
                    pnl, bpn = pans[hf]
                    cc = c + 24 * hf
                    q = (c % 3) * 2 + hf
                    pt, bp = proj_chunk(pnl, bpn, ci * 128, 128, h, b_h, 8)
                    S.op("act", lambda e: e.copy(out=uh[q][:, 2:514], in_=pt[:]), reads=[bp], writes=[b_uh[q]])
                    S.op("act", lambda e: e.activation(out=uc[q][:], in_=pt[:], func=AF.Identity, scale=ffcw[:, l, 2, cc:cc + 1], bias=ffcb[:, l, cc:cc + 1]),
                         reads=[bp, b_ffc], writes=[b_uc[q]])
                    S.op("pool", lambda e: e.tensor_copy(out=uh[q][:, 0:2], in_=ff_hist[:, l, cc, :]), reads=[b_ffhist[l][cc]], writes=[b_uh[q]])
                    S.op("pool", lambda e: e.tensor_copy(out=ff_hist[:, l, cc, :], in_=uh[q][:, 512:514]), reads=[b_uh[q]], writes=[b_ffhist[l][cc]])
                    if smp:
                        S.op("dve", lambda e: e.tensor_copy(out=so_ffc[:, cc, :], in_=uh[q][:, 5:21:4]), reads=[b_uh[q]], writes=[b_so])
                        S.op("dve", lambda e: e.tensor_copy(out=uh[q][:, 2:18].rearrange("p (b k) -> p b k", k=4)[:, :, 1:3], in_=s_ffh[:, cc, :, :]),
                             reads=[b_sst], writes=[b_uh[q]])

            def stageB(c):
                for hf in range(2):
                    cc = c + 24 * hf
                    q = (c % 3) * 2 + hf
                    for k in (1, 0):
                        S.op("dve", lambda e: e.scalar_tensor_tensor(out=uc[q][:], in0=uh[q][:, k:k + 512], scalar=ffcw[:, l, k, cc:cc + 1], in1=uc[q][:],
                                                                      op0=ALU.mult, op1=ALU.add), reads=[b_uh[q], b_ffc, b_uc[q]], writes=[b_uc[q]])
                q0 = (c % 3) * 2
                S.op("act", lambda e: e.activation(out=uc[q0][:], in_=uc[q0][:], func=AF.Gelu), reads=[b_uc[q0]], writes=[b_uc[q0]])
                S.op("pool", lambda e: e.tensor_tensor(out=gbuf[:, c, :], in0=uc[q0][:], in1=uc[q0 + 1][:], op=ALU.mult),
                     reads=[b_uc[q0], b_uc[q0 + 1]], writes=[b_g[c]])
            stageA(0)
            for c in range(24):
                if c + 1 < 24:
                    stageA(c + 1)
                stageB(c)
            for oc in range(8):
                pnl, bpn = st.get()
                pt, bp = ps_next()
                for k in range(24):
                    S.op("pe", lambda e: e.matmul(pt[:], lhsT=pnl[:, k, :], rhs=gbuf[:, k, :], start=(k == 0), stop=(k == 23)),
                         reads=[bpn, b_g[k]], writes=[bp])
                S.op("dve", lambda e: e.tensor_tensor(out=x[:, oc, :], in0=x[:, oc, :], in1=pt[:], op=ALU.add), reads=[b_x[oc], bp], writes=[b_x[oc]])
            if smp:
                S.dma("sp", o_s_ffc_new[:, l], so_ffc[:], reads=[b_so])
            if smp:
                S.dma("pool", pT[:], ps_s[l], writes=b_pT)
            else:
                transpose_in(ps_in[l, ti * 512:(ti + 1) * 512, :], 256, pT, b_pT)
            rmsnorm(8 + l, h, b_h)
            pin, bpin = st.get()
            pg = [st.get(), st.get()]
            for oc in range(8):
                q = oc % 2
                pe_, bpe = proj_chunk(pin, bpin, oc * 128, 128, pT, b_pT, 2)
                pgp, bpg = pg[oc // 4]
                pt, bp = proj_chunk(pgp, bpg, (oc % 4) * 128, 128, h, b_h, 8)
                S.op("act", lambda e: e.activation(out=sig[q][:], in_=pt[:], func=AF.Sigmoid), reads=[bp], writes=[b_sig[q]])
                S.op("dve", lambda e: e.tensor_tensor(out=sig[q][:], in0=sig[q][:], in1=pe_[:], op=ALU.mult), reads=[b_sig[q], bpe], writes=[b_sig[q]])
                S.op("dve", lambda e: e.tensor_tensor(out=x[:, oc, :], in0=x[:, oc, :], in1=sig[q][:], op=ALU.add), reads=[b_x[oc], b_sig[q]], writes=[b_x[oc]])


        def attn_layer(l, ti, es2, qbs, smp=False):
            j = l - 2

            def sb2(name, shape, dt=F32):
                uid[0] += 1
                return es2.enter_context(nc.sbuf_tensor(f"s{uid[0]}_{name}", list(shape), dt))
            qT = sb2("qT", [128, 8, 512], BF16); b_qT = [Buf() for _ in range(8)]
            qrT = sb2("qrT", [128, 8, 512], BF16); b_qrT = [Buf() for _ in range(8)]
            gT = sb2("gT", [128, 512], BF16); b_gT = Buf()
            oT = sb2("oT", [128, 8, 512], BF16); b_oT = Buf()
            rl = sb2("rl", [128, 1024]); b_rl = Buf()
            rc_t = rl[:, 0:512]; rs_t = rl[:, 512:1024]; b_rope = b_rl
            rt1 = sb2("rt1", [128, 512]); rt2 = sb2("rt2", [128, 512]); b_rt1 = Buf(); b_rt2 = Buf()
            if smp:
                qbs = []
            PW = 8 if smp else 1024
            qpad = [sb2(f"qpad{i}", [128, 8, 128 if not smp else 2], BF16) for i in range(2)]; b_qpad = [Buf(), Buf()]
            qrpad = [sb2(f"qrpad{i}", [128, 8, 128 if not smp else 2], BF16) for i in range(2)]; b_qrpad = [Buf(), Buf()]
            cbt = [sb2(f"cbt{i}", [128, 256], BF16) for i in range(2)]; b_cbt = [Buf(), Buf()]
            wbt = [sb2(f"wbt{i}", [128, 640], BF16) for i in range(2)]; b_wbt = [Buf(), Buf()]
            vat = [sb2(f"vat{i}", [128, 2, 64]) for i in range(2)]; b_vat = [Buf(), Buf()]
            Pc = sb2("Pc", [128, 2, PW], BF16); b_Pc = [Buf(), Buf()]
            Pn = Pc; b_Pn = b_Pc
            sqf = sq[:].rearrange("p k t -> p (k t)")
            Pst = [sqf[:, i * 1024:(i + 1) * 1024] for i in range(2)]; b_Pst = [Buf() for _ in range(2)]
            pst_i = [0]
            osb = [sb2(f"osb{i}", [64, PW]) for i in range(3)]; b_osb = [Buf() for _ in range(3)]
            rlx = sb2("rlx", [64, PW]); b_rlx = Buf()
            impf = sb2("impf", [128, 64]); b_impf = Buf()
            impt = sb2("impt", [128, 64]); b_impt = Buf()
            m8 = sb2("m8", [128, 16]); b_m8 = Buf()
            thr = sb2("thr", [128, 1]); b_thr = Buf()
            nbf = sb2("nbf", [128, 64]); b_nbf = Buf()
            nb = sb2("nb", [128, 64], BF16); b_nb = Buf()
            nbd = sb2("nbd", [128, 128], BF16); b_nbd = Buf()
            nbe = [sb2(f"nbe{i}", [128, 128], BF16) for i in range(4)]; b_nbe = [Buf() for _ in range(4)]
            nbe_i = [0]
            for t_ in qpad + qrpad:
                S.op("pool", lambda e: e.memset(t_[:], 0.0), writes=b_qpad + b_qrpad)

            S.dma("sp", rc_t, ropec_s if smp else ropec[:, ti * 512:(ti + 1) * 512], writes=[b_rope])
            S.dma("sp", rs_t, ropes_s if smp else ropes[:, ti * 512:(ti + 1) * 512], writes=[b_rope])
            rmsnorm(l, h, b_h)
            for bq in b_Pst:
                bq.w = b_sq.w
                bq.r = list(b_sq.r)
            st = Stream([(W[f"q{j}"], 0), (W[f"qsw{j}"], 0), (W[f"q{j}"], 1), (W[f"qsw{j}"], 1), (W[f"qg{j}"], 0), (W[f"wo{j}"], 0), (W[f"wo{j}"], 1)], depth=2)
            for c in range(8):
                if c % 4 == 0:
                    pq, bpq = st.get()
                    psw, bpsw = st.get()
                pt, bp = proj_chunk(pq, bpq, (c % 4) * 128, 128, h, b_h, 8)
                pt2, bp2 = proj_chunk(psw, bpsw, (c % 4) * 128, 128, h, b_h, 8)
                S.op("dve", lambda e: e.tensor_copy(out=qT[:, c, :], in_=pt[:]), reads=[bp], writes=[b_qT[c]])
                S.op("dve", lambda e: e.tensor_tensor(out=rt1[:], in0=pt[:], in1=rc_t, op=ALU.mult), reads=[bp, b_rope], writes=[b_rt1])
                S.op("dve", lambda e: e.tensor_tensor(out=rt2[:], in0=pt2[:], in1=rs_t, op=ALU.mult), reads=[bp2, b_rope], writes=[b_rt2])
                S.op("dve", lambda e: e.tensor_tensor(out=qrT[:, c, :], in0=rt1[:], in1=rt2[:], op=ALU.add), reads=[b_rt1, b_rt2], writes=[b_qrT[c]])
            pg, bpg = st.get()
            pt, bp = proj_chunk(pg, bpg, 0, 128, h, b_h, 8)
            S.op("act", lambda e: e.activation(out=gT[:], in_=pt[:], func=AF.Sigmoid), reads=[bp], writes=[b_gT])
            identb4 = ident_bf[:].unsqueeze(1).broadcast_to([128, 4, 128])

            def scores(lhs_bias, bias_bufs, kmat, kbufs, qp, bqp):
                pp, bpp = ps_pair()
                for bank in range(2):
                    S.op("pe", lambda e: e.matmul(pp[:, bank * 512:(bank + 1) * 512], lhsT=lhs_bias, rhs=identb4, start=True, stop=False),
                         reads=bias_bufs + [b_identb], writes=[bpp[bank]])
                    for h4 in range(4):
                        hh = bank * 4 + h4
                        S.op("pe", lambda e: e.matmul(pp[:, hh * 128:(hh + 1) * 128], lhsT=kmat, rhs=qp[:, hh, :], start=False, stop=(h4 == 3)),
                             reads=kbufs + [bqp], writes=[bpp[bank]])
                return pp, bpp

            def pv_acc(vmat, vbufs, pt_, bpt_, first, last):
                for bank in range(2):
                    S.op("pe", lambda e: e.matmul(acc_ps[:, bank * 512:(bank + 1) * 512], lhsT=vmat, rhs=pt_[:, bank * 512:(bank + 1) * 512], start=first, stop=last),
                         reads=vbufs + [bpt_], writes=[b_acc[bank]])

            if len(qbs) < 4:
                S.op("pool", lambda e: e.memset(oT[:], 0.0), writes=[b_oT])
            for qb in (qbs if DBG >= 5 else []):
                qbo = (ti - OWN0) * 4 + qb + 1
                kd = 4 * ti + qb
                tb = qbo % 2
                S.dma("pool", cbt[tb][:], t_cb[qbo], writes=[b_cbt[tb]])
                S.dma("pool", wbt[tb][:], t_wb[qbo], writes=[b_wbt[tb]])
                S.dma("sp", vat[tb][:], t_va[qbo], writes=[b_vat[tb]])
                tsl = slice(qb * 128, (qb + 1) * 128)
                for g in range(2):
                    pi = g
                    S.op("pool", lambda e: e.tensor_copy(out=qpad[pi][0:64, 0:8:2, :], in_=qT[0:64, 4 * g:4 * g + 4, tsl]), reads=b_qT[4 * g:4 * g + 4], writes=[b_qpad[pi]])
                    S.op("pool", lambda e: e.tensor_copy(out=qpad[pi][64:128, 1:8:2, :], in_=qT[64:128, 4 * g:4 * g + 4, tsl]), reads=b_qT[4 * g:4 * g + 4], writes=[b_qpad[pi]])
                    S.op("pool", lambda e: e.tensor_copy(out=qrpad[pi][0:64, 0:8:2, :], in_=qrT[0:64, 4 * g:4 * g + 4, tsl]), reads=b_qrT[4 * g:4 * g + 4], writes=[b_qrpad[pi]])
                    S.op("pool", lambda e: e.tensor_copy(out=qrpad[pi][64:128, 1:8:2, :], in_=qrT[64:128, 4 * g:4 * g + 4, tsl]), reads=b_qrT[4 * g:4 * g + 4], writes=[b_qrpad[pi]])
                    nbc = 1 if 32 * ti + 31 < 128 else 2
                    for bc in range(nbc):
                        pp, bpp = scores(cbt[tb][:, bc * 128:(bc + 1) * 128], [b_cbt[tb]], kcK[:, g, bc * 128:(bc + 1) * 128], [b_kcK], qpad[pi], b_qpad[pi])
                        S.op("act", lambda e: e.activation(out=Pc[:, bc, :], in_=pp[:], func=AF.Exp, scale=0.125), reads=bpp, writes=[b_Pc[bc]])

                    def win_front(wi):
                        rch = (kd - 4 + wi) % 8
                        pp, bpp = scores(wbt[tb][:, wi * 128:(wi + 1) * 128], [b_wbt[tb]], winK[:, g, rch * 128:(rch + 1) * 128], [b_winK], qrpad[pi], b_qrpad[pi])
                        pz = pst_i[0] % 2
                        pst_i[0] += 1
                        S.op("act", lambda e: e.activation(out=Pst[pz], in_=pp[:], func=AF.Exp, scale=0.125), reads=bpp, writes=[b_Pst[pz]])
                        return pz
                    cur = win_front(0)
                    lp, blp = ps_pair()
                    for bank in range(2):
                        for bc in range(nbc):
                            S.op("pe", lambda e: e.matmul(lp[:, bank * 512:(bank + 1) * 512], lhsT=ones1[:], rhs=Pc[:, bc, bank * 512:(bank + 1) * 512],
                                                          start=(bc == 0), stop=(bc == nbc - 1)), reads=[b_ones1, b_Pc[bc]], writes=[blp[bank]])
                    S.op("dve", lambda e: e.tensor_scalar(out=rl[:], in0=lp[:], scalar1=1e-30, scalar2=None, op0=ALU.add), reads=blp, writes=[b_rl])
                    S.op("dve", lambda e: e.reciprocal(out=rl[:], in_=rl[:]), reads=[b_rl], writes=[b_rl])
                    for bc in range(nbc):
                        S.op("dve", lambda e: e.tensor_tensor(out=Pn[:, bc, :], in0=Pc[:, bc, :], in1=rl[:], op=ALU.mult), reads=[b_Pc[bc], b_rl], writes=[b_Pn[bc]])
                    for wi in range(5):
                        nxt = win_front(wi + 1) if wi < 4 else None
                        rch = (kd - 4 + wi) % 8
                        pv_acc(winV[:, rch, g, :], [b_winV], Pst[cur], b_Pst[cur], wi == 0, wi == 4)
                        cur = nxt
                        if wi == 2:
                            ocp, bocp = ps_pair()
                            for bank in range(2):
                                for bc in range(nbc):
                                    S.op("pe", lambda e: e.matmul(ocp[:, bank * 512:(bank + 1) * 512], lhsT=vcVb[:, bc, g, :], rhs=Pn[:, bc, bank * 512:(bank + 1) * 512],
                                                                  start=(bc == 0), stop=(bc == nbc - 1)), reads=[b_vcVb, b_Pn[bc]], writes=[bocp[bank]])
                            S.op("act", lambda e: e.copy(out=osb[0][:], in_=ocp[0:64, :]), reads=bocp, writes=[b_osb[0]])
                            ip, bip = ps_next()
                            n_ = 0
                            for bc in range(nbc):
                                for hh in range(8):
                                    S.op("pe", lambda e: e.matmul(ip[:, 0:64], lhsT=Pn[:, bc, hh * 128:(hh + 1) * 128], rhs=mcs[:, bc, :], start=(n_ == 0), stop=(n_ == 8 * nbc - 1)),
                                         reads=[b_Pn[bc], b_mcs], writes=[bip])
                                    n_ += 1
                    S.op("dve", lambda e: e.tensor_tensor(out=impf[:], in0=ip[:, 0:64], in1=vat[tb][:, 0, :], op=ALU.mult), reads=[bip, b_vat[tb]], writes=[b_impf])
                    S.op("dve", lambda e: e.tensor_tensor(out=impf[:], in0=impf[:], in1=vat[tb][:, 1, :], op=ALU.add), reads=[b_impf, b_vat[tb]], writes=[b_impf])
                    S.op("dve", lambda e: e.max(out=m8[:, 0:8], in_=impf[:]), reads=[b_impf], writes=[b_m8])
                    S.op("dve", lambda e: e.match_replace(out=impt[:], in_to_replace=m8[:, 0:8], in_values=impf[:], imm_value=-2.0), reads=[b_impf, b_m8], writes=[b_impt])
                    S.op("dve", lambda e: e.max(out=m8[:, 8:16], in_=impt[:]), reads=[b_impt], writes=[b_m8])
                    S.op("dve", lambda e: e.tensor_scalar(out=thr[:], in0=m8[:, 15:16], scalar1=-0.5, scalar2=None, op0=ALU.max), reads=[b_m8], writes=[b_thr])
                    S.op("dve", lambda e: e.tensor_scalar(out=nbf[:], in0=impf[:], scalar1=thr[:, 0:1], scalar2=BIGV, op0=ALU.is_ge, op1=ALU.mult),
                         reads=[b_impf, b_thr], writes=[b_nbf])
                    S.op("dve", lambda e: e.tensor_scalar(out=nb[:], in0=nbf[:], scalar1=-BIGV, scalar2=None, op0=ALU.add), reads=[b_nbf], writes=[b_nb])
                    S.op("dve", lambda e: e.tensor_tensor(out=nbd[:].rearrange("p (s k) -> p s k", s=2), in0=nb[:, 2 * kd:2 * kd + 2].unsqueeze(2).broadcast_to([128, 2, 64]),
                                                           in1=tri[:].rearrange("p (s k) -> p s k", s=2), op=ALU.add), reads=[b_nb, b_tri], writes=[b_nbd])
                    S.op("dve", lambda e: e.reciprocal(out=rlx[:], in_=acc_ps[64:128, :]), reads=b_acc, writes=[b_rlx])
                    S.op("dve", lambda e: e.tensor_tensor(out=osb[2][:], in0=acc_ps[0:64, :], in1=rlx[:], op=ALU.mult), reads=b_acc + [b_rlx], writes=[b_osb[2]])
                    def slc_front(kc):
                        if kc == kd:
                            lb, lbb = nbd[:], [b_nbd]
                        else:
                            z = nbe_i[0] % 4
                            nbe_i[0] += 1
                            S.op("dve", lambda e: e.tensor_copy(out=nbe[z][:].rearrange("p (s k) -> p s k", s=2),
                                                                 in_=nb[:, 2 * kc:2 * kc + 2].unsqueeze(2).broadcast_to([128, 2, 64])), reads=[b_nb], writes=[b_nbe[z]])
                            lb, lbb = nbe[z][:], [b_nbe[z]]
                        pp, bpp = scores(lb, lbb, slcK[:, g, kc * 128:(kc + 1) * 128], [b_slcK], qrpad[pi], b_qrpad[pi])
                        pz = pst_i[0] % 2
                        pst_i[0] += 1
                        S.op("act", lambda e: e.activation(out=Pst[pz], in_=pp[:], func=AF.Exp, scale=0.125), reads=bpp, writes=[b_Pst[pz]])
                        return pz
                    cur = slc_front(0)
                    for kc in range(kd + 1):
                        nxt = slc_front(kc + 1) if kc < kd else None
                        pv_acc(slcV[:, kc, g, :], [b_slcV], Pst[cur], b_Pst[cur], kc == 0, kc == kd)
                        cur = nxt
                    S.op("dve", lambda e: e.reciprocal(out=rlx[:], in_=acc_ps[64:128, :]), reads=b_acc, writes=[b_rlx])
                    S.op("dve", lambda e: e.tensor_tensor(out=osb[1][:], in0=acc_ps[0:64, :], in1=rlx[:], op=ALU.mult), reads=b_acc + [b_rlx], writes=[b_osb[1]])
                    for br in range(3):
                        gp, bgp = ps_pair()
                        for hh in range(8):
                            f0 = 3 * (8 * g + hh) + br
                            S.op("pe", lambda e: e.matmul(gp[:, hh * 128:(hh + 1) * 128], lhsT=ident_bf[:, f0:f0 + 1].broadcast_to([128, 128]), rhs=gT[:, tsl], start=True, stop=True),
                                 reads=[b_identb, b_gT], writes=[bgp[hh // 4]])
                        S.op("dve", lambda e: e.tensor_tensor(out=osb[br][:], in0=osb[br][:], in1=gp[0:64, :], op=ALU.mult), reads=[b_osb[br]] + bgp, writes=[b_osb[br]])
                    S.op("dve", lambda e: e.tensor_tensor(out=osb[0][:], in0=osb[0][:], in1=osb[1][:], op=ALU.add), reads=[b_osb[0], b_osb[1]], writes=[b_osb[0]])
                    av = osb[0][:].rearrange("p (h t) -> p h t", h=8)
                    tv = osb[2][:].rearrange("p (h t) -> p h t", h=8)
                    S.op("dve", lambda e: e.tensor_tensor(out=oT[0:64, 4 * g:4 * g + 4, tsl], in0=av[:, 0:8:2, :], in1=tv[:, 0:8:2, :], op=ALU.add),
                         reads=[b_osb[0], b_osb[2]], writes=[b_oT])
                    S.op("dve", lambda e: e.tensor_tensor(out=oT[64:128, 4 * g:4 * g + 4, tsl], in0=av[:, 1:8:2, :], in1=tv[:, 1:8:2, :], op=ALU.add),
                         reads=[b_osb[0], b_osb[2]], writes=[b_oT])

            if smp:
                S.op("pool", lambda e: e.memset(oT[:], 0.0), writes=[b_oT])
                QS = sb2("QS", [128, 16], BF16); QRS = sb2("QRS", [128, 16], BF16); b_QS = Buf()
                S.op("pool", lambda e: e.memset(QS[:], 0.0), writes=[b_QS])
                S.op("pool", lambda e: e.memset(QRS[:], 0.0), writes=[b_QS])
                PcS = sb2("PcS", [128, 4, 16], BF16); b_PcS = Buf()
                PnG = sb2("PnG", [128, 4, 2], BF16); b_PnG = Buf()
                PnGf = sb2("PnGf", [128, 4, 2]); b_PnGf = Buf()
                rlS = sb2("rlS", [128, 16]); b_rlS = Buf()
                tpin = sb2("tpin", [128, 2, 128]); b_tpin = Buf()
                S.op("pool", lambda e: e.memset(tpin[:], 0.0), writes=[b_tpin])
                impS = sb2("impS", [128, 256]); impS2 = sb2("impS2", [128, 256]); b_impS = Buf()
                m8s = sb2("m8s", [128, 16]); thrs = sb2("thrs", [128, 1])
                nbS = sb2("nbS", [128, 256], BF16); b_nbS = Buf()
                gb = [sb2(f"gb{i}", [128, 256]) for i in range(4)]; b_gb = [Buf() for _ in range(4)]
                KTc = [sb2(f"KTc{i}", [128, 128], BF16) for i in range(3)]; b_KTc = [Buf() for _ in range(3)]
                wbuf = sb2("wbuf", [128, 4, 256]); b_wbuf = Buf()
                PsS = [sb2(f"PsS{i}", [128, 16], BF16) for i in range(3)]; b_PsS = [Buf() for _ in range(3)]
                GS = sb2("GS", [128, 48, 4]); b_GS = Buf()
                ocS = sb2("ocS", [64, 16]); osS = sb2("osS", [64, 2, 16]); rlS2 = sb2("rlS2", [64, 2, 16]); b_ocS = Buf()
                oS = sb2("oS", [64, 16]); tS = sb2("tS", [64, 16]); b_oS = Buf()
                gpt, bgpt = ps_next()
                for f0 in range(48):
                    S.op("pe", lambda e: e.matmul(gpt[:, f0 * 4:(f0 + 1) * 4], lhsT=ident_bf[:, f0:f0 + 1].broadcast_to([128, 128]), rhs=gT[:, 3:19:4], start=True, stop=True),
                         reads=[b_identb, b_gT], writes=[bgpt])
                S.op("dve", lambda e: e.tensor_copy(out=GS[:].rearrange("p f b -> p (f b)"), in_=gpt[:, 0:192]), reads=[bgpt], writes=[b_GS])
                gi_c = [0]
                kt_c = [0]
                ps_c = [0]
                ne_c = [0]
                for i in range(4):
                    col = 4 * i + 3
                    for g in range(2):
                        for par in range(2):
                            S.op("dve", lambda e: e.tensor_copy(out=QS[64 * g:64 * g + 64, 8 * g + par:8 * g + 8:2], in_=qT[64 * par:64 * par + 64, 4 * g:4 * g + 4, col]),
                                 reads=b_qT[4 * g:4 * g + 4], writes=[b_QS])
                            S.op("dve", lambda e: e.tensor_copy(out=QRS[64 * g:64 * g + 64, 8 * g + par:8 * g + 8:2], in_=qrT[64 * par:64 * par + 64, 4 * g:4 * g + 4, col]),
                                 reads=b_qrT[4 * g:4 * g + 4], writes=[b_QS])
                    for pc in range(4):
                        pp, bpp = ps_next()
                        if pc == 0:
                            S.op("pe", lambda e: e.matmul(pp[:, 0:16], lhsT=cb0, rhs=brhs, start=True, stop=False), reads=[b_stab], writes=[bpp])
                        S.op("pe", lambda e: e.matmul(pp[:, 0:16], lhsT=kcS[:, i, pc * 128:(pc + 1) * 128], rhs=QS[:], start=(pc != 0), stop=True),
                             reads=[b_kcS, b_QS], writes=[bpp])
                        S.op("act", lambda e: e.activation(out=PcS[:, pc, :], in_=pp[:, 0:16], func=AF.Exp, scale=0.125), reads=[bpp], writes=[b_PcS])
                    lp, blp = ps_next()
                    for pc in range(4):
                        S.op("pe", lambda e: e.matmul(lp[:, 0:16], lhsT=ones1[:], rhs=PcS[:, pc, :], start=(pc == 0), stop=(pc == 3)), reads=[b_ones1, b_PcS], writes=[blp])
                    S.op("dve", lambda e: e.tensor_scalar(out=rlS[:], in0=lp[:, 0:16], scalar1=1e-30, scalar2=None, op0=ALU.add), reads=[blp], writes=[b_rlS])
                    S.op("dve", lambda e: e.reciprocal(out=rlS[:], in_=rlS[:]), reads=[b_rlS], writes=[b_rlS])
                    for pc in range(4):
                        S.op("dve", lambda e: e.tensor_tensor(out=PcS[:, pc, :], in0=PcS[:, pc, :], in1=rlS[:], op=ALU.mult), reads=[b_PcS, b_rlS], writes=[b_PcS])
                    for pc in range(4):
                        S.op("pe", lambda e: e.matmul(acc_ps[:, 0:16], lhsT=vcS[:, i, pc, :], rhs=PcS[:, pc, :], start=(pc == 0), stop=(pc == 3)),
                             reads=[b_vcS, b_PcS], writes=[b_acc[0]])
                    S.op("dve", lambda e: e.tensor_copy(out=ocS[:, 0:8], in_=acc_ps[0:64, 0:8]), reads=[b_acc[0]], writes=[b_ocS])
                    S.op("dve", lambda e: e.tensor_copy(out=ocS[:, 8:16], in_=acc_ps[64:128, 8:16]), reads=[b_acc[0]], writes=[b_ocS])
                    for pc in range(4):
                        S.op("dve", lambda e: e.tensor_reduce(out=PnGf[:, pc, :], in_=PcS[:, pc, :].rearrange("p (g h) -> p g h", g=2), axis=mybir.AxisListType.X, op=ALU.add),
                             reads=[b_PcS], writes=[b_PnGf])
                    S.op("dve", lambda e: e.tensor_copy(out=PnG[:], in_=PnGf[:]), reads=[b_PnGf], writes=[b_PnG])
                    ipT, bipT = ps_next()
                    for sc in range(2):
                        for pc in range(4):
                            S.op("pe", lambda e: e.matmul(ipT[:, 2 * sc:2 * sc + 2], lhsT=mcsS[:, pc, sc, :], rhs=PnG[:, pc, :], start=(pc == 0), stop=(pc == 3)),
                                 reads=[b_stab, b_PnG], writes=[bipT])
                    S.op("dve", lambda e: e.tensor_copy(out=tpin[:, :, 0:2], in_=ipT[:, 0:4].rearrange("p (s g) -> p s g", s=2)), reads=[bipT], writes=[b_tpin])
                    tp, btp = ps_next()
                    for sc in range(2):
                        S.op("pe", lambda e: e.transpose(tp[:, sc * 128:(sc + 1) * 128], tpin[:, sc, :], ident[:]), reads=[b_tpin, b_ident], writes=[btp])
                    S.op("dve", lambda e: e.tensor_tensor(out=impS[:], in0=tp[:, 0:256], in1=tas[:, 0, :], op=ALU.mult), reads=[btp, b_stab], writes=[b_impS])
                    S.op("dve", lambda e: e.tensor_tensor(out=impS[:], in0=impS[:], in1=tas[:, 1, :], op=ALU.add), reads=[b_impS, b_stab], writes=[b_impS])
                    S.op("dve", lambda e: e.max(out=m8s[:, 0:8], in_=impS[:]), reads=[b_impS], writes=[b_impS])
                    S.op("dve", lambda e: e.match_replace(out=impS2[:], in_to_replace=m8s[:, 0:8], in_values=impS[:], imm_value=-2.0), reads=[b_impS], writes=[b_impS])
                    S.op("dve", lambda e: e.max(out=m8s[:, 8:16], in_=impS2[:]), reads=[b_impS], writes=[b_impS])
                    S.op("dve", lambda e: e.tensor_scalar(out=thrs[:], in0=m8s[:, 15:16], scalar1=-0.5, scalar2=None, op0=ALU.max), reads=[b_impS], writes=[b_impS])
                    S.op("dve", lambda e: e.tensor_scalar(out=impS2[:], in0=impS[:], scalar1=thrs[:, 0:1], scalar2=BIGV, op0=ALU.is_ge, op1=ALU.mult), reads=[b_impS], writes=[b_impS])
                    S.op("dve", lambda e: e.tensor_scalar(out=nbS[:], in0=impS2[:], scalar1=-BIGV, scalar2=None, op0=ALU.add), reads=[b_impS], writes=[b_nbS])

                    def s_front(bias_l, bias_b, kt, ktb):
                        pp, bpp = ps_next()
                        if bias_l is not None:
                            S.op("pe", lambda e: e.matmul(pp[:, 0:16], lhsT=bias_l, rhs=brhs, start=True, stop=False), reads=bias_b + [b_stab], writes=[bpp])
                        S.op("pe", lambda e: e.matmul(pp[:, 0:16], lhsT=kt, rhs=QRS[:], start=(bias_l is None), stop=True), reads=ktb + [b_QS], writes=[bpp])
                        z = ps_c[0] % 3
                        ps_c[0] += 1
                        S.op("act", lambda e: e.activation(out=PsS[z][:], in_=pp[:, 0:16], func=AF.Exp, scale=0.125), reads=[bpp], writes=[b_PsS[z]])
                        return z

                    def s_pv(z, v0, v1, vb, first, last):
                        for g, vv in enumerate((v0, v1)):
                            S.op("pe", lambda e: e.matmul(acc_ps[:, 512 + 8 * g:512 + 8 * g + 8], lhsT=vv, rhs=PsS[z][:, 8 * g:8 * g + 8],
                                                          start=(first and g == 0), stop=(last and g == 1)),
                                 reads=vb + [b_PsS[z]], writes=[b_acc[1]])

                    def gather(kc):
                        gz = gi_c[0] % 4
                        gi_c[0] += 1
                        S._deps("pool", [b_idxS], [b_gb[gz]])
                        tk = S.dma_ind(gb[gz][:], c_slc, idxS[:, i, kc:kc + 1])
                        S._mark(tk, [b_idxS], [b_gb[gz]])
                        return gz

                    def slc_front_s(kc, gz):
                        if kc == 64:
                            z = s_front(b64[:, i, :], [], KTn[:, 0, :], [b_KTn])
                            return (z, Vn[:, 0, 0, :], Vn[:, 0, 1, :], [b_Vn])
                        zz = ne_c[0] % 4
                        ne_c[0] += 1
                        tq, btq = ps_next()
                        S.op("pe", lambda e: e.transpose(tq[:, 0:128], gb[gz][:, 0:128], ident[:]), reads=[b_gb[gz], b_ident], writes=[btq])
                        kz = kt_c[0] % 3
                        kt_c[0] += 1
                        S.op("act", lambda e: e.copy(out=KTc[kz][:], in_=tq[:, 0:128]), reads=[btq], writes=[b_KTc[kz]])
                        vz = 2 + (kc % 4)
                        S.op("dve", lambda e: e.tensor_copy(out=slcV[:, vz, :, 0:64], in_=gb[gz][:, 128:256].rearrange("p (g d) -> p g d", g=2)),
                             reads=[b_gb[gz]], writes=[b_Vc[kc % 4]])
                        S.op("dve", lambda e: e.tensor_copy(out=nbe[zz][:].rearrange("p (s k) -> p s k", s=2),
                                                             in_=nbS[:, 2 * kc:2 * kc + 2].unsqueeze(2).broadcast_to([128, 2, 64])), reads=[b_nbS], writes=[b_nbe[zz]])
                        z = s_front(nbe[zz][:], [b_nbe[zz]], KTc[kz][:], [b_KTc[kz]])
                        return (z, slcV[:, vz, 0, :], slcV[:, vz, 1, :], [b_Vc[kc % 4]])

                    gq = [gather(kc) for kc in range(3)]
                    cur = slc_front_s(0, gq[0])
                    for kc in range(65):
                        if kc + 3 < 64:
                            gq.append(gather(kc + 3))
                        nxt = slc_front_s(kc + 1, gq[kc + 1] if kc + 1 < 64 else None) if kc < 64 else None
                        s_pv(cur[0], cur[1], cur[2], cur[3], kc == 0, kc == 64)
                        cur = nxt
                    S.op("dve", lambda e: e.reciprocal(out=rlS2[:, 0, :], in_=acc_ps[64:128, 512:528]), reads=[b_acc[1]], writes=[b_ocS])
                    S.op("dve", lambda e: e.tensor_tensor(out=osS[:, 0, :], in0=acc_ps[0:64, 512:528], in1=rlS2[:, 0, :], op=ALU.mult), reads=[b_acc[1], b_ocS], writes=[b_ocS])
                    S.dma("sp", wbuf[:], c_win[i].rearrange("(c p) f -> p c f", p=128), writes=[b_wbuf])

                    def win_front_s(wc):
                        if wc == 4:
                            z = s_front(b64[:, i, :], [], KTn[:, 1, :], [b_KTn])
                            return (z, Vn[:, 1, 0, :], Vn[:, 1, 1, :], [b_Vn])
                        tq, btq = ps_next()
                        S.op("pe", lambda e: e.transpose(tq[:, 0:128], wbuf[:, wc, 0:128], ident[:]), reads=[b_wbuf, b_ident], writes=[btq])
                        kz = kt_c[0] % 3
                        kt_c[0] += 1
                        S.op("act", lambda e: e.copy(out=KTc[kz][:], in_=tq[:, 0:128]), reads=[btq], writes=[b_KTc[kz]])
                        vz = 2 + wc
                        S.op("dve", lambda e: e.tensor_copy(out=slcV[:, vz, :, 0:64], in_=wbuf[:, wc, 128:256].rearrange("p (g d) -> p g d", g=2)),
                             reads=[b_wbuf], writes=[b_Vc[wc]])
                        z = s_front(wb0 if wc == 0 else None, [], KTc[kz][:], [b_KTc[kz]])
                        return (z, slcV[:, vz, 0, :], slcV[:, vz, 1, :], [b_Vc[wc]])
                    cur = win_front_s(0)
                    for wc in range(5):
                        nxt = win_front_s(wc + 1) if wc < 4 else None
                        s_pv(cur[0], cur[1], cur[2], cur[3], wc == 0, wc == 4)
                        cur = nxt
                    S.op("dve", lambda e: e.reciprocal(out=rlS2[:, 1, :], in_=acc_ps[64:128, 512:528]), reads=[b_acc[1]], writes=[b_ocS])
                    S.op("dve", lambda e: e.tensor_tensor(out=osS[:, 1, :], in0=acc_ps[0:64, 512:528], in1=rlS2[:, 1, :], op=ALU.mult), reads=[b_acc[1], b_ocS], writes=[b_ocS])
                    Gv = GS[0:64, :, i].rearrange("p (h r) -> p h r", r=3)
                    S.op("dve", lambda e: e.tensor_tensor(out=oS[:], in0=ocS[:], in1=Gv[:, :, 0], op=ALU.mult), reads=[b_ocS, b_GS], writes=[b_oS])
                    for br in (1, 2):
                        S.op("dve", lambda e: e.tensor_tensor(out=tS[:], in0=osS[:, br - 1, :], in1=Gv[:, :, br], op=ALU.mult), reads=[b_ocS, b_GS, b_oS], writes=[b_oS])
                        S.op("dve", lambda e: e.tensor_tensor(out=oS[:], in0=oS[:], in1=tS[:], op=ALU.add), reads=[b_oS], writes=[b_oS])
                    S.op("dve", lambda e: e.tensor_copy(out=oT[0:64, :, col], in_=oS[:, 0:16:2]), reads=[b_oS], writes=[b_oT])
                    S.op("dve", lambda e: e.tensor_copy(out=oT[64:128, :, col], in_=oS[:, 1:16:2]), reads=[b_oS], writes=[b_oT])
            for oc in range(8):
                if oc % 4 == 0:
                    pnl, bpn = st.get()
                pt, bp = ps_next()
                for k in range(8):
                    S.op("pe", lambda e: e.matmul(pt[:], lhsT=pnl[:, k, (oc % 4) * 128:(oc % 4 + 1) * 128], rhs=oT[:, k, :], start=(k == 0), stop=(k == 7)),
                         reads=[bpn, b_oT], writes=[bp])
                S.op("dve", lambda e: e.tensor_tensor(out=x[:, oc, :], in0=x[:, oc, :], in1=pt[:], op=ALU.add), reads=[b_x[oc], bp], writes=[b_x[oc]])

        def final_out(ti, es2, smp=False):
            uid[0] += 1
            yf = es2.enter_context(nc.sbuf_tensor(f"s{uid[0]}_yf", [128, 8, 512], F32)); b_yf = Buf()
            rmsnorm_f32(12, yf, b_yf)
            if smp:
                uid[0] += 1
                yo = es2.enter_context(nc.sbuf_tensor(f"s{uid[0]}_yo", [128, 8, 4], F32)); b_yo = Buf()
                S.op("dve", lambda e: e.tensor_copy(out=yo[:], in_=yf[:, :, 3:19:4]), reads=[b_yf], writes=[b_yo])
                S.dma("sp", o_s_y, yo[:], reads=[b_yo])
                return
            for blk in range(4):
                for kh in range(2):
                    pt, bp = ps_next()
                    for k4 in range(4):
                        k = kh * 4 + k4
                        S.op("pe", lambda e: e.transpose(pt[:, k4 * 128:(k4 + 1) * 128], yf[:, k, blk * 128:(blk + 1) * 128], ident[:]),
                             reads=[b_yf, b_ident], writes=[bp])
                    S.op("act", lambda e: e.copy(out=xt[:, blk, kh * 512:(kh + 1) * 512], in_=pt[:]), reads=[bp], writes=[b_xt])
            r0 = (ti - OWN0) * 512
            S.dma("sp", o_y[r0:r0 + 512, :].rearrange("(b p) f -> p b f", p=128), xt[:], reads=[b_xt])

        def rmsnorm_f32(gi, yf, b_yf):
            for k in range(8):
                S.op("act", lambda e: e.activation(out=sq[:, k, :], in_=x[:, k, :], func=AF.Square), reads=[b_x[k]], writes=[b_sq])
            pt, bp = ps_next()
            for k in range(8):
                S.op("pe", lambda e: e.matmul(pt[:], lhsT=ones_bf[:], rhs=sq[:, k, :], start=(k == 0), stop=(k == 7)),
                     reads=[b_sq, b_ones], writes=[bp])
            S.op("act", lambda e: e.activation(out=rstd[:], in_=pt[:], func=AF.Sqrt, bias=epsb[:, 0:1]), reads=[bp, b_eps], writes=[b_rstd])
            S.op("dve", lambda e: e.reciprocal(out=rstd[:], in_=rstd[:]), reads=[b_rstd], writes=[b_rstd])
            for k in range(8):
                S.op("dve", lambda e: e.scalar_tensor_tensor(out=yf[:, k, :], in0=x[:, k, :], scalar=gvec[:, gi, k:k + 1], in1=rstd[:],
                                                              op0=ALU.mult, op1=ALU.mult),
                     reads=[b_x[k], b_gvec, b_rstd], writes=[b_yf])


        def compress_round(sb2, cT, b_cT, P0, kdst, vdst, w2vs, pre=None):
            if pre is None:
                hid0 = sb2("hid0", [128, 32], BF16); b_hid0 = Buf()
                hidp = sb2("hidp", [128, 128], BF16); b_hidp = Buf()
            else:
                hid0, b_hid0, hidp, b_hidp = pre
            off = P0 % 128
            stc = Stream([(W["w1_00"], 0), (W["w1_01"], 0), (W["w1_10"], 0), (W["w1_11"], 0)], depth=2)
            for jv in range(2):
                for g in range(2):
                    pnl, bpn = stc.get()
                    pt, bp = ps_next()
                    for l_ in range(32):
                        S.op("pe", lambda e: e.matmul(pt[:, 0:32], lhsT=pnl[:, l_, :], rhs=cT[:, jv, l_:l_ + 497:16], start=(l_ == 0), stop=(l_ == 31)),
                             reads=[bpn, b_cT], writes=[bp])
                    if jv == 0:
                        S.op("act", lambda e: e.activation(out=hid0[:], in_=pt[:, 0:32], func=AF.Gelu, bias=cb1[:, 0:1]), reads=[bp, b_cb1], writes=[b_hid0])
                        pt2, bp2 = ps_next()
                        S.op("pe", lambda e: e.matmul(pt2[:, 0:32], lhsT=w2k[:], rhs=hid0[:], start=True, stop=True), reads=[b_w2, b_hid0], writes=[bp2])
                        kdst(g, pt2, bp2)
                    else:
                        S.op("dve", lambda e: e.memset(hidp[:], 0.0), writes=[b_hidp])
                        S.op("act", lambda e: e.activation(out=hidp[:, off:off + 32], in_=pt[:, 0:32], func=AF.Gelu, bias=cb1[:, 1:2]), reads=[bp, b_cb1], writes=[b_hidp])
                        pt2, bp2 = ps_next()
                        S.op("pe", lambda e: e.matmul(pt2[:, 0:128], lhsT=hidp[:], rhs=w2vs[g], start=True, stop=True), reads=[b_w2, b_hidp], writes=[bp2])
                        vdst(g, pt2, bp2)

        kvsw = Wt(nc, "kvsw", None, D, 256, 256)
        kvsw.name = "kvsw"
        W["kvsw"] = kvsw

        def kvswjob():
            tks = []
            with nc.allow_non_contiguous_dma(reason="one-time rope column swap"):
                for c, cb in enumerate((256, 512)):
                    for k in range(8):
                        srcv = w_kv[k * 128:(k + 1) * 128, cb:cb + 128].rearrange("p (g hh dd) -> p g hh dd", g=2, hh=2)
                        dstv = kvsw.dst[0].rearrange("p (k c g hh dd) -> p k c g hh dd", k=8, c=2, g=2, hh=2)
                        for hh in range(2):
                            tks.append(S.dma("pool", dstv[:, k, c, :, hh, :], srcv[:, :, 1 - hh, :]))
            kvsw.buf.w = tks
        jobs["kvsw"] = kvswjob

        def kv_gen(ti, es2, smp=False):
            def sb2(name, shape, dt=F32):
                uid[0] += 1
                return es2.enter_context(nc.sbuf_tensor(f"s{uid[0]}_{name}", list(shape), dt))
            kvf = sb2("kvf", [128, 6, 512]); b_kvf = [Buf() for _ in range(6)]
            rc_t = sb2("rc_t", [128, 512]); rs_t = sb2("rs_t", [128, 512]); b_rope = Buf()
            rt1 = sb2("rt1", [128, 512]); b_rt1 = Buf()
            S.dma("sp", rc_t[:], ropec_s if smp else ropec[:, ti * 512:(ti + 1) * 512], writes=[b_rope])
            S.dma("sp", rs_t[:], ropes_s if smp else ropes[:, ti * 512:(ti + 1) * 512], writes=[b_rope])
            rmsnorm(13, h, b_h)
            st = Stream([(W["kv"], 0), (W["kv"], 1), (W["kv"], 2), (W["kvsw"], 0)], depth=3)
            pk = [st.get(), st.get(), st.get()]
            psw, bpsw = st.get()
            for c in range(6):
                pnl, bpn = pk[c // 2]
                pt, bp = proj_chunk(pnl, bpn, (c % 2) * 128, 128, h, b_h, 8)
                if c in (2, 4):
                    pt2, bp2 = proj_chunk(psw, bpsw, (c // 2 - 1) * 128, 128, h, b_h, 8)
                    S.op("dve", lambda e: e.tensor_tensor(out=kvf[:, c, :], in0=pt[:], in1=rc_t[:], op=ALU.mult), reads=[bp, b_rope], writes=[b_kvf[c]])
                    S.op("dve", lambda e: e.tensor_tensor(out=rt1[:], in0=pt2[:], in1=rs_t[:], op=ALU.mult), reads=[bp2, b_rope], writes=[b_rt1])
                    S.op("dve", lambda e: e.tensor_tensor(out=kvf[:, c, :], in0=kvf[:, c, :], in1=rt1[:], op=ALU.add), reads=[b_kvf[c], b_rt1], writes=[b_kvf[c]])
                else:
                    S.op("act", lambda e: e.copy(out=kvf[:, c, :], in_=pt[:]), reads=[bp], writes=[b_kvf[c]])
            if smp:
                kvo = sb2("kvo", [128, 6, 4]); b_kvo = Buf()
                S.op("dve", lambda e: e.tensor_copy(out=kvo[:], in_=kvf[:, :, 3:19:4]), reads=b_kvf, writes=[b_kvo])
                S.dma("sp", o_s_kv, kvo[:], reads=[b_kvo])
                S.op("dve", lambda e: e.tensor_copy(out=KTn[:, 0, :], in_=kvf[:, 2, 0:128]), reads=[b_kvf[2]], writes=[b_KTn])
                S.op("dve", lambda e: e.tensor_copy(out=KTn[:, 1, :], in_=kvf[:, 4, 0:128]), reads=[b_kvf[4]], writes=[b_KTn])
                for ci, c in enumerate((3, 5)):
                    pt, bp = ps_next()
                    S.op("pe", lambda e: e.transpose(pt[:, 0:128], kvf[:, c, 0:128], ident[:]), reads=[b_kvf[c], b_ident], writes=[bp])
                    S.op("dve", lambda e: e.tensor_copy(out=Vn[:, ci, :, 0:64], in_=pt[:, 0:128].rearrange("p (g d) -> p g d", g=2)), reads=[bp], writes=[b_Vn])
                return
            if ti >= OWN0 or DO_B:
                for blk in range(4):
                    for hf in range(2):
                        pt, bp = ps_next()
                        ncn = 4 if hf == 0 else 2
                        for c4 in range(ncn):
                            c = hf * 4 + c4
                            S.op("pe", lambda e: e.transpose(pt[:, c4 * 128:(c4 + 1) * 128], kvf[:, c, blk * 128:(blk + 1) * 128], ident[:]),
                                 reads=[b_kvf[c], b_ident], writes=[bp])
                        S.op("act", lambda e: e.copy(out=xt[:, blk, hf * 512:hf * 512 + ncn * 128], in_=pt[:, 0:ncn * 128]), reads=[bp], writes=[b_xt])
                        if DO_B and DBG >= 2 and not cfg.get("NO_V"):
                            kch = 4 * ti + blk
                            if hf == 0:
                                S.op("dve", lambda e: e.tensor_copy(out=slcV[:, kch, :, 0:64], in_=xt[:, blk, 384:512].rearrange("p (g d) -> p g d", g=2)),
                                     reads=[b_xt], writes=[b_slcV])
                            else:
                                S.op("dve", lambda e: e.tensor_copy(out=winV[:, kch % 8, :, 0:64], in_=xt[:, blk, 640:768].rearrange("p (g d) -> p g d", g=2)),
                                     reads=[b_xt], writes=[b_winV])
            if DO_B and DBG >= 2 and not cfg.get("NO_K"):
                c0 = ti * 512
                r0w = (ti % 2) * 512
                for g in range(2):
                    for hfp in range(2):
                        S.op("dve", lambda e: e.tensor_copy(out=slcK[hfp * 64:(hfp + 1) * 64, g, c0:c0 + 512], in_=kvf[g * 64:(g + 1) * 64, 2, :]),
                             reads=[b_kvf[2]], writes=[b_slcK])
                        S.op("dve", lambda e: e.tensor_copy(out=winK[hfp * 64:(hfp + 1) * 64, g, r0w:r0w + 512], in_=kvf[g * 64:(g + 1) * 64, 4, :]),
                             reads=[b_kvf[4]], writes=[b_winK])
            if DO_B and DBG >= 3:
                P0 = 32 * ti
                S.op("pool", lambda e: e.tensor_copy(out=cmpT[:, :, 0:16], in_=cmpT[:, :, 512:528]), reads=[b_cmpT], writes=[b_cmpT])
                for jv in range(2):
                    S.op("pool", lambda e: e.tensor_copy(out=cmpT[:, jv, 16:528], in_=kvf[:, jv, :]), reads=[b_kvf[jv], b_cmpT], writes=[b_cmpT])
                def kdst(g, pt2, bp2):
                    S.op("act", lambda e: e.copy(out=kcK[:, g, P0:P0 + 32], in_=pt2[:, 0:32]), reads=[bp2], writes=[b_kcK])

                def vdst(g, pt2, bp2):
                    pch = P0 // 128
                    S.op("dve", lambda e: e.tensor_tensor(out=vcV[:, pch, g, :], in0=vcV[:, pch, g, :], in1=pt2[:, 0:128], op=ALU.add),
                         reads=[bp2, b_vcV], writes=[b_vcV])
                    S.op("dve", lambda e: e.tensor_copy(out=vcVb[:, pch, g, :], in_=vcV[:, pch, g, :]), reads=[b_vcV], writes=[b_vcVb])
                compress_round(sb2, cmpT, b_cmpT, P0, kdst, vdst, [w2v[:], w2v[:]])
            if ti >= OWN0:
                r0 = (ti - OWN0) * 512
                for oi, od in enumerate((o_cmp, o_slc, o_win)):
                    S.dma("sp", od[r0:r0 + 512, :].rearrange("(b p) f -> p b f", p=128), xt[:, :, oi * 256:(oi + 1) * 256], reads=[b_xt])

        for l_ in range(NL_A):
            cast_order.extend([f"rgin{l_}", f"band{l_}_0", f"band{l_}_1", f"rgout{l_}", f"up{l_}", f"down{l_}", f"plein{l_}", f"pleg{l_}"])
        cast_order.extend(["kv", "kvsw"])
        if DO_B:
            cast_order.extend(["w1_00", "w1_01", "w1_10", "w1_11"])
            for j_ in range(2):
                cast_order.extend([f"q{j_}", f"qsw{j_}", f"qg{j_}", f"wo{j_}", f"up{2 + j_}", f"down{2 + j_}", f"plein{2 + j_}", f"pleg{2 + j_}"])
        for ti in range(NT):
            transpose_in(xs[ti * 512:(ti + 1) * 512, :], 1024, x, b_x)
            for l in range(NL_A):
                with ExitStack() as es2:
                    rg_layer(l, ti, es2)
                    S.barrier()
                with ExitStack() as es2:
                    ffn_ple(l, ti, es2)
                    S.barrier()
            with ExitStack() as es2:
                kv_gen(ti, es2)
                S.barrier()
            if DO_B and ti >= OWN0 - 1:
                for l in (2, 3):
                    if DBG >= 4:
                        with ExitStack() as es2:
                            attn_layer(l, ti, es2, [3] if ti == OWN0 - 1 else [0, 1, 2, 3])
                            S.barrier()
                    with ExitStack() as es2:
                        ffn_ple(l, ti, es2)
                        S.barrier()
            if ti >= OWN0:
                with ExitStack() as es2:
                    final_out(ti, es2)
                    S.barrier()

        with nc.allow_non_contiguous_dma(reason="small state outputs"):
            for l in range(2):
                for k in range(3):
                    S.dma("sp", o_rgc[l, k].rearrange("(c p) -> p c", p=128), rg_hist[:, l, :, k], reads=b_rghist[l])
                S.dma("sp", o_rgh[l].rearrange("(c p) -> p c", p=128), rg_hc[:, l, :], reads=b_rghc[l])
            for l in range(4):
                for k in range(2):
                    S.dma("sp", o_ffc[l, k].rearrange("(c p) -> p c", p=128), ff_hist[:, l, :, k], reads=b_ffhist[l])

        if SMP:
            S.barrier()
            with ExitStack() as es3:
                def sb3(name, shape, dt=F32):
                    uid[0] += 1
                    return es3.enter_context(nc.sbuf_tensor(f"s{uid[0]}_{name}", list(shape), dt))
                assert NTOK >= 2048
                flatK = slcK[:].rearrange("p g t -> p (g t)")
                flatV = slcV[:].rearrange("p c g f -> p (c g f)")
                kcS = flatK[:, 0:2048].rearrange("p (b t) -> p b t", b=4)
                vcS = flatK[:, 2048:4096].rearrange("p (b c f) -> p b c f", b=4, c=4)
                b_kcS = Buf(); b_vcS = Buf(); b_KTn = Buf()
                Vn = slcV[:, 0:2, :, :]; b_Vn = Buf()
                b_Vc = [Buf() for _ in range(4)]
                vo = [1536]

                def vview(n):
                    a = vo[0]
                    vo[0] += n
                    return flatV[:, a:a + n]
                mcsS = vview(1024).rearrange("p (a b c) -> p a b c", a=4, b=2)
                b64 = vview(512).rearrange("p (a c) -> p a c", a=4)
                KTn = vview(256).rearrange("p (a t) -> p a t", a=2)
                cb0 = vview(128); wb0 = vview(128); w2v1 = vview(128); brhs = vview(16)
                xtf = xt[:].rearrange("p b f -> p (b f)")
                tas = xtf[:, 0:512].rearrange("p (a s) -> p a s", a=2)
                b_stab = Buf()
                S.op("pool", lambda e: e.memset(w2v1, 0.0), writes=[b_stab])
                S.dma("pool", w2v1[:, 64:128], cmp_w2[1], writes=[b_stab])
                S.dma("pool", cb0, t_cb0, writes=[b_stab]); S.dma("pool", brhs, t_brhs, writes=[b_stab])
                S.dma("pool", mcsS, t_mcs_s, writes=[b_stab]); S.dma("sp", tas, t_as, writes=[b_stab])
                S.dma("pool", b64, t_b64, writes=[b_stab]); S.dma("pool", wb0, t_wb0, writes=[b_stab])
                pgi = xtf[:, 512:768].bitcast(mybir.dt.int32); pgf = xtf[:, 768:1024]; pidf = sb3("pidf", [128, 1])
                idxS_t = xtf[:, 1024:1280].bitcast(mybir.dt.int32); b_idxS = Buf()
                idxS = idxS_t.rearrange("p (b g) -> p b g", b=4)
                S.dma("sp", pgi, pg_s.rearrange("b g -> (b g)").unsqueeze(0).broadcast_to([128, 256]), writes=[b_idxS])
                S.op("pool", lambda e: e.iota(pidf[:], pattern=[[0, 1]], base=0, channel_multiplier=1, allow_small_or_imprecise_dtypes=True), writes=[b_idxS])
                S.op("dve", lambda e: e.tensor_copy(out=pgf, in_=pgi), reads=[b_idxS], writes=[b_idxS])
                S.op("dve", lambda e: e.tensor_scalar(out=pgf, in0=pgf, scalar1=128.0, scalar2=pidf[:, 0:1], op0=ALU.mult, op1=ALU.add), reads=[b_idxS], writes=[b_idxS])
                S.op("dve", lambda e: e.tensor_copy(out=idxS_t, in_=pgf), reads=[b_idxS], writes=[b_idxS])
                cT = cmpT; b_cT = b_cmpT
                hid_pre = (sb3("hid0s", [128, 32], BF16), Buf(), sb3("hidps", [128, 128], BF16), Buf())
                vcSf = xtf[:, 1280:1792].rearrange("p (c f) -> p c f", c=4); b_vcSf = Buf()
                gbc = [xtf[:, 1792 + 256 * i:2048 + 256 * i] for i in range(4)]; b_gbc = [Buf() for _ in range(4)]
                for i in range(4):
                    S.op("pool", lambda e: e.memset(cT[:], 0.0), writes=[b_cT])
                    S.op("pool", lambda e: e.memset(vcSf, 0.0), writes=[b_vcSf])
                    for r in range(16):
                        for jq in range(4):
                            S._deps("pool", [b_idxS], [b_gbc[jq]])
                            tk = S.dma_ind(gbc[jq], c_cmp, idxS[:, i, 4 * r + jq:4 * r + jq + 1])
                            S._mark(tk, [b_idxS], [b_gbc[jq]])
                            tq, btq = ps_next()
                            for jv in range(2):
                                S.op("pe", lambda e: e.transpose(tq[:, jv * 128:(jv + 1) * 128], gbc[jq][:, jv * 128:(jv + 1) * 128], ident[:]),
                                     reads=[b_gbc[jq], b_ident], writes=[btq])
                            S.op("act", lambda e: e.copy(out=cT[:, :, 16 + 128 * jq:16 + 128 * (jq + 1)], in_=tq[:, 0:256].rearrange("p (j t) -> p j t", j=2)),
                                 reads=[btq], writes=[b_cT])
                        P0 = 32 * r

                        def kdst(g, pt2, bp2):
                            S.op("act", lambda e: e.copy(out=kcS[64 * g:64 * g + 64, i, P0:P0 + 32], in_=pt2[64 * g:64 * g + 64, 0:32]), reads=[bp2], writes=[b_kcS])

                        def vdst(g, pt2, bp2):
                            pch = P0 // 128
                            S.op("dve", lambda e: e.tensor_tensor(out=vcSf[:, pch, :], in0=vcSf[:, pch, :], in1=pt2[:, 0:128], op=ALU.add),
                                 reads=[bp2, b_vcSf], writes=[b_vcSf])
                        compress_round(None, cT, b_cT, P0, kdst, vdst, [w2v[:], w2v1], pre=hid_pre)
                        S.op("pool", lambda e: e.tensor_copy(out=cT[:, :, 0:16], in_=cT[:, :, 512:528]), reads=[b_cT], writes=[b_cT])
                    S.op("dve", lambda e: e.tensor_copy(out=vcS[:, i, :, :], in_=vcSf), reads=[b_vcSf], writes=[b_vcS])
                S.dma("sp", x[:], xs_s, writes=b_x)
                for l in range(2):
                    with ExitStack() as es2:
                        rg_layer(l, -1, es2, smp=True)
                        S.barrier()
                    with ExitStack() as es2:
                        ffn_ple(l, -1, es2, smp=True)
                        S.barrier()
                with ExitStack() as es2:
                    kv_gen(-1, es2, smp=True)
                    S.barrier()
                for l in (2, 3):
                    with ExitStack() as es2:
                        attn_layer(l, -1, es2, [], smp=True)
                        S.barrier()
                    with ExitStack() as es2:
                        ffn_ple(l, -1, es2, smp=True)
                        S.barrier()
                with ExitStack() as es2:
                    final_out(-1, es2, smp=True)
                    S.barrier()
                with nc.allow_non_contiguous_dma(reason="state passthrough"):
                    for i in range(4):
                        S.dma("sp", o_s_win[i], c_win[i, 1:512, :])
                    S.dma("sp", o_s_rgc_old, st_rgc[:, :, 1:3, :])
                    S.dma("sp", o_s_ffc_old, st_ffc[:, :, 1, :])
        S.finish()
    return nc

_WNAMES = ['g_mix', 'g_ffn', 'g_ple', 'g_final', 'g_kv', 'rg_w_in', 'rg_conv_w', 'rg_conv_b', 'rg_w_a', 'rg_w_x', 'rg_lambda',
           'rg_w_out', 'w_kv', 'ffn_w_up', 'ffn_conv_w', 'ffn_conv_b', 'ffn_w_down', 'ple_w_in', 'ple_w_gate',
           'attn_w_qg', 'attn_w_o', 'cmp_pos', 'cmp_w1', 'cmp_b1', 'cmp_w2']


def make_tables(half, NT, OWN0):
    BIG = 30000.0
    NTOK = NT * 512
    pre = OWN0 * 512
    nown = NTOK - pre + 128
    nqb = nown // 128
    i = np.arange(nown) - 128
    tau = pre + i
    hv = np.where(i < 0, 1, half)[:, None]
    P = np.arange(256)
    c = P - 1
    vc = (P >= 1)[None, :] & ((hv == 1) | (16 * c >= pre)[None, :])
    ok = vc & ((16 * c + 31)[None, :] <= tau[:, None])
    t_cb = np.where(ok, 0.0, -BIG).astype(np.float32).reshape(nqb, 128, 256)
    kk = np.arange(640)
    tau0 = pre + 128 * (i // 128)
    kap = tau0[:, None] - 512 + kk[None, :]
    dist = tau[:, None] - kap
    okw = (dist >= 0) & (dist < 512) & (kap >= 0) & ((hv == 1) | (kap >= pre))
    t_wb = np.where(okw, 0.0, -BIG).astype(np.float32).reshape(nqb, 128, 640)
    sblk = np.arange(64)
    vb = ((hv == 1) | (64 * sblk >= pre)[None, :]) & (64 * sblk < NTOK)[None, :]
    V = (vb & ((64 * sblk)[None, :] <= tau[:, None])).astype(np.float32)
    blk0 = np.where(hv[:, 0] == 1, 0, pre // 64)
    cur = tau // 64
    forced = (sblk[None, :] == blk0[:, None]) | (sblk[None, :] == cur[:, None]) | (sblk[None, :] == (cur - 1)[:, None])
    A = 100.0 * forced.astype(np.float32) * V + (V - 1.0)
    t_va = np.stack([V, A], axis=1).astype(np.float32).reshape(nqb, 128, 2, 64)
    Pm = np.arange(256)
    c0 = 16 * (Pm - 1)
    s0 = 64 * sblk
    m = ((c0[:, None] < s0[None, :] + 64) & (c0[:, None] + 32 > s0[None, :]) & (Pm[:, None] >= 1)).astype(np.float32)
    t_mcs = np.ascontiguousarray(m.reshape(2, 128, 64).transpose(1, 0, 2))
    t_tri = np.where(np.arange(128)[None, :] > np.arange(128)[:, None], -BIG, 0.0).astype(np.float32)
    return dict(t_cb=t_cb, t_wb=t_wb, t_va=t_va, t_mcs=t_mcs, t_tri=t_tri)


def core_inputs(inp, b, half, NT=8, OWN0=4):
    NTOK = NT * 512
    pre = OWN0 * 512
    own = NTOK - pre
    if half == 1:
        xs = inp['x_prompt'][b, :NTOK]
        ps = inp['p_prompt'][:, b, :NTOK]
        pos = np.arange(NTOK)
    else:
        xs = np.concatenate([inp['x_prompt'][b, :pre], inp['x_prompt'][b, :own]], 0)
        ps = np.concatenate([inp['p_prompt'][:, b, :pre], inp['p_prompt'][:, b, :own]], 1)
        pos = np.concatenate([np.arange(pre), np.arange(own)])
    d = np.arange(128) % 64
    inv = (10000.0 ** (-(d % 32).astype(np.float32) / 32)).astype(np.float32)
    ang = pos[None, :].astype(np.float32) * inv[:, None]
    sgn = np.where(d < 32, -1.0, 1.0)[:, None]
    m = dict(xs=np.ascontiguousarray(xs, dtype=np.float32), ps=np.ascontiguousarray(ps, dtype=np.float32),
             flag=np.tile(np.array([[half, 1 - half]], np.float32), (128, 1)),
             ident=np.eye(128, dtype=np.float32),
             ropec=np.cos(ang).astype(np.float32), ropes=(np.sin(ang) * sgn).astype(np.float32))
    for k in _WNAMES:
        m[k] = np.ascontiguousarray(inp[k], dtype=np.float32)
    m['rg_b_a'] = np.ascontiguousarray(inp['rg_b_a'], dtype=np.float32).reshape(2, 1280)
    m['rg_b_x'] = np.ascontiguousarray(inp['rg_b_x'], dtype=np.float32).reshape(2, 1280)
    m.update(make_tables(half, NT, OWN0))
    return m


def sample_inputs(inp, c):
    BIG = 30000.0
    b0 = 4 * c
    m = {}
    xs = np.zeros((128, 8, 512), np.float32)
    ps = np.zeros((4, 128, 2, 512), np.float32)
    for i in range(4):
        xs[:, :, 4 * i + 3] = inp['x_sample'][b0 + i, 0].reshape(8, 128).T
        for l in range(4):
            ps[l, :, :, 4 * i + 3] = inp['p_sample'][l, b0 + i, 0].reshape(2, 128).T
    m['xs_s'] = xs
    m['ps_s'] = ps
    rgc = inp['state_rg_conv'][:, b0:b0 + 4]
    m['s_rgh'] = np.ascontiguousarray(rgc.reshape(2, 4, 3, 10, 128).transpose(4, 0, 3, 1, 2), dtype=np.float32)
    rgh = inp['state_rg_h'][:, b0:b0 + 4]
    m['s_h0'] = np.ascontiguousarray(rgh.reshape(2, 4, 10, 128).transpose(3, 0, 2, 1), dtype=np.float32)
    ffc = inp['state_ffn_conv'][:, b0:b0 + 4]
    m['s_ffh'] = np.ascontiguousarray(ffc.reshape(4, 4, 2, 48, 128).transpose(4, 0, 3, 1, 2), dtype=np.float32)
    d = np.arange(128) % 64
    inv = (10000.0 ** (-(d % 32).astype(np.float32) / 32)).astype(np.float32)
    ang = np.full((1, 512), 8192.0, np.float32) * inv[:, None]
    sgn = np.where(d < 32, -1.0, 1.0)[:, None]
    m['ropec_s'] = np.cos(ang).astype(np.float32)
    m['ropes_s'] = (np.sin(ang) * sgn).astype(np.float32)
    m['pg_s'] = np.ascontiguousarray(inp['page_table'][b0:b0 + 4], dtype=np.int32)
    nphys = inp['cache_cmp_kv'].shape[0]
    m['cache_cmp_kv'] = np.ascontiguousarray(inp['cache_cmp_kv'], dtype=np.float32).reshape(nphys * 128, 256)
    m['cache_slc_kv'] = np.ascontiguousarray(inp['cache_slc_kv'], dtype=np.float32).reshape(nphys * 128, 256)
    m['c_win'] = np.ascontiguousarray(inp['cache_win_kv'][b0:b0 + 4], dtype=np.float32).reshape(4, 512, 256)
    m['st_rgc'] = np.ascontiguousarray(rgc, dtype=np.float32)
    m['st_ffc'] = np.ascontiguousarray(ffc, dtype=np.float32)
    P = np.arange(512)
    c0 = 16 * (P - 1)
    sb = np.arange(256)
    mm = ((c0[:, None] < 64 * sb[None, :] + 64) & (c0[:, None] + 32 > 64 * sb[None, :]) & (P[:, None] >= 1) & (sb[None, :] < 129)).astype(np.float32)
    m['t_mcs_s'] = np.ascontiguousarray(mm.reshape(4, 128, 2, 128).transpose(1, 0, 2, 3))
    V = (sb < 129).astype(np.float32)
    forced = ((sb == 0) | (sb == 128) | (sb == 127)).astype(np.float32)
    A = 100.0 * forced * V + (V - 1.0)
    m['t_as'] = np.ascontiguousarray(np.tile(np.stack([V, A])[None], (128, 1, 1)), dtype=np.float32)
    br = np.zeros((128, 16), np.float32); br[0, 0:8] = 1.0; br[1, 8:16] = 1.0
    m['t_brhs'] = br
    b64 = np.zeros((128, 4, 128), np.float32)
    for i in range(4):
        b64[0:2, i, :] = -BIG
        b64[0:2, i, 4 * i + 3] = 0.0
    m['t_b64'] = b64
    wb0 = np.zeros((128, 128), np.float32); wb0[0:2, 0] = -BIG
    m['t_wb0'] = wb0
    m['t_cb0'] = wb0.copy()
    return m


_BUILD_CACHE = {}


def kernel(**inputs):
    inp = {k: np.asarray(v) for k, v in inputs.items()}
    nphys = inp['cache_cmp_kv'].shape[0]
    nc = build(dict(NT=8, OWN0=4, NL_A=2, DO_B=True, SMP=True, NPHYS=nphys))
    in_maps = []
    for c in range(8):
        m = core_inputs(inp, c // 2, c % 2)
        m.update(sample_inputs(inp, c))
        in_maps.append(m)
    res = run_bass_kernel_spmd(nc, in_maps, core_ids=list(range(8)))
    R = res.results
    B, T = 4, 4096
    y_prompt = np.zeros((B, T, D), np.float32)
    cmp_p = np.zeros((B, T, 2, 2, 64), np.float32)
    slc_p = np.zeros((B, T, 2, 2, 64), np.float32)
    win_p = np.zeros((B, 512, 2, 2, 64), np.float32)
    rgc_p = np.zeros((2, B, 3, DR), np.float32)
    rgh_p = np.zeros((2, B, DR), np.float32)
    ffc_p = np.zeros((4, B, 2, 2 * DFF), np.float32)
    DB = 32
    y_sample = np.zeros((DB, 1, D), np.float32)
    cmp_s = np.zeros((DB, 1, 2, 2, 64), np.float32)
    slc_s = np.zeros((DB, 1, 2, 2, 64), np.float32)
    win_s = np.zeros((DB, 512, 2, 2, 64), np.float32)
    rgc_s = np.zeros((2, DB, 3, DR), np.float32)
    rgh_s = np.zeros((2, DB, DR), np.float32)
    ffc_s = np.zeros((4, DB, 2, 2 * DFF), np.float32)
    for c in range(8):
        b, half = c // 2, c % 2
        sl = slice(half * 2048, half * 2048 + 2048)
        y_prompt[b, sl] = R[c]['o_y']
        cmp_p[b, sl] = R[c]['o_cmp'].reshape(2048, 2, 2, 64)
        slc_p[b, sl] = R[c]['o_slc'].reshape(2048, 2, 2, 64)
        if half == 1:
            win_p[b] = R[c]['o_win'][-512:].reshape(512, 2, 2, 64)
            rgc_p[:, b] = R[c]['o_rgc']
            rgh_p[:, b] = R[c]['o_rgh']
            ffc_p[:, b] = R[c]['o_ffc']
        assemble_sample(R[c], c, y_sample, cmp_s, slc_s, win_s, rgc_s, rgh_s, ffc_s)
    return (y_prompt, y_sample, cmp_p, cmp_s, slc_p, slc_s, win_p, win_s, rgc_p, rgc_s, rgh_p, rgh_s, ffc_p, ffc_s)


def assemble_sample(r, c, y_sample, cmp_s, slc_s, win_s, rgc_s, rgh_s, ffc_s):
    b0 = 4 * c
    for i in range(4):
        b = b0 + i
        y_sample[b, 0] = r['o_s_y'][:, :, i].T.reshape(1024)
        kv = r['o_s_kv'][:, :, i]
        cmp_s[b, 0] = kv[:, 0:2].T.reshape(2, 2, 64)
        slc_s[b, 0] = kv[:, 2:4].T.reshape(2, 2, 64)
        win_s[b, 0:511] = r['o_s_win'][i].reshape(511, 2, 2, 64)
        win_s[b, 511] = kv[:, 4:6].T.reshape(2, 2, 64)
        for l in range(2):
            rgc_s[l, b, 0:2] = r['o_s_rgc_old'][l, i]
            rgc_s[l, b, 2] = r['o_s_rgc_new'][:, l, :, i].T.reshape(1280)
            rgh_s[l, b] = r['o_s_rgh'][:, l, :, i].T.reshape(1280)
        for l in range(4):
            ffc_s[l, b, 0] = r['o_s_ffc_old'][l, i]
            ffc_s[l, b, 1] = r['o_s_ffc_new'][:, l, :, i].T.reshape(6144)
```

```python
import numpy as np
import ml_dtypes
from contextlib import ExitStack
import concourse.bass as bass
import concourse.mybir as mybir
from concourse.bass_utils import run_bass_kernel_spmd

F32 = mybir.dt.float32
BF16 = mybir.dt.bfloat16
AF = mybir.ActivationFunctionType
ALU = mybir.AluOpType

D = 1024
NT_TOK = 512
DR = 1280
DFF = 3072
EPS = 1e-6


class Buf:
    __slots__ = ("w", "r")

    def __init__(self):
        self.w = None
        self.r = []


class Sched:
    ND = 12

    def __init__(self, nc, es):
        self.nc = nc
        self.E = dict(pe=nc.tensor, act=nc.scalar, dve=nc.vector, pool=nc.gpsimd, sp=nc.sync)
        self.sem = {}
        self.cnt = {}
        for e in ("pe", "act", "dve", "pool"):
            self.sem[e] = es.enter_context(nc.semaphore("s_" + e))
            self.cnt[e] = 0
        self.seen = {e: {} for e in self.E}
        self.dsem = {}
        self.dval = {}
        self.didx = {}
        self.NDQ = {"sp": 12, "pool": 40, "act": 4}
        for q in ("sp", "pool", "act"):
            self.dsem[q] = [es.enter_context(nc.semaphore(f"d_{q}{i}")) for i in range(self.NDQ[q])]
            self.dval[q] = [0] * self.NDQ[q]
            self.didx[q] = 0
        self.all_dma = []

    def _wait(self, e, tk):
        key, sem, val = tk
        if key == e and e == "pe":
            return
        if self.seen[e].get(key, 0) >= val:
            return
        self.E[e].wait_ge(sem, val)
        self.seen[e][key] = val

    def _deps(self, e, reads, writes):
        for b in reads:
            if b.w is not None:
                for tk in (b.w if isinstance(b.w, list) else [b.w]):
                    self._wait(e, tk)
        for b in writes:
            if b.w is not None:
                for tk in (b.w if isinstance(b.w, list) else [b.w]):
                    self._wait(e, tk)
            for tk in b.r:
                self._wait(e, tk)

    def _mark(self, tk, reads, writes):
        for b in reads:
            b.r.append(tk)
            if len(b.r) > 24:
                b.r = b.r[-24:]
        for b in writes:
            b.w = tk
            b.r = []

    def op(self, e, fn, reads=(), writes=()):
        self._deps(e, reads, writes)
        inst = fn(self.E[e])
        inst.then_inc(self.sem[e], 1)
        self.cnt[e] += 1
        tk = (e, self.sem[e], self.cnt[e])
        self._mark(tk, reads, writes)
        return tk

    def dma(self, q, out, in_, reads=(), writes=(), **kw):
        slot = self.didx[q] % self.NDQ[q]
        self.didx[q] += 1
        sem = self.dsem[q][slot]
        key = (q, slot)
        if self.dval[q][slot] > 0:
            self._wait(q, (key, sem, self.dval[q][slot]))
        self._deps(q, reads, writes)
        inst = self.E[q].dma_start(out=out, in_=in_, **kw)
        inst.then_inc(sem, 16)
        self.dval[q][slot] += 16
        tk = (key, sem, self.dval[q][slot])
        self._mark(tk, reads, writes)
        self.all_dma.append(tk)
        return tk

    def dma_ind(self, out, in_, idx_ap):
        q = "pool"
        slot = self.didx[q] % self.NDQ[q]
        self.didx[q] += 1
        sem = self.dsem[q][slot]
        key = (q, slot)
        if self.dval[q][slot] > 0:
            self._wait(q, (key, sem, self.dval[q][slot]))
        inst = self.nc.gpsimd.indirect_dma_start(out=out, out_offset=None, in_=in_, in_offset=bass.IndirectOffsetOnAxis(ap=idx_ap, axis=0))
        inst.then_inc(sem, 16)
        self.dval[q][slot] += 16
        return (key, sem, self.dval[q][slot])

    def barrier(self):
        tks = [(e, self.sem[e], self.cnt[e]) for e in self.cnt if self.cnt[e] > 0]
        for q in self.dsem:
            for slot in range(self.NDQ[q]):
                if self.dval[q][slot] > 0:
                    tks.append(((q, slot), self.dsem[q][slot], self.dval[q][slot]))
        for e in self.E:
            for tk in tks:
                if tk[0] == e and e == "pe":
                    continue
                self._wait(e, tk)

    def finish(self):
        for q in self.dsem:
            for slot in range(self.NDQ[q]):
                if self.dval[q][slot] > 0:
                    self._wait("sp", ((q, slot), self.dsem[q][slot], self.dval[q][slot]))


class Wt:
    def __init__(self, nc, name, src, K, M, MW):
        self.KC = K // 128
        self.MW = MW
        self.NP = M // MW
        assert self.KC * MW <= 4096
        self.src = src
        self.dst = nc.dram_tensor("wb_" + name, [self.NP, 128, self.KC * MW], BF16, kind="Internal").ap()
        self.buf = Buf()


def build(cfg):
    NT = cfg.get("NT", 8)
    OWN0 = cfg.get("OWN0", 4)
    NL_A = cfg.get("NL_A", 2)
    DO_B = cfg.get("DO_B", False)
    DBG = cfg.get("DBG", 99)
    NTOK = NT * NT_TOK
    NOWN = (NT - OWN0) * NT_TOK

    nc = bass.Bass("TRN2", target_bir_lowering=False)

    def din(name, shape, dt=F32):
        return nc.dram_tensor(name, list(shape), dt, kind="ExternalInput").ap()

    def dout(name, shape, dt=F32):
        return nc.dram_tensor(name, list(shape), dt, kind="ExternalOutput").ap()

    xs = din("xs", [NTOK, D])
    ps_in = din("ps", [4, NTOK, 256])
    flag = din("flag", [128, 2])
    ident_d = din("ident", [128, 128])
    ropec = din("ropec", [128, NTOK])
    ropes = din("ropes", [128, NTOK])
    g_mix = din("g_mix", [4, D]); g_ffn = din("g_ffn", [4, D]); g_ple = din("g_ple", [4, D])
    g_final = din("g_final", [D]); g_kv = din("g_kv", [D])
    rg_w_in = din("rg_w_in", [2, D, 2 * DR]); rg_conv_w = din("rg_conv_w", [2, 4, DR]); rg_conv_b = din("rg_conv_b", [2, DR])
    rg_w_a = din("rg_w_a", [2, 16, 80, 80]); rg_b_a = din("rg_b_a", [2, DR])
    rg_w_x = din("rg_w_x", [2, 16, 80, 80]); rg_b_x = din("rg_b_x", [2, DR])
    rg_lambda = din("rg_lambda", [2, DR]); rg_w_out = din("rg_w_out", [2, DR, D])
    w_kv = din("w_kv", [D, 768])
    ffn_w_up = din("ffn_w_up", [4, D, 2 * DFF]); ffn_conv_w = din("ffn_conv_w", [4, 3, 2 * DFF]); ffn_conv_b = din("ffn_conv_b", [4, 2 * DFF])
    ffn_w_down = din("ffn_w_down", [4, DFF, D])
    ple_w_in = din("ple_w_in", [4, 256, D]); ple_w_gate = din("ple_w_gate", [4, D, D])

    attn_w_qg = din("attn_w_qg", [2, D, 1072]); attn_w_o = din("attn_w_o", [2, D, D])
    cmp_pos = din("cmp_pos", [32, 2, 64]); cmp_w1 = din("cmp_w1", [2, 2048, 128]); cmp_b1 = din("cmp_b1", [2, 128]); cmp_w2 = din("cmp_w2", [2, 128, 64])
    NQB = NOWN // 128 + 1
    t_cb = din("t_cb", [NQB, 128, 256]); t_wb = din("t_wb", [NQB, 128, 640]); t_va = din("t_va", [NQB, 128, 2, 64])
    t_mcs = din("t_mcs", [128, 2, 64]); t_tri = din("t_tri", [128, 128])

    SMP = cfg.get("SMP", False)
    if SMP:
        xs_s = din("xs_s", [128, 8, 512]); ps_s = din("ps_s", [4, 128, 2, 512])
        s_rgh_d = din("s_rgh", [128, 2, 10, 4, 3]); s_h0_d = din("s_h0", [128, 2, 10, 4]); s_ffh_d = din("s_ffh", [128, 4, 48, 4, 2])
        ropec_s = din("ropec_s", [128, 512]); ropes_s = din("ropes_s", [128, 512])
        pg_s = din("pg_s", [4, 64], mybir.dt.int32)
        NPHYS = cfg.get("NPHYS", 2560)
        c_cmp = din("cache_cmp_kv", [NPHYS * 128, 256]); c_slc = din("cache_slc_kv", [NPHYS * 128, 256]); c_win = din("c_win", [4, 512, 256])
        st_rgc = din("st_rgc", [2, 4, 3, DR]); st_ffc = din("st_ffc", [4, 4, 2, 2 * DFF])
        t_mcs_s = din("t_mcs_s", [128, 4, 2, 128]); t_as = din("t_as", [128, 2, 256]); t_brhs = din("t_brhs", [128, 16])
        t_b64 = din("t_b64", [128, 4, 128]); t_wb0 = din("t_wb0", [128, 128]); t_cb0 = din("t_cb0", [128, 128])
        o_s_y = dout("o_s_y", [128, 8, 4]); o_s_kv = dout("o_s_kv", [128, 6, 4]); o_s_win = dout("o_s_win", [4, 511, 256])
        o_s_rgc_new = dout("o_s_rgc_new", [128, 2, 10, 4]); o_s_rgc_old = dout("o_s_rgc_old", [2, 4, 2, DR]); o_s_rgh = dout("o_s_rgh", [128, 2, 10, 4])
        o_s_ffc_new = dout("o_s_ffc_new", [128, 4, 48, 4]); o_s_ffc_old = dout("o_s_ffc_old", [4, 4, 2 * DFF])

    o_y = dout("o_y", [NOWN, D])
    o_cmp = dout("o_cmp", [NOWN, 256]); o_slc = dout("o_slc", [NOWN, 256]); o_win = dout("o_win", [NOWN, 256])
    o_rgc = dout("o_rgc", [2, 3, DR]); o_rgh = dout("o_rgh", [2, DR]); o_ffc = dout("o_ffc", [4, 2, 2 * DFF])

    zpad = nc.dram_tensor("zpad", [9, 128, 4096], BF16, kind="Internal").ap()
    zband = nc.dram_tensor("zband", [2, 2, 128, 10 * 384], BF16, kind="Internal").ap()

    es = ExitStack()
    with es:
        S = Sched(nc, es)

        uid = [0]

        def sb(name, shape, dt=F32):
            uid[0] += 1
            return es.enter_context(nc.sbuf_tensor(f"s{uid[0]}_{name}", list(shape), dt))

        ident = sb("ident", [128, 128]); b_ident = Buf()
        ones_bf = sb("ones_bf", [128, 128], BF16); b_ones = Buf()
        flag_sb = sb("flag_sb", [128, 2]); b_flag = Buf()
        gvec = sb("gvec", [128, 14, 8]); b_gvec = Buf()
        rgcw = sb("rgcw", [128, 2, 4, 10]); rgcb = sb("rgcb", [128, 2, 10]); b_rgc = Buf()
        rgba = sb("rgba", [128, 2, 10]); rgbx = sb("rgbx", [128, 2, 10]); rgc8 = sb("rgc8", [128, 2, 10]); b_rgp = Buf()
        ffcw = sb("ffcw", [128, 4, 3, 48]); ffcb = sb("ffcb", [128, 4, 48]); b_ffc = Buf()
        rg_hist = sb("rg_hist", [128, 2, 10, 3]); b_rghist = [[Buf() for _ in range(10)] for _ in range(2)]
        rg_hc = sb("rg_hc", [128, 2, 10]); b_rghc = [[Buf() for _ in range(10)] for _ in range(2)]
        ff_hist = sb("ff_hist", [128, 4, 48, 2]); b_ffhist = [[Buf() for _ in range(48)] for _ in range(4)]
        epsb = sb("epsb", [128, 1]); b_eps = Buf()

        x = sb("x", [128, 8, 512]); b_x = [Buf() for _ in range(8)]
        xt = sb("xt", [128, 4, 1024]); b_xt = Buf()
        h = sb("h", [128, 8, 512], BF16); b_h = Buf()
        sq = sb("sq", [128, 8, 512], BF16); b_sq = Buf()
        rstd = sb("rstd", [128, 512]); b_rstd = Buf()
        NPAN = 4
        pan = [sb(f"pan{i}", [128, 4096], BF16) for i in range(NPAN)]
        b_pan = [Buf() for _ in range(NPAN)]
        psum2 = [es.enter_context(nc.psum_tensor(f"psum{i}", [128, 1024], F32)) for i in range(4)]
        psum = [psum2[i // 2][:, (i % 2) * 512:(i % 2 + 1) * 512] for i in range(8)]
        b_ps = [Buf() for _ in range(8)]
        ps_i = [0]

        def ps_next():
            i = ps_i[0] % 6
            ps_i[0] += 1
            return psum[i], b_ps[i]

        def ps_pair():
            if ps_i[0] % 2 == 1:
                ps_i[0] += 1
            i = ps_i[0] % 6
            ps_i[0] += 2
            return psum2[i // 2], [b_ps[i], b_ps[i + 1]]

        acc_ps = psum2[3]
        b_acc = [b_ps[6], b_ps[7]]

        W = {}
        band = {}

        jobs = {}
        cast_order = []

        def ensure(name):
            if name in jobs:
                jobs.pop(name)()

        def prefetch_casts(k):
            n = 0
            for nm in cast_order:
                if n >= k:
                    break
                if nm in jobs:
                    ensure(nm)
                    n += 1

        def mkw(name, src, K, M, MW):
            w = Wt(nc, name, src, K, M, MW)
            w.name = name
            W[name] = w

            def job():
                tks = []
                for pnl in range(w.NP):
                    srcv = src[:, pnl * MW:(pnl + 1) * MW].rearrange("(k p) m -> p k m", p=128)
                    dstv = w.dst[pnl].rearrange("p (k m) -> p k m", k=w.KC)
                    tks.append(S.dma("pool", dstv, srcv))
                w.buf.w = tks
            jobs[name] = job
            return w

        zt_full = sq[:].rearrange("p k t -> p (k t)")
        zt = zt_full[:, 0:3840]; b_zt = b_sq
        S.op("dve", lambda e: e.memset(sq[:], 0.0), writes=[b_zt])
        for l in range(NL_A):
            for gi, wsrc in enumerate((rg_w_a, rg_w_x)):
                bw = Buf()
                band[(l, gi)] = bw
                tz = S.dma("sp", zband[l, gi], zt, reads=[b_zt])
                wv = Wt.__new__(Wt)
                wv.KC = 30; wv.MW = 128; wv.NP = 1; wv.dst = zband[l, gi:gi + 1]; wv.buf = bw; wv.name = f"band{l}_{gi}"
                W[wv.name] = wv

                def bjob(l=l, gi=gi, wsrc=wsrc, bw=bw, tz=tz):
                    S._wait("pool", tz)
                    tks = []
                    dv = zband[l, gi].rearrange("p (i b m) -> p i b m", i=10, b=3)
                    for n in range(16):
                        r0, r1 = 80 * n, 80 * n + 80
                        for i in range(r0 // 128, (r1 - 1) // 128 + 1):
                            ra, rb = max(r0, 128 * i), min(r1, 128 * i + 128)
                            for j in range(r0 // 128, (r1 - 1) // 128 + 1):
                                ca, cb = max(r0, 128 * j), min(r1, 128 * j + 128)
                                tks.append(S.dma("pool", dv[ra - 128 * i:rb - 128 * i, i, j - i + 1, ca - 128 * j:cb - 128 * j],
                                                 wsrc[l, n, ra - r0:rb - r0, ca - r0:cb - r0]))
                    bw.w = tks
                jobs[wv.name] = bjob

        for l in range(NL_A):
            mkw(f"rgin{l}", rg_w_in[l], D, 2 * DR, 512)
            mkw(f"rgout{l}", rg_w_out[l], DR, D, 256)
        for l in range(4 if DO_B else NL_A):
            mkw(f"up{l}", ffn_w_up[l], D, 2 * DFF, 512)
            mkw(f"down{l}", ffn_w_down[l], DFF, D, 128)
            mkw(f"plein{l}", ple_w_in[l], 256, D, 1024)
            mkw(f"pleg{l}", ple_w_gate[l], D, D, 512)
        mkw("kv", w_kv, D, 768, 256)


        BIGV = 30000.0
        if DO_B:
            for j in range(2):
                mkw(f"q{j}", attn_w_qg[j][:, 0:1024], D, 1024, 512)
                mkw(f"wo{j}", attn_w_o[j], D, D, 512)
                wsw = Wt(nc, f"qsw{j}", None, D, 1024, 512)
                wsw.name = f"qsw{j}"
                W[f"qsw{j}"] = wsw

                def qswjob(j=j, wsw=wsw):
                    tks = []
                    with nc.allow_non_contiguous_dma(reason="one-time rope column swap"):
                        for pnl in range(2):
                            for k in range(8):
                                srcv = attn_w_qg[j][k * 128:(k + 1) * 128, pnl * 512:(pnl + 1) * 512].rearrange("p (hd hh dd) -> p hd hh dd", hd=8, hh=2)
                                dstv = wsw.dst[pnl].rearrange("p (k hd hh dd) -> p k hd hh dd", k=8, hd=8, hh=2)
                                for hh in range(2):
                                    tks.append(S.dma("pool", dstv[:, k, :, hh, :], srcv[:, :, 1 - hh, :]))
                    wsw.buf.w = tks
                jobs[wsw.name] = qswjob
                wg = Wt.__new__(Wt)
                wg.KC = 8; wg.MW = 128; wg.NP = 1; wg.dst = zpad[j:j + 1, :, 0:1024]; wg.buf = Buf()
                W[f"qg{j}"] = wg
                wg.name = f"qg{j}"
                tz = S.dma("sp", zpad[j, :, 0:1024], zt[:, 0:1024], reads=[b_zt])

                def qgjob(j=j, wg=wg, tz=tz):
                    S._wait("pool", tz)
                    with nc.allow_non_contiguous_dma(reason="gate cols"):
                        tk = S.dma("pool", zpad[j, :, 0:1024].rearrange("p (k m) -> p k m", k=8)[:, :, 0:48],
                                   attn_w_qg[j][:, 1024:1072].rearrange("(k p) m -> p k m", p=128))
                    wg.buf.w = [tk]
                jobs[wg.name] = qgjob
            for jv in range(2):
                for g in range(2):
                    idx = 2 + jv * 2 + g
                    w1 = Wt.__new__(Wt)
                    w1.KC = 32; w1.MW = 128; w1.NP = 1; w1.dst = zpad[idx:idx + 1]; w1.buf = Buf()
                    W[f"w1_{jv}{g}"] = w1
                    w1.name = f"w1_{jv}{g}"
                    tz = S.dma("sp", zpad[idx], zt_full, reads=[b_zt])

                    def w1job(jv=jv, g=g, idx=idx, w1=w1, tz=tz):
                        S._wait("pool", tz)
                        tk = S.dma("pool", zpad[idx, g * 64:(g + 1) * 64, :].rearrange("p (l e) -> p l e", l=32),
                                   cmp_w1[jv].rearrange("(l d) e -> d l e", d=64))
                        w1.buf.w = [tk]
                    jobs[w1.name] = w1job

            slcK = sb("slcK", [128, 2, NTOK], BF16); b_slcK = Buf()
            slcV = sb("slcV", [128, NTOK // 128, 2, 128], BF16); b_slcV = Buf()
            winK = sb("winK", [128, 2, 1024], BF16); b_winK = Buf()
            winV = sb("winV", [128, 8, 2, 128], BF16); b_winV = Buf()
            cmpT = sb("cmpT", [128, 2, 528], BF16); b_cmpT = Buf()
            kcK = sb("kcK", [128, 2, 256], BF16); b_kcK = Buf()
            vcV = sb("vcV", [128, 2, 2, 128], F32); b_vcV = Buf()
            vcVb = sb("vcVb", [128, 2, 2, 128], BF16); b_vcVb = Buf()
            ident_bf = sb("ident_bf", [128, 128], BF16); b_identb = Buf()
            ones1 = sb("ones1", [128, 128], BF16); b_ones1 = Buf()
            mcs = sb("mcs", [128, 2, 64], BF16); b_mcs = Buf()
            tri = sb("tri", [128, 128], BF16); b_tri = Buf()
            w2k = sb("w2k", [128, 128], BF16); w2v = sb("w2v", [128, 128], BF16); b_w2 = Buf()
            cb1 = sb("cb1", [128, 2]); b_cb1 = Buf()
            b1t = sb("b1t", [128, 2]); posT = sb("posT", [128, 2, 32], BF16); b_posT = Buf()
            S.op("pool", lambda e: e.memset(slcV[:], 1.0), writes=[b_slcV])
            S.op("pool", lambda e: e.memset(winV[:], 1.0), writes=[b_winV])
            S.op("pool", lambda e: e.memset(slcK[:], 0.0), writes=[b_slcK])
            S.op("pool", lambda e: e.memset(winK[:], 0.0), writes=[b_winK])
            S.op("pool", lambda e: e.memset(cmpT[:], 0.0), writes=[b_cmpT])
            S.op("pool", lambda e: e.memset(kcK[:], 0.0), writes=[b_kcK])
            S.op("pool", lambda e: e.memset(vcV[:], 0.0), writes=[b_vcV])
            S.op("pool", lambda e: e.memset(vcVb[:], 0.0), writes=[b_vcVb])
            S.op("pool", lambda e: e.memset(ones1[:], 1.0), writes=[b_ones1])
            S.op("pool", lambda e: e.memset(w2v[:], 0.0), writes=[b_w2])
            S.op("pool", lambda e: e.memset(posT[:], 0.0), writes=[b_posT])
            S.dma("pool", ident_bf[:], ident_d, writes=[b_identb])
            S.dma("pool", mcs[:], t_mcs, writes=[b_mcs])
            S.dma("pool", tri[:], t_tri, writes=[b_tri])
            S.dma("pool", w2k[:, 0:64], cmp_w2[0], writes=[b_w2])
            S.dma("pool", w2k[:, 64:128], cmp_w2[0], writes=[b_w2])
            S.dma("pool", w2v[:, 0:64], cmp_w2[1], writes=[b_w2])
            with nc.allow_non_contiguous_dma(reason="small"):
                S.dma("sp", b1t[:], cmp_b1.rearrange("j e -> e j"), writes=[b_cb1])
                for jv_ in range(2):
                    S.dma("pool", posT[0:64, jv_, :], cmp_pos[:, jv_, :].rearrange("l d -> d l"), writes=[b_posT])

        S.dma("sp", ident[:], ident_d, writes=[b_ident])
        S.dma("sp", flag_sb[:], flag, writes=[b_flag])
        S.op("dve", lambda e: e.memset(ones_bf[:], 1.0 / D), writes=[b_ones])
        S.op("dve", lambda e: e.memset(epsb[:], EPS), writes=[b_eps])
        with nc.allow_non_contiguous_dma(reason="small param vectors to feature-major"):
            for i in range(4):
                S.dma("sp", gvec[:, i, :], g_mix[i].rearrange("(c p) -> p c", p=128), writes=[b_gvec])
                S.dma("sp", gvec[:, 4 + i, :], g_ffn[i].rearrange("(c p) -> p c", p=128), writes=[b_gvec])
                S.dma("sp", gvec[:, 8 + i, :], g_ple[i].rearrange("(c p) -> p c", p=128), writes=[b_gvec])
            S.dma("sp", gvec[:, 12, :], g_final.rearrange("(c p) -> p c", p=128), writes=[b_gvec])
            S.dma("sp", gvec[:, 13, :], g_kv.rearrange("(c p) -> p c", p=128), writes=[b_gvec])
            for l in range(2):
                for k in range(4):
                    S.dma("sp", rgcw[:, l, k, :], rg_conv_w[l, k].rearrange("(c p) -> p c", p=128), writes=[b_rgc])
                S.dma("sp", rgcb[:, l, :], rg_conv_b[l].rearrange("(c p) -> p c", p=128), writes=[b_rgc])
                S.dma("sp", rgba[:, l, :], rg_b_a[l].rearrange("(c p) -> p c", p=128), writes=[b_rgp])
                S.dma("sp", rgbx[:, l, :], rg_b_x[l].rearrange("(c p) -> p c", p=128), writes=[b_rgp])
                S.dma("sp", rgc8[:, l, :], rg_lambda[l].rearrange("(c p) -> p c", p=128), writes=[b_rgp])
            for l in range(4):
                for k in range(3):
                    S.dma("sp", ffcw[:, l, k, :], ffn_conv_w[l, k].rearrange("(c p) -> p c", p=128), writes=[b_ffc])
                S.dma("sp", ffcb[:, l, :], ffn_conv_b[l].rearrange("(c p) -> p c", p=128), writes=[b_ffc])
        S.op("act", lambda e: e.activation(out=rgc8[:], in_=rgc8[:], func=AF.Exp, scale=-1.0), reads=[b_rgp], writes=[b_rgp])
        S.op("act", lambda e: e.activation(out=rgc8[:], in_=rgc8[:], func=AF.Ln, bias=1.0), reads=[b_rgp], writes=[b_rgp])
        S.op("dve", lambda e: e.tensor_scalar(out=rgc8[:], in0=rgc8[:], scalar1=-8.0, scalar2=None, op0=ALU.mult), reads=[b_rgp], writes=[b_rgp])
        allh = [b for row in b_rghist for b in row] + [b for row in b_rghc for b in row] + [b for row in b_ffhist for b in row]
        S.op("dve", lambda e: e.memset(rg_hist[:], 0.0), writes=[b for row in b_rghist for b in row])
        S.op("dve", lambda e: e.memset(rg_hc[:], 0.0), writes=[b for row in b_rghc for b in row])
        S.op("dve", lambda e: e.memset(ff_hist[:], 0.0), writes=[b for row in b_ffhist for b in row])

        pan_i = [0]
        NWv = [512]

        def load_panel(w, pnl):
            i = pan_i[0] % NPAN
            pan_i[0] += 1
            n = w.KC * w.MW
            S.dma("sp", pan[i][:, 0:n], w.dst[pnl], reads=[w.buf], writes=[b_pan[i]])
            return pan[i][:, 0:n].rearrange("p (k m) -> p k m", k=w.KC), b_pan[i]

        class Stream:
            def __init__(self, items, depth=2):
                for w_, _p in items:
                    ensure(w_.name)
                prefetch_casts(4)
                self.items = items
                self.loaded = []
                self.depth = depth
                self.pos = 0

            def get(self):
                while len(self.loaded) < min(len(self.items), self.pos + 1 + self.depth):
                    w, pnl = self.items[len(self.loaded)]
                    self.loaded.append(load_panel(w, pnl))
                r = self.loaded[self.pos]
                self.pos += 1
                return r

        if DO_B:
            for jv in range(2):
                ensure(f"w1_{jv}0")
                pnl, bpn = load_panel(W[f"w1_{jv}0"], 0)
                pt, bp = ps_next()
                for l_ in range(32):
                    S.op("pe", lambda e: e.matmul(pt[:, 0:1], lhsT=pnl[:, l_, :], rhs=posT[:, jv, l_:l_ + 1], start=(l_ == 0), stop=(l_ == 31)),
                         reads=[bpn, b_posT], writes=[bp])
                S.op("dve", lambda e: e.tensor_tensor(out=cb1[:, jv:jv + 1], in0=b1t[:, jv:jv + 1], in1=pt[:, 0:1], op=ALU.add), reads=[bp, b_cb1], writes=[b_cb1])

        def transpose_in(src_rows, ncols, dst, dst_bufs, dst_dt_bf16=False):
            nk = ncols // 128
            S.dma("sp", xt[:, :, 0:ncols], src_rows.rearrange("(b p) f -> p b f", p=128), writes=[b_xt])
            for k in range(nk):
                pt, bp = ps_next()
                for blk in range(4):
                    S.op("pe", lambda e: e.transpose(pt[:, blk * 128:(blk + 1) * 128], xt[:, blk, k * 128:(k + 1) * 128], ident[:]),
                         reads=[b_xt, b_ident], writes=[bp])
                S.op("act", lambda e: e.copy(out=dst[:, k, :], in_=pt[:]), reads=[bp], writes=[dst_bufs[k]])

        def rmsnorm(gi, out_t, out_buf):
            for k in range(8):
                S.op("act", lambda e: e.activation(out=sq[:, k, :], in_=x[:, k, :], func=AF.Square), reads=[b_x[k]], writes=[b_sq])
            pt, bp = ps_next()
            for k in range(8):
                S.op("pe", lambda e: e.matmul(pt[:], lhsT=ones_bf[:], rhs=sq[:, k, :], start=(k == 0), stop=(k == 7)),
                     reads=[b_sq, b_ones], writes=[bp])
            S.op("act", lambda e: e.activation(out=rstd[:], in_=pt[:], func=AF.Sqrt, bias=epsb[:, 0:1]), reads=[bp, b_eps], writes=[b_rstd])
            S.op("dve", lambda e: e.reciprocal(out=rstd[:], in_=rstd[:]), reads=[b_rstd], writes=[b_rstd])
            for k in range(8):
                S.op("dve", lambda e: e.scalar_tensor_tensor(out=out_t[:, k, :], in0=x[:, k, :], scalar=gvec[:, gi, k:k + 1], in1=rstd[:],
                                                              op0=ALU.mult, op1=ALU.mult),
                     reads=[b_x[k], b_gvec, b_rstd], writes=[out_buf])

        def proj_chunk(pnl_ap, pnl_buf, col0, mcols, rhs_t, rhs_buf, nk):
            pt, bp = ps_next()
            for k in range(nk):
                S.op("pe", lambda e: e.matmul(pt[0:mcols, 0:NWv[0]], lhsT=pnl_ap[:, k, col0:col0 + mcols], rhs=rhs_t[:, k, 0:NWv[0]],
                                              start=(k == 0), stop=(k == nk - 1)),
                     reads=[pnl_buf] + (rhs_buf if isinstance(rhs_buf, list) else [rhs_buf]), writes=[bp])
            return pt, bp

        def rg_layer(l, ti, es2, smp=False):
            def sb2(name, shape, dt=F32):
                uid[0] += 1
                return es2.enter_context(nc.sbuf_tensor(f"s{uid[0]}_{name}", list(shape), dt))
            y = sb2("rg_y", [128, 10, 512], BF16); b_y = [Buf() for _ in range(10)]
            xc = sb2("rg_xc", [128, 10, 512]); b_xc = [Buf() for _ in range(10)]
            xcb = sb2("rg_xcb", [128, 10, 512], BF16); b_xcb = [Buf() for _ in range(10)]
            xrh = [sb2(f"rg_xrh{i}", [128, 515]) for i in range(2)]; b_xrh = [Buf(), Buf()]
            t1 = [sb2(f"rg_t1{i}", [128, 512]) for i in range(2)]; b_t1 = [Buf(), Buf()]
            gr = [sb2(f"rg_r{i}", [128, 512]) for i in range(2)]; b_gr = [Buf(), Buf()]
            gi_ = [sb2(f"rg_i{i}", [128, 512]) for i in range(2)]; b_gi = [Buf(), Buf()]
            ga = [sb2(f"rg_a{i}", [128, 512]) for i in range(2)]; b_ga = [Buf(), Buf()]
            gm = [sb2(f"rg_m{i}", [128, 512]) for i in range(2)]; b_gm = [Buf(), Buf()]
            hs = [sb2(f"rg_hs{i}", [128, 512]) for i in range(2)]; b_hs = [Buf(), Buf()]
            if smp:
                s_rgh = sb2("s_rgh", [128, 10, 4, 3]); s_h0 = sb2("s_h0", [128, 10, 4]); b_sst = Buf()
                so_rgc = sb2("so_rgc", [128, 10, 4]); so_rgh = sb2("so_rgh", [128, 10, 4]); b_so = Buf()
                S.dma("sp", s_rgh[:], s_rgh_d[:, l], writes=[b_sst])
                S.dma("sp", s_h0[:], s_h0_d[:, l], writes=[b_sst])
            rmsnorm(l, h, b_h)
            st = Stream([(W[f"rgin{l}"], p) for p in range(5)] + [(W[f"band{l}_0"], 0), (W[f"band{l}_1"], 0)]
                        + [(W[f"rgout{l}"], p) for p in range(4)])
            for jc in range(20):
                if jc % 4 == 0:
                    pnl, bpn = st.get()
                pt, bp = proj_chunk(pnl, bpn, (jc % 4) * 128, 128, h, b_h, 8)
                if jc < 10:
                    S.op("act", lambda e: e.activation(out=y[:, jc, :], in_=pt[:], func=AF.Gelu), reads=[bp], writes=[b_y[jc]])
                else:
                    j = jc - 10
                    q = j % 2
                    S.op("act", lambda e: e.copy(out=xrh[q][:, 3:515], in_=pt[:]), reads=[bp], writes=[b_xrh[q]])
                    S.op("act", lambda e: e.activation(out=xc[:, j, :], in_=pt[:], func=AF.Identity, scale=rgcw[:, l, 3, j:j + 1], bias=rgcb[:, l, j:j + 1]),
                         reads=[bp, b_rgc], writes=[b_xc[j]])
                    if ti == OWN0:
                        S.op("pool", lambda e: e.tensor_scalar(out=rg_hist[:, l, j, :], in0=rg_hist[:, l, j, :], scalar1=flag_sb[:, 0:1], scalar2=None, op0=ALU.mult),
                             reads=[b_rghist[l][j], b_flag], writes=[b_rghist[l][j]])
                    S.op("pool", lambda e: e.tensor_copy(out=xrh[q][:, 0:3], in_=rg_hist[:, l, j, :]), reads=[b_rghist[l][j]], writes=[b_xrh[q]])
                    S.op("pool", lambda e: e.tensor_copy(out=rg_hist[:, l, j, :], in_=xrh[q][:, 512:515]), reads=[b_xrh[q]], writes=[b_rghist[l][j]])
                    if smp:
                        S.op("dve", lambda e: e.tensor_copy(out=so_rgc[:, j, :], in_=xrh[q][:, 6:22:4]), reads=[b_xrh[q]], writes=[b_so])
                        S.op("dve", lambda e: e.tensor_copy(out=xrh[q][:, 3:19].rearrange("p (b k) -> p b k", k=4)[:, :, 0:3], in_=s_rgh[:, j, :, :]),
                             reads=[b_sst], writes=[b_xrh[q]])
                    for k in (2, 1, 0):
                        S.op("dve", lambda e: e.scalar_tensor_tensor(out=xc[:, j, :], in0=xrh[q][:, k:k + 512], scalar=rgcw[:, l, k, j:j + 1], in1=xc[:, j, :],
                                                                      op0=ALU.mult, op1=ALU.add), reads=[b_xrh[q], b_rgc, b_xc[j]], writes=[b_xc[j]])
                    S.op("pool", lambda e: e.tensor_copy(out=xcb[:, j, :], in_=xc[:, j, :]), reads=[b_xc[j]], writes=[b_xcb[j]])
            bands = [st.get(), st.get()]
            for jp in range(5):
                js = (2 * jp, 2 * jp + 1)
                for j in js:
                    q = j % 2
                    ins = [i for i in (j - 1, j, j + 1) if 0 <= i < 10]
                    for gidx, (dst, bdst, bias_t) in enumerate(((gr, b_gr, rgba), (gi_, b_gi, rgbx))):
                        pt, bp = ps_next()
                        bv, b_band = bands[gidx]
                        for n_, i in enumerate(ins):
                            S.op("pe", lambda e: e.matmul(pt[:, 0:NWv[0]], lhsT=bv[:, i * 3 + (j - i + 1), :], rhs=xcb[:, i, 0:NWv[0]], start=(n_ == 0), stop=(n_ == len(ins) - 1)),
                                 reads=[b_band, b_xcb[i]], writes=[bp])
                        S.op("act", lambda e: e.activation(out=dst[q][:], in_=pt[:], func=AF.Sigmoid, bias=bias_t[:, l, j:j + 1]),
                             reads=[bp, b_rgp], writes=[bdst[q]])
                for j in js:
                    q = j % 2
                    S.op("act", lambda e: e.activation(out=ga[q][:], in_=gr[q][:], func=AF.Exp, scale=rgc8[:, l, j:j + 1]),
                         reads=[b_gr[q], b_rgp], writes=[b_ga[q]])
                    S.op("dve", lambda e: e.tensor_tensor(out=gm[q][:], in0=ga[q][:], in1=ga[q][:], op=ALU.mult), reads=[b_ga[q]], writes=[b_gm[q]])
                    S.op("dve", lambda e: e.tensor_scalar(out=gm[q][:], in0=gm[q][:], scalar1=-1.0, scalar2=1.0, op0=ALU.mult, op1=ALU.add),
                         reads=[b_gm[q]], writes=[b_gm[q]])
                    S.op("dve", lambda e: e.tensor_tensor(out=gi_[q][:], in0=gi_[q][:], in1=xc[:, j, :], op=ALU.mult), reads=[b_gi[q], b_xc[j]], writes=[b_gi[q]])
                for j in js:
                    q = j % 2
                    S.op("act", lambda e: e.activation(out=gm[q][:], in_=gm[q][:], func=AF.Sqrt), reads=[b_gm[q]], writes=[b_gm[q]])
                for j in js:
                    q = j % 2
                    if ti == 0:
                        S.op("dve", lambda e: e.memset(gm[q][:, 0:1], 1.0), reads=[], writes=[b_gm[q]])
                    elif ti == OWN0:
                        S.op("dve", lambda e: e.tensor_scalar(out=gm[q][:, 0:1], in0=gm[q][:, 0:1], scalar1=flag_sb[:, 1:2], scalar2=None, op0=ALU.max),
                             reads=[b_gm[q], b_flag], writes=[b_gm[q]])
                        S.op("dve", lambda e: e.tensor_scalar(out=rg_hc[:, l, j:j + 1], in0=rg_hc[:, l, j:j + 1], scalar1=flag_sb[:, 0:1], scalar2=None, op0=ALU.mult),
                             reads=[b_rghc[l][j], b_flag], writes=[b_rghc[l][j]])
                    S.op("dve", lambda e: e.tensor_tensor(out=gi_[q][:], in0=gi_[q][:], in1=gm[q][:], op=ALU.mult), reads=[b_gi[q], b_gm[q]], writes=[b_gi[q]])
                    if smp:
                        S.op("dve", lambda e: e.memset(ga[q][:, 2:18:4], 0.0), reads=[], writes=[b_ga[q]])
                        S.op("dve", lambda e: e.tensor_copy(out=gi_[q][:, 2:18:4], in_=s_h0[:, j, :]), reads=[b_sst], writes=[b_gi[q]])
                    S.op("dve", lambda e: e.tensor_tensor_scan(out=hs[q][:], data0=ga[q][:], data1=gi_[q][:], initial=rg_hc[:, l, j:j + 1], op0=ALU.mult, op1=ALU.add),
                         reads=[b_ga[q], b_gi[q], b_rghc[l][j]], writes=[b_hs[q]])
                    if smp:
                        S.op("dve", lambda e: e.tensor_copy(out=so_rgh[:, j, :], in_=hs[q][:, 3:19:4]), reads=[b_hs[q]], writes=[b_so])
                    S.op("dve", lambda e: e.tensor_copy(out=rg_hc[:, l, j:j + 1], in_=hs[q][:, 511:512]), reads=[b_hs[q]], writes=[b_rghc[l][j]])
                    S.op("dve", lambda e: e.tensor_tensor(out=y[:, j, :], in0=y[:, j, :], in1=hs[q][:], op=ALU.mult), reads=[b_y[j], b_hs[q]], writes=[b_y[j]])
            for oc in range(8):
                if oc % 2 == 0:
                    pnl, bpn = st.get()
                pt, bp = ps_next()
                for k in range(10):
                    S.op("pe", lambda e: e.matmul(pt[:, 0:NWv[0]], lhsT=pnl[:, k, (oc % 2) * 128:(oc % 2 + 1) * 128], rhs=y[:, k, 0:NWv[0]], start=(k == 0), stop=(k == 9)),
                         reads=[bpn, b_y[k]], writes=[bp])
                S.op("dve", lambda e: e.tensor_tensor(out=x[:, oc, :], in0=x[:, oc, :], in1=pt[:], op=ALU.add), reads=[b_x[oc], bp], writes=[b_x[oc]])
            if smp:
                S.dma("sp", o_s_rgc_new[:, l], so_rgc[:], reads=[b_so])
                S.dma("sp", o_s_rgh[:, l], so_rgh[:], reads=[b_so])

        def ffn_ple(l, ti, es2, smp=False):
            def sb2(name, shape, dt=F32):
                uid[0] += 1
                return es2.enter_context(nc.sbuf_tensor(f"s{uid[0]}_{name}", list(shape), dt))
            gbuf = sb2("ff_g", [128, 24, 512], BF16); b_g = [Buf() for _ in range(24)]
            uh = [sb2(f"ff_uh{i}", [128, 514]) for i in range(6)]; b_uh = [Buf() for _ in range(6)]
            uc = [sb2(f"ff_uc{i}", [128, 512]) for i in range(6)]; b_uc = [Buf() for _ in range(6)]
            pT = sb2("ple_pT", [128, 2, 512], BF16); b_pT = [Buf(), Buf()]
            sig = [sb2(f"ple_sig{i}", [128, 512]) for i in range(2)]; b_sig = [Buf(), Buf()]

            if smp:
                s_ffh = sb2("s_ffh", [128, 48, 4, 2]); b_sst = Buf()
                so_ffc = sb2("so_ffc", [128, 48, 4]); b_so = Buf()
                S.dma("sp", s_ffh[:], s_ffh_d[:, l], writes=[b_sst])
            rmsnorm(4 + l, h, b_h)
            items = []
            for c4 in range(6):
                items += [(W[f"up{l}"], c4), (W[f"up{l}"], c4 + 6)]
            items += [(W[f"down{l}"], p) for p in range(8)]
            items += [(W[f"plein{l}"], 0), (W[f"pleg{l}"], 0), (W[f"pleg{l}"], 1)]
            st = Stream(items)
            if ti == OWN0:
                for cc in range(48):
                    S.op("pool", lambda e: e.tensor_scalar(out=ff_hist[:, l, cc, :], in0=ff_hist[:, l, cc, :], scalar1=flag_sb[:, 0:1], scalar2=None, op0=ALU.mult),
                         reads=[b_ffhist[l][cc], b_flag], writes=[b_ffhist[l][cc]])
            pans = [None, None]

            def stageA(c):
                if c % 4 == 0:
                    pans[0] = st.get()
                    pans[1] = st.get()
                ci = c % 4
                for hf in range(2):
                    pnl, bpn = pans[hf]
                    cc = c + 24 * hf
                    q = (c % 3) * 2 + hf
                    pt, bp = proj_chunk(pnl, bpn, ci * 128, 128, h, b_h, 8)
                    S.op("act", lambda e: e.copy(out=uh[q][:, 2:514], in_=pt[:]), reads=[bp], writes=[b_uh[q]])
                    S.op("act", lambda e: e.activation(out=uc[q][:], in_=pt[:], func=AF.Identity, scale=ffcw[:, l, 2, cc:cc + 1], bias=ffcb[:, l, cc:cc + 1]),
                         reads=[bp, b_ffc], writes=[b_uc[q]])
                    S.op("pool", lambda e: e.tensor_copy(out=uh[q][:, 0:2], in_=ff_hist[:, l, cc, :]), reads=[b_ffhist[l][cc]], writes=[b_uh[q]])
                    S.op("pool", lambda e: e.tensor_copy(out=ff_hist[:, l, cc, :], in_=uh[q][:, 512:514]), reads=[b_uh[q]], writes=[b_ffhist[l][cc]])
                    if smp:
                        S.op("dve", lambda e: e.tensor_copy(out=so_ffc[:, cc, :], in_=uh[q][:, 5:21:4]), reads=[b_uh[q]], writes=[b_so])
                        S.op("dve", lambda e: e.tensor_copy(out=uh[q][:, 2:18].rearrange("p (b k) -> p b k", k=4)[:, :, 1:3], in_=s_ffh[:, cc, :, :]),
                             reads=[b_sst], writes=[b_uh[q]])

            def stageB(c):
                for hf in range(2):
                    cc = c + 24 * hf
                    q = (c % 3) * 2 + hf
                    for k in (1, 0):
                        S.op("dve", lambda e: e.scalar_tensor_tensor(out=uc[q][:], in0=uh[q][:, k:k + 512], scalar=ffcw[:, l, k, cc:cc + 1], in1=uc[q][:],
                                                                      op0=ALU.mult, op1=ALU.add), reads=[b_uh[q], b_ffc, b_uc[q]], writes=[b_uc[q]])
                q0 = (c % 3) * 2
                S.op("act", lambda e: e.activation(out=uc[q0][:], in_=uc[q0][:], func=AF.Gelu), reads=[b_uc[q0]], writes=[b_uc[q0]])
                S.op("pool", lambda e: e.tensor_tensor(out=gbuf[:, c, :], in0=uc[q0][:], in1=uc[q0 + 1][:], op=ALU.mult),
                     reads=[b_uc[q0], b_uc[q0 + 1]], writes=[b_g[c]])
            stageA(0)
            for c in range(24):
                if c + 1 < 24:
                    stageA(c + 1)
                stageB(c)
            for oc in range(8):
                pnl, bpn = st.get()
                pt, bp = ps_next()
                for k in range(24):
                    S.op("pe", lambda e: e.matmul(pt[:, 0:NWv[0]], lhsT=pnl[:, k, :], rhs=gbuf[:, k, 0:NWv[0]], start=(k == 0), stop=(k == 23)),
                         reads=[bpn, b_g[k]], writes=[bp])
                S.op("dve", lambda e: e.tensor_tensor(out=x[:, oc, :], in0=x[:, oc, :], in1=pt[:], op=ALU.add), reads=[b_x[oc], bp], writes=[b_x[oc]])
            if smp:
                S.dma("sp", o_s_ffc_new[:, l], so_ffc[:], reads=[b_so])
            if smp:
                S.dma("pool", pT[:], ps_s[l], writes=b_pT)
            else:
                transpose_in(ps_in[l, ti * 512:(ti + 1) * 512, :], 256, pT, b_pT)
            rmsnorm(8 + l, h, b_h)
            pin, bpin = st.get()
            pg = [st.get(), st.get()]
            for oc in range(8):
                q = oc % 2
                pe_, bpe = proj_chunk(pin, bpin, oc * 128, 128, pT, b_pT, 2)
                pgp, bpg = pg[oc // 4]
                pt, bp = proj_chunk(pgp, bpg, (oc % 4) * 128, 128, h, b_h, 8)
                S.op("act", lambda e: e.activation(out=sig[q][:], in_=pt[:], func=AF.Sigmoid), reads=[bp], writes=[b_sig[q]])
                S.op("dve", lambda e: e.tensor_tensor(out=sig[q][:], in0=sig[q][:], in1=pe_[:], op=ALU.mult), reads=[b_sig[q], bpe], writes=[b_sig[q]])
                S.op("dve", lambda e: e.tensor_tensor(out=x[:, oc, :], in0=x[:, oc, :], in1=sig[q][:], op=ALU.add), reads=[b_x[oc], b_sig[q]], writes=[b_x[oc]])


        def attn_layer(l, ti, es2, qbs, smp=False):
            j = l - 2

            def sb2(name, shape, dt=F32):
                uid[0] += 1
                return es2.enter_context(nc.sbuf_tensor(f"s{uid[0]}_{name}", list(shape), dt))
            qT = sb2("qT", [128, 8, 512], BF16); b_qT = [Buf() for _ in range(8)]
            qrT = sb2("qrT", [128, 8, 512], BF16); b_qrT = [Buf() for _ in range(8)]
            gT = sb2("gT", [128, 512], BF16); b_gT = Buf()
            oT = sb2("oT", [128, 8, 512], BF16); b_oT = Buf()
            rl = sb2("rl", [128, 1024]); b_rl = Buf()
            rc_t = rl[:, 0:512]; rs_t = rl[:, 512:1024]; b_rope = b_rl
            rt1 = sb2("rt1", [128, 512]); rt2 = sb2("rt2", [128, 512]); b_rt1 = Buf(); b_rt2 = Buf()
            if smp:
                qbs = []
            PW = 8 if smp else 1024
            qpad = [sb2(f"qpad{i}", [128, 8, 128 if not smp else 2], BF16) for i in range(2)]; b_qpad = [Buf(), Buf()]
            qrpad = [sb2(f"qrpad{i}", [128, 8, 128 if not smp else 2], BF16) for i in range(2)]; b_qrpad = [Buf(), Buf()]
            cbt = [sb2(f"cbt{i}", [128, 256], BF16) for i in range(2)]; b_cbt = [Buf(), Buf()]
            wbt = [sb2(f"wbt{i}", [128, 640], BF16) for i in range(2)]; b_wbt = [Buf(), Buf()]
            vat = [sb2(f"vat{i}", [128, 2, 64]) for i in range(2)]; b_vat = [Buf(), Buf()]
            Pc = sb2("Pc", [128, 2, PW], BF16); b_Pc = [Buf(), Buf()]
            Pn = Pc; b_Pn = b_Pc
            sqf = sq[:].rearrange("p k t -> p (k t)")
            Pst = [sqf[:, i * 1024:(i + 1) * 1024] for i in range(2)]; b_Pst = [Buf() for _ in range(2)]
            pst_i = [0]
            osb = [sb2(f"osb{i}", [64, PW]) for i in range(3)]; b_osb = [Buf() for _ in range(3)]
            rlx = sb2("rlx", [64, PW]); b_rlx = Buf()
            impf = sb2("impf", [128, 64]); b_impf = Buf()
            impt = sb2("impt", [128, 64]); b_impt = Buf()
            m8 = sb2("m8", [128, 16]); b_m8 = Buf()
            thr = sb2("thr", [128, 1]); b_thr = Buf()
            nbf = sb2("nbf", [128, 64]); b_nbf = Buf()
            nb = sb2("nb", [128, 64], BF16); b_nb = Buf()
            nbd = sb2("nbd", [128, 128], BF16); b_nbd = Buf()
            nbe = [sb2(f"nbe{i}", [128, 128], BF16) for i in range(4)]; b_nbe = [Buf() for _ in range(4)]
            nbe_i = [0]
            for t_ in qpad + qrpad:
                S.op("pool", lambda e: e.memset(t_[:], 0.0), writes=b_qpad + b_qrpad)

            S.dma("sp", rc_t, ropec_s if smp else ropec[:, ti * 512:(ti + 1) * 512], writes=[b_rope])
            S.dma("sp", rs_t, ropes_s if smp else ropes[:, ti * 512:(ti + 1) * 512], writes=[b_rope])
            rmsnorm(l, h, b_h)
            for bq in b_Pst:
                bq.w = b_sq.w
                bq.r = list(b_sq.r)
            st = Stream([(W[f"q{j}"], 0), (W[f"qsw{j}"], 0), (W[f"q{j}"], 1), (W[f"qsw{j}"], 1), (W[f"qg{j}"], 0), (W[f"wo{j}"], 0), (W[f"wo{j}"], 1)], depth=2)
            for c in range(8):
                if c % 4 == 0:
                    pq, bpq = st.get()
                    psw, bpsw = st.get()
                pt, bp = proj_chunk(pq, bpq, (c % 4) * 128, 128, h, b_h, 8)
                pt2, bp2 = proj_chunk(psw, bpsw, (c % 4) * 128, 128, h, b_h, 8)
                S.op("dve", lambda e: e.tensor_copy(out=qT[:, c, :], in_=pt[:]), reads=[bp], writes=[b_qT[c]])
                S.op("dve", lambda e: e.tensor_tensor(out=rt1[:], in0=pt[:], in1=rc_t, op=ALU.mult), reads=[bp, b_rope], writes=[b_rt1])
                S.op("dve", lambda e: e.tensor_tensor(out=rt2[:], in0=pt2[:], in1=rs_t, op=ALU.mult), reads=[bp2, b_rope], writes=[b_rt2])
                S.op("dve", lambda e: e.tensor_tensor(out=qrT[:, c, :], in0=rt1[:], in1=rt2[:], op=ALU.add), reads=[b_rt1, b_rt2], writes=[b_qrT[c]])
            pg, bpg = st.get()
            pt, bp = proj_chunk(pg, bpg, 0, 128, h, b_h, 8)
            S.op("act", lambda e: e.activation(out=gT[:], in_=pt[:], func=AF.Sigmoid), reads=[bp], writes=[b_gT])
            identb4 = ident_bf[:].unsqueeze(1).broadcast_to([128, 4, 128])

            def scores(lhs_bias, bias_bufs, kmat, kbufs, qp, bqp):
                pp, bpp = ps_pair()
                for bank in range(2):
                    S.op("pe", lambda e: e.matmul(pp[:, bank * 512:(bank + 1) * 512], lhsT=lhs_bias, rhs=identb4, start=True, stop=False),
                         reads=bias_bufs + [b_identb], writes=[bpp[bank]])
                    for h4 in range(4):
                        hh = bank * 4 + h4
                        S.op("pe", lambda e: e.matmul(pp[:, hh * 128:(hh + 1) * 128], lhsT=kmat, rhs=qp[:, hh, :], start=False, stop=(h4 == 3)),
                             reads=kbufs + [bqp], writes=[bpp[bank]])
                return pp, bpp

            def pv_acc(vmat, vbufs, pt_, bpt_, first, last):
                for bank in range(2):
                    S.op("pe", lambda e: e.matmul(acc_ps[:, bank * 512:(bank + 1) * 512], lhsT=vmat, rhs=pt_[:, bank * 512:(bank + 1) * 512], start=first, stop=last),
                         reads=vbufs + [bpt_], writes=[b_acc[bank]])

            if len(qbs) < 4:
                S.op("pool", lambda e: e.memset(oT[:], 0.0), writes=[b_oT])
            for qb in (qbs if DBG >= 5 else []):
                qbo = (ti - OWN0) * 4 + qb + 1
                kd = 4 * ti + qb
                tb = qbo % 2
                S.dma("pool", cbt[tb][:], t_cb[qbo], writes=[b_cbt[tb]])
                S.dma("pool", wbt[tb][:], t_wb[qbo], writes=[b_wbt[tb]])
                S.dma("sp", vat[tb][:], t_va[qbo], writes=[b_vat[tb]])
                tsl = slice(qb * 128, (qb + 1) * 128)
                for g in range(2):
                    pi = g
                    S.op("pool", lambda e: e.tensor_copy(out=qpad[pi][0:64, 0:8:2, :], in_=qT[0:64, 4 * g:4 * g + 4, tsl]), reads=b_qT[4 * g:4 * g + 4], writes=[b_qpad[pi]])
                    S.op("pool", lambda e: e.tensor_copy(out=qpad[pi][64:128, 1:8:2, :], in_=qT[64:128, 4 * g:4 * g + 4, tsl]), reads=b_qT[4 * g:4 * g + 4], writes=[b_qpad[pi]])
                    S.op("pool", lambda e: e.tensor_copy(out=qrpad[pi][0:64, 0:8:2, :], in_=qrT[0:64, 4 * g:4 * g + 4, tsl]), reads=b_qrT[4 * g:4 * g + 4], writes=[b_qrpad[pi]])
                    S.op("pool", lambda e: e.tensor_copy(out=qrpad[pi][64:128, 1:8:2, :], in_=qrT[64:128, 4 * g:4 * g + 4, tsl]), reads=b_qrT[4 * g:4 * g + 4], writes=[b_qrpad[pi]])
                    nbc = 1 if 32 * ti + 31 < 128 else 2
                    for bc in range(nbc):
                        pp, bpp = scores(cbt[tb][:, bc * 128:(bc + 1) * 128], [b_cbt[tb]], kcK[:, g, bc * 128:(bc + 1) * 128], [b_kcK], qpad[pi], b_qpad[pi])
                        S.op("act", lambda e: e.activation(out=Pc[:, bc, :], in_=pp[:], func=AF.Exp, scale=0.125), reads=bpp, writes=[b_Pc[bc]])

                    def win_front(wi):
                        rch = (kd - 4 + wi) % 8
                        pp, bpp = scores(wbt[tb][:, wi * 128:(wi + 1) * 128], [b_wbt[tb]], winK[:, g, rch * 128:(rch + 1) * 128], [b_winK], qrpad[pi], b_qrpad[pi])
                        pz = pst_i[0] % 2
                        pst_i[0] += 1
                        S.op("act", lambda e: e.activation(out=Pst[pz], in_=pp[:], func=AF.Exp, scale=0.125), reads=bpp, writes=[b_Pst[pz]])
                        return pz
                    cur = win_front(0)
                    lp, blp = ps_pair()
                    for bank in range(2):
                        for bc in range(nbc):
                            S.op("pe", lambda e: e.matmul(lp[:, bank * 512:(bank + 1) * 512], lhsT=ones1[:], rhs=Pc[:, bc, bank * 512:(bank + 1) * 512],
                                                          start=(bc == 0), stop=(bc == nbc - 1)), reads=[b_ones1, b_Pc[bc]], writes=[blp[bank]])
                    S.op("dve", lambda e: e.tensor_scalar(out=rl[:], in0=lp[:], scalar1=1e-30, scalar2=None, op0=ALU.add), reads=blp, writes=[b_rl])
                    S.op("dve", lambda e: e.reciprocal(out=rl[:], in_=rl[:]), reads=[b_rl], writes=[b_rl])
                    for bc in range(nbc):
                        S.op("dve", lambda e: e.tensor_tensor(out=Pn[:, bc, :], in0=Pc[:, bc, :], in1=rl[:], op=ALU.mult), reads=[b_Pc[bc], b_rl], writes=[b_Pn[bc]])
                    for wi in range(5):
                        nxt = win_front(wi + 1) if wi < 4 else None
                        rch = (kd - 4 + wi) % 8
                        pv_acc(winV[:, rch, g, :], [b_winV], Pst[cur], b_Pst[cur], wi == 0, wi == 4)
                        cur = nxt
                        if wi == 2:
                            ocp, bocp = ps_pair()
                            for bank in range(2):
                                for bc in range(nbc):
                                    S.op("pe", lambda e: e.matmul(ocp[:, bank * 512:(bank + 1) * 512], lhsT=vcVb[:, bc, g, :], rhs=Pn[:, bc, bank * 512:(bank + 1) * 512],
                                                                  start=(bc == 0), stop=(bc == nbc - 1)), reads=[b_vcVb, b_Pn[bc]], writes=[bocp[bank]])
                            S.op("act", lambda e: e.copy(out=osb[0][:], in_=ocp[0:64, :]), reads=bocp, writes=[b_osb[0]])
                            ip, bip = ps_next()
                            n_ = 0
                            for bc in range(nbc):
                                for hh in range(8):
                                    S.op("pe", lambda e: e.matmul(ip[:, 0:64], lhsT=Pn[:, bc, hh * 128:(hh + 1) * 128], rhs=mcs[:, bc, :], start=(n_ == 0), stop=(n_ == 8 * nbc - 1)),
                                         reads=[b_Pn[bc], b_mcs], writes=[bip])
                                    n_ += 1
                    S.op("dve", lambda e: e.tensor_tensor(out=impf[:], in0=ip[:, 0:64], in1=vat[tb][:, 0, :], op=ALU.mult), reads=[bip, b_vat[tb]], writes=[b_impf])
                    S.op("dve", lambda e: e.tensor_tensor(out=impf[:], in0=impf[:], in1=vat[tb][:, 1, :], op=ALU.add), reads=[b_impf, b_vat[tb]], writes=[b_impf])
                    S.op("dve", lambda e: e.max(out=m8[:, 0:8], in_=impf[:]), reads=[b_impf], writes=[b_m8])
                    S.op("dve", lambda e: e.match_replace(out=impt[:], in_to_replace=m8[:, 0:8], in_values=impf[:], imm_value=-2.0), reads=[b_impf, b_m8], writes=[b_impt])
                    S.op("dve", lambda e: e.max(out=m8[:, 8:16], in_=impt[:]), reads=[b_impt], writes=[b_m8])
                    S.op("dve", lambda e: e.tensor_scalar(out=thr[:], in0=m8[:, 15:16], scalar1=-0.5, scalar2=None, op0=ALU.max), reads=[b_m8], writes=[b_thr])
                    S.op("dve", lambda e: e.tensor_scalar(out=nbf[:], in0=impf[:], scalar1=thr[:, 0:1], scalar2=BIGV, op0=ALU.is_ge, op1=ALU.mult),
                         reads=[b_impf, b_thr], writes=[b_nbf])
                    S.op("dve", lambda e: e.tensor_scalar(out=nb[:], in0=nbf[:], scalar1=-BIGV, scalar2=None, op0=ALU.add), reads=[b_nbf], writes=[b_nb])
                    S.op("dve", lambda e: e.tensor_tensor(out=nbd[:].rearrange("p (s k) -> p s k", s=2), in0=nb[:, 2 * kd:2 * kd + 2].unsqueeze(2).broadcast_to([128, 2, 64]),
                                                           in1=tri[:].rearrange("p (s k) -> p s k", s=2), op=ALU.add), reads=[b_nb, b_tri], writes=[b_nbd])
                    S.op("dve", lambda e: e.reciprocal(out=rlx[:], in_=acc_ps[64:128, :]), reads=b_acc, writes=[b_rlx])
                    S.op("dve", lambda e: e.tensor_tensor(out=osb[2][:], in0=acc_ps[0:64, :], in1=rlx[:], op=ALU.mult), reads=b_acc + [b_rlx], writes=[b_osb[2]])
                    def slc_front(kc):
                        if kc == kd:
                            lb, lbb = nbd[:], [b_nbd]
                        else:
                            z = nbe_i[0] % 4
                            nbe_i[0] += 1
                            S.op("dve", lambda e: e.tensor_copy(out=nbe[z][:].rearrange("p (s k) -> p s k", s=2),
                                                                 in_=nb[:, 2 * kc:2 * kc + 2].unsqueeze(2).broadcast_to([128, 2, 64])), reads=[b_nb], writes=[b_nbe[z]])
                            lb, lbb = nbe[z][:], [b_nbe[z]]
                        pp, bpp = scores(lb, lbb, slcK[:, g, kc * 128:(kc + 1) * 128], [b_slcK], qrpad[pi], b_qrpad[pi])
                        pz = pst_i[0] % 2
                        pst_i[0] += 1
                        S.op("act", lambda e: e.activation(out=Pst[pz], in_=pp[:], func=AF.Exp, scale=0.125), reads=bpp, writes=[b_Pst[pz]])
                        return pz
                    cur = slc_front(0)
                    for kc in range(kd + 1):
                        nxt = slc_front(kc + 1) if kc < kd else None
                        pv_acc(slcV[:, kc, g, :], [b_slcV], Pst[cur], b_Pst[cur], kc == 0, kc == kd)
                        cur = nxt
                    S.op("dve", lambda e: e.reciprocal(out=rlx[:], in_=acc_ps[64:128, :]), reads=b_acc, writes=[b_rlx])
                    S.op("dve", lambda e: e.tensor_tensor(out=osb[1][:], in0=acc_ps[0:64, :], in1=rlx[:], op=ALU.mult), reads=b_acc + [b_rlx], writes=[b_osb[1]])
                    for br in range(3):
                        gp, bgp = ps_pair()
                        for hh in range(8):
                            f0 = 3 * (8 * g + hh) + br
                            S.op("pe", lambda e: e.matmul(gp[:, hh * 128:(hh + 1) * 128], lhsT=ident_bf[:, f0:f0 + 1].broadcast_to([128, 128]), rhs=gT[:, tsl], start=True, stop=True),
                                 reads=[b_identb, b_gT], writes=[bgp[hh // 4]])
                        S.op("dve", lambda e: e.tensor_tensor(out=osb[br][:], in0=osb[br][:], in1=gp[0:64, :], op=ALU.mult), reads=[b_osb[br]] + bgp, writes=[b_osb[br]])
                    S.op("dve", lambda e: e.tensor_tensor(out=osb[0][:], in0=osb[0][:], in1=osb[1][:], op=ALU.add), reads=[b_osb[0], b_osb[1]], writes=[b_osb[0]])
                    av = osb[0][:].rearrange("p (h t) -> p h t", h=8)
                    tv = osb[2][:].rearrange("p (h t) -> p h t", h=8)
                    S.op("dve", lambda e: e.tensor_tensor(out=oT[0:64, 4 * g:4 * g + 4, tsl], in0=av[:, 0:8:2, :], in1=tv[:, 0:8:2, :], op=ALU.add),
                         reads=[b_osb[0], b_osb[2]], writes=[b_oT])
                    S.op("dve", lambda e: e.tensor_tensor(out=oT[64:128, 4 * g:4 * g + 4, tsl], in0=av[:, 1:8:2, :], in1=tv[:, 1:8:2, :], op=ALU.add),
                         reads=[b_osb[0], b_osb[2]], writes=[b_oT])

            if smp:
                S.op("pool", lambda e: e.memset(oT[:], 0.0), writes=[b_oT])
                QS = sb2("QS", [128, 16], BF16); QRS = sb2("QRS", [128, 16], BF16); b_QS = Buf()
                S.op("pool", lambda e: e.memset(QS[:], 0.0), writes=[b_QS])
                S.op("pool", lambda e: e.memset(QRS[:], 0.0), writes=[b_QS])
                PcS = sb2("PcS", [128, 4, 16], BF16); b_PcS = Buf()
                PnG = sb2("PnG", [128, 4, 2], BF16); b_PnG = Buf()
                PnGf = sb2("PnGf", [128, 4, 2]); b_PnGf = Buf()
                rlS = sb2("rlS", [128, 16]); b_rlS = Buf()
                tpin = sb2("tpin", [128, 2, 128]); b_tpin = Buf()
                S.op("pool", lambda e: e.memset(tpin[:], 0.0), writes=[b_tpin])
                impS = sb2("impS", [128, 256]); impS2 = sb2("impS2", [128, 256]); b_impS = Buf()
                m8s = sb2("m8s", [128, 16]); thrs = sb2("thrs", [128, 1])
                nbS = sb2("nbS", [128, 256], BF16); b_nbS = Buf()
                gb = [sb2(f"gb{i}", [128, 256]) for i in range(4)]; b_gb = [Buf() for _ in range(4)]
                KTc = [sb2(f"KTc{i}", [128, 128], BF16) for i in range(3)]; b_KTc = [Buf() for _ in range(3)]
                wbuf = sb2("wbuf", [128, 4, 256]); b_wbuf = Buf()
                PsS = [sb2(f"PsS{i}", [128, 16], BF16) for i in range(3)]; b_PsS = [Buf() for _ in range(3)]
                GS = sb2("GS", [128, 48, 4]); b_GS = Buf()
                ocS = sb2("ocS", [64, 16]); osS = sb2("osS", [64, 2, 16]); rlS2 = sb2("rlS2", [64, 2, 16]); b_ocS = Buf()
                oS = sb2("oS", [64, 16]); tS = sb2("tS", [64, 16]); b_oS = Buf()
                gpt, bgpt = ps_next()
                for f0 in range(48):
                    S.op("pe", lambda e: e.matmul(gpt[:, f0 * 4:(f0 + 1) * 4], lhsT=ident_bf[:, f0:f0 + 1].broadcast_to([128, 128]), rhs=gT[:, 3:19:4], start=True, stop=True),
                         reads=[b_identb, b_gT], writes=[bgpt])
                S.op("dve", lambda e: e.tensor_copy(out=GS[:].rearrange("p f b -> p (f b)"), in_=gpt[:, 0:192]), reads=[bgpt], writes=[b_GS])
                gi_c = [0]
                kt_c = [0]
                ps_c = [0]
                ne_c = [0]
                for i in range(4):
                    col = 4 * i + 3
                    for g in range(2):
                        for par in range(2):
                            S.op("dve", lambda e: e.tensor_copy(out=QS[64 * g:64 * g + 64, 8 * g + par:8 * g + 8:2], in_=qT[64 * par:64 * par + 64, 4 * g:4 * g + 4, col]),
                                 reads=b_qT[4 * g:4 * g + 4], writes=[b_QS])
                            S.op("dve", lambda e: e.tensor_copy(out=QRS[64 * g:64 * g + 64, 8 * g + par:8 * g + 8:2], in_=qrT[64 * par:64 * par + 64, 4 * g:4 * g + 4, col]),
                                 reads=b_qrT[4 * g:4 * g + 4], writes=[b_QS])
                    for pc in range(4):
                        pp, bpp = ps_next()
                        if pc == 0:
                            S.op("pe", lambda e: e.matmul(pp[:, 0:16], lhsT=cb0, rhs=brhs, start=True, stop=False), reads=[b_stab], writes=[bpp])
                        S.op("pe", lambda e: e.matmul(pp[:, 0:16], lhsT=kcS[:, i, pc * 128:(pc + 1) * 128], rhs=QS[:], start=(pc != 0), stop=True),
                             reads=[b_kcS, b_QS], writes=[bpp])
                        S.op("act", lambda e: e.activation(out=PcS[:, pc, :], in_=pp[:, 0:16], func=AF.Exp, scale=0.125), reads=[bpp], writes=[b_PcS])
                    lp, blp = ps_next()
                    for pc in range(4):
                        S.op("pe", lambda e: e.matmul(lp[:, 0:16], lhsT=ones1[:], rhs=PcS[:, pc, :], start=(pc == 0), stop=(pc == 3)), reads=[b_ones1, b_PcS], writes=[blp])
                    S.op("dve", lambda e: e.tensor_scalar(out=rlS[:], in0=lp[:, 0:16], scalar1=1e-30, scalar2=None, op0=ALU.add), reads=[blp], writes=[b_rlS])
                    S.op("dve", lambda e: e.reciprocal(out=rlS[:], in_=rlS[:]), reads=[b_rlS], writes=[b_rlS])
                    for pc in range(4):
                        S.op("dve", lambda e: e.tensor_tensor(out=PcS[:, pc, :], in0=PcS[:, pc, :], in1=rlS[:], op=ALU.mult), reads=[b_PcS, b_rlS], writes=[b_PcS])
                    for pc in range(4):
                        S.op("pe", lambda e: e.matmul(acc_ps[:, 0:16], lhsT=vcS[:, i, pc, :], rhs=PcS[:, pc, :], start=(pc == 0), stop=(pc == 3)),
                             reads=[b_vcS, b_PcS], writes=[b_acc[0]])
                    S.op("dve", lambda e: e.tensor_copy(out=ocS[:, 0:8], in_=acc_ps[0:64, 0:8]), reads=[b_acc[0]], writes=[b_ocS])
                    S.op("dve", lambda e: e.tensor_copy(out=ocS[:, 8:16], in_=acc_ps[64:128, 8:16]), reads=[b_acc[0]], writes=[b_ocS])
                    for pc in range(4):
                        S.op("dve", lambda e: e.tensor_reduce(out=PnGf[:, pc, :], in_=PcS[:, pc, :].rearrange("p (g h) -> p g h", g=2), axis=mybir.AxisListType.X, op=ALU.add),
                             reads=[b_PcS], writes=[b_PnGf])
                    S.op("dve", lambda e: e.tensor_copy(out=PnG[:], in_=PnGf[:]), reads=[b_PnGf], writes=[b_PnG])
                    ipT, bipT = ps_next()
                    for sc in range(2):
                        for pc in range(4):
                            S.op("pe", lambda e: e.matmul(ipT[:, 2 * sc:2 * sc + 2], lhsT=mcsS[:, pc, sc, :], rhs=PnG[:, pc, :], start=(pc == 0), stop=(pc == 3)),
                                 reads=[b_stab, b_PnG], writes=[bipT])
                    S.op("dve", lambda e: e.tensor_copy(out=tpin[:, :, 0:2], in_=ipT[:, 0:4].rearrange("p (s g) -> p s g", s=2)), reads=[bipT], writes=[b_tpin])
                    tp, btp = ps_next()
                    for sc in range(2):
                        S.op("pe", lambda e: e.transpose(tp[:, sc * 128:(sc + 1) * 128], tpin[:, sc, :], ident[:]), reads=[b_tpin, b_ident], writes=[btp])
                    S.op("dve", lambda e: e.tensor_tensor(out=impS[:], in0=tp[:, 0:256], in1=tas[:, 0, :], op=ALU.mult), reads=[btp, b_stab], writes=[b_impS])
                    S.op("dve", lambda e: e.tensor_tensor(out=impS[:], in0=impS[:], in1=tas[:, 1, :], op=ALU.add), reads=[b_impS, b_stab], writes=[b_impS])
                    S.op("dve", lambda e: e.max(out=m8s[:, 0:8], in_=impS[:]), reads=[b_impS], writes=[b_impS])
                    S.op("dve", lambda e: e.match_replace(out=impS2[:], in_to_replace=m8s[:, 0:8], in_values=impS[:], imm_value=-2.0), reads=[b_impS], writes=[b_impS])
                    S.op("dve", lambda e: e.max(out=m8s[:, 8:16], in_=impS2[:]), reads=[b_impS], writes=[b_impS])
                    S.op("dve", lambda e: e.tensor_scalar(out=thrs[:], in0=m8s[:, 15:16], scalar1=-0.5, scalar2=None, op0=ALU.max), reads=[b_impS], writes=[b_impS])
                    S.op("dve", lambda e: e.tensor_scalar(out=impS2[:], in0=impS[:], scalar1=thrs[:, 0:1], scalar2=BIGV, op0=ALU.is_ge, op1=ALU.mult), reads=[b_impS], writes=[b_impS])
                    S.op("dve", lambda e: e.tensor_scalar(out=nbS[:], in0=impS2[:], scalar1=-BIGV, scalar2=None, op0=ALU.add), reads=[b_impS], writes=[b_nbS])

                    def s_front(bias_l, bias_b, kt, ktb):
                        pp, bpp = ps_next()
                        if bias_l is not None:
                            S.op("pe", lambda e: e.matmul(pp[:, 0:16], lhsT=bias_l, rhs=brhs, start=True, stop=False), reads=bias_b + [b_stab], writes=[bpp])
                        S.op("pe", lambda e: e.matmul(pp[:, 0:16], lhsT=kt, rhs=QRS[:], start=(bias_l is None), stop=True), reads=ktb + [b_QS], writes=[bpp])
                        z = ps_c[0] % 3
                        ps_c[0] += 1
                        S.op("act", lambda e: e.activation(out=PsS[z][:], in_=pp[:, 0:16], func=AF.Exp, scale=0.125), reads=[bpp], writes=[b_PsS[z]])
                        return z

                    def s_pv(z, v0, v1, vb, first, last):
                        for g, vv in enumerate((v0, v1)):
                            S.op("pe", lambda e: e.matmul(acc_ps[:, 512 + 8 * g:512 + 8 * g + 8], lhsT=vv, rhs=PsS[z][:, 8 * g:8 * g + 8],
                                                          start=(first and g == 0), stop=(last and g == 1)),
                                 reads=vb + [b_PsS[z]], writes=[b_acc[1]])

                    def gather(kc):
                        gz = gi_c[0] % 4
                        gi_c[0] += 1
                        S._deps("pool", [b_idxS], [b_gb[gz]])
                        tk = S.dma_ind(gb[gz][:], c_slc, idxS[:, i, kc:kc + 1])
                        S._mark(tk, [b_idxS], [b_gb[gz]])
                        return gz

                    def slc_front_s(kc, gz):
                        if kc == 64:
                            z = s_front(b64[:, i, :], [], KTn[:, 0, :], [b_KTn])
                            return (z, Vn[:, 0, 0, :], Vn[:, 0, 1, :], [b_Vn])
                        zz = ne_c[0] % 4
                        ne_c[0] += 1
                        tq, btq = ps_next()
                        S.op("pe", lambda e: e.transpose(tq[:, 0:128], gb[gz][:, 0:128], ident[:]), reads=[b_gb[gz], b_ident], writes=[btq])
                        kz = kt_c[0] % 3
                        kt_c[0] += 1
                        S.op("act", lambda e: e.copy(out=KTc[kz][:], in_=tq[:, 0:128]), reads=[btq], writes=[b_KTc[kz]])
                        vz = 2 + (kc % 4)
                        S.op("dve", lambda e: e.tensor_copy(out=slcV[:, vz, :, 0:64], in_=gb[gz][:, 128:256].rearrange("p (g d) -> p g d", g=2)),
                             reads=[b_gb[gz]], writes=[b_Vc[kc % 4]])
                        S.op("dve", lambda e: e.tensor_copy(out=nbe[zz][:].rearrange("p (s k) -> p s k", s=2),
                                                             in_=nbS[:, 2 * kc:2 * kc + 2].unsqueeze(2).broadcast_to([128, 2, 64])), reads=[b_nbS], writes=[b_nbe[zz]])
                        z = s_front(nbe[zz][:], [b_nbe[zz]], KTc[kz][:], [b_KTc[kz]])
                        return (z, slcV[:, vz, 0, :], slcV[:, vz, 1, :], [b_Vc[kc % 4]])

                    gq = [gather(kc) for kc in range(3)]
                    cur = slc_front_s(0, gq[0])
                    for kc in range(65):
                        if kc + 3 < 64:
                            gq.append(gather(kc + 3))
                        nxt = slc_front_s(kc + 1, gq[kc + 1] if kc + 1 < 64 else None) if kc < 64 else None
                        s_pv(cur[0], cur[1], cur[2], cur[3], kc == 0, kc == 64)
                        cur = nxt
                    S.op("dve", lambda e: e.reciprocal(out=rlS2[:, 0, :], in_=acc_ps[64:128, 512:528]), reads=[b_acc[1]], writes=[b_ocS])
                    S.op("dve", lambda e: e.tensor_tensor(out=osS[:, 0, :], in0=acc_ps[0:64, 512:528], in1=rlS2[:, 0, :], op=ALU.mult), reads=[b_acc[1], b_ocS], writes=[b_ocS])
                    S.dma("sp", wbuf[:], c_win[i].rearrange("(c p) f -> p c f", p=128), writes=[b_wbuf])

                    def win_front_s(wc):
                        if wc == 4:
                            z = s_front(b64[:, i, :], [], KTn[:, 1, :], [b_KTn])
                            return (z, Vn[:, 1, 0, :], Vn[:, 1, 1, :], [b_Vn])
                        tq, btq = ps_next()
                        S.op("pe", lambda e: e.transpose(tq[:, 0:128], wbuf[:, wc, 0:128], ident[:]), reads=[b_wbuf, b_ident], writes=[btq])
                        kz = kt_c[0] % 3
                        kt_c[0] += 1
                        S.op("act", lambda e: e.copy(out=KTc[kz][:], in_=tq[:, 0:128]), reads=[btq], writes=[b_KTc[kz]])
                        vz = 2 + wc
                        S.op("dve", lambda e: e.tensor_copy(out=slcV[:, vz, :, 0:64], in_=wbuf[:, wc, 128:256].rearrange("p (g d) -> p g d", g=2)),
                             reads=[b_wbuf], writes=[b_Vc[wc]])
                        z = s_front(wb0 if wc == 0 else None, [], KTc[kz][:], [b_KTc[kz]])
                        return (z, slcV[:, vz, 0, :], slcV[:, vz, 1, :], [b_Vc[wc]])
                    cur = win_front_s(0)
                    for wc in range(5):
                        nxt = win_front_s(wc + 1) if wc < 4 else None
                        s_pv(cur[0], cur[1], cur[2], cur[3], wc == 0, wc == 4)
                        cur = nxt
                    S.op("dve", lambda e: e.reciprocal(out=rlS2[:, 1, :], in_=acc_ps[64:128, 512:528]), reads=[b_acc[1]], writes=[b_ocS])
                    S.op("dve", lambda e: e.tensor_tensor(out=osS[:, 1, :], in0=acc_ps[0:64, 512:528], in1=rlS2[:, 1, :], op=ALU.mult), reads=[b_acc[1], b_ocS], writes=[b_ocS])
                    Gv = GS[0:64, :, i].rearrange("p (h r) -> p h r", r=3)
                    S.op("dve", lambda e: e.tensor_tensor(out=oS[:], in0=ocS[:], in1=Gv[:, :, 0], op=ALU.mult), reads=[b_ocS, b_GS], writes=[b_oS])
                    for br in (1, 2):
                        S.op("dve", lambda e: e.tensor_tensor(out=tS[:], in0=osS[:, br - 1, :], in1=Gv[:, :, br], op=ALU.mult), reads=[b_ocS, b_GS, b_oS], writes=[b_oS])
                        S.op("dve", lambda e: e.tensor_tensor(out=oS[:], in0=oS[:], in1=tS[:], op=ALU.add), reads=[b_oS], writes=[b_oS])
                    S.op("dve", lambda e: e.tensor_copy(out=oT[0:64, :, col], in_=oS[:, 0:16:2]), reads=[b_oS], writes=[b_oT])
                    S.op("dve", lambda e: e.tensor_copy(out=oT[64:128, :, col], in_=oS[:, 1:16:2]), reads=[b_oS], writes=[b_oT])
            for oc in range(8):
                if oc % 4 == 0:
                    pnl, bpn = st.get()
                pt, bp = ps_next()
                for k in range(8):
                    S.op("pe", lambda e: e.matmul(pt[:, 0:NWv[0]], lhsT=pnl[:, k, (oc % 4) * 128:(oc % 4 + 1) * 128], rhs=oT[:, k, 0:NWv[0]], start=(k == 0), stop=(k == 7)),
                         reads=[bpn, b_oT], writes=[bp])
                S.op("dve", lambda e: e.tensor_tensor(out=x[:, oc, :], in0=x[:, oc, :], in1=pt[:], op=ALU.add), reads=[b_x[oc], bp], writes=[b_x[oc]])

        def final_out(ti, es2, smp=False):
            uid[0] += 1
            yf = es2.enter_context(nc.sbuf_tensor(f"s{uid[0]}_yf", [128, 8, 512], F32)); b_yf = Buf()
            rmsnorm_f32(12, yf, b_yf)
            if smp:
                uid[0] += 1
                yo = es2.enter_context(nc.sbuf_tensor(f"s{uid[0]}_yo", [128, 8, 4], F32)); b_yo = Buf()
                S.op("dve", lambda e: e.tensor_copy(out=yo[:], in_=yf[:, :, 3:19:4]), reads=[b_yf], writes=[b_yo])
                S.dma("sp", o_s_y, yo[:], reads=[b_yo])
                return
            for blk in range(4):
                for kh in range(2):
                    pt, bp = ps_next()
                    for k4 in range(4):
                        k = kh * 4 + k4
                        S.op("pe", lambda e: e.transpose(pt[:, k4 * 128:(k4 + 1) * 128], yf[:, k, blk * 128:(blk + 1) * 128], ident[:]),
                             reads=[b_yf, b_ident], writes=[bp])
                    S.op("act", lambda e: e.copy(out=xt[:, blk, kh * 512:(kh + 1) * 512], in_=pt[:]), reads=[bp], writes=[b_xt])
            r0 = (ti - OWN0) * 512
            S.dma("sp", o_y[r0:r0 + 512, :].rearrange("(b p) f -> p b f", p=128), xt[:], reads=[b_xt])

        def rmsnorm_f32(gi, yf, b_yf):
            for k in range(8):
                S.op("act", lambda e: e.activation(out=sq[:, k, :], in_=x[:, k, :], func=AF.Square), reads=[b_x[k]], writes=[b_sq])
            pt, bp = ps_next()
            for k in range(8):
                S.op("pe", lambda e: e.matmul(pt[:], lhsT=ones_bf[:], rhs=sq[:, k, :], start=(k == 0), stop=(k == 7)),
                     reads=[b_sq, b_ones], writes=[bp])
            S.op("act", lambda e: e.activation(out=rstd[:], in_=pt[:], func=AF.Sqrt, bias=epsb[:, 0:1]), reads=[bp, b_eps], writes=[b_rstd])
            S.op("dve", lambda e: e.reciprocal(out=rstd[:], in_=rstd[:]), reads=[b_rstd], writes=[b_rstd])
            for k in range(8):
                S.op("dve", lambda e: e.scalar_tensor_tensor(out=yf[:, k, :], in0=x[:, k, :], scalar=gvec[:, gi, k:k + 1], in1=rstd[:],
                                                              op0=ALU.mult, op1=ALU.mult),
                     reads=[b_x[k], b_gvec, b_rstd], writes=[b_yf])


        def compress_round(sb2, cT, b_cT, P0, kdst, vdst, w2vs, pre=None):
            if pre is None:
                hid0 = sb2("hid0", [128, 32], BF16); b_hid0 = Buf()
                hidp = sb2("hidp", [128, 128], BF16); b_hidp = Buf()
            else:
                hid0, b_hid0, hidp, b_hidp = pre
            off = P0 % 128
            stc = Stream([(W["w1_00"], 0), (W["w1_01"], 0), (W["w1_10"], 0), (W["w1_11"], 0)], depth=2)
            for jv in range(2):
                for g in range(2):
                    pnl, bpn = stc.get()
                    pt, bp = ps_next()
                    for l_ in range(32):
                        S.op("pe", lambda e: e.matmul(pt[:, 0:32], lhsT=pnl[:, l_, :], rhs=cT[:, jv, l_:l_ + 497:16], start=(l_ == 0), stop=(l_ == 31)),
                             reads=[bpn, b_cT], writes=[bp])
                    if jv == 0:
                        S.op("act", lambda e: e.activation(out=hid0[:], in_=pt[:, 0:32], func=AF.Gelu, bias=cb1[:, 0:1]), reads=[bp, b_cb1], writes=[b_hid0])
                        pt2, bp2 = ps_next()
                        S.op("pe", lambda e: e.matmul(pt2[:, 0:32], lhsT=w2k[:], rhs=hid0[:], start=True, stop=True), reads=[b_w2, b_hid0], writes=[bp2])
                        kdst(g, pt2, bp2)
                    else:
                        S.op("dve", lambda e: e.memset(hidp[:], 0.0), writes=[b_hidp])
                        S.op("act", lambda e: e.activation(out=hidp[:, off:off + 32], in_=pt[:, 0:32], func=AF.Gelu, bias=cb1[:, 1:2]), reads=[bp, b_cb1], writes=[b_hidp])
                        pt2, bp2 = ps_next()
                        S.op("pe", lambda e: e.matmul(pt2[:, 0:128], lhsT=hidp[:], rhs=w2vs[g], start=True, stop=True), reads=[b_w2, b_hidp], writes=[bp2])
                        vdst(g, pt2, bp2)

        kvsw = Wt(nc, "kvsw", None, D, 256, 256)
        kvsw.name = "kvsw"
        W["kvsw"] = kvsw

        def kvswjob():
            tks = []
            with nc.allow_non_contiguous_dma(reason="one-time rope column swap"):
                for c, cb in enumerate((256, 512)):
                    for k in range(8):
                        srcv = w_kv[k * 128:(k + 1) * 128, cb:cb + 128].rearrange("p (g hh dd) -> p g hh dd", g=2, hh=2)
                        dstv = kvsw.dst[0].rearrange("p (k c g hh dd) -> p k c g hh dd", k=8, c=2, g=2, hh=2)
                        for hh in range(2):
                            tks.append(S.dma("pool", dstv[:, k, c, :, hh, :], srcv[:, :, 1 - hh, :]))
            kvsw.buf.w = tks
        jobs["kvsw"] = kvswjob

        def kv_gen(ti, es2, smp=False):
            def sb2(name, shape, dt=F32):
                uid[0] += 1
                return es2.enter_context(nc.sbuf_tensor(f"s{uid[0]}_{name}", list(shape), dt))
            kvf = sb2("kvf", [128, 6, 512]); b_kvf = [Buf() for _ in range(6)]
            rc_t = sb2("rc_t", [128, 512]); rs_t = sb2("rs_t", [128, 512]); b_rope = Buf()
            rt1 = sb2("rt1", [128, 512]); b_rt1 = Buf()
            S.dma("sp", rc_t[:], ropec_s if smp else ropec[:, ti * 512:(ti + 1) * 512], writes=[b_rope])
            S.dma("sp", rs_t[:], ropes_s if smp else ropes[:, ti * 512:(ti + 1) * 512], writes=[b_rope])
            rmsnorm(13, h, b_h)
            st = Stream([(W["kv"], 0), (W["kv"], 1), (W["kv"], 2), (W["kvsw"], 0)], depth=3)
            pk = [st.get(), st.get(), st.get()]
            psw, bpsw = st.get()
            for c in range(6):
                pnl, bpn = pk[c // 2]
                pt, bp = proj_chunk(pnl, bpn, (c % 2) * 128, 128, h, b_h, 8)
                if c in (2, 4):
                    pt2, bp2 = proj_chunk(psw, bpsw, (c // 2 - 1) * 128, 128, h, b_h, 8)
                    S.op("dve", lambda e: e.tensor_tensor(out=kvf[:, c, :], in0=pt[:], in1=rc_t[:], op=ALU.mult), reads=[bp, b_rope], writes=[b_kvf[c]])
                    S.op("dve", lambda e: e.tensor_tensor(out=rt1[:], in0=pt2[:], in1=rs_t[:], op=ALU.mult), reads=[bp2, b_rope], writes=[b_rt1])
                    S.op("dve", lambda e: e.tensor_tensor(out=kvf[:, c, :], in0=kvf[:, c, :], in1=rt1[:], op=ALU.add), reads=[b_kvf[c], b_rt1], writes=[b_kvf[c]])
                else:
                    S.op("act", lambda e: e.copy(out=kvf[:, c, :], in_=pt[:]), reads=[bp], writes=[b_kvf[c]])
            if smp:
                S.op("dve", lambda e: e.memset(kvf[:, :, 16:128], 0.0), reads=[], writes=b_kvf)
                kvo = sb2("kvo", [128, 6, 4]); b_kvo = Buf()
                S.op("dve", lambda e: e.tensor_copy(out=kvo[:], in_=kvf[:, :, 3:19:4]), reads=b_kvf, writes=[b_kvo])
                S.dma("sp", o_s_kv, kvo[:], reads=[b_kvo])
                S.op("dve", lambda e: e.tensor_copy(out=KTn[:, 0, :], in_=kvf[:, 2, 0:128]), reads=[b_kvf[2]], writes=[b_KTn])
                S.op("dve", lambda e: e.tensor_copy(out=KTn[:, 1, :], in_=kvf[:, 4, 0:128]), reads=[b_kvf[4]], writes=[b_KTn])
                for ci, c in enumerate((3, 5)):
                    pt, bp = ps_next()
                    S.op("pe", lambda e: e.transpose(pt[:, 0:128], kvf[:, c, 0:128], ident[:]), reads=[b_kvf[c], b_ident], writes=[bp])
                    S.op("dve", lambda e: e.tensor_copy(out=Vn[:, ci, :, 0:64], in_=pt[:, 0:128].rearrange("p (g d) -> p g d", g=2)), reads=[bp], writes=[b_Vn])
                return
            if ti >= OWN0 or DO_B:
                for blk in range(4):
                    for hf in range(2):
                        pt, bp = ps_next()
                        ncn = 4 if hf == 0 else 2
                        for c4 in range(ncn):
                            c = hf * 4 + c4
                            S.op("pe", lambda e: e.transpose(pt[:, c4 * 128:(c4 + 1) * 128], kvf[:, c, blk * 128:(blk + 1) * 128], ident[:]),
                                 reads=[b_kvf[c], b_ident], writes=[bp])
                        S.op("act", lambda e: e.copy(out=xt[:, blk, hf * 512:hf * 512 + ncn * 128], in_=pt[:, 0:ncn * 128]), reads=[bp], writes=[b_xt])
                        if DO_B and DBG >= 2 and not cfg.get("NO_V"):
                            kch = 4 * ti + blk
                            if hf == 0:
                                S.op("dve", lambda e: e.tensor_copy(out=slcV[:, kch, :, 0:64], in_=xt[:, blk, 384:512].rearrange("p (g d) -> p g d", g=2)),
                                     reads=[b_xt], writes=[b_slcV])
                            else:
                                S.op("dve", lambda e: e.tensor_copy(out=winV[:, kch % 8, :, 0:64], in_=xt[:, blk, 640:768].rearrange("p (g d) -> p g d", g=2)),
                                     reads=[b_xt], writes=[b_winV])
            if DO_B and DBG >= 2 and not cfg.get("NO_K"):
                c0 = ti * 512
                r0w = (ti % 2) * 512
                for g in range(2):
                    for hfp in range(2):
                        S.op("dve", lambda e: e.tensor_copy(out=slcK[hfp * 64:(hfp + 1) * 64, g, c0:c0 + 512], in_=kvf[g * 64:(g + 1) * 64, 2, :]),
                             reads=[b_kvf[2]], writes=[b_slcK])
                        S.op("dve", lambda e: e.tensor_copy(out=winK[hfp * 64:(hfp + 1) * 64, g, r0w:r0w + 512], in_=kvf[g * 64:(g + 1) * 64, 4, :]),
                             reads=[b_kvf[4]], writes=[b_winK])
            if DO_B and DBG >= 3:
                P0 = 32 * ti
                S.op("pool", lambda e: e.tensor_copy(out=cmpT[:, :, 0:16], in_=cmpT[:, :, 512:528]), reads=[b_cmpT], writes=[b_cmpT])
                for jv in range(2):
                    S.op("pool", lambda e: e.tensor_copy(out=cmpT[:, jv, 16:528], in_=kvf[:, jv, :]), reads=[b_kvf[jv], b_cmpT], writes=[b_cmpT])
                def kdst(g, pt2, bp2):
                    S.op("act", lambda e: e.copy(out=kcK[:, g, P0:P0 + 32], in_=pt2[:, 0:32]), reads=[bp2], writes=[b_kcK])

                def vdst(g, pt2, bp2):
                    pch = P0 // 128
                    S.op("dve", lambda e: e.tensor_tensor(out=vcV[:, pch, g, :], in0=vcV[:, pch, g, :], in1=pt2[:, 0:128], op=ALU.add),
                         reads=[bp2, b_vcV], writes=[b_vcV])
                    S.op("dve", lambda e: e.tensor_copy(out=vcVb[:, pch, g, :], in_=vcV[:, pch, g, :]), reads=[b_vcV], writes=[b_vcVb])
                compress_round(sb2, cmpT, b_cmpT, P0, kdst, vdst, [w2v[:], w2v[:]])
            if ti >= OWN0:
                r0 = (ti - OWN0) * 512
                for oi, od in enumerate((o_cmp, o_slc, o_win)):
                    S.dma("sp", od[r0:r0 + 512, :].rearrange("(b p) f -> p b f", p=128), xt[:, :, oi * 256:(oi + 1) * 256], reads=[b_xt])

        for l_ in range(NL_A):
            cast_order.extend([f"rgin{l_}", f"band{l_}_0", f"band{l_}_1", f"rgout{l_}", f"up{l_}", f"down{l_}", f"plein{l_}", f"pleg{l_}"])
        cast_order.extend(["kv", "kvsw"])
        if DO_B:
            cast_order.extend(["w1_00", "w1_01", "w1_10", "w1_11"])
            for j_ in range(2):
                cast_order.extend([f"q{j_}", f"qsw{j_}", f"qg{j_}", f"wo{j_}", f"up{2 + j_}", f"down{2 + j_}", f"plein{2 + j_}", f"pleg{2 + j_}"])
        for ti in range(NT):
            transpose_in(xs[ti * 512:(ti + 1) * 512, :], 1024, x, b_x)
            for l in range(NL_A):
                with ExitStack() as es2:
                    rg_layer(l, ti, es2)
                    S.barrier()
                with ExitStack() as es2:
                    ffn_ple(l, ti, es2)
                    S.barrier()
            with ExitStack() as es2:
                kv_gen(ti, es2)
                S.barrier()
            if DO_B and ti >= OWN0 - 1:
                for l in (2, 3):
                    if DBG >= 4:
                        with ExitStack() as es2:
                            attn_layer(l, ti, es2, [3] if ti == OWN0 - 1 else [0, 1, 2, 3])
                            S.barrier()
                    with ExitStack() as es2:
                        ffn_ple(l, ti, es2)
                        S.barrier()
            if ti >= OWN0:
                with ExitStack() as es2:
                    final_out(ti, es2)
                    S.barrier()

        with nc.allow_non_contiguous_dma(reason="small state outputs"):
            for l in range(2):
                for k in range(3):
                    S.dma("sp", o_rgc[l, k].rearrange("(c p) -> p c", p=128), rg_hist[:, l, :, k], reads=b_rghist[l])
                S.dma("sp", o_rgh[l].rearrange("(c p) -> p c", p=128), rg_hc[:, l, :], reads=b_rghc[l])
            for l in range(4):
                for k in range(2):
                    S.dma("sp", o_ffc[l, k].rearrange("(c p) -> p c", p=128), ff_hist[:, l, :, k], reads=b_ffhist[l])

        if SMP:
            S.barrier()
            with ExitStack() as es3:
                def sb3(name, shape, dt=F32):
                    uid[0] += 1
                    return es3.enter_context(nc.sbuf_tensor(f"s{uid[0]}_{name}", list(shape), dt))
                assert NTOK >= 2048
                flatK = slcK[:].rearrange("p g t -> p (g t)")
                flatV = slcV[:].rearrange("p c g f -> p (c g f)")
                kcS = flatK[:, 0:2048].rearrange("p (b t) -> p b t", b=4)
                vcS = flatK[:, 2048:4096].rearrange("p (b c f) -> p b c f", b=4, c=4)
                b_kcS = Buf(); b_vcS = Buf(); b_KTn = Buf()
                Vn = slcV[:, 0:2, :, :]; b_Vn = Buf()
                b_Vc = [Buf() for _ in range(4)]
                vo = [1536]

                def vview(n):
                    a = vo[0]
                    vo[0] += n
                    return flatV[:, a:a + n]
                mcsS = vview(1024).rearrange("p (a b c) -> p a b c", a=4, b=2)
                b64 = vview(512).rearrange("p (a c) -> p a c", a=4)
                KTn = vview(256).rearrange("p (a t) -> p a t", a=2)
                cb0 = vview(128); wb0 = vview(128); w2v1 = vview(128); brhs = vview(16)
                xtf = xt[:].rearrange("p b f -> p (b f)")
                tas = xtf[:, 0:512].rearrange("p (a s) -> p a s", a=2)
                b_stab = Buf()
                S.op("pool", lambda e: e.memset(w2v1, 0.0), writes=[b_stab])
                S.dma("pool", w2v1[:, 64:128], cmp_w2[1], writes=[b_stab])
                S.dma("pool", cb0, t_cb0, writes=[b_stab]); S.dma("pool", brhs, t_brhs, writes=[b_stab])
                S.dma("pool", mcsS, t_mcs_s, writes=[b_stab]); S.dma("sp", tas, t_as, writes=[b_stab])
                S.dma("pool", b64, t_b64, writes=[b_stab]); S.dma("pool", wb0, t_wb0, writes=[b_stab])
                pgi = xtf[:, 512:768].bitcast(mybir.dt.int32); pgf = xtf[:, 768:1024]; pidf = sb3("pidf", [128, 1])
                idxS_t = xtf[:, 1024:1280].bitcast(mybir.dt.int32); b_idxS = Buf()
                idxS = idxS_t.rearrange("p (b g) -> p b g", b=4)
                S.dma("sp", pgi, pg_s.rearrange("b g -> (b g)").unsqueeze(0).broadcast_to([128, 256]), writes=[b_idxS])
                S.op("pool", lambda e: e.iota(pidf[:], pattern=[[0, 1]], base=0, channel_multiplier=1, allow_small_or_imprecise_dtypes=True), writes=[b_idxS])
                S.op("dve", lambda e: e.tensor_copy(out=pgf, in_=pgi), reads=[b_idxS], writes=[b_idxS])
                S.op("dve", lambda e: e.tensor_scalar(out=pgf, in0=pgf, scalar1=128.0, scalar2=pidf[:, 0:1], op0=ALU.mult, op1=ALU.add), reads=[b_idxS], writes=[b_idxS])
                S.op("dve", lambda e: e.tensor_copy(out=idxS_t, in_=pgf), reads=[b_idxS], writes=[b_idxS])
                cT = cmpT; b_cT = b_cmpT
                hid_pre = (sb3("hid0s", [128, 32], BF16), Buf(), sb3("hidps", [128, 128], BF16), Buf())
                vcSf = xtf[:, 1280:1792].rearrange("p (c f) -> p c f", c=4); b_vcSf = Buf()
                gbc = [xtf[:, 1792 + 256 * i:2048 + 256 * i] for i in range(4)]; b_gbc = [Buf() for _ in range(4)]
                for i in range(4):
                    S.op("pool", lambda e: e.memset(cT[:], 0.0), writes=[b_cT])
                    S.op("pool", lambda e: e.memset(vcSf, 0.0), writes=[b_vcSf])
                    for r in range(16):
                        for jq in range(4):
                            S._deps("pool", [b_idxS], [b_gbc[jq]])
                            tk = S.dma_ind(gbc[jq], c_cmp, idxS[:, i, 4 * r + jq:4 * r + jq + 1])
                            S._mark(tk, [b_idxS], [b_gbc[jq]])
                            tq, btq = ps_next()
                            for jv in range(2):
                                S.op("pe", lambda e: e.transpose(tq[:, jv * 128:(jv + 1) * 128], gbc[jq][:, jv * 128:(jv + 1) * 128], ident[:]),
                                     reads=[b_gbc[jq], b_ident], writes=[btq])
                            S.op("act", lambda e: e.copy(out=cT[:, :, 16 + 128 * jq:16 + 128 * (jq + 1)], in_=tq[:, 0:256].rearrange("p (j t) -> p j t", j=2)),
                                 reads=[btq], writes=[b_cT])
                        P0 = 32 * r

                        def kdst(g, pt2, bp2):
                            S.op("act", lambda e: e.copy(out=kcS[64 * g:64 * g + 64, i, P0:P0 + 32], in_=pt2[64 * g:64 * g + 64, 0:32]), reads=[bp2], writes=[b_kcS])

                        def vdst(g, pt2, bp2):
                            pch = P0 // 128
                            S.op("dve", lambda e: e.tensor_tensor(out=vcSf[:, pch, :], in0=vcSf[:, pch, :], in1=pt2[:, 0:128], op=ALU.add),
                                 reads=[bp2, b_vcSf], writes=[b_vcSf])
                        compress_round(None, cT, b_cT, P0, kdst, vdst, [w2v[:], w2v1], pre=hid_pre)
                        S.op("pool", lambda e: e.tensor_copy(out=cT[:, :, 0:16], in_=cT[:, :, 512:528]), reads=[b_cT], writes=[b_cT])
                    S.op("dve", lambda e: e.tensor_copy(out=vcS[:, i, :, :], in_=vcSf), reads=[b_vcSf], writes=[b_vcS])
                S.dma("sp", x[:], xs_s, writes=b_x)
                NWv[0] = 16
                for l in range(2):
                    with ExitStack() as es2:
                        rg_layer(l, -1, es2, smp=True)
                        S.barrier()
                    with ExitStack() as es2:
                        ffn_ple(l, -1, es2, smp=True)
                        S.barrier()
                with ExitStack() as es2:
                    kv_gen(-1, es2, smp=True)
                    S.barrier()
                for l in (2, 3):
                    with ExitStack() as es2:
                        attn_layer(l, -1, es2, [], smp=True)
                        S.barrier()
                    with ExitStack() as es2:
                        ffn_ple(l, -1, es2, smp=True)
                        S.barrier()
                with ExitStack() as es2:
                    final_out(-1, es2, smp=True)
                    S.barrier()
                with nc.allow_non_contiguous_dma(reason="state passthrough"):
                    for i in range(4):
                        S.dma("sp", o_s_win[i], c_win[i, 1:512, :])
                    S.dma("sp", o_s_rgc_old, st_rgc[:, :, 1:3, :])
                    S.dma("sp", o_s_ffc_old, st_ffc[:, :, 1, :])
        S.finish()
    return nc

_WNAMES = ['g_mix', 'g_ffn', 'g_ple', 'g_final', 'g_kv', 'rg_w_in', 'rg_conv_w', 'rg_conv_b', 'rg_w_a', 'rg_w_x', 'rg_lambda',
           'rg_w_out', 'w_kv', 'ffn_w_up', 'ffn_conv_w', 'ffn_conv_b', 'ffn_w_down', 'ple_w_in', 'ple_w_gate',
           'attn_w_qg', 'attn_w_o', 'cmp_pos', 'cmp_w1', 'cmp_b1', 'cmp_w2']


def make_tables(half, NT, OWN0):
    BIG = 30000.0
    NTOK = NT * 512
    pre = OWN0 * 512
    nown = NTOK - pre + 128
    nqb = nown // 128
    i = np.arange(nown) - 128
    tau = pre + i
    hv = np.where(i < 0, 1, half)[:, None]
    P = np.arange(256)
    c = P - 1
    vc = (P >= 1)[None, :] & ((hv == 1) | (16 * c >= pre)[None, :])
    ok = vc & ((16 * c + 31)[None, :] <= tau[:, None])
    t_cb = np.where(ok, 0.0, -BIG).astype(np.float32).reshape(nqb, 128, 256)
    kk = np.arange(640)
    tau0 = pre + 128 * (i // 128)
    kap = tau0[:, None] - 512 + kk[None, :]
    dist = tau[:, None] - kap
    okw = (dist >= 0) & (dist < 512) & (kap >= 0) & ((hv == 1) | (kap >= pre))
    t_wb = np.where(okw, 0.0, -BIG).astype(np.float32).reshape(nqb, 128, 640)
    sblk = np.arange(64)
    vb = ((hv == 1) | (64 * sblk >= pre)[None, :]) & (64 * sblk < NTOK)[None, :]
    V = (vb & ((64 * sblk)[None, :] <= tau[:, None])).astype(np.float32)
    blk0 = np.where(hv[:, 0] == 1, 0, pre // 64)
    cur = tau // 64
    forced = (sblk[None, :] == blk0[:, None]) | (sblk[None, :] == cur[:, None]) | (sblk[None, :] == (cur - 1)[:, None])
    A = 100.0 * forced.astype(np.float32) * V + (V - 1.0)
    t_va = np.stack([V, A], axis=1).astype(np.float32).reshape(nqb, 128, 2, 64)
    Pm = np.arange(256)
    c0 = 16 * (Pm - 1)
    s0 = 64 * sblk
    m = ((c0[:, None] < s0[None, :] + 64) & (c0[:, None] + 32 > s0[None, :]) & (Pm[:, None] >= 1)).astype(np.float32)
    t_mcs = np.ascontiguousarray(m.reshape(2, 128, 64).transpose(1, 0, 2))
    t_tri = np.where(np.arange(128)[None, :] > np.arange(128)[:, None], -BIG, 0.0).astype(np.float32)
    return dict(t_cb=t_cb, t_wb=t_wb, t_va=t_va, t_mcs=t_mcs, t_tri=t_tri)


def core_inputs(inp, b, half, NT=8, OWN0=4):
    NTOK = NT * 512
    pre = OWN0 * 512
    own = NTOK - pre
    if half == 1:
        xs = inp['x_prompt'][b, :NTOK]
        ps = inp['p_prompt'][:, b, :NTOK]
        pos = np.arange(NTOK)
    else:
        xs = np.concatenate([inp['x_prompt'][b, :pre], inp['x_prompt'][b, :own]], 0)
        ps = np.concatenate([inp['p_prompt'][:, b, :pre], inp['p_prompt'][:, b, :own]], 1)
        pos = np.concatenate([np.arange(pre), np.arange(own)])
    d = np.arange(128) % 64
    inv = (10000.0 ** (-(d % 32).astype(np.float32) / 32)).astype(np.float32)
    ang = pos[None, :].astype(np.float32) * inv[:, None]
    sgn = np.where(d < 32, -1.0, 1.0)[:, None]
    m = dict(xs=np.ascontiguousarray(xs, dtype=np.float32), ps=np.ascontiguousarray(ps, dtype=np.float32),
             flag=np.tile(np.array([[half, 1 - half]], np.float32), (128, 1)),
             ident=np.eye(128, dtype=np.float32),
             ropec=np.cos(ang).astype(np.float32), ropes=(np.sin(ang) * sgn).astype(np.float32))
    for k in _WNAMES:
        m[k] = np.ascontiguousarray(inp[k], dtype=np.float32)
    m['rg_b_a'] = np.ascontiguousarray(inp['rg_b_a'], dtype=np.float32).reshape(2, 1280)
    m['rg_b_x'] = np.ascontiguousarray(inp['rg_b_x'], dtype=np.float32).reshape(2, 1280)
    m.update(make_tables(half, NT, OWN0))
    return m


def sample_inputs(inp, c):
    BIG = 30000.0
    b0 = 4 * c
    m = {}
    xs = np.zeros((128, 8, 512), np.float32)
    ps = np.zeros((4, 128, 2, 512), np.float32)
    for i in range(4):
        xs[:, :, 4 * i + 3] = inp['x_sample'][b0 + i, 0].reshape(8, 128).T
        for l in range(4):
            ps[l, :, :, 4 * i + 3] = inp['p_sample'][l, b0 + i, 0].reshape(2, 128).T
    m['xs_s'] = xs
    m['ps_s'] = ps
    rgc = inp['state_rg_conv'][:, b0:b0 + 4]
    m['s_rgh'] = np.ascontiguousarray(rgc.reshape(2, 4, 3, 10, 128).transpose(4, 0, 3, 1, 2), dtype=np.float32)
    rgh = inp['state_rg_h'][:, b0:b0 + 4]
    m['s_h0'] = np.ascontiguousarray(rgh.reshape(2, 4, 10, 128).transpose(3, 0, 2, 1), dtype=np.float32)
    ffc = inp['state_ffn_conv'][:, b0:b0 + 4]
    m['s_ffh'] = np.ascontiguousarray(ffc.reshape(4, 4, 2, 48, 128).transpose(4, 0, 3, 1, 2), dtype=np.float32)
    d = np.arange(128) % 64
    inv = (10000.0 ** (-(d % 32).astype(np.float32) / 32)).astype(np.float32)
    ang = np.full((1, 512), 8192.0, np.float32) * inv[:, None]
    sgn = np.where(d < 32, -1.0, 1.0)[:, None]
    m['ropec_s'] = np.cos(ang).astype(np.float32)
    m['ropes_s'] = (np.sin(ang) * sgn).astype(np.float32)
    m['pg_s'] = np.ascontiguousarray(inp['page_table'][b0:b0 + 4], dtype=np.int32)
    nphys = inp['cache_cmp_kv'].shape[0]
    m['cache_cmp_kv'] = np.ascontiguousarray(inp['cache_cmp_kv'], dtype=np.float32).reshape(nphys * 128, 256)
    m['cache_slc_kv'] = np.ascontiguousarray(inp['cache_slc_kv'], dtype=np.float32).reshape(nphys * 128, 256)
    m['c_win'] = np.ascontiguousarray(inp['cache_win_kv'][b0:b0 + 4], dtype=np.float32).reshape(4, 512, 256)
    m['st_rgc'] = np.ascontiguousarray(rgc, dtype=np.float32)
    m['st_ffc'] = np.ascontiguousarray(ffc, dtype=np.float32)
    P = np.arange(512)
    c0 = 16 * (P - 1)
    sb = np.arange(256)
    mm = ((c0[:, None] < 64 * sb[None, :] + 64) & (c0[:, None] + 32 > 64 * sb[None, :]) & (P[:, None] >= 1) & (sb[None, :] < 129)).astype(np.float32)
    m['t_mcs_s'] = np.ascontiguousarray(mm.reshape(4, 128, 2, 128).transpose(1, 0, 2, 3))
    V = (sb < 129).astype(np.float32)
    forced = ((sb == 0) | (sb == 128) | (sb == 127)).astype(np.float32)
    A = 100.0 * forced * V + (V - 1.0)
    m['t_as'] = np.ascontiguousarray(np.tile(np.stack([V, A])[None], (128, 1, 1)), dtype=np.float32)
    br = np.zeros((128, 16), np.float32); br[0, 0:8] = 1.0; br[1, 8:16] = 1.0
    m['t_brhs'] = br
    b64 = np.zeros((128, 4, 128), np.float32)
    for i in range(4):
        b64[0:2, i, :] = -BIG
        b64[0:2, i, 4 * i + 3] = 0.0
    m['t_b64'] = b64
    wb0 = np.zeros((128, 128), np.float32); wb0[0:2, 0] = -BIG
    m['t_wb0'] = wb0
    m['t_cb0'] = wb0.copy()
    return m


_BUILD_CACHE = {}


def kernel(**inputs):
    inp = {k: np.asarray(v) for k, v in inputs.items()}
    nphys = inp['cache_cmp_kv'].shape[0]
    nc = build(dict(NT=8, OWN0=4, NL_A=2, DO_B=True, SMP=True, NPHYS=nphys))
    in_maps = []
    for c in range(8):
        m = core_inputs(inp, c // 2, c % 2)
        m.update(sample_inputs(inp, c))
        in_maps.append(m)
    res = run_bass_kernel_spmd(nc, in_maps, core_ids=list(range(8)))
    R = res.results
    B, T = 4, 4096
    y_prompt = np.zeros((B, T, D), np.float32)
    cmp_p = np.zeros((B, T, 2, 2, 64), np.float32)
    slc_p = np.zeros((B, T, 2, 2, 64), np.float32)
    win_p = np.zeros((B, 512, 2, 2, 64), np.float32)
    rgc_p = np.zeros((2, B, 3, DR), np.float32)
    rgh_p = np.zeros((2, B, DR), np.float32)
    ffc_p = np.zeros((4, B, 2, 2 * DFF), np.float32)
    DB = 32
    y_sample = np.zeros((DB, 1, D), np.float32)
    cmp_s = np.zeros((DB, 1, 2, 2, 64), np.float32)
    slc_s = np.zeros((DB, 1, 2, 2, 64), np.float32)
    win_s = np.zeros((DB, 512, 2, 2, 64), np.float32)
    rgc_s = np.zeros((2, DB, 3, DR), np.float32)
    rgh_s = np.zeros((2, DB, DR), np.float32)
    ffc_s = np.zeros((4, DB, 2, 2 * DFF), np.float32)
    for c in range(8):
        b, half = c // 2, c % 2
        sl = slice(half * 2048, half * 2048 + 2048)
        y_prompt[b, sl] = R[c]['o_y']
        cmp_p[b, sl] = R[c]['o_cmp'].reshape(2048, 2, 2, 64)
        slc_p[b, sl] = R[c]['o_slc'].reshape(2048, 2, 2, 64)
        if half == 1:
            win_p[b] = R[c]['o_win'][-512:].reshape(512, 2, 2, 64)
            rgc_p[:, b] = R[c]['o_rgc']
            rgh_p[:, b] = R[c]['o_rgh']
            ffc_p[:, b] = R[c]['o_ffc']
        assemble_sample(R[c], c, y_sample, cmp_s, slc_s, win_s, rgc_s, rgh_s, ffc_s)
    return (y_prompt, y_sample, cmp_p, cmp_s, slc_p, slc_s, win_p, win_s, rgc_p, rgc_s, rgh_p, rgh_s, ffc_p, ffc_s)


def assemble_sample(r, c, y_sample, cmp_s, slc_s, win_s, rgc_s, rgh_s, ffc_s):
    b0 = 4 * c
    for i in range(4):
        b = b0 + i
        y_sample[b, 0] = r['o_s_y'][:, :, i].T.reshape(1024)
        kv = r['o_s_kv'][:, :, i]
        cmp_s[b, 0] = kv[:, 0:2].T.reshape(2, 2, 64)
        slc_s[b, 0] = kv[:, 2:4].T.reshape(2, 2, 64)
        win_s[b, 0:511] = r['o_s_win'][i].reshape(511, 2, 2, 64)
        win_s[b, 511] = kv[:, 4:6].T.reshape(2, 2, 64)
        for l in range(2):
            rgc_s[l, b, 0:2] = r['o_s_rgc_old'][l, i]
            rgc_s[l, b, 2] = r['o_s_rgc_new'][:, l, :, i].T.reshape(1280)
            rgh_s[l, b] = r['o_s_rgh'][:, l, :, i].T.reshape(1280)
        for l in range(4):
            ffc_s[l, b, 0] = r['o_s_ffc_old'][l, i]
            ffc_s[l, b, 1] = r['o_s_ffc_new'][:, l, :, i].T.reshape(6144)
```

```python
import numpy as np
import ml_dtypes
from contextlib import ExitStack
import concourse.bass as bass
import concourse.mybir as mybir
from concourse.bass_utils import run_bass_kernel_spmd

F32 = mybir.dt.float32
BF16 = mybir.dt.bfloat16
AF = mybir.ActivationFunctionType
ALU = mybir.AluOpType

D = 1024
NT_TOK = 512
DR = 1280
DFF = 3072
EPS = 1e-6


class Buf:
    __slots__ = ("w", "r")

    def __init__(self):
        self.w = None
        self.r = []


class Sched:
    ND = 12

    def __init__(self, nc, es):
        self.nc = nc
        self.E = dict(pe=nc.tensor, act=nc.scalar, dve=nc.vector, pool=nc.gpsimd, sp=nc.sync)
        self.sem = {}
        self.cnt = {}
        for e in ("pe", "act", "dve", "pool"):
            self.sem[e] = es.enter_context(nc.semaphore("s_" + e))
            self.cnt[e] = 0
        self.seen = {e: {} for e in self.E}
        self.dsem = {}
        self.dval = {}
        self.didx = {}
        self.NDQ = {"sp": 12, "pool": 40, "act": 4}
        for q in ("sp", "pool", "act"):
            self.dsem[q] = [es.enter_context(nc.semaphore(f"d_{q}{i}")) for i in range(self.NDQ[q])]
            self.dval[q] = [0] * self.NDQ[q]
            self.didx[q] = 0
        self.all_dma = []

    def _wait(self, e, tk):
        key, sem, val = tk
        if key == e and e == "pe":
            return
        if self.seen[e].get(key, 0) >= val:
            return
        self.E[e].wait_ge(sem, val)
        self.seen[e][key] = val

    def _deps(self, e, reads, writes):
        for b in reads:
            if b.w is not None:
                for tk in (b.w if isinstance(b.w, list) else [b.w]):
                    self._wait(e, tk)
        for b in writes:
            if b.w is not None:
                for tk in (b.w if isinstance(b.w, list) else [b.w]):
                    self._wait(e, tk)
            for tk in b.r:
                self._wait(e, tk)

    def _mark(self, tk, reads, writes):
        for b in reads:
            b.r.append(tk)
            if len(b.r) > 24:
                b.r = b.r[-24:]
        for b in writes:
            b.w = tk
            b.r = []

    def op(self, e, fn, reads=(), writes=()):
        self._deps(e, reads, writes)
        inst = fn(self.E[e])
        inst.then_inc(self.sem[e], 1)
        self.cnt[e] += 1
        tk = (e, self.sem[e], self.cnt[e])
        self._mark(tk, reads, writes)
        return tk

    def dma(self, q, out, in_, reads=(), writes=(), **kw):
        slot = self.didx[q] % self.NDQ[q]
        self.didx[q] += 1
        sem = self.dsem[q][slot]
        key = (q, slot)
        if self.dval[q][slot] > 0:
            self._wait(q, (key, sem, self.dval[q][slot]))
        self._deps(q, reads, writes)
        inst = self.E[q].dma_start(out=out, in_=in_, **kw)
        inst.then_inc(sem, 16)
        self.dval[q][slot] += 16
        tk = (key, sem, self.dval[q][slot])
        self._mark(tk, reads, writes)
        self.all_dma.append(tk)
        return tk

    def dma_ind(self, out, in_, idx_ap):
        q = "pool"
        slot = self.didx[q] % self.NDQ[q]
        self.didx[q] += 1
        sem = self.dsem[q][slot]
        key = (q, slot)
        if self.dval[q][slot] > 0:
            self._wait(q, (key, sem, self.dval[q][slot]))
        inst = self.nc.gpsimd.indirect_dma_start(out=out, out_offset=None, in_=in_, in_offset=bass.IndirectOffsetOnAxis(ap=idx_ap, axis=0))
        inst.then_inc(sem, 16)
        self.dval[q][slot] += 16
        return (key, sem, self.dval[q][slot])

    def barrier(self):
        tks = [(e, self.sem[e], self.cnt[e]) for e in self.cnt if self.cnt[e] > 0]
        for q in self.dsem:
            for slot in range(self.NDQ[q]):
                if self.dval[q][slot] > 0:
                    tks.append(((q, slot), self.dsem[q][slot], self.dval[q][slot]))
        for e in self.E:
            for tk in tks:
                if tk[0] == e and e == "pe":
                    continue
                self._wait(e, tk)

    def finish(self):
        for q in self.dsem:
            for slot in range(self.NDQ[q]):
                if self.dval[q][slot] > 0:
                    self._wait("sp", ((q, slot), self.dsem[q][slot], self.dval[q][slot]))


class Wt:
    def __init__(self, nc, name, src, K, M, MW):
        self.KC = K // 128
        self.MW = MW
        self.NP = M // MW
        assert self.KC * MW <= 4096
        self.src = src
        self.dst = nc.dram_tensor("wb_" + name, [self.NP, 128, self.KC * MW], BF16, kind="Internal").ap()
        self.buf = Buf()


def build(cfg):
    NT = cfg.get("NT", 8)
    OWN0 = cfg.get("OWN0", 4)
    NL_A = cfg.get("NL_A", 2)
    DO_B = cfg.get("DO_B", False)
    DBG = cfg.get("DBG", 99)
    NTOK = NT * NT_TOK
    NOWN = (NT - OWN0) * NT_TOK

    nc = bass.Bass("TRN2", target_bir_lowering=False)

    def din(name, shape, dt=F32):
        return nc.dram_tensor(name, list(shape), dt, kind="ExternalInput").ap()

    def dout(name, shape, dt=F32):
        return nc.dram_tensor(name, list(shape), dt, kind="ExternalOutput").ap()

    xs = din("xs", [NTOK, D])
    ps_in = din("ps", [4, NTOK, 256])
    flag = din("flag", [128, 2])
    ident_d = din("ident", [128, 128])
    ropec = din("ropec", [128, NTOK])
    ropes = din("ropes", [128, NTOK])
    g_mix = din("g_mix", [4, D]); g_ffn = din("g_ffn", [4, D]); g_ple = din("g_ple", [4, D])
    g_final = din("g_final", [D]); g_kv = din("g_kv", [D])
    rg_w_in = din("rg_w_in", [2, D, 2 * DR]); rg_conv_w = din("rg_conv_w", [2, 4, DR]); rg_conv_b = din("rg_conv_b", [2, DR])
    rg_w_a = din("rg_w_a", [2, 16, 80, 80]); rg_b_a = din("rg_b_a", [2, DR])
    rg_w_x = din("rg_w_x", [2, 16, 80, 80]); rg_b_x = din("rg_b_x", [2, DR])
    rg_lambda = din("rg_lambda", [2, DR]); rg_w_out = din("rg_w_out", [2, DR, D])
    w_kv = din("w_kv", [D, 768])
    ffn_w_up = din("ffn_w_up", [4, D, 2 * DFF]); ffn_conv_w = din("ffn_conv_w", [4, 3, 2 * DFF]); ffn_conv_b = din("ffn_conv_b", [4, 2 * DFF])
    ffn_w_down = din("ffn_w_down", [4, DFF, D])
    ple_w_in = din("ple_w_in", [4, 256, D]); ple_w_gate = din("ple_w_gate", [4, D, D])

    attn_w_qg = din("attn_w_qg", [2, D, 1072]); attn_w_o = din("attn_w_o", [2, D, D])
    cmp_pos = din("cmp_pos", [32, 2, 64]); cmp_w1 = din("cmp_w1", [2, 2048, 128]); cmp_b1 = din("cmp_b1", [2, 128]); cmp_w2 = din("cmp_w2", [2, 128, 64])
    NQB = NOWN // 128 + 1
    t_cb = din("t_cb", [NQB, 128, 256]); t_wb = din("t_wb", [NQB, 128, 640]); t_va = din("t_va", [NQB, 128, 2, 64])
    t_mcs = din("t_mcs", [128, 2, 64]); t_tri = din("t_tri", [128, 128])

    SMP = cfg.get("SMP", False)
    if SMP:
        xs_s = din("xs_s", [128, 8, 512]); ps_s = din("ps_s", [4, 128, 2, 512])
        s_rgh_d = din("s_rgh", [128, 2, 10, 4, 3]); s_h0_d = din("s_h0", [128, 2, 10, 4]); s_ffh_d = din("s_ffh", [128, 4, 48, 4, 2])
        ropec_s = din("ropec_s", [128, 512]); ropes_s = din("ropes_s", [128, 512])
        pg_s = din("pg_s", [4, 64], mybir.dt.int32)
        NPHYS = cfg.get("NPHYS", 2560)
        c_cmp = din("cache_cmp_kv", [NPHYS * 128, 256]); c_slc = din("cache_slc_kv", [NPHYS * 128, 256]); c_win = din("c_win", [4, 512, 256])
        st_rgc = din("st_rgc", [2, 4, 3, DR]); st_ffc = din("st_ffc", [4, 4, 2, 2 * DFF])
        t_mcs_s = din("t_mcs_s", [128, 4, 2, 128]); t_as = din("t_as", [128, 2, 256]); t_brhs = din("t_brhs", [128, 16])
        t_b64 = din("t_b64", [128, 4, 128]); t_wb0 = din("t_wb0", [128, 128]); t_cb0 = din("t_cb0", [128, 128])
        o_s_y = dout("o_s_y", [128, 8, 4]); o_s_kv = dout("o_s_kv", [128, 6, 4]); o_s_win = dout("o_s_win", [4, 511, 256])
        o_s_rgc_new = dout("o_s_rgc_new", [128, 2, 10, 4]); o_s_rgc_old = dout("o_s_rgc_old", [2, 4, 2, DR]); o_s_rgh = dout("o_s_rgh", [128, 2, 10, 4])
        o_s_ffc_new = dout("o_s_ffc_new", [128, 4, 48, 4]); o_s_ffc_old = dout("o_s_ffc_old", [4, 4, 2 * DFF])

    o_y = dout("o_y", [NOWN, D])
    o_cmp = dout("o_cmp", [NOWN, 256]); o_slc = dout("o_slc", [NOWN, 256]); o_win = dout("o_win", [NOWN, 256])
    o_rgc = dout("o_rgc", [2, 3, DR]); o_rgh = dout("o_rgh", [2, DR]); o_ffc = dout("o_ffc", [4, 2, 2 * DFF])

    zpad = nc.dram_tensor("zpad", [9, 128, 4096], BF16, kind="Internal").ap()
    zband = nc.dram_tensor("zband", [2, 2, 128, 10 * 384], BF16, kind="Internal").ap()

    es = ExitStack()
    with es:
        S = Sched(nc, es)

        uid = [0]

        def sb(name, shape, dt=F32):
            uid[0] += 1
            return es.enter_context(nc.sbuf_tensor(f"s{uid[0]}_{name}", list(shape), dt))

        ident = sb("ident", [128, 128]); b_ident = Buf()
        ones_bf = sb("ones_bf", [128, 128], BF16); b_ones = Buf()
        flag_sb = sb("flag_sb", [128, 2]); b_flag = Buf()
        gvec = sb("gvec", [128, 14, 8]); b_gvec = Buf()
        rgcw = sb("rgcw", [128, 2, 4, 10]); rgcb = sb("rgcb", [128, 2, 10]); b_rgc = Buf()
        rgba = sb("rgba", [128, 2, 10]); rgbx = sb("rgbx", [128, 2, 10]); rgc8 = sb("rgc8", [128, 2, 10]); b_rgp = Buf()
        ffcw = sb("ffcw", [128, 4, 3, 48]); ffcb = sb("ffcb", [128, 4, 48]); b_ffc = Buf()
        rg_hist = sb("rg_hist", [128, 2, 10, 3]); b_rghist = [[Buf() for _ in range(10)] for _ in range(2)]
        rg_hc = sb("rg_hc", [128, 2, 10]); b_rghc = [[Buf() for _ in range(10)] for _ in range(2)]
        ff_hist = sb("ff_hist", [128, 4, 48, 2]); b_ffhist = [[Buf() for _ in range(48)] for _ in range(4)]
        epsb = sb("epsb", [128, 1]); b_eps = Buf()

        x = sb("x", [128, 8, 512]); b_x = [Buf() for _ in range(8)]
        xt = sb("xt", [128, 4, 1024]); b_xt = Buf()
        h = sb("h", [128, 8, 512], BF16); b_h = Buf()
        sq = sb("sq", [128, 8, 512], BF16); b_sq = Buf()
        rstd = sb("rstd", [128, 512]); b_rstd = Buf()
        NPAN = 4
        pan = [sb(f"pan{i}", [128, 4096], BF16) for i in range(NPAN)]
        b_pan = [Buf() for _ in range(NPAN)]
        psum2 = [es.enter_context(nc.psum_tensor(f"psum{i}", [128, 1024], F32)) for i in range(4)]
        psum = [psum2[i // 2][:, (i % 2) * 512:(i % 2 + 1) * 512] for i in range(8)]
        b_ps = [Buf() for _ in range(8)]
        ps_i = [0]

        def ps_next():
            i = ps_i[0] % 6
            ps_i[0] += 1
            return psum[i], b_ps[i]

        def ps_pair():
            if ps_i[0] % 2 == 1:
                ps_i[0] += 1
            i = ps_i[0] % 6
            ps_i[0] += 2
            return psum2[i // 2], [b_ps[i], b_ps[i + 1]]

        acc_ps = psum2[3]
        b_acc = [b_ps[6], b_ps[7]]

        W = {}
        band = {}

        jobs = {}
        cast_order = []

        def ensure(name):
            if name in jobs:
                jobs.pop(name)()

        def prefetch_casts(k):
            n = 0
            for nm in cast_order:
                if n >= k:
                    break
                if nm in jobs:
                    ensure(nm)
                    n += 1

        def mkw(name, src, K, M, MW):
            w = Wt(nc, name, src, K, M, MW)
            w.name = name
            W[name] = w

            def job():
                tks = []
                for pnl in range(w.NP):
                    srcv = src[:, pnl * MW:(pnl + 1) * MW].rearrange("(k p) m -> p k m", p=128)
                    dstv = w.dst[pnl].rearrange("p (k m) -> p k m", k=w.KC)
                    tks.append(S.dma("pool", dstv, srcv))
                w.buf.w = tks
            jobs[name] = job
            return w

        zt_full = sq[:].rearrange("p k t -> p (k t)")
        zt = zt_full[:, 0:3840]; b_zt = b_sq
        S.op("dve", lambda e: e.memset(sq[:], 0.0), writes=[b_zt])
        for l in range(NL_A):
            for gi, wsrc in enumerate((rg_w_a, rg_w_x)):
                bw = Buf()
                band[(l, gi)] = bw
                tz = S.dma("sp", zband[l, gi], zt, reads=[b_zt])
                wv = Wt.__new__(Wt)
                wv.KC = 30; wv.MW = 128; wv.NP = 1; wv.dst = zband[l, gi:gi + 1]; wv.buf = bw; wv.name = f"band{l}_{gi}"
                W[wv.name] = wv

                def bjob(l=l, gi=gi, wsrc=wsrc, bw=bw, tz=tz):
                    S._wait("pool", tz)
                    tks = []
                    dv = zband[l, gi].rearrange("p (i b m) -> p i b m", i=10, b=3)
                    for n in range(16):
                        r0, r1 = 80 * n, 80 * n + 80
                        for i in range(r0 // 128, (r1 - 1) // 128 + 1):
                            ra, rb = max(r0, 128 * i), min(r1, 128 * i + 128)
                            for j in range(r0 // 128, (r1 - 1) // 128 + 1):
                                ca, cb = max(r0, 128 * j), min(r1, 128 * j + 128)
                                tks.append(S.dma("pool", dv[ra - 128 * i:rb - 128 * i, i, j - i + 1, ca - 128 * j:cb - 128 * j],
                                                 wsrc[l, n, ra - r0:rb - r0, ca - r0:cb - r0]))
                    bw.w = tks
                jobs[wv.name] = bjob

        for l in range(NL_A):
            mkw(f"rgin{l}", rg_w_in[l], D, 2 * DR, 512)
            mkw(f"rgout{l}", rg_w_out[l], DR, D, 256)
        for l in range(4 if DO_B else NL_A):
            mkw(f"up{l}", ffn_w_up[l], D, 2 * DFF, 512)
            mkw(f"down{l}", ffn_w_down[l], DFF, D, 128)
            mkw(f"plein{l}", ple_w_in[l], 256, D, 1024)
            mkw(f"pleg{l}", ple_w_gate[l], D, D, 512)
        mkw("kv", w_kv, D, 768, 256)


        BIGV = 30000.0
        if DO_B:
            for j in range(2):
                mkw(f"q{j}", attn_w_qg[j][:, 0:1024], D, 1024, 512)
                mkw(f"wo{j}", attn_w_o[j], D, D, 512)
                wsw = Wt(nc, f"qsw{j}", None, D, 1024, 512)
                wsw.name = f"qsw{j}"
                W[f"qsw{j}"] = wsw

                def qswjob(j=j, wsw=wsw):
                    tks = []
                    with nc.allow_non_contiguous_dma(reason="one-time rope column swap"):
                        for pnl in range(2):
                            for k in range(8):
                                srcv = attn_w_qg[j][k * 128:(k + 1) * 128, pnl * 512:(pnl + 1) * 512].rearrange("p (hd hh dd) -> p hd hh dd", hd=8, hh=2)
                                dstv = wsw.dst[pnl].rearrange("p (k hd hh dd) -> p k hd hh dd", k=8, hd=8, hh=2)
                                for hh in range(2):
                                    tks.append(S.dma("pool", dstv[:, k, :, hh, :], srcv[:, :, 1 - hh, :]))
                    wsw.buf.w = tks
                jobs[wsw.name] = qswjob
                wg = Wt.__new__(Wt)
                wg.KC = 8; wg.MW = 128; wg.NP = 1; wg.dst = zpad[j:j + 1, :, 0:1024]; wg.buf = Buf()
                W[f"qg{j}"] = wg
                wg.name = f"qg{j}"
                tz = S.dma("sp", zpad[j, :, 0:1024], zt[:, 0:1024], reads=[b_zt])

                def qgjob(j=j, wg=wg, tz=tz):
                    S._wait("pool", tz)
                    with nc.allow_non_contiguous_dma(reason="gate cols"):
                        tk = S.dma("pool", zpad[j, :, 0:1024].rearrange("p (k m) -> p k m", k=8)[:, :, 0:48],
                                   attn_w_qg[j][:, 1024:1072].rearrange("(k p) m -> p k m", p=128))
                    wg.buf.w = [tk]
                jobs[wg.name] = qgjob
            for jv in range(2):
                for g in range(2):
                    idx = 2 + jv * 2 + g
                    w1 = Wt.__new__(Wt)
                    w1.KC = 32; w1.MW = 128; w1.NP = 1; w1.dst = zpad[idx:idx + 1]; w1.buf = Buf()
                    W[f"w1_{jv}{g}"] = w1
                    w1.name = f"w1_{jv}{g}"
                    tz = S.dma("sp", zpad[idx], zt_full, reads=[b_zt])

                    def w1job(jv=jv, g=g, idx=idx, w1=w1, tz=tz):
                        S._wait("pool", tz)
                        tk = S.dma("pool", zpad[idx, g * 64:(g + 1) * 64, :].rearrange("p (l e) -> p l e", l=32),
                                   cmp_w1[jv].rearrange("(l d) e -> d l e", d=64))
                        w1.buf.w = [tk]
                    jobs[w1.name] = w1job

            slcK = sb("slcK", [128, 2, NTOK], BF16); b_slcK = Buf()
            slcV = sb("slcV", [128, NTOK // 128, 2, 128], BF16); b_slcV = Buf()
            winK = sb("winK", [128, 2, 1024], BF16); b_winK = Buf()
            winV = sb("winV", [128, 8, 2, 128], BF16); b_winV = Buf()
            cmpT = sb("cmpT", [128, 2, 528], BF16); b_cmpT = Buf()
            kcK = sb("kcK", [128, 2, 256], BF16); b_kcK = Buf()
            vcV = sb("vcV", [128, 2, 2, 128], F32); b_vcV = Buf()
            vcVb = sb("vcVb", [128, 2, 2, 128], BF16); b_vcVb = Buf()
            ident_bf = sb("ident_bf", [128, 128], BF16); b_identb = Buf()
            ones1 = sb("ones1", [128, 128], BF16); b_ones1 = Buf()
            mcs = sb("mcs", [128, 2, 64], BF16); b_mcs = Buf()
            tri = sb("tri", [128, 128], BF16); b_tri = Buf()
            w2k = sb("w2k", [128, 128], BF16); w2v = sb("w2v", [128, 128], BF16); b_w2 = Buf()
            cb1 = sb("cb1", [128, 2]); b_cb1 = Buf()
            b1t = sb("b1t", [128, 2]); posT = sb("posT", [128, 2, 32], BF16); b_posT = Buf()
            S.op("pool", lambda e: e.memset(slcV[:], 1.0), writes=[b_slcV])
            S.op("pool", lambda e: e.memset(winV[:], 1.0), writes=[b_winV])
            S.op("pool", lambda e: e.memset(slcK[:], 0.0), writes=[b_slcK])
            S.op("pool", lambda e: e.memset(winK[:], 0.0), writes=[b_winK])
            S.op("pool", lambda e: e.memset(cmpT[:], 0.0), writes=[b_cmpT])
            S.op("pool", lambda e: e.memset(kcK[:], 0.0), writes=[b_kcK])
            S.op("pool", lambda e: e.memset(vcV[:], 0.0), writes=[b_vcV])
            S.op("pool", lambda e: e.memset(vcVb[:], 0.0), writes=[b_vcVb])
            S.op("pool", lambda e: e.memset(ones1[:], 1.0), writes=[b_ones1])
            S.op("pool", lambda e: e.memset(w2v[:], 0.0), writes=[b_w2])
            S.op("pool", lambda e: e.memset(posT[:], 0.0), writes=[b_posT])
            S.dma("pool", ident_bf[:], ident_d, writes=[b_identb])
            S.dma("pool", mcs[:], t_mcs, writes=[b_mcs])
            S.dma("pool", tri[:], t_tri, writes=[b_tri])
            S.dma("pool", w2k[:, 0:64], cmp_w2[0], writes=[b_w2])
            S.dma("pool", w2k[:, 64:128], cmp_w2[0], writes=[b_w2])
            S.dma("pool", w2v[:, 0:64], cmp_w2[1], writes=[b_w2])
            with nc.allow_non_contiguous_dma(reason="small"):
                S.dma("sp", b1t[:], cmp_b1.rearrange("j e -> e j"), writes=[b_cb1])
                for jv_ in range(2):
                    S.dma("pool", posT[0:64, jv_, :], cmp_pos[:, jv_, :].rearrange("l d -> d l"), writes=[b_posT])

        S.dma("sp", ident[:], ident_d, writes=[b_ident])
        S.dma("sp", flag_sb[:], flag, writes=[b_flag])
        S.op("dve", lambda e: e.memset(ones_bf[:], 1.0 / D), writes=[b_ones])
        S.op("dve", lambda e: e.memset(epsb[:], EPS), writes=[b_eps])
        with nc.allow_non_contiguous_dma(reason="small param vectors to feature-major"):
            for i in range(4):
                S.dma("sp", gvec[:, i, :], g_mix[i].rearrange("(c p) -> p c", p=128), writes=[b_gvec])
                S.dma("sp", gvec[:, 4 + i, :], g_ffn[i].rearrange("(c p) -> p c", p=128), writes=[b_gvec])
                S.dma("sp", gvec[:, 8 + i, :], g_ple[i].rearrange("(c p) -> p c", p=128), writes=[b_gvec])
            S.dma("sp", gvec[:, 12, :], g_final.rearrange("(c p) -> p c", p=128), writes=[b_gvec])
            S.dma("sp", gvec[:, 13, :], g_kv.rearrange("(c p) -> p c", p=128), writes=[b_gvec])
            for l in range(2):
                for k in range(4):
                    S.dma("sp", rgcw[:, l, k, :], rg_conv_w[l, k].rearrange("(c p) -> p c", p=128), writes=[b_rgc])
                S.dma("sp", rgcb[:, l, :], rg_conv_b[l].rearrange("(c p) -> p c", p=128), writes=[b_rgc])
                S.dma("sp", rgba[:, l, :], rg_b_a[l].rearrange("(c p) -> p c", p=128), writes=[b_rgp])
                S.dma("sp", rgbx[:, l, :], rg_b_x[l].rearrange("(c p) -> p c", p=128), writes=[b_rgp])
                S.dma("sp", rgc8[:, l, :], rg_lambda[l].rearrange("(c p) -> p c", p=128), writes=[b_rgp])
            for l in range(4):
                for k in range(3):
                    S.dma("sp", ffcw[:, l, k, :], ffn_conv_w[l, k].rearrange("(c p) -> p c", p=128), writes=[b_ffc])
                S.dma("sp", ffcb[:, l, :], ffn_conv_b[l].rearrange("(c p) -> p c", p=128), writes=[b_ffc])
        S.op("act", lambda e: e.activation(out=rgc8[:], in_=rgc8[:], func=AF.Exp, scale=-1.0), reads=[b_rgp], writes=[b_rgp])
        S.op("act", lambda e: e.activation(out=rgc8[:], in_=rgc8[:], func=AF.Ln, bias=1.0), reads=[b_rgp], writes=[b_rgp])
        S.op("dve", lambda e: e.tensor_scalar(out=rgc8[:], in0=rgc8[:], scalar1=-8.0, scalar2=None, op0=ALU.mult), reads=[b_rgp], writes=[b_rgp])
        allh = [b for row in b_rghist for b in row] + [b for row in b_rghc for b in row] + [b for row in b_ffhist for b in row]
        S.op("dve", lambda e: e.memset(rg_hist[:], 0.0), writes=[b for row in b_rghist for b in row])
        S.op("dve", lambda e: e.memset(rg_hc[:], 0.0), writes=[b for row in b_rghc for b in row])
        S.op("dve", lambda e: e.memset(ff_hist[:], 0.0), writes=[b for row in b_ffhist for b in row])

        pan_i = [0]
        NWv = [512]

        def load_panel(w, pnl):
            i = pan_i[0] % NPAN
            pan_i[0] += 1
            n = w.KC * w.MW
            S.dma("sp", pan[i][:, 0:n], w.dst[pnl], reads=[w.buf], writes=[b_pan[i]])
            return pan[i][:, 0:n].rearrange("p (k m) -> p k m", k=w.KC), b_pan[i]

        class Stream:
            def __init__(self, items, depth=2):
                for w_, _p in items:
                    ensure(w_.name)
                prefetch_casts(4)
                self.items = items
                self.loaded = []
                self.depth = depth
                self.pos = 0

            def get(self):
                while len(self.loaded) < min(len(self.items), self.pos + 1 + self.depth):
                    w, pnl = self.items[len(self.loaded)]
                    self.loaded.append(load_panel(w, pnl))
                r = self.loaded[self.pos]
                self.pos += 1
                return r

        if DO_B:
            for jv in range(2):
                ensure(f"w1_{jv}0")
                pnl, bpn = load_panel(W[f"w1_{jv}0"], 0)
                pt, bp = ps_next()
                for l_ in range(32):
                    S.op("pe", lambda e: e.matmul(pt[:, 0:1], lhsT=pnl[:, l_, :], rhs=posT[:, jv, l_:l_ + 1], start=(l_ == 0), stop=(l_ == 31)),
                         reads=[bpn, b_posT], writes=[bp])
                S.op("dve", lambda e: e.tensor_tensor(out=cb1[:, jv:jv + 1], in0=b1t[:, jv:jv + 1], in1=pt[:, 0:1], op=ALU.add), reads=[bp, b_cb1], writes=[b_cb1])

        def transpose_in(src_rows, ncols, dst, dst_bufs, dst_dt_bf16=False):
            nk = ncols // 128
            S.dma("sp", xt[:, :, 0:ncols], src_rows.rearrange("(b p) f -> p b f", p=128), writes=[b_xt])
            for k in range(nk):
                pt, bp = ps_next()
                for blk in range(4):
                    S.op("pe", lambda e: e.transpose(pt[:, blk * 128:(blk + 1) * 128], xt[:, blk, k * 128:(k + 1) * 128], ident[:]),
                         reads=[b_xt, b_ident], writes=[bp])
                S.op("act", lambda e: e.copy(out=dst[:, k, :], in_=pt[:]), reads=[bp], writes=[dst_bufs[k]])

        def rmsnorm(gi, out_t, out_buf):
            for k in range(8):
                S.op("act", lambda e: e.activation(out=sq[:, k, :], in_=x[:, k, :], func=AF.Square), reads=[b_x[k]], writes=[b_sq])
            pt, bp = ps_next()
            for k in range(8):
                S.op("pe", lambda e: e.matmul(pt[:], lhsT=ones_bf[:], rhs=sq[:, k, :], start=(k == 0), stop=(k == 7)),
                     reads=[b_sq, b_ones], writes=[bp])
            S.op("act", lambda e: e.activation(out=rstd[:], in_=pt[:], func=AF.Sqrt, bias=epsb[:, 0:1]), reads=[bp, b_eps], writes=[b_rstd])
            S.op("dve", lambda e: e.reciprocal(out=rstd[:], in_=rstd[:]), reads=[b_rstd], writes=[b_rstd])
            for k in range(8):
                S.op("dve", lambda e: e.scalar_tensor_tensor(out=out_t[:, k, :], in0=x[:, k, :], scalar=gvec[:, gi, k:k + 1], in1=rstd[:],
                                                              op0=ALU.mult, op1=ALU.mult),
                     reads=[b_x[k], b_gvec, b_rstd], writes=[out_buf])

        def proj_chunk(pnl_ap, pnl_buf, col0, mcols, rhs_t, rhs_buf, nk):
            pt, bp = ps_next()
            for k in range(nk):
                S.op("pe", lambda e: e.matmul(pt[0:mcols, 0:NWv[0]], lhsT=pnl_ap[:, k, col0:col0 + mcols], rhs=rhs_t[:, k, 0:NWv[0]],
                                              start=(k == 0), stop=(k == nk - 1)),
                     reads=[pnl_buf] + (rhs_buf if isinstance(rhs_buf, list) else [rhs_buf]), writes=[bp])
            return pt, bp

        def rg_layer(l, ti, es2, smp=False):
            def sb2(name, shape, dt=F32):
                uid[0] += 1
                return es2.enter_context(nc.sbuf_tensor(f"s{uid[0]}_{name}", list(shape), dt))
            y = sb2("rg_y", [128, 10, 512], BF16); b_y = [Buf() for _ in range(10)]
            xc = sb2("rg_xc", [128, 10, 512]); b_xc = [Buf() for _ in range(10)]
            xcb = sb2("rg_xcb", [128, 10, 512], BF16); b_xcb = [Buf() for _ in range(10)]
            xrh = [sb2(f"rg_xrh{i}", [128, 515]) for i in range(2)]; b_xrh = [Buf(), Buf()]
            t1 = [sb2(f"rg_t1{i}", [128, 512]) for i in range(2)]; b_t1 = [Buf(), Buf()]
            gr = [sb2(f"rg_r{i}", [128, 512]) for i in range(2)]; b_gr = [Buf(), Buf()]
            gi_ = [sb2(f"rg_i{i}", [128, 512]) for i in range(2)]; b_gi = [Buf(), Buf()]
            ga = [sb2(f"rg_a{i}", [128, 512]) for i in range(2)]; b_ga = [Buf(), Buf()]
            gm = [sb2(f"rg_m{i}", [128, 512]) for i in range(2)]; b_gm = [Buf(), Buf()]
            hs = [sb2(f"rg_hs{i}", [128, 512]) for i in range(2)]; b_hs = [Buf(), Buf()]
            if smp:
                s_rgh = sb2("s_rgh", [128, 10, 4, 3]); s_h0 = sb2("s_h0", [128, 10, 4]); b_sst = Buf()
                so_rgc = sb2("so_rgc", [128, 10, 4]); so_rgh = sb2("so_rgh", [128, 10, 4]); b_so = Buf()
                S.dma("sp", s_rgh[:], s_rgh_d[:, l], writes=[b_sst])
                S.dma("sp", s_h0[:], s_h0_d[:, l], writes=[b_sst])
            rmsnorm(l, h, b_h)
            st = Stream([(W[f"rgin{l}"], p) for p in range(5)] + [(W[f"band{l}_0"], 0), (W[f"band{l}_1"], 0)]
                        + [(W[f"rgout{l}"], p) for p in range(4)])
            for jc in range(20):
                if jc % 4 == 0:
                    pnl, bpn = st.get()
                pt, bp = proj_chunk(pnl, bpn, (jc % 4) * 128, 128, h, b_h, 8)
                if jc < 10:
                    S.op("act", lambda e: e.activation(out=y[:, jc, :], in_=pt[:], func=AF.Gelu), reads=[bp], writes=[b_y[jc]])
                else:
                    j = jc - 10
                    q = j % 2
                    S.op("act", lambda e: e.copy(out=xrh[q][:, 3:515], in_=pt[:]), reads=[bp], writes=[b_xrh[q]])
                    S.op("act", lambda e: e.activation(out=xc[:, j, :], in_=pt[:], func=AF.Identity, scale=rgcw[:, l, 3, j:j + 1], bias=rgcb[:, l, j:j + 1]),
                         reads=[bp, b_rgc], writes=[b_xc[j]])
                    if ti == OWN0:
                        S.op("pool", lambda e: e.tensor_scalar(out=rg_hist[:, l, j, :], in0=rg_hist[:, l, j, :], scalar1=flag_sb[:, 0:1], scalar2=None, op0=ALU.mult),
                             reads=[b_rghist[l][j], b_flag], writes=[b_rghist[l][j]])
                    S.op("pool", lambda e: e.tensor_copy(out=xrh[q][:, 0:3], in_=rg_hist[:, l, j, :]), reads=[b_rghist[l][j]], writes=[b_xrh[q]])
                    S.op("pool", lambda e: e.tensor_copy(out=rg_hist[:, l, j, :], in_=xrh[q][:, 512:515]), reads=[b_xrh[q]], writes=[b_rghist[l][j]])
                    if smp:
                        S.op("dve", lambda e: e.tensor_copy(out=so_rgc[:, j, :], in_=xrh[q][:, 6:22:4]), reads=[b_xrh[q]], writes=[b_so])
                        S.op("dve", lambda e: e.tensor_copy(out=xrh[q][:, 3:19].rearrange("p (b k) -> p b k", k=4)[:, :, 0:3], in_=s_rgh[:, j, :, :]),
                             reads=[b_sst], writes=[b_xrh[q]])
                    for k in (2, 1, 0):
                        S.op("dve", lambda e: e.scalar_tensor_tensor(out=xc[:, j, :], in0=xrh[q][:, k:k + 512], scalar=rgcw[:, l, k, j:j + 1], in1=xc[:, j, :],
                                                                      op0=ALU.mult, op1=ALU.add), reads=[b_xrh[q], b_rgc, b_xc[j]], writes=[b_xc[j]])
                    S.op("pool", lambda e: e.tensor_copy(out=xcb[:, j, :], in_=xc[:, j, :]), reads=[b_xc[j]], writes=[b_xcb[j]])
            bands = [st.get(), st.get()]
            for jp in range(5):
                js = (2 * jp, 2 * jp + 1)
                for j in js:
                    q = j % 2
                    ins = [i for i in (j - 1, j, j + 1) if 0 <= i < 10]
                    for gidx, (dst, bdst, bias_t) in enumerate(((gr, b_gr, rgba), (gi_, b_gi, rgbx))):
                        pt, bp = ps_next()
                        bv, b_band = bands[gidx]
                        for n_, i in enumerate(ins):
                            S.op("pe", lambda e: e.matmul(pt[:, 0:NWv[0]], lhsT=bv[:, i * 3 + (j - i + 1), :], rhs=xcb[:, i, 0:NWv[0]], start=(n_ == 0), stop=(n_ == len(ins) - 1)),
                                 reads=[b_band, b_xcb[i]], writes=[bp])
                        S.op("act", lambda e: e.activation(out=dst[q][:], in_=pt[:], func=AF.Sigmoid, bias=bias_t[:, l, j:j + 1]),
                             reads=[bp, b_rgp], writes=[bdst[q]])
                for j in js:
                    q = j % 2
                    S.op("act", lambda e: e.activation(out=ga[q][:], in_=gr[q][:], func=AF.Exp, scale=rgc8[:, l, j:j + 1]),
                         reads=[b_gr[q], b_rgp], writes=[b_ga[q]])
                    S.op("dve", lambda e: e.tensor_tensor(out=gm[q][:], in0=ga[q][:], in1=ga[q][:], op=ALU.mult), reads=[b_ga[q]], writes=[b_gm[q]])
                    S.op("dve", lambda e: e.tensor_scalar(out=gm[q][:], in0=gm[q][:], scalar1=-1.0, scalar2=1.0, op0=ALU.mult, op1=ALU.add),
                         reads=[b_gm[q]], writes=[b_gm[q]])
                    S.op("dve", lambda e: e.tensor_tensor(out=gi_[q][:], in0=gi_[q][:], in1=xc[:, j, :], op=ALU.mult), reads=[b_gi[q], b_xc[j]], writes=[b_gi[q]])
                for j in js:
                    q = j % 2
                    S.op("act", lambda e: e.activation(out=gm[q][:], in_=gm[q][:], func=AF.Sqrt), reads=[b_gm[q]], writes=[b_gm[q]])
                for j in js:
                    q = j % 2
                    if ti == 0:
                        S.op("dve", lambda e: e.memset(gm[q][:, 0:1], 1.0), reads=[], writes=[b_gm[q]])
                    elif ti == OWN0:
                        S.op("dve", lambda e: e.tensor_scalar(out=gm[q][:, 0:1], in0=gm[q][:, 0:1], scalar1=flag_sb[:, 1:2], scalar2=None, op0=ALU.max),
                             reads=[b_gm[q], b_flag], writes=[b_gm[q]])
                        S.op("dve", lambda e: e.tensor_scalar(out=rg_hc[:, l, j:j + 1], in0=rg_hc[:, l, j:j + 1], scalar1=flag_sb[:, 0:1], scalar2=None, op0=ALU.mult),
                             reads=[b_rghc[l][j], b_flag], writes=[b_rghc[l][j]])
                    S.op("dve", lambda e: e.tensor_tensor(out=gi_[q][:], in0=gi_[q][:], in1=gm[q][:], op=ALU.mult), reads=[b_gi[q], b_gm[q]], writes=[b_gi[q]])
                    if smp:
                        S.op("dve", lambda e: e.memset(ga[q][:, 2:18:4], 0.0), reads=[], writes=[b_ga[q]])
                        S.op("dve", lambda e: e.tensor_copy(out=gi_[q][:, 2:18:4], in_=s_h0[:, j, :]), reads=[b_sst], writes=[b_gi[q]])
                    S.op("dve", lambda e: e.tensor_tensor_scan(out=hs[q][:], data0=ga[q][:], data1=gi_[q][:], initial=rg_hc[:, l, j:j + 1], op0=ALU.mult, op1=ALU.add),
                         reads=[b_ga[q], b_gi[q], b_rghc[l][j]], writes=[b_hs[q]])
                    if smp:
                        S.op("dve", lambda e: e.tensor_copy(out=so_rgh[:, j, :], in_=hs[q][:, 3:19:4]), reads=[b_hs[q]], writes=[b_so])
                    S.op("dve", lambda e: e.tensor_copy(out=rg_hc[:, l, j:j + 1], in_=hs[q][:, 511:512]), reads=[b_hs[q]], writes=[b_rghc[l][j]])
                    S.op("dve", lambda e: e.tensor_tensor(out=y[:, j, :], in0=y[:, j, :], in1=hs[q][:], op=ALU.mult), reads=[b_y[j], b_hs[q]], writes=[b_y[j]])
            for oc in range(8):
                if oc % 2 == 0:
                    pnl, bpn = st.get()
                pt, bp = ps_next()
                for k in range(10):
                    S.op("pe", lambda e: e.matmul(pt[:, 0:NWv[0]], lhsT=pnl[:, k, (oc % 2) * 128:(oc % 2 + 1) * 128], rhs=y[:, k, 0:NWv[0]], start=(k == 0), stop=(k == 9)),
                         reads=[bpn, b_y[k]], writes=[bp])
                S.op("dve", lambda e: e.tensor_tensor(out=x[:, oc, :], in0=x[:, oc, :], in1=pt[:], op=ALU.add), reads=[b_x[oc], bp], writes=[b_x[oc]])
            if smp:
                S.dma("sp", o_s_rgc_new[:, l], so_rgc[:], reads=[b_so])
                S.dma("sp", o_s_rgh[:, l], so_rgh[:], reads=[b_so])

        def ffn_ple(l, ti, es2, smp=False):
            def sb2(name, shape, dt=F32):
                uid[0] += 1
                return es2.enter_context(nc.sbuf_tensor(f"s{uid[0]}_{name}", list(shape), dt))
            gbuf = sb2("ff_g", [128, 24, 512], BF16); b_g = [Buf() for _ in range(24)]
            uh = [sb2(f"ff_uh{i}", [128, 514]) for i in range(6)]; b_uh = [Buf() for _ in range(6)]
            uc = [sb2(f"ff_uc{i}", [128, 512]) for i in range(6)]; b_uc = [Buf() for _ in range(6)]
            pT = sb2("ple_pT", [128, 2, 512], BF16); b_pT = [Buf(), Buf()]
            sig = [sb2(f"ple_sig{i}", [128, 512]) for i in range(2)]; b_sig = [Buf(), Buf()]

            if smp:
                s_ffh = sb2("s_ffh", [128, 48, 4, 2]); b_sst = Buf()
                so_ffc = sb2("so_ffc", [128, 48, 4]); b_so = Buf()
                S.dma("sp", s_ffh[:], s_ffh_d[:, l], writes=[b_sst])
            rmsnorm(4 + l, h, b_h)
            items = []
            for c4 in range(6):
                items += [(W[f"up{l}"], c4), (W[f"up{l}"], c4 + 6)]
            items += [(W[f"down{l}"], p) for p in range(8)]
            items += [(W[f"plein{l}"], 0), (W[f"pleg{l}"], 0), (W[f"pleg{l}"], 1)]
            st = Stream(items)
            if ti == OWN0:
                for cc in range(48):
                    S.op("pool", lambda e: e.tensor_scalar(out=ff_hist[:, l, cc, :], in0=ff_hist[:, l, cc, :], scalar1=flag_sb[:, 0:1], scalar2=None, op0=ALU.mult),
                         reads=[b_ffhist[l][cc], b_flag], writes=[b_ffhist[l][cc]])
            pans = [None, None]

            def stageA(c):
                if c % 4 == 0:
                    pans[0] = st.get()
                    pans[1] = st.get()
                ci = c % 4
                for hf in range(2):
                    pnl, bpn = pans[hf]
                    cc = c + 24 * hf
                    q = (c % 3) * 2 + hf
                    pt, bp = proj_chunk(pnl, bpn, ci * 128, 128, h, b_h, 8)
                    S.op("act", lambda e: e.copy(out=uh[q][:, 2:2 + NWv[0]], in_=pt[:, 0:NWv[0]]), reads=[bp], writes=[b_uh[q]])
                    S.op("act", lambda e: e.activation(out=uc[q][:, 0:NWv[0]], in_=pt[:, 0:NWv[0]], func=AF.Identity, scale=ffcw[:, l, 2, cc:cc + 1], bias=ffcb[:, l, cc:cc + 1]),
                         reads=[bp, b_ffc], writes=[b_uc[q]])
                    S.op("pool", lambda e: e.tensor_copy(out=uh[q][:, 0:2], in_=ff_hist[:, l, cc, :]), reads=[b_ffhist[l][cc]], writes=[b_uh[q]])
                    S.op("pool", lambda e: e.tensor_copy(out=ff_hist[:, l, cc, :], in_=uh[q][:, NWv[0]:NWv[0] + 2]), reads=[b_uh[q]], writes=[b_ffhist[l][cc]])
                    if smp:
                        S.op("dve", lambda e: e.tensor_copy(out=so_ffc[:, cc, :], in_=uh[q][:, 5:21:4]), reads=[b_uh[q]], writes=[b_so])
                        S.op("dve", lambda e: e.tensor_copy(out=uh[q][:, 2:18].rearrange("p (b k) -> p b k", k=4)[:, :, 1:3], in_=s_ffh[:, cc, :, :]),
                             reads=[b_sst], writes=[b_uh[q]])

            def stageB(c):
                for hf in range(2):
                    cc = c + 24 * hf
                    q = (c % 3) * 2 + hf
                    for k in (1, 0):
                        S.op("dve", lambda e: e.scalar_tensor_tensor(out=uc[q][:, 0:NWv[0]], in0=uh[q][:, k:k + NWv[0]], scalar=ffcw[:, l, k, cc:cc + 1], in1=uc[q][:, 0:NWv[0]],
                                                                      op0=ALU.mult, op1=ALU.add), reads=[b_uh[q], b_ffc, b_uc[q]], writes=[b_uc[q]])
                q0 = (c % 3) * 2
                S.op("act", lambda e: e.activation(out=uc[q0][:, 0:NWv[0]], in_=uc[q0][:, 0:NWv[0]], func=AF.Gelu), reads=[b_uc[q0]], writes=[b_uc[q0]])
                S.op("pool", lambda e: e.tensor_tensor(out=gbuf[:, c, 0:NWv[0]], in0=uc[q0][:, 0:NWv[0]], in1=uc[q0 + 1][:, 0:NWv[0]], op=ALU.mult),
                     reads=[b_uc[q0], b_uc[q0 + 1]], writes=[b_g[c]])
            stageA(0)
            for c in range(24):
                if c + 1 < 24:
                    stageA(c + 1)
                stageB(c)
            for oc in range(8):
                pnl, bpn = st.get()
                pt, bp = ps_next()
                for k in range(24):
                    S.op("pe", lambda e: e.matmul(pt[:, 0:NWv[0]], lhsT=pnl[:, k, :], rhs=gbuf[:, k, 0:NWv[0]], start=(k == 0), stop=(k == 23)),
                         reads=[bpn, b_g[k]], writes=[bp])
                S.op("dve", lambda e: e.tensor_tensor(out=x[:, oc, 0:NWv[0]], in0=x[:, oc, 0:NWv[0]], in1=pt[:, 0:NWv[0]], op=ALU.add), reads=[b_x[oc], bp], writes=[b_x[oc]])
            if smp:
                S.dma("sp", o_s_ffc_new[:, l], so_ffc[:], reads=[b_so])
            if smp:
                S.dma("pool", pT[:], ps_s[l], writes=b_pT)
            else:
                transpose_in(ps_in[l, ti * 512:(ti + 1) * 512, :], 256, pT, b_pT)
            rmsnorm(8 + l, h, b_h)
            pin, bpin = st.get()
            pg = [st.get(), st.get()]
            for oc in range(8):
                q = oc % 2
                pe_, bpe = proj_chunk(pin, bpin, oc * 128, 128, pT, b_pT, 2)
                pgp, bpg = pg[oc // 4]
                pt, bp = proj_chunk(pgp, bpg, (oc % 4) * 128, 128, h, b_h, 8)
                S.op("act", lambda e: e.activation(out=sig[q][:, 0:NWv[0]], in_=pt[:, 0:NWv[0]], func=AF.Sigmoid), reads=[bp], writes=[b_sig[q]])
                S.op("dve", lambda e: e.tensor_tensor(out=sig[q][:, 0:NWv[0]], in0=sig[q][:, 0:NWv[0]], in1=pe_[:, 0:NWv[0]], op=ALU.mult), reads=[b_sig[q], bpe], writes=[b_sig[q]])
                S.op("dve", lambda e: e.tensor_tensor(out=x[:, oc, 0:NWv[0]], in0=x[:, oc, 0:NWv[0]], in1=sig[q][:, 0:NWv[0]], op=ALU.add), reads=[b_x[oc], b_sig[q]], writes=[b_x[oc]])


        def attn_layer(l, ti, es2, qbs, smp=False):
            j = l - 2

            def sb2(name, shape, dt=F32):
                uid[0] += 1
                return es2.enter_context(nc.sbuf_tensor(f"s{uid[0]}_{name}", list(shape), dt))
            qT = sb2("qT", [128, 8, 512], BF16); b_qT = [Buf() for _ in range(8)]
            qrT = sb2("qrT", [128, 8, 512], BF16); b_qrT = [Buf() for _ in range(8)]
            gT = sb2("gT", [128, 512], BF16); b_gT = Buf()
            oT = sb2("oT", [128, 8, 512], BF16); b_oT = Buf()
            rl = sb2("rl", [128, 1024]); b_rl = Buf()
            rc_t = rl[:, 0:512]; rs_t = rl[:, 512:1024]; b_rope = b_rl
            rt1 = sb2("rt1", [128, 512]); rt2 = sb2("rt2", [128, 512]); b_rt1 = Buf(); b_rt2 = Buf()
            if smp:
                qbs = []
            PW = 8 if smp else 1024
            qpad = [sb2(f"qpad{i}", [128, 8, 128 if not smp else 2], BF16) for i in range(2)]; b_qpad = [Buf(), Buf()]
            qrpad = [sb2(f"qrpad{i}", [128, 8, 128 if not smp else 2], BF16) for i in range(2)]; b_qrpad = [Buf(), Buf()]
            cbt = [sb2(f"cbt{i}", [128, 256], BF16) for i in range(2)]; b_cbt = [Buf(), Buf()]
            wbt = [sb2(f"wbt{i}", [128, 640], BF16) for i in range(2)]; b_wbt = [Buf(), Buf()]
            vat = [sb2(f"vat{i}", [128, 2, 64]) for i in range(2)]; b_vat = [Buf(), Buf()]
            Pc = sb2("Pc", [128, 2, PW], BF16); b_Pc = [Buf(), Buf()]
            Pn = Pc; b_Pn = b_Pc
            sqf = sq[:].rearrange("p k t -> p (k t)")
            Pst = [sqf[:, i * 1024:(i + 1) * 1024] for i in range(2)]; b_Pst = [Buf() for _ in range(2)]
            pst_i = [0]
            osb = [sb2(f"osb{i}", [64, PW]) for i in range(3)]; b_osb = [Buf() for _ in range(3)]
            rlx = sb2("rlx", [64, PW]); b_rlx = Buf()
            impf = sb2("impf", [128, 64]); b_impf = Buf()
            impt = sb2("impt", [128, 64]); b_impt = Buf()
            m8 = sb2("m8", [128, 16]); b_m8 = Buf()
            thr = sb2("thr", [128, 1]); b_thr = Buf()
            nbf = sb2("nbf", [128, 64]); b_nbf = Buf()
            nb = sb2("nb", [128, 64], BF16); b_nb = Buf()
            nbd = sb2("nbd", [128, 128], BF16); b_nbd = Buf()
            nbe = [sb2(f"nbe{i}", [128, 128], BF16) for i in range(4)]; b_nbe = [Buf() for _ in range(4)]
            nbe_i = [0]
            for t_ in qpad + qrpad:
                S.op("pool", lambda e: e.memset(t_[:], 0.0), writes=b_qpad + b_qrpad)

            S.dma("sp", rc_t, ropec_s if smp else ropec[:, ti * 512:(ti + 1) * 512], writes=[b_rope])
            S.dma("sp", rs_t, ropes_s if smp else ropes[:, ti * 512:(ti + 1) * 512], writes=[b_rope])
            rmsnorm(l, h, b_h)
            for bq in b_Pst:
                bq.w = b_sq.w
                bq.r = list(b_sq.r)
            st = Stream([(W[f"q{j}"], 0), (W[f"qsw{j}"], 0), (W[f"q{j}"], 1), (W[f"qsw{j}"], 1), (W[f"qg{j}"], 0), (W[f"wo{j}"], 0), (W[f"wo{j}"], 1)], depth=2)
            for c in range(8):
                if c % 4 == 0:
                    pq, bpq = st.get()
                    psw, bpsw = st.get()
                pt, bp = proj_chunk(pq, bpq, (c % 4) * 128, 128, h, b_h, 8)
                pt2, bp2 = proj_chunk(psw, bpsw, (c % 4) * 128, 128, h, b_h, 8)
                S.op("dve", lambda e: e.tensor_copy(out=qT[:, c, :], in_=pt[:]), reads=[bp], writes=[b_qT[c]])
                S.op("dve", lambda e: e.tensor_tensor(out=rt1[:], in0=pt[:], in1=rc_t, op=ALU.mult), reads=[bp, b_rope], writes=[b_rt1])
                S.op("dve", lambda e: e.tensor_tensor(out=rt2[:], in0=pt2[:], in1=rs_t, op=ALU.mult), reads=[bp2, b_rope], writes=[b_rt2])
                S.op("dve", lambda e: e.tensor_tensor(out=qrT[:, c, :], in0=rt1[:], in1=rt2[:], op=ALU.add), reads=[b_rt1, b_rt2], writes=[b_qrT[c]])
            pg, bpg = st.get()
            pt, bp = proj_chunk(pg, bpg, 0, 128, h, b_h, 8)
            S.op("act", lambda e: e.activation(out=gT[:], in_=pt[:], func=AF.Sigmoid), reads=[bp], writes=[b_gT])
            identb4 = ident_bf[:].unsqueeze(1).broadcast_to([128, 4, 128])

            def scores(lhs_bias, bias_bufs, kmat, kbufs, qp, bqp):
                pp, bpp = ps_pair()
                for bank in range(2):
                    S.op("pe", lambda e: e.matmul(pp[:, bank * 512:(bank + 1) * 512], lhsT=lhs_bias, rhs=identb4, start=True, stop=False),
                         reads=bias_bufs + [b_identb], writes=[bpp[bank]])
                    for h4 in range(4):
                        hh = bank * 4 + h4
                        S.op("pe", lambda e: e.matmul(pp[:, hh * 128:(hh + 1) * 128], lhsT=kmat, rhs=qp[:, hh, :], start=False, stop=(h4 == 3)),
                             reads=kbufs + [bqp], writes=[bpp[bank]])
                return pp, bpp

            def pv_acc(vmat, vbufs, pt_, bpt_, first, last):
                for bank in range(2):
                    S.op("pe", lambda e: e.matmul(acc_ps[:, bank * 512:(bank + 1) * 512], lhsT=vmat, rhs=pt_[:, bank * 512:(bank + 1) * 512], start=first, stop=last),
                         reads=vbufs + [bpt_], writes=[b_acc[bank]])

            if len(qbs) < 4:
                S.op("pool", lambda e: e.memset(oT[:], 0.0), writes=[b_oT])
            for qb in (qbs if DBG >= 5 else []):
                qbo = (ti - OWN0) * 4 + qb + 1
                kd = 4 * ti + qb
                tb = qbo % 2
                S.dma("pool", cbt[tb][:], t_cb[qbo], writes=[b_cbt[tb]])
                S.dma("pool", wbt[tb][:], t_wb[qbo], writes=[b_wbt[tb]])
                S.dma("sp", vat[tb][:], t_va[qbo], writes=[b_vat[tb]])
                tsl = slice(qb * 128, (qb + 1) * 128)
                for g in range(2):
                    pi = g
                    S.op("pool", lambda e: e.tensor_copy(out=qpad[pi][0:64, 0:8:2, :], in_=qT[0:64, 4 * g:4 * g + 4, tsl]), reads=b_qT[4 * g:4 * g + 4], writes=[b_qpad[pi]])
                    S.op("pool", lambda e: e.tensor_copy(out=qpad[pi][64:128, 1:8:2, :], in_=qT[64:128, 4 * g:4 * g + 4, tsl]), reads=b_qT[4 * g:4 * g + 4], writes=[b_qpad[pi]])
                    S.op("pool", lambda e: e.tensor_copy(out=qrpad[pi][0:64, 0:8:2, :], in_=qrT[0:64, 4 * g:4 * g + 4, tsl]), reads=b_qrT[4 * g:4 * g + 4], writes=[b_qrpad[pi]])
                    S.op("pool", lambda e: e.tensor_copy(out=qrpad[pi][64:128, 1:8:2, :], in_=qrT[64:128, 4 * g:4 * g + 4, tsl]), reads=b_qrT[4 * g:4 * g + 4], writes=[b_qrpad[pi]])
                    nbc = 1 if 32 * ti + 31 < 128 else 2
                    for bc in range(nbc):
                        pp, bpp = scores(cbt[tb][:, bc * 128:(bc + 1) * 128], [b_cbt[tb]], kcK[:, g, bc * 128:(bc + 1) * 128], [b_kcK], qpad[pi], b_qpad[pi])
                        S.op("act", lambda e: e.activation(out=Pc[:, bc, :], in_=pp[:], func=AF.Exp, scale=0.125), reads=bpp, writes=[b_Pc[bc]])

                    def win_front(wi):
                        rch = (kd - 4 + wi) % 8
                        pp, bpp = scores(wbt[tb][:, wi * 128:(wi + 1) * 128], [b_wbt[tb]], winK[:, g, rch * 128:(rch + 1) * 128], [b_winK], qrpad[pi], b_qrpad[pi])
                        pz = pst_i[0] % 2
                        pst_i[0] += 1
                        S.op("act", lambda e: e.activation(out=Pst[pz], in_=pp[:], func=AF.Exp, scale=0.125), reads=bpp, writes=[b_Pst[pz]])
                        return pz
                    cur = win_front(0)
                    lp, blp = ps_pair()
                    for bank in range(2):
                        for bc in range(nbc):
                            S.op("pe", lambda e: e.matmul(lp[:, bank * 512:(bank + 1) * 512], lhsT=ones1[:], rhs=Pc[:, bc, bank * 512:(bank + 1) * 512],
                                                          start=(bc == 0), stop=(bc == nbc - 1)), reads=[b_ones1, b_Pc[bc]], writes=[blp[bank]])
                    S.op("dve", lambda e: e.tensor_scalar(out=rl[:], in0=lp[:], scalar1=1e-30, scalar2=None, op0=ALU.add), reads=blp, writes=[b_rl])
                    S.op("dve", lambda e: e.reciprocal(out=rl[:], in_=rl[:]), reads=[b_rl], writes=[b_rl])
                    for bc in range(nbc):
                        S.op("dve", lambda e: e.tensor_tensor(out=Pn[:, bc, :], in0=Pc[:, bc, :], in1=rl[:], op=ALU.mult), reads=[b_Pc[bc], b_rl], writes=[b_Pn[bc]])
                    for wi in range(5):
                        nxt = win_front(wi + 1) if wi < 4 else None
                        rch = (kd - 4 + wi) % 8
                        pv_acc(winV[:, rch, g, :], [b_winV], Pst[cur], b_Pst[cur], wi == 0, wi == 4)
                        cur = nxt
                        if wi == 2:
                            ocp, bocp = ps_pair()
                            for bank in range(2):
                                for bc in range(nbc):
                                    S.op("pe", lambda e: e.matmul(ocp[:, bank * 512:(bank + 1) * 512], lhsT=vcVb[:, bc, g, :], rhs=Pn[:, bc, bank * 512:(bank + 1) * 512],
                                                                  start=(bc == 0), stop=(bc == nbc - 1)), reads=[b_vcVb, b_Pn[bc]], writes=[bocp[bank]])
                            S.op("act", lambda e: e.copy(out=osb[0][:], in_=ocp[0:64, :]), reads=bocp, writes=[b_osb[0]])
                            ip, bip = ps_next()
                            n_ = 0
                            for bc in range(nbc):
                                for hh in range(8):
                                    S.op("pe", lambda e: e.matmul(ip[:, 0:64], lhsT=Pn[:, bc, hh * 128:(hh + 1) * 128], rhs=mcs[:, bc, :], start=(n_ == 0), stop=(n_ == 8 * nbc - 1)),
                                         reads=[b_Pn[bc], b_mcs], writes=[bip])
                                    n_ += 1
                    S.op("dve", lambda e: e.tensor_tensor(out=impf[:], in0=ip[:, 0:64], in1=vat[tb][:, 0, :], op=ALU.mult), reads=[bip, b_vat[tb]], writes=[b_impf])
                    S.op("dve", lambda e: e.tensor_tensor(out=impf[:], in0=impf[:], in1=vat[tb][:, 1, :], op=ALU.add), reads=[b_impf, b_vat[tb]], writes=[b_impf])
                    S.op("dve", lambda e: e.max(out=m8[:, 0:8], in_=impf[:]), reads=[b_impf], writes=[b_m8])
                    S.op("dve", lambda e: e.match_replace(out=impt[:], in_to_replace=m8[:, 0:8], in_values=impf[:], imm_value=-2.0), reads=[b_impf, b_m8], writes=[b_impt])
                    S.op("dve", lambda e: e.max(out=m8[:, 8:16], in_=impt[:]), reads=[b_impt], writes=[b_m8])
                    S.op("dve", lambda e: e.tensor_scalar(out=thr[:], in0=m8[:, 15:16], scalar1=-0.5, scalar2=None, op0=ALU.max), reads=[b_m8], writes=[b_thr])
                    S.op("dve", lambda e: e.tensor_scalar(out=nbf[:], in0=impf[:], scalar1=thr[:, 0:1], scalar2=BIGV, op0=ALU.is_ge, op1=ALU.mult),
                         reads=[b_impf, b_thr], writes=[b_nbf])
                    S.op("dve", lambda e: e.tensor_scalar(out=nb[:], in0=nbf[:], scalar1=-BIGV, scalar2=None, op0=ALU.add), reads=[b_nbf], writes=[b_nb])
                    S.op("dve", lambda e: e.tensor_tensor(out=nbd[:].rearrange("p (s k) -> p s k", s=2), in0=nb[:, 2 * kd:2 * kd + 2].unsqueeze(2).broadcast_to([128, 2, 64]),
                                                           in1=tri[:].rearrange("p (s k) -> p s k", s=2), op=ALU.add), reads=[b_nb, b_tri], writes=[b_nbd])
                    S.op("dve", lambda e: e.reciprocal(out=rlx[:], in_=acc_ps[64:128, :]), reads=b_acc, writes=[b_rlx])
                    S.op("dve", lambda e: e.tensor_tensor(out=osb[2][:], in0=acc_ps[0:64, :], in1=rlx[:], op=ALU.mult), reads=b_acc + [b_rlx], writes=[b_osb[2]])
                    def slc_front(kc):
                        if kc == kd:
                            lb, lbb = nbd[:], [b_nbd]
                        else:
                            z = nbe_i[0] % 4
                            nbe_i[0] += 1
                            S.op("dve", lambda e: e.tensor_copy(out=nbe[z][:].rearrange("p (s k) -> p s k", s=2),
                                                                 in_=nb[:, 2 * kc:2 * kc + 2].unsqueeze(2).broadcast_to([128, 2, 64])), reads=[b_nb], writes=[b_nbe[z]])
                            lb, lbb = nbe[z][:], [b_nbe[z]]
                        pp, bpp = scores(lb, lbb, slcK[:, g, kc * 128:(kc + 1) * 128], [b_slcK], qrpad[pi], b_qrpad[pi])
                        pz = pst_i[0] % 2
                        pst_i[0] += 1
                        S.op("act", lambda e: e.activation(out=Pst[pz], in_=pp[:], func=AF.Exp, scale=0.125), reads=bpp, writes=[b_Pst[pz]])
                        return pz
                    cur = slc_front(0)
                    for kc in range(kd + 1):
                        nxt = slc_front(kc + 1) if kc < kd else None
                        pv_acc(slcV[:, kc, g, :], [b_slcV], Pst[cur], b_Pst[cur], kc == 0, kc == kd)
                        cur = nxt
                    S.op("dve", lambda e: e.reciprocal(out=rlx[:], in_=acc_ps[64:128, :]), reads=b_acc, writes=[b_rlx])
                    S.op("dve", lambda e: e.tensor_tensor(out=osb[1][:], in0=acc_ps[0:64, :], in1=rlx[:], op=ALU.mult), reads=b_acc + [b_rlx], writes=[b_osb[1]])
                    for br in range(3):
                        gp, bgp = ps_pair()
                        for hh in range(8):
                            f0 = 3 * (8 * g + hh) + br
                            S.op("pe", lambda e: e.matmul(gp[:, hh * 128:(hh + 1) * 128], lhsT=ident_bf[:, f0:f0 + 1].broadcast_to([128, 128]), rhs=gT[:, tsl], start=True, stop=True),
                                 reads=[b_identb, b_gT], writes=[bgp[hh // 4]])
                        S.op("dve", lambda e: e.tensor_tensor(out=osb[br][:], in0=osb[br][:], in1=gp[0:64, :], op=ALU.mult), reads=[b_osb[br]] + bgp, writes=[b_osb[br]])
                    S.op("dve", lambda e: e.tensor_tensor(out=osb[0][:], in0=osb[0][:], in1=osb[1][:], op=ALU.add), reads=[b_osb[0], b_osb[1]], writes=[b_osb[0]])
                    av = osb[0][:].rearrange("p (h t) -> p h t", h=8)
                    tv = osb[2][:].rearrange("p (h t) -> p h t", h=8)
                    S.op("dve", lambda e: e.tensor_tensor(out=oT[0:64, 4 * g:4 * g + 4, tsl], in0=av[:, 0:8:2, :], in1=tv[:, 0:8:2, :], op=ALU.add),
                         reads=[b_osb[0], b_osb[2]], writes=[b_oT])
                    S.op("dve", lambda e: e.tensor_tensor(out=oT[64:128, 4 * g:4 * g + 4, tsl], in0=av[:, 1:8:2, :], in1=tv[:, 1:8:2, :], op=ALU.add),
                         reads=[b_osb[0], b_osb[2]], writes=[b_oT])

            if smp:
                S.op("pool", lambda e: e.memset(oT[:], 0.0), writes=[b_oT])
                QS = sb2("QS", [128, 16], BF16); QRS = sb2("QRS", [128, 16], BF16); b_QS = Buf()
                S.op("pool", lambda e: e.memset(QS[:], 0.0), writes=[b_QS])
                S.op("pool", lambda e: e.memset(QRS[:], 0.0), writes=[b_QS])
                PcS = sb2("PcS", [128, 4, 16], BF16); b_PcS = Buf()
                PnG = sb2("PnG", [128, 4, 2], BF16); b_PnG = Buf()
                PnGf = sb2("PnGf", [128, 4, 2]); b_PnGf = Buf()
                rlS = sb2("rlS", [128, 16]); b_rlS = Buf()
                tpin = sb2("tpin", [128, 2, 128]); b_tpin = Buf()
                S.op("pool", lambda e: e.memset(tpin[:], 0.0), writes=[b_tpin])
                impS = sb2("impS", [128, 256]); impS2 = sb2("impS2", [128, 256]); b_impS = Buf()
                m8s = sb2("m8s", [128, 16]); thrs = sb2("thrs", [128, 1])
                nbS = sb2("nbS", [128, 256], BF16); b_nbS = Buf()
                gb = [sb2(f"gb{i}", [128, 256]) for i in range(4)]; b_gb = [Buf() for _ in range(4)]
                KTc = [sb2(f"KTc{i}", [128, 128], BF16) for i in range(3)]; b_KTc = [Buf() for _ in range(3)]
                wbuf = sb2("wbuf", [128, 4, 256]); b_wbuf = Buf()
                PsS = [sb2(f"PsS{i}", [128, 16], BF16) for i in range(3)]; b_PsS = [Buf() for _ in range(3)]
                GS = sb2("GS", [128, 48, 4]); b_GS = Buf()
                ocS = sb2("ocS", [64, 16]); osS = sb2("osS", [64, 2, 16]); rlS2 = sb2("rlS2", [64, 2, 16]); b_ocS = Buf()
                oS = sb2("oS", [64, 16]); tS = sb2("tS", [64, 16]); b_oS = Buf()
                gpt, bgpt = ps_next()
                for f0 in range(48):
                    S.op("pe", lambda e: e.matmul(gpt[:, f0 * 4:(f0 + 1) * 4], lhsT=ident_bf[:, f0:f0 + 1].broadcast_to([128, 128]), rhs=gT[:, 3:19:4], start=True, stop=True),
                         reads=[b_identb, b_gT], writes=[bgpt])
                S.op("dve", lambda e: e.tensor_copy(out=GS[:].rearrange("p f b -> p (f b)"), in_=gpt[:, 0:192]), reads=[bgpt], writes=[b_GS])
                gi_c = [0]
                kt_c = [0]
                ps_c = [0]
                ne_c = [0]
                for i in range(4):
                    col = 4 * i + 3
                    for g in range(2):
                        for par in range(2):
                            S.op("dve", lambda e: e.tensor_copy(out=QS[64 * g:64 * g + 64, 8 * g + par:8 * g + 8:2], in_=qT[64 * par:64 * par + 64, 4 * g:4 * g + 4, col]),
                                 reads=b_qT[4 * g:4 * g + 4], writes=[b_QS])
                            S.op("dve", lambda e: e.tensor_copy(out=QRS[64 * g:64 * g + 64, 8 * g + par:8 * g + 8:2], in_=qrT[64 * par:64 * par + 64, 4 * g:4 * g + 4, col]),
                                 reads=b_qrT[4 * g:4 * g + 4], writes=[b_QS])
                    for pc in range(4):
                        pp, bpp = ps_next()
                        if pc == 0:
                            S.op("pe", lambda e: e.matmul(pp[:, 0:16], lhsT=cb0, rhs=brhs, start=True, stop=False), reads=[b_stab], writes=[bpp])
                        S.op("pe", lambda e: e.matmul(pp[:, 0:16], lhsT=kcS[:, i, pc * 128:(pc + 1) * 128], rhs=QS[:], start=(pc != 0), stop=True),
                             reads=[b_kcS, b_QS], writes=[bpp])
                        S.op("act", lambda e: e.activation(out=PcS[:, pc, :], in_=pp[:, 0:16], func=AF.Exp, scale=0.125), reads=[bpp], writes=[b_PcS])
                    lp, blp = ps_next()
                    for pc in range(4):
                        S.op("pe", lambda e: e.matmul(lp[:, 0:16], lhsT=ones1[:], rhs=PcS[:, pc, :], start=(pc == 0), stop=(pc == 3)), reads=[b_ones1, b_PcS], writes=[blp])
                    S.op("dve", lambda e: e.tensor_scalar(out=rlS[:], in0=lp[:, 0:16], scalar1=1e-30, scalar2=None, op0=ALU.add), reads=[blp], writes=[b_rlS])
                    S.op("dve", lambda e: e.reciprocal(out=rlS[:], in_=rlS[:]), reads=[b_rlS], writes=[b_rlS])
                    for pc in range(4):
                        S.op("dve", lambda e: e.tensor_tensor(out=PcS[:, pc, :], in0=PcS[:, pc, :], in1=rlS[:], op=ALU.mult), reads=[b_PcS, b_rlS], writes=[b_PcS])
                    for pc in range(4):
                        S.op("pe", lambda e: e.matmul(acc_ps[:, 0:16], lhsT=vcS[:, i, pc, :], rhs=PcS[:, pc, :], start=(pc == 0), stop=(pc == 3)),
                             reads=[b_vcS, b_PcS], writes=[b_acc[0]])
                    S.op("dve", lambda e: e.tensor_copy(out=ocS[:, 0:8], in_=acc_ps[0:64, 0:8]), reads=[b_acc[0]], writes=[b_ocS])
                    S.op("dve", lambda e: e.tensor_copy(out=ocS[:, 8:16], in_=acc_ps[64:128, 8:16]), reads=[b_acc[0]], writes=[b_ocS])
                    for pc in range(4):
                        S.op("dve", lambda e: e.tensor_reduce(out=PnGf[:, pc, :], in_=PcS[:, pc, :].rearrange("p (g h) -> p g h", g=2), axis=mybir.AxisListType.X, op=ALU.add),
                             reads=[b_PcS], writes=[b_PnGf])
                    S.op("dve", lambda e: e.tensor_copy(out=PnG[:], in_=PnGf[:]), reads=[b_PnGf], writes=[b_PnG])
                    ipT, bipT = ps_next()
                    for sc in range(2):
                        for pc in range(4):
                            S.op("pe", lambda e: e.matmul(ipT[:, 2 * sc:2 * sc + 2], lhsT=mcsS[:, pc, sc, :], rhs=PnG[:, pc, :], start=(pc == 0), stop=(pc == 3)),
                                 reads=[b_stab, b_PnG], writes=[bipT])
                    S.op("dve", lambda e: e.tensor_copy(out=tpin[:, :, 0:2], in_=ipT[:, 0:4].rearrange("p (s g) -> p s g", s=2)), reads=[bipT], writes=[b_tpin])
                    tp, btp = ps_next()
                    for sc in range(2):
                        S.op("pe", lambda e: e.transpose(tp[:, sc * 128:(sc + 1) * 128], tpin[:, sc, :], ident[:]), reads=[b_tpin, b_ident], writes=[btp])
                    S.op("dve", lambda e: e.tensor_tensor(out=impS[:], in0=tp[:, 0:256], in1=tas[:, 0, :], op=ALU.mult), reads=[btp, b_stab], writes=[b_impS])
                    S.op("dve", lambda e: e.tensor_tensor(out=impS[:], in0=impS[:], in1=tas[:, 1, :], op=ALU.add), reads=[b_impS, b_stab], writes=[b_impS])
                    S.op("dve", lambda e: e.max(out=m8s[:, 0:8], in_=impS[:]), reads=[b_impS], writes=[b_impS])
                    S.op("dve", lambda e: e.match_replace(out=impS2[:], in_to_replace=m8s[:, 0:8], in_values=impS[:], imm_value=-2.0), reads=[b_impS], writes=[b_impS])
                    S.op("dve", lambda e: e.max(out=m8s[:, 8:16], in_=impS2[:]), reads=[b_impS], writes=[b_impS])
                    S.op("dve", lambda e: e.tensor_scalar(out=thrs[:], in0=m8s[:, 15:16], scalar1=-0.5, scalar2=None, op0=ALU.max), reads=[b_impS], writes=[b_impS])
                    S.op("dve", lambda e: e.tensor_scalar(out=impS2[:], in0=impS[:], scalar1=thrs[:, 0:1], scalar2=BIGV, op0=ALU.is_ge, op1=ALU.mult), reads=[b_impS], writes=[b_impS])
                    S.op("dve", lambda e: e.tensor_scalar(out=nbS[:], in0=impS2[:], scalar1=-BIGV, scalar2=None, op0=ALU.add), reads=[b_impS], writes=[b_nbS])

                    def s_front(bias_l, bias_b, kt, ktb):
                        pp, bpp = ps_next()
                        if bias_l is not None:
                            S.op("pe", lambda e: e.matmul(pp[:, 0:16], lhsT=bias_l, rhs=brhs, start=True, stop=False), reads=bias_b + [b_stab], writes=[bpp])
                        S.op("pe", lambda e: e.matmul(pp[:, 0:16], lhsT=kt, rhs=QRS[:], start=(bias_l is None), stop=True), reads=ktb + [b_QS], writes=[bpp])
                        z = ps_c[0] % 3
                        ps_c[0] += 1
                        S.op("act", lambda e: e.activation(out=PsS[z][:], in_=pp[:, 0:16], func=AF.Exp, scale=0.125), reads=[bpp], writes=[b_PsS[z]])
                        return z

                    def s_pv(z, v0, v1, vb, first, last):
                        for g, vv in enumerate((v0, v1)):
                            S.op("pe", lambda e: e.matmul(acc_ps[:, 512 + 8 * g:512 + 8 * g + 8], lhsT=vv, rhs=PsS[z][:, 8 * g:8 * g + 8],
                                                          start=(first and g == 0), stop=(last and g == 1)),
                                 reads=vb + [b_PsS[z]], writes=[b_acc[1]])

                    def gather(kc):
                        gz = gi_c[0] % 4
                        gi_c[0] += 1
                        S._deps("pool", [b_idxS], [b_gb[gz]])
                        tk = S.dma_ind(gb[gz][:], c_slc, idxS[:, i, kc:kc + 1])
                        S._mark(tk, [b_idxS], [b_gb[gz]])
                        return gz

                    def slc_front_s(kc, gz):
                        if kc == 64:
                            z = s_front(b64[:, i, :], [], KTn[:, 0, :], [b_KTn])
                            return (z, Vn[:, 0, 0, :], Vn[:, 0, 1, :], [b_Vn])
                        zz = ne_c[0] % 4
                        ne_c[0] += 1
                        tq, btq = ps_next()
                        S.op("pe", lambda e: e.transpose(tq[:, 0:128], gb[gz][:, 0:128], ident[:]), reads=[b_gb[gz], b_ident], writes=[btq])
                        kz = kt_c[0] % 3
                        kt_c[0] += 1
                        S.op("act", lambda e: e.copy(out=KTc[kz][:], in_=tq[:, 0:128]), reads=[btq], writes=[b_KTc[kz]])
                        vz = 2 + (kc % 4)
                        S.op("dve", lambda e: e.tensor_copy(out=slcV[:, vz, :, 0:64], in_=gb[gz][:, 128:256].rearrange("p (g d) -> p g d", g=2)),
                             reads=[b_gb[gz]], writes=[b_Vc[kc % 4]])
                        S.op("dve", lambda e: e.tensor_copy(out=nbe[zz][:].rearrange("p (s k) -> p s k", s=2),
                                                             in_=nbS[:, 2 * kc:2 * kc + 2].unsqueeze(2).broadcast_to([128, 2, 64])), reads=[b_nbS], writes=[b_nbe[zz]])
                        z = s_front(nbe[zz][:], [b_nbe[zz]], KTc[kz][:], [b_KTc[kz]])
                        return (z, slcV[:, vz, 0, :], slcV[:, vz, 1, :], [b_Vc[kc % 4]])

                    gq = [gather(kc) for kc in range(3)]
                    cur = slc_front_s(0, gq[0])
                    for kc in range(65):
                        if kc + 3 < 64:
                            gq.append(gather(kc + 3))
                        nxt = slc_front_s(kc + 1, gq[kc + 1] if kc + 1 < 64 else None) if kc < 64 else None
                        s_pv(cur[0], cur[1], cur[2], cur[3], kc == 0, kc == 64)
                        cur = nxt
                    S.op("dve", lambda e: e.reciprocal(out=rlS2[:, 0, :], in_=acc_ps[64:128, 512:528]), reads=[b_acc[1]], writes=[b_ocS])
                    S.op("dve", lambda e: e.tensor_tensor(out=osS[:, 0, :], in0=acc_ps[0:64, 512:528], in1=rlS2[:, 0, :], op=ALU.mult), reads=[b_acc[1], b_ocS], writes=[b_ocS])
                    S.dma("sp", wbuf[:], c_win[i].rearrange("(c p) f -> p c f", p=128), writes=[b_wbuf])

                    def win_front_s(wc):
                        if wc == 4:
                            z = s_front(b64[:, i, :], [], KTn[:, 1, :], [b_KTn])
                            return (z, Vn[:, 1, 0, :], Vn[:, 1, 1, :], [b_Vn])
                        tq, btq = ps_next()
                        S.op("pe", lambda e: e.transpose(tq[:, 0:128], wbuf[:, wc, 0:128], ident[:]), reads=[b_wbuf, b_ident], writes=[btq])
                        kz = kt_c[0] % 3
                        kt_c[0] += 1
                        S.op("act", lambda e: e.copy(out=KTc[kz][:], in_=tq[:, 0:128]), reads=[btq], writes=[b_KTc[kz]])
                        vz = 2 + wc
                        S.op("dve", lambda e: e.tensor_copy(out=slcV[:, vz, :, 0:64], in_=wbuf[:, wc, 128:256].rearrange("p (g d) -> p g d", g=2)),
                             reads=[b_wbuf], writes=[b_Vc[wc]])
                        z = s_front(wb0 if wc == 0 else None, [], KTc[kz][:], [b_KTc[kz]])
                        return (z, slcV[:, vz, 0, :], slcV[:, vz, 1, :], [b_Vc[wc]])
                    cur = win_front_s(0)
                    for wc in range(5):
                        nxt = win_front_s(wc + 1) if wc < 4 else None
                        s_pv(cur[0], cur[1], cur[2], cur[3], wc == 0, wc == 4)
                        cur = nxt
                    S.op("dve", lambda e: e.reciprocal(out=rlS2[:, 1, :], in_=acc_ps[64:128, 512:528]), reads=[b_acc[1]], writes=[b_ocS])
                    S.op("dve", lambda e: e.tensor_tensor(out=osS[:, 1, :], in0=acc_ps[0:64, 512:528], in1=rlS2[:, 1, :], op=ALU.mult), reads=[b_acc[1], b_ocS], writes=[b_ocS])
                    Gv = GS[0:64, :, i].rearrange("p (h r) -> p h r", r=3)
                    S.op("dve", lambda e: e.tensor_tensor(out=oS[:], in0=ocS[:], in1=Gv[:, :, 0], op=ALU.mult), reads=[b_ocS, b_GS], writes=[b_oS])
                    for br in (1, 2):
                        S.op("dve", lambda e: e.tensor_tensor(out=tS[:], in0=osS[:, br - 1, :], in1=Gv[:, :, br], op=ALU.mult), reads=[b_ocS, b_GS, b_oS], writes=[b_oS])
                        S.op("dve", lambda e: e.tensor_tensor(out=oS[:], in0=oS[:], in1=tS[:], op=ALU.add), reads=[b_oS], writes=[b_oS])
                    S.op("dve", lambda e: e.tensor_copy(out=oT[0:64, :, col], in_=oS[:, 0:16:2]), reads=[b_oS], writes=[b_oT])
                    S.op("dve", lambda e: e.tensor_copy(out=oT[64:128, :, col], in_=oS[:, 1:16:2]), reads=[b_oS], writes=[b_oT])
            for oc in range(8):
                if oc % 4 == 0:
                    pnl, bpn = st.get()
                pt, bp = ps_next()
                for k in range(8):
                    S.op("pe", lambda e: e.matmul(pt[:, 0:NWv[0]], lhsT=pnl[:, k, (oc % 4) * 128:(oc % 4 + 1) * 128], rhs=oT[:, k, 0:NWv[0]], start=(k == 0), stop=(k == 7)),
                         reads=[bpn, b_oT], writes=[bp])
                S.op("dve", lambda e: e.tensor_tensor(out=x[:, oc, :], in0=x[:, oc, :], in1=pt[:], op=ALU.add), reads=[b_x[oc], bp], writes=[b_x[oc]])

        def final_out(ti, es2, smp=False):
            uid[0] += 1
            yf = es2.enter_context(nc.sbuf_tensor(f"s{uid[0]}_yf", [128, 8, 512], F32)); b_yf = Buf()
            rmsnorm_f32(12, yf, b_yf)
            if smp:
                uid[0] += 1
                yo = es2.enter_context(nc.sbuf_tensor(f"s{uid[0]}_yo", [128, 8, 4], F32)); b_yo = Buf()
                S.op("dve", lambda e: e.tensor_copy(out=yo[:], in_=yf[:, :, 3:19:4]), reads=[b_yf], writes=[b_yo])
                S.dma("sp", o_s_y, yo[:], reads=[b_yo])
                return
            for blk in range(4):
                for kh in range(2):
                    pt, bp = ps_next()
                    for k4 in range(4):
                        k = kh * 4 + k4
                        S.op("pe", lambda e: e.transpose(pt[:, k4 * 128:(k4 + 1) * 128], yf[:, k, blk * 128:(blk + 1) * 128], ident[:]),
                             reads=[b_yf, b_ident], writes=[bp])
                    S.op("act", lambda e: e.copy(out=xt[:, blk, kh * 512:(kh + 1) * 512], in_=pt[:]), reads=[bp], writes=[b_xt])
            r0 = (ti - OWN0) * 512
            S.dma("sp", o_y[r0:r0 + 512, :].rearrange("(b p) f -> p b f", p=128), xt[:], reads=[b_xt])

        def rmsnorm_f32(gi, yf, b_yf):
            for k in range(8):
                S.op("act", lambda e: e.activation(out=sq[:, k, :], in_=x[:, k, :], func=AF.Square), reads=[b_x[k]], writes=[b_sq])
            pt, bp = ps_next()
            for k in range(8):
                S.op("pe", lambda e: e.matmul(pt[:], lhsT=ones_bf[:], rhs=sq[:, k, :], start=(k == 0), stop=(k == 7)),
                     reads=[b_sq, b_ones], writes=[bp])
            S.op("act", lambda e: e.activation(out=rstd[:], in_=pt[:], func=AF.Sqrt, bias=epsb[:, 0:1]), reads=[bp, b_eps], writes=[b_rstd])
            S.op("dve", lambda e: e.reciprocal(out=rstd[:], in_=rstd[:]), reads=[b_rstd], writes=[b_rstd])
            for k in range(8):
                S.op("dve", lambda e: e.scalar_tensor_tensor(out=yf[:, k, :], in0=x[:, k, :], scalar=gvec[:, gi, k:k + 1], in1=rstd[:],
                                                              op0=ALU.mult, op1=ALU.mult),
                     reads=[b_x[k], b_gvec, b_rstd], writes=[b_yf])


        def compress_round(sb2, cT, b_cT, P0, kdst, vdst, w2vs, pre=None):
            if pre is None:
                hid0 = sb2("hid0", [128, 32], BF16); b_hid0 = Buf()
                hidp = sb2("hidp", [128, 128], BF16); b_hidp = Buf()
            else:
                hid0, b_hid0, hidp, b_hidp = pre
            off = P0 % 128
            stc = Stream([(W["w1_00"], 0), (W["w1_01"], 0), (W["w1_10"], 0), (W["w1_11"], 0)], depth=2)
            for jv in range(2):
                for g in range(2):
                    pnl, bpn = stc.get()
                    pt, bp = ps_next()
                    for l_ in range(32):
                        S.op("pe", lambda e: e.matmul(pt[:, 0:32], lhsT=pnl[:, l_, :], rhs=cT[:, jv, l_:l_ + 497:16], start=(l_ == 0), stop=(l_ == 31)),
                             reads=[bpn, b_cT], writes=[bp])
                    if jv == 0:
                        S.op("act", lambda e: e.activation(out=hid0[:], in_=pt[:, 0:32], func=AF.Gelu, bias=cb1[:, 0:1]), reads=[bp, b_cb1], writes=[b_hid0])
                        pt2, bp2 = ps_next()
                        S.op("pe", lambda e: e.matmul(pt2[:, 0:32], lhsT=w2k[:], rhs=hid0[:], start=True, stop=True), reads=[b_w2, b_hid0], writes=[bp2])
                        kdst(g, pt2, bp2)
                    else:
                        S.op("dve", lambda e: e.memset(hidp[:], 0.0), writes=[b_hidp])
                        S.op("act", lambda e: e.activation(out=hidp[:, off:off + 32], in_=pt[:, 0:32], func=AF.Gelu, bias=cb1[:, 1:2]), reads=[bp, b_cb1], writes=[b_hidp])
                        pt2, bp2 = ps_next()
                        S.op("pe", lambda e: e.matmul(pt2[:, 0:128], lhsT=hidp[:], rhs=w2vs[g], start=True, stop=True), reads=[b_w2, b_hidp], writes=[bp2])
                        vdst(g, pt2, bp2)

        kvsw = Wt(nc, "kvsw", None, D, 256, 256)
        kvsw.name = "kvsw"
        W["kvsw"] = kvsw

        def kvswjob():
            tks = []
            with nc.allow_non_contiguous_dma(reason="one-time rope column swap"):
                for c, cb in enumerate((256, 512)):
                    for k in range(8):
                        srcv = w_kv[k * 128:(k + 1) * 128, cb:cb + 128].rearrange("p (g hh dd) -> p g hh dd", g=2, hh=2)
                        dstv = kvsw.dst[0].rearrange("p (k c g hh dd) -> p k c g hh dd", k=8, c=2, g=2, hh=2)
                        for hh in range(2):
                            tks.append(S.dma("pool", dstv[:, k, c, :, hh, :], srcv[:, :, 1 - hh, :]))
            kvsw.buf.w = tks
        jobs["kvsw"] = kvswjob

        def kv_gen(ti, es2, smp=False):
            def sb2(name, shape, dt=F32):
                uid[0] += 1
                return es2.enter_context(nc.sbuf_tensor(f"s{uid[0]}_{name}", list(shape), dt))
            kvf = sb2("kvf", [128, 6, 512]); b_kvf = [Buf() for _ in range(6)]
            rc_t = sb2("rc_t", [128, 512]); rs_t = sb2("rs_t", [128, 512]); b_rope = Buf()
            rt1 = sb2("rt1", [128, 512]); b_rt1 = Buf()
            S.dma("sp", rc_t[:], ropec_s if smp else ropec[:, ti * 512:(ti + 1) * 512], writes=[b_rope])
            S.dma("sp", rs_t[:], ropes_s if smp else ropes[:, ti * 512:(ti + 1) * 512], writes=[b_rope])
            rmsnorm(13, h, b_h)
            st = Stream([(W["kv"], 0), (W["kv"], 1), (W["kv"], 2), (W["kvsw"], 0)], depth=3)
            pk = [st.get(), st.get(), st.get()]
            psw, bpsw = st.get()
            for c in range(6):
                pnl, bpn = pk[c // 2]
                pt, bp = proj_chunk(pnl, bpn, (c % 2) * 128, 128, h, b_h, 8)
                if c in (2, 4):
                    pt2, bp2 = proj_chunk(psw, bpsw, (c // 2 - 1) * 128, 128, h, b_h, 8)
                    S.op("dve", lambda e: e.tensor_tensor(out=kvf[:, c, :], in0=pt[:], in1=rc_t[:], op=ALU.mult), reads=[bp, b_rope], writes=[b_kvf[c]])
                    S.op("dve", lambda e: e.tensor_tensor(out=rt1[:], in0=pt2[:], in1=rs_t[:], op=ALU.mult), reads=[bp2, b_rope], writes=[b_rt1])
                    S.op("dve", lambda e: e.tensor_tensor(out=kvf[:, c, :], in0=kvf[:, c, :], in1=rt1[:], op=ALU.add), reads=[b_kvf[c], b_rt1], writes=[b_kvf[c]])
                else:
                    S.op("act", lambda e: e.copy(out=kvf[:, c, :], in_=pt[:]), reads=[bp], writes=[b_kvf[c]])
            if smp:
                S.op("dve", lambda e: e.memset(kvf[:, :, 16:128], 0.0), reads=[], writes=b_kvf)
                kvo = sb2("kvo", [128, 6, 4]); b_kvo = Buf()
                S.op("dve", lambda e: e.tensor_copy(out=kvo[:], in_=kvf[:, :, 3:19:4]), reads=b_kvf, writes=[b_kvo])
                S.dma("sp", o_s_kv, kvo[:], reads=[b_kvo])
                S.op("dve", lambda e: e.tensor_copy(out=KTn[:, 0, :], in_=kvf[:, 2, 0:128]), reads=[b_kvf[2]], writes=[b_KTn])
                S.op("dve", lambda e: e.tensor_copy(out=KTn[:, 1, :], in_=kvf[:, 4, 0:128]), reads=[b_kvf[4]], writes=[b_KTn])
                for ci, c in enumerate((3, 5)):
                    pt, bp = ps_next()
                    S.op("pe", lambda e: e.transpose(pt[:, 0:128], kvf[:, c, 0:128], ident[:]), reads=[b_kvf[c], b_ident], writes=[bp])
                    S.op("dve", lambda e: e.tensor_copy(out=Vn[:, ci, :, 0:64], in_=pt[:, 0:128].rearrange("p (g d) -> p g d", g=2)), reads=[bp], writes=[b_Vn])
                return
            if ti >= OWN0 or DO_B:
                for blk in range(4):
                    for hf in range(2):
                        pt, bp = ps_next()
                        ncn = 4 if hf == 0 else 2
                        for c4 in range(ncn):
                            c = hf * 4 + c4
                            S.op("pe", lambda e: e.transpose(pt[:, c4 * 128:(c4 + 1) * 128], kvf[:, c, blk * 128:(blk + 1) * 128], ident[:]),
                                 reads=[b_kvf[c], b_ident], writes=[bp])
                        S.op("act", lambda e: e.copy(out=xt[:, blk, hf * 512:hf * 512 + ncn * 128], in_=pt[:, 0:ncn * 128]), reads=[bp], writes=[b_xt])
                        if DO_B and DBG >= 2 and not cfg.get("NO_V"):
                            kch = 4 * ti + blk
                            if hf == 0:
                                S.op("dve", lambda e: e.tensor_copy(out=slcV[:, kch, :, 0:64], in_=xt[:, blk, 384:512].rearrange("p (g d) -> p g d", g=2)),
                                     reads=[b_xt], writes=[b_slcV])
                            else:
                                S.op("dve", lambda e: e.tensor_copy(out=winV[:, kch % 8, :, 0:64], in_=xt[:, blk, 640:768].rearrange("p (g d) -> p g d", g=2)),
                                     reads=[b_xt], writes=[b_winV])
            if DO_B and DBG >= 2 and not cfg.get("NO_K"):
                c0 = ti * 512
                r0w = (ti % 2) * 512
                for g in range(2):
                    for hfp in range(2):
                        S.op("dve", lambda e: e.tensor_copy(out=slcK[hfp * 64:(hfp + 1) * 64, g, c0:c0 + 512], in_=kvf[g * 64:(g + 1) * 64, 2, :]),
                             reads=[b_kvf[2]], writes=[b_slcK])
                        S.op("dve", lambda e: e.tensor_copy(out=winK[hfp * 64:(hfp + 1) * 64, g, r0w:r0w + 512], in_=kvf[g * 64:(g + 1) * 64, 4, :]),
                             reads=[b_kvf[4]], writes=[b_winK])
            if DO_B and DBG >= 3:
                P0 = 32 * ti
                S.op("pool", lambda e: e.tensor_copy(out=cmpT[:, :, 0:16], in_=cmpT[:, :, 512:528]), reads=[b_cmpT], writes=[b_cmpT])
                for jv in range(2):
                    S.op("pool", lambda e: e.tensor_copy(out=cmpT[:, jv, 16:528], in_=kvf[:, jv, :]), reads=[b_kvf[jv], b_cmpT], writes=[b_cmpT])
                def kdst(g, pt2, bp2):
                    S.op("act", lambda e: e.copy(out=kcK[:, g, P0:P0 + 32], in_=pt2[:, 0:32]), reads=[bp2], writes=[b_kcK])

                def vdst(g, pt2, bp2):
                    pch = P0 // 128
                    S.op("dve", lambda e: e.tensor_tensor(out=vcV[:, pch, g, :], in0=vcV[:, pch, g, :], in1=pt2[:, 0:128], op=ALU.add),
                         reads=[bp2, b_vcV], writes=[b_vcV])
                    S.op("dve", lambda e: e.tensor_copy(out=vcVb[:, pch, g, :], in_=vcV[:, pch, g, :]), reads=[b_vcV], writes=[b_vcVb])
                compress_round(sb2, cmpT, b_cmpT, P0, kdst, vdst, [w2v[:], w2v[:]])
            if ti >= OWN0:
                r0 = (ti - OWN0) * 512
                for oi, od in enumerate((o_cmp, o_slc, o_win)):
                    S.dma("sp", od[r0:r0 + 512, :].rearrange("(b p) f -> p b f", p=128), xt[:, :, oi * 256:(oi + 1) * 256], reads=[b_xt])

        for l_ in range(NL_A):
            cast_order.extend([f"rgin{l_}", f"band{l_}_0", f"band{l_}_1", f"rgout{l_}", f"up{l_}", f"down{l_}", f"plein{l_}", f"pleg{l_}"])
        cast_order.extend(["kv", "kvsw"])
        if DO_B:
            cast_order.extend(["w1_00", "w1_01", "w1_10", "w1_11"])
            for j_ in range(2):
                cast_order.extend([f"q{j_}", f"qsw{j_}", f"qg{j_}", f"wo{j_}", f"up{2 + j_}", f"down{2 + j_}", f"plein{2 + j_}", f"pleg{2 + j_}"])
        for ti in range(NT):
            transpose_in(xs[ti * 512:(ti + 1) * 512, :], 1024, x, b_x)
            for l in range(NL_A):
                with ExitStack() as es2:
                    rg_layer(l, ti, es2)
                    S.barrier()
                with ExitStack() as es2:
                    ffn_ple(l, ti, es2)
                    S.barrier()
            with ExitStack() as es2:
                kv_gen(ti, es2)
                S.barrier()
            if DO_B and ti >= OWN0 - 1:
                for l in (2, 3):
                    if DBG >= 4:
                        with ExitStack() as es2:
                            attn_layer(l, ti, es2, [3] if ti == OWN0 - 1 else [0, 1, 2, 3])
                            S.barrier()
                    with ExitStack() as es2:
                        ffn_ple(l, ti, es2)
                        S.barrier()
            if ti >= OWN0:
                with ExitStack() as es2:
                    final_out(ti, es2)
                    S.barrier()

        with nc.allow_non_contiguous_dma(reason="small state outputs"):
            for l in range(2):
                for k in range(3):
                    S.dma("sp", o_rgc[l, k].rearrange("(c p) -> p c", p=128), rg_hist[:, l, :, k], reads=b_rghist[l])
                S.dma("sp", o_rgh[l].rearrange("(c p) -> p c", p=128), rg_hc[:, l, :], reads=b_rghc[l])
            for l in range(4):
                for k in range(2):
                    S.dma("sp", o_ffc[l, k].rearrange("(c p) -> p c", p=128), ff_hist[:, l, :, k], reads=b_ffhist[l])

        if SMP:
            S.barrier()
            with ExitStack() as es3:
                def sb3(name, shape, dt=F32):
                    uid[0] += 1
                    return es3.enter_context(nc.sbuf_tensor(f"s{uid[0]}_{name}", list(shape), dt))
                assert NTOK >= 2048
                flatK = slcK[:].rearrange("p g t -> p (g t)")
                flatV = slcV[:].rearrange("p c g f -> p (c g f)")
                kcS = flatK[:, 0:2048].rearrange("p (b t) -> p b t", b=4)
                vcS = flatK[:, 2048:4096].rearrange("p (b c f) -> p b c f", b=4, c=4)
                b_kcS = Buf(); b_vcS = Buf(); b_KTn = Buf()
                Vn = slcV[:, 0:2, :, :]; b_Vn = Buf()
                b_Vc = [Buf() for _ in range(4)]
                vo = [1536]

                def vview(n):
                    a = vo[0]
                    vo[0] += n
                    return flatV[:, a:a + n]
                mcsS = vview(1024).rearrange("p (a b c) -> p a b c", a=4, b=2)
                b64 = vview(512).rearrange("p (a c) -> p a c", a=4)
                KTn = vview(256).rearrange("p (a t) -> p a t", a=2)
                cb0 = vview(128); wb0 = vview(128); w2v1 = vview(128); brhs = vview(16)
                xtf = xt[:].rearrange("p b f -> p (b f)")
                tas = xtf[:, 0:512].rearrange("p (a s) -> p a s", a=2)
                b_stab = Buf()
                S.op("pool", lambda e: e.memset(w2v1, 0.0), writes=[b_stab])
                S.dma("pool", w2v1[:, 64:128], cmp_w2[1], writes=[b_stab])
                S.dma("pool", cb0, t_cb0, writes=[b_stab]); S.dma("pool", brhs, t_brhs, writes=[b_stab])
                S.dma("pool", mcsS, t_mcs_s, writes=[b_stab]); S.dma("sp", tas, t_as, writes=[b_stab])
                S.dma("pool", b64, t_b64, writes=[b_stab]); S.dma("pool", wb0, t_wb0, writes=[b_stab])
                pgi = xtf[:, 512:768].bitcast(mybir.dt.int32); pgf = xtf[:, 768:1024]; pidf = sb3("pidf", [128, 1])
                idxS_t = xtf[:, 1024:1280].bitcast(mybir.dt.int32); b_idxS = Buf()
                idxS = idxS_t.rearrange("p (b g) -> p b g", b=4)
                S.dma("sp", pgi, pg_s.rearrange("b g -> (b g)").unsqueeze(0).broadcast_to([128, 256]), writes=[b_idxS])
                S.op("pool", lambda e: e.iota(pidf[:], pattern=[[0, 1]], base=0, channel_multiplier=1, allow_small_or_imprecise_dtypes=True), writes=[b_idxS])
                S.op("dve", lambda e: e.tensor_copy(out=pgf, in_=pgi), reads=[b_idxS], writes=[b_idxS])
                S.op("dve", lambda e: e.tensor_scalar(out=pgf, in0=pgf, scalar1=128.0, scalar2=pidf[:, 0:1], op0=ALU.mult, op1=ALU.add), reads=[b_idxS], writes=[b_idxS])
                S.op("dve", lambda e: e.tensor_copy(out=idxS_t, in_=pgf), reads=[b_idxS], writes=[b_idxS])
                cT = cmpT; b_cT = b_cmpT
                hid_pre = (sb3("hid0s", [128, 32], BF16), Buf(), sb3("hidps", [128, 128], BF16), Buf())
                vcSf = xtf[:, 1280:1792].rearrange("p (c f) -> p c f", c=4); b_vcSf = Buf()
                gbc = [xtf[:, 1792 + 256 * i:2048 + 256 * i] for i in range(4)]; b_gbc = [Buf() for _ in range(4)]
                for i in range(4):
                    S.op("pool", lambda e: e.memset(cT[:], 0.0), writes=[b_cT])
                    S.op("pool", lambda e: e.memset(vcSf, 0.0), writes=[b_vcSf])
                    for r in range(16):
                        for jq in range(4):
                            S._deps("pool", [b_idxS], [b_gbc[jq]])
                            tk = S.dma_ind(gbc[jq], c_cmp, idxS[:, i, 4 * r + jq:4 * r + jq + 1])
                            S._mark(tk, [b_idxS], [b_gbc[jq]])
                            tq, btq = ps_next()
                            for jv in range(2):
                                S.op("pe", lambda e: e.transpose(tq[:, jv * 128:(jv + 1) * 128], gbc[jq][:, jv * 128:(jv + 1) * 128], ident[:]),
                                     reads=[b_gbc[jq], b_ident], writes=[btq])
                            S.op("act", lambda e: e.copy(out=cT[:, :, 16 + 128 * jq:16 + 128 * (jq + 1)], in_=tq[:, 0:256].rearrange("p (j t) -> p j t", j=2)),
                                 reads=[btq], writes=[b_cT])
                        P0 = 32 * r

                        def kdst(g, pt2, bp2):
                            S.op("act", lambda e: e.copy(out=kcS[64 * g:64 * g + 64, i, P0:P0 + 32], in_=pt2[64 * g:64 * g + 64, 0:32]), reads=[bp2], writes=[b_kcS])

                        def vdst(g, pt2, bp2):
                            pch = P0 // 128
                            S.op("dve", lambda e: e.tensor_tensor(out=vcSf[:, pch, :], in0=vcSf[:, pch, :], in1=pt2[:, 0:128], op=ALU.add),
                                 reads=[bp2, b_vcSf], writes=[b_vcSf])
                        compress_round(None, cT, b_cT, P0, kdst, vdst, [w2v[:], w2v1], pre=hid_pre)
                        S.op("pool", lambda e: e.tensor_copy(out=cT[:, :, 0:16], in_=cT[:, :, 512:528]), reads=[b_cT], writes=[b_cT])
                    S.op("dve", lambda e: e.tensor_copy(out=vcS[:, i, :, :], in_=vcSf), reads=[b_vcSf], writes=[b_vcS])
                S.dma("sp", x[:], xs_s, writes=b_x)
                NWv[0] = 16
                for l in range(2):
                    with ExitStack() as es2:
                        rg_layer(l, -1, es2, smp=True)
                        S.barrier()
                    with ExitStack() as es2:
                        ffn_ple(l, -1, es2, smp=True)
                        S.barrier()
                with ExitStack() as es2:
                    kv_gen(-1, es2, smp=True)
                    S.barrier()
                for l in (2, 3):
                    with ExitStack() as es2:
                        attn_layer(l, -1, es2, [], smp=True)
                        S.barrier()
                    with ExitStack() as es2:
                        ffn_ple(l, -1, es2, smp=True)
                        S.barrier()
                with ExitStack() as es2:
                    final_out(-1, es2, smp=True)
                    S.barrier()
                with nc.allow_non_contiguous_dma(reason="state passthrough"):
                    for i in range(4):
                        S.dma("sp", o_s_win[i], c_win[i, 1:512, :])
                    S.dma("sp", o_s_rgc_old, st_rgc[:, :, 1:3, :])
                    S.dma("sp", o_s_ffc_old, st_ffc[:, :, 1, :])
        S.finish()
    return nc

_WNAMES = ['g_mix', 'g_ffn', 'g_ple', 'g_final', 'g_kv', 'rg_w_in', 'rg_conv_w', 'rg_conv_b', 'rg_w_a', 'rg_w_x', 'rg_lambda',
           'rg_w_out', 'w_kv', 'ffn_w_up', 'ffn_conv_w', 'ffn_conv_b', 'ffn_w_down', 'ple_w_in', 'ple_w_gate',
           'attn_w_qg', 'attn_w_o', 'cmp_pos', 'cmp_w1', 'cmp_b1', 'cmp_w2']


def make_tables(half, NT, OWN0):
    BIG = 30000.0
    NTOK = NT * 512
    pre = OWN0 * 512
    nown = NTOK - pre + 128
    nqb = nown // 128
    i = np.arange(nown) - 128
    tau = pre + i
    hv = np.where(i < 0, 1, half)[:, None]
    P = np.arange(256)
    c = P - 1
    vc = (P >= 1)[None, :] & ((hv == 1) | (16 * c >= pre)[None, :])
    ok = vc & ((16 * c + 31)[None, :] <= tau[:, None])
    t_cb = np.where(ok, 0.0, -BIG).astype(np.float32).reshape(nqb, 128, 256)
    kk = np.arange(640)
    tau0 = pre + 128 * (i // 128)
    kap = tau0[:, None] - 512 + kk[None, :]
    dist = tau[:, None] - kap
    okw = (dist >= 0) & (dist < 512) & (kap >= 0) & ((hv == 1) | (kap >= pre))
    t_wb = np.where(okw, 0.0, -BIG).astype(np.float32).reshape(nqb, 128, 640)
    sblk = np.arange(64)
    vb = ((hv == 1) | (64 * sblk >= pre)[None, :]) & (64 * sblk < NTOK)[None, :]
    V = (vb & ((64 * sblk)[None, :] <= tau[:, None])).astype(np.float32)
    blk0 = np.where(hv[:, 0] == 1, 0, pre // 64)
    cur = tau // 64
    forced = (sblk[None, :] == blk0[:, None]) | (sblk[None, :] == cur[:, None]) | (sblk[None, :] == (cur - 1)[:, None])
    A = 100.0 * forced.astype(np.float32) * V + (V - 1.0)
    t_va = np.stack([V, A], axis=1).astype(np.float32).reshape(nqb, 128, 2, 64)
    Pm = np.arange(256)
    c0 = 16 * (Pm - 1)
    s0 = 64 * sblk
    m = ((c0[:, None] < s0[None, :] + 64) & (c0[:, None] + 32 > s0[None, :]) & (Pm[:, None] >= 1)).astype(np.float32)
    t_mcs = np.ascontiguousarray(m.reshape(2, 128, 64).transpose(1, 0, 2))
    t_tri = np.where(np.arange(128)[None, :] > np.arange(128)[:, None], -BIG, 0.0).astype(np.float32)
    return dict(t_cb=t_cb, t_wb=t_wb, t_va=t_va, t_mcs=t_mcs, t_tri=t_tri)


def core_inputs(inp, b, half, NT=8, OWN0=4):
    NTOK = NT * 512
    pre = OWN0 * 512
    own = NTOK - pre
    if half == 1:
        xs = inp['x_prompt'][b, :NTOK]
        ps = inp['p_prompt'][:, b, :NTOK]
        pos = np.arange(NTOK)
    else:
        xs = np.concatenate([inp['x_prompt'][b, :pre], inp['x_prompt'][b, :own]], 0)
        ps = np.concatenate([inp['p_prompt'][:, b, :pre], inp['p_prompt'][:, b, :own]], 1)
        pos = np.concatenate([np.arange(pre), np.arange(own)])
    d = np.arange(128) % 64
    inv = (10000.0 ** (-(d % 32).astype(np.float32) / 32)).astype(np.float32)
    ang = pos[None, :].astype(np.float32) * inv[:, None]
    sgn = np.where(d < 32, -1.0, 1.0)[:, None]
    m = dict(xs=np.ascontiguousarray(xs, dtype=np.float32), ps=np.ascontiguousarray(ps, dtype=np.float32),
             flag=np.tile(np.array([[half, 1 - half]], np.float32), (128, 1)),
             ident=np.eye(128, dtype=np.float32),
             ropec=np.cos(ang).astype(np.float32), ropes=(np.sin(ang) * sgn).astype(np.float32))
    for k in _WNAMES:
        m[k] = np.ascontiguousarray(inp[k], dtype=np.float32)
    m['rg_b_a'] = np.ascontiguousarray(inp['rg_b_a'], dtype=np.float32).reshape(2, 1280)
    m['rg_b_x'] = np.ascontiguousarray(inp['rg_b_x'], dtype=np.float32).reshape(2, 1280)
    m.update(make_tables(half, NT, OWN0))
    return m


def sample_inputs(inp, c):
    BIG = 30000.0
    b0 = 4 * c
    m = {}
    xs = np.zeros((128, 8, 512), np.float32)
    ps = np.zeros((4, 128, 2, 512), np.float32)
    for i in range(4):
        xs[:, :, 4 * i + 3] = inp['x_sample'][b0 + i, 0].reshape(8, 128).T
        for l in range(4):
            ps[l, :, :, 4 * i + 3] = inp['p_sample'][l, b0 + i, 0].reshape(2, 128).T
    m['xs_s'] = xs
    m['ps_s'] = ps
    rgc = inp['state_rg_conv'][:, b0:b0 + 4]
    m['s_rgh'] = np.ascontiguousarray(rgc.reshape(2, 4, 3, 10, 128).transpose(4, 0, 3, 1, 2), dtype=np.float32)
    rgh = inp['state_rg_h'][:, b0:b0 + 4]
    m['s_h0'] = np.ascontiguousarray(rgh.reshape(2, 4, 10, 128).transpose(3, 0, 2, 1), dtype=np.float32)
    ffc = inp['state_ffn_conv'][:, b0:b0 + 4]
    m['s_ffh'] = np.ascontiguousarray(ffc.reshape(4, 4, 2, 48, 128).transpose(4, 0, 3, 1, 2), dtype=np.float32)
    d = np.arange(128) % 64
    inv = (10000.0 ** (-(d % 32).astype(np.float32) / 32)).astype(np.float32)
    ang = np.full((1, 512), 8192.0, np.float32) * inv[:, None]
    sgn = np.where(d < 32, -1.0, 1.0)[:, None]
    m['ropec_s'] = np.cos(ang).astype(np.float32)
    m['ropes_s'] = (np.sin(ang) * sgn).astype(np.float32)
    m['pg_s'] = np.ascontiguousarray(inp['page_table'][b0:b0 + 4], dtype=np.int32)
    nphys = inp['cache_cmp_kv'].shape[0]
    m['cache_cmp_kv'] = np.ascontiguousarray(inp['cache_cmp_kv'], dtype=np.float32).reshape(nphys * 128, 256)
    m['cache_slc_kv'] = np.ascontiguousarray(inp['cache_slc_kv'], dtype=np.float32).reshape(nphys * 128, 256)
    m['c_win'] = np.ascontiguousarray(inp['cache_win_kv'][b0:b0 + 4], dtype=np.float32).reshape(4, 512, 256)
    m['st_rgc'] = np.ascontiguousarray(rgc, dtype=np.float32)
    m['st_ffc'] = np.ascontiguousarray(ffc, dtype=np.float32)
    P = np.arange(512)
    c0 = 16 * (P - 1)
    sb = np.arange(256)
    mm = ((c0[:, None] < 64 * sb[None, :] + 64) & (c0[:, None] + 32 > 64 * sb[None, :]) & (P[:, None] >= 1) & (sb[None, :] < 129)).astype(np.float32)
    m['t_mcs_s'] = np.ascontiguousarray(mm.reshape(4, 128, 2, 128).transpose(1, 0, 2, 3))
    V = (sb < 129).astype(np.float32)
    forced = ((sb == 0) | (sb == 128) | (sb == 127)).astype(np.float32)
    A = 100.0 * forced * V + (V - 1.0)
    m['t_as'] = np.ascontiguousarray(np.tile(np.stack([V, A])[None], (128, 1, 1)), dtype=np.float32)
    br = np.zeros((128, 16), np.float32); br[0, 0:8] = 1.0; br[1, 8:16] = 1.0
    m['t_brhs'] = br
    b64 = np.zeros((128, 4, 128), np.float32)
    for i in range(4):
        b64[0:2, i, :] = -BIG
        b64[0:2, i, 4 * i + 3] = 0.0
    m['t_b64'] = b64
    wb0 = np.zeros((128, 128), np.float32); wb0[0:2, 0] = -BIG
    m['t_wb0'] = wb0
    m['t_cb0'] = wb0.copy()
    return m


_BUILD_CACHE = {}


def kernel(**inputs):
    inp = {k: np.asarray(v) for k, v in inputs.items()}
    nphys = inp['cache_cmp_kv'].shape[0]
    nc = build(dict(NT=8, OWN0=4, NL_A=2, DO_B=True, SMP=True, NPHYS=nphys))
    in_maps = []
    for c in range(8):
        m = core_inputs(inp, c // 2, c % 2)
        m.update(sample_inputs(inp, c))
        in_maps.append(m)
    res = run_bass_kernel_spmd(nc, in_maps, core_ids=list(range(8)))
    R = res.results
    B, T = 4, 4096
    y_prompt = np.zeros((B, T, D), np.float32)
    cmp_p = np.zeros((B, T, 2, 2, 64), np.float32)
    slc_p = np.zeros((B, T, 2, 2, 64), np.float32)
    win_p = np.zeros((B, 512, 2, 2, 64), np.float32)
    rgc_p = np.zeros((2, B, 3, DR), np.float32)
    rgh_p = np.zeros((2, B, DR), np.float32)
    ffc_p = np.zeros((4, B, 2, 2 * DFF), np.float32)
    DB = 32
    y_sample = np.zeros((DB, 1, D), np.float32)
    cmp_s = np.zeros((DB, 1, 2, 2, 64), np.float32)
    slc_s = np.zeros((DB, 1, 2, 2, 64), np.float32)
    win_s = np.zeros((DB, 512, 2, 2, 64), np.float32)
    rgc_s = np.zeros((2, DB, 3, DR), np.float32)
    rgh_s = np.zeros((2, DB, DR), np.float32)
    ffc_s = np.zeros((4, DB, 2, 2 * DFF), np.float32)
    for c in range(8):
        b, half = c // 2, c % 2
        sl = slice(half * 2048, half * 2048 + 2048)
        y_prompt[b, sl] = R[c]['o_y']
        cmp_p[b, sl] = R[c]['o_cmp'].reshape(2048, 2, 2, 64)
        slc_p[b, sl] = R[c]['o_slc'].reshape(2048, 2, 2, 64)
        if half == 1:
            win_p[b] = R[c]['o_win'][-512:].reshape(512, 2, 2, 64)
            rgc_p[:, b] = R[c]['o_rgc']
            rgh_p[:, b] = R[c]['o_rgh']
            ffc_p[:, b] = R[c]['o_ffc']
        assemble_sample(R[c], c, y_sample, cmp_s, slc_s, win_s, rgc_s, rgh_s, ffc_s)
    return (y_prompt, y_sample, cmp_p, cmp_s, slc_p, slc_s, win_p, win_s, rgc_p, rgc_s, rgh_p, rgh_s, ffc_p, ffc_s)


def assemble_sample(r, c, y_sample, cmp_s, slc_s, win_s, rgc_s, rgh_s, ffc_s):
    b0 = 4 * c
    for i in range(4):
        b = b0 + i
        y_sample[b, 0] = r['o_s_y'][:, :, i].T.reshape(1024)
        kv = r['o_s_kv'][:, :, i]
        cmp_s[b, 0] = kv[:, 0:2].T.reshape(2, 2, 64)
        slc_s[b, 0] = kv[:, 2:4].T.reshape(2, 2, 64)
        win_s[b, 0:511] = r['o_s_win'][i].reshape(511, 2, 2, 64)
        win_s[b, 511] = kv[:, 4:6].T.reshape(2, 2, 64)
        for l in range(2):
            rgc_s[l, b, 0:2] = r['o_s_rgc_old'][l, i]
            rgc_s[l, b, 2] = r['o_s_rgc_new'][:, l, :, i].T.reshape(1280)
            rgh_s[l, b] = r['o_s_rgh'][:, l, :, i].T.reshape(1280)
        for l in range(4):
            ffc_s[l, b, 0] = r['o_s_ffc_old'][l, i]
            ffc_s[l, b, 1] = r['o_s_ffc_new'][:, l, :, i].T.reshape(6144)
```
